# Optimizing a Trainium2 kernel written in Bass

```python
import math
import jax, jax.numpy as jnp
from jax import lax
import numpy as np

D_MODEL = 1024
BATCH = 1
SEQ = 16384
DEPTH = 4
DEC_BATCH = 4
DEC_SEQ = 8192
PAST_LEN = 128

N_Q_HEADS = 8
N_KV_HEADS = 2
HEAD_DIM = 64
D_ATTN = N_Q_HEADS * HEAD_DIM
D_KV = N_KV_HEADS * HEAD_DIM
D_QKV = D_ATTN + 2 * D_KV
WINDOW = 128
BLOCK = 128
ROPE_THETA = 500000.0
ROPE_DIM = HEAD_DIM // 4
D_HYENA = 512
HYENA_GROUPS = 8
SHORT_CONV = 3
FILTER_EMB = 33
FILTER_HID = 64
FAST_DECAY_PCT = 0.3
SLOW_DECAY_PCT = 1.5
DECAY_TARGET = 1e-2
MAX_DECAY = math.log(DECAY_TARGET) / FAST_DECAY_PCT
MIN_DECAY = math.log(DECAY_TARGET) / SLOW_DECAY_PCT
D_MIX = D_ATTN + D_HYENA
D_IN = D_QKV + 3 * D_HYENA
D_FF = 2816
FFN_CONV = 3
D_PLE = 256
EPS = 1e-6

kernel_name = 'hybrid_hyena_swa_encoder'


def _rmsnorm(x, g):
    xf = x.astype(jnp.float32)
    y = xf * lax.rsqrt(jnp.mean(xf * xf, axis=-1, keepdims=True) + EPS)
    return (y * g.astype(jnp.float32)).astype(x.dtype)


def _dwconv3(x, w, b):
    xp = jnp.pad(x, ((0, 0), (1, 1), (0, 0)))
    return xp[:, :-2] * w[0] + xp[:, 1:-1] * w[1] + xp[:, 2:] * w[2] + b


def _rope_tables(L):
    inv = ROPE_THETA ** (-jnp.arange(0, ROPE_DIM, 2, dtype=jnp.float32) / ROPE_DIM)
    ang = jnp.arange(L, dtype=jnp.float32)[:, None] * inv[None]
    return jnp.cos(ang), jnp.sin(ang)


def _partial_rope(x, cos, sin):
    half = ROPE_DIM // 2
    xf = x.astype(jnp.float32)
    x1 = xf[..., :half]
    x2 = xf[..., half:ROPE_DIM]
    c = cos[None, :, None, :]
    s = sin[None, :, None, :]
    out = jnp.concatenate([x1 * c - x2 * s, x2 * c + x1 * s, xf[..., ROPE_DIM:]], axis=-1)
    return out.astype(x.dtype)


def _window_attention(q, k, v, sink):
    B, L = q.shape[0], q.shape[1]
    nb = L // BLOCK
    G = N_Q_HEADS // N_KV_HEADS
    qb = q.reshape(B, nb, BLOCK, N_KV_HEADS, G, HEAD_DIM)

    def band(t):
        tp = jnp.pad(t, ((0, 0), (BLOCK, BLOCK), (0, 0), (0, 0)))
        tp = tp.reshape(B, nb + 2, BLOCK, N_KV_HEADS, HEAD_DIM)
        return jnp.concatenate([tp[:, :-2], tp[:, 1:-1], tp[:, 2:]], axis=2)

    kb = band(k)
    vb = band(v)
    s = jnp.einsum('bnqhgd,bnkhd->bnhgqk', qb, kb,
                   preferred_element_type=jnp.float32) * (HEAD_DIM ** -0.5)
    qpos = jnp.arange(nb)[:, None, None] * BLOCK + jnp.arange(BLOCK)[None, :, None]
    kpos = jnp.arange(nb)[:, None, None] * BLOCK - BLOCK + jnp.arange(3 * BLOCK)[None, None, :]
    valid = (jnp.abs(kpos - qpos) <= WINDOW) & (kpos >= 0) & (kpos < L)
    s = jnp.where(valid[None, :, None, None], s, -1e30)
    sk = sink.astype(jnp.float32).reshape(N_KV_HEADS, G)[None, None, :, :, None, None]
    m = jnp.maximum(jnp.max(s, axis=-1, keepdims=True), sk)
    e = jnp.exp(s - m)
    pr = e / (jnp.sum(e, axis=-1, keepdims=True) + jnp.exp(sk - m))
    o = jnp.einsum('bnhgqk,bnkhd->bnqhgd', pr.astype(v.dtype), vb)
    return o.reshape(B, L, D_ATTN)


def _hyena_filter(L, w1, b1, fr1, w2, b2, fr2, w3):
    f32 = jnp.float32
    t = jnp.linspace(0.0, 1.0, L, dtype=f32)[:, None]
    bands = (FILTER_EMB - 1) // 2
    w = 2.0 * math.pi * jnp.arange(L, dtype=f32)[:, None] / L
    f = jnp.linspace(1e-4, bands - 1, bands, dtype=f32)[None]
    z = jnp.concatenate([t, jnp.cos(f * w), -jnp.sin(f * w)], axis=-1)
    hdn = jnp.sin(fr1.astype(f32) * (z @ w1.astype(f32) + b1.astype(f32)))
    hdn = jnp.sin(fr2.astype(f32) * (hdn @ w2.astype(f32) + b2.astype(f32)))
    h = (hdn @ w3.astype(f32)).reshape(L, 2, D_HYENA)
    deltas = jnp.linspace(MIN_DECAY, MAX_DECAY, D_HYENA, dtype=f32)
    decay = jnp.exp(-t * jnp.abs(deltas)[None])
    h = h * decay[:, None, :]
    return h[:, 0], h[:, 1]


def _bidir_long_conv(v, h_fwd, h_bwd, d_bias):
    B, L, C = v.shape
    f32 = jnp.float32
    k_full = jnp.concatenate([h_fwd, jnp.zeros((1, C), f32), h_bwd[:0:-1]], axis=0)
    kf = jnp.fft.rfft(k_full, n=2 * L, axis=0)
    vf32 = v.astype(f32)
    vf = jnp.fft.rfft(vf32, n=2 * L, axis=1)
    y = jnp.fft.irfft(vf * kf[None], n=2 * L, axis=1)[:, :L]
    return y + vf32 * d_bias.astype(f32)


def _trunk(x, p, weights):
    (rms_mix, w_in, q_norm, k_norm, sink, w_short, b_short,
     filt_w1, filt_b1, filt_freq1, filt_w2, filt_b2, filt_freq2, filt_w3, hyena_bias,
     norm_attn_out, norm_hyena_out, w_out, rms_ffn, w_up, w_ffconv, b_ffconv, w_down,
     w_ple_gate, w_ple_proj) = weights
    B, L, _ = x.shape
    cos, sin = _rope_tables(L)
    h = x
    for i in range(DEPTH):
        n = _rmsnorm(h, rms_mix[i])
        z = n @ w_in[i]
        q = z[..., :D_ATTN].reshape(B, L, N_Q_HEADS, HEAD_DIM)
        k = z[..., D_ATTN:D_ATTN + D_KV].reshape(B, L, N_KV_HEADS, HEAD_DIM)
        v = z[..., D_ATTN + D_KV:D_QKV].reshape(B, L, N_KV_HEADS, HEAD_DIM)
        q = _partial_rope(_rmsnorm(q, q_norm[i]), cos, sin)
        k = _partial_rope(_rmsnorm(k, k_norm[i]), cos, sin)
        attn = _window_attention(q, k, v, sink[i])
        u = _dwconv3(z[..., D_QKV:], w_short[i], b_short[i])
        x0, x1, hv = jnp.split(u, 3, axis=-1)
        hf, hb = _hyena_filter(L, filt_w1[i], filt_b1[i], filt_freq1[i],
                               filt_w2[i], filt_b2[i], filt_freq2[i], filt_w3[i])
        hy = x0.astype(jnp.float32) * _bidir_long_conv(x1 * hv, hf, hb, hyena_bias[i])
        hy = hy.astype(h.dtype)
        mix = jnp.concatenate([_rmsnorm(attn, norm_attn_out[i]),
                               _rmsnorm(hy, norm_hyena_out[i])], axis=-1)
        h = h + mix @ w_out[i]
        n2 = _rmsnorm(h, rms_ffn[i])
        uu = _dwconv3(n2 @ w_up[i], w_ffconv[i], b_ffconv[i])
        a, g = jnp.split(uu, 2, axis=-1)
        h = h + (jax.nn.silu(g) * a) @ w_down[i]
        h = h + jax.nn.sigmoid(h @ w_ple_gate[i]) * (p[i] @ w_ple_proj[i])
    return h


def setup_inputs(seed: int = 0) -> dict:
    key = jax.random.key(seed)
    ks = jax.random.split(key, 32)
    f32 = jnp.float32

    def nrm(k, shape, scale):
        return jax.random.normal(k, shape, f32) * scale

    def gain(k, shape):
        return 1.0 + 0.05 * jax.random.normal(k, shape, f32)

    return {
        'x_prompt': nrm(ks[0], (BATCH, SEQ, D_MODEL), 1.0),
        'x_sample': nrm(ks[1], (DEC_BATCH, DEC_SEQ, D_MODEL), 1.0),
        'p_prompt': nrm(ks[2], (DEPTH, BATCH, SEQ, D_PLE), 1.0),
        'p_sample': nrm(ks[3], (DEPTH, DEC_BATCH, DEC_SEQ, D_PLE), 1.0),
        'rms_mix': gain(ks[4], (DEPTH, D_MODEL)),
        'w_in': nrm(ks[5], (DEPTH, D_MODEL, D_IN), D_MODEL ** -0.5),
        'q_norm': gain(ks[6], (DEPTH, HEAD_DIM)),
        'k_norm': gain(ks[7], (DEPTH, HEAD_DIM)),
        'sink': nrm(ks[8], (DEPTH, N_Q_HEADS), 0.5),
        'w_short': nrm(ks[9], (DEPTH, SHORT_CONV, 3 * D_HYENA), SHORT_CONV ** -0.5),
        'b_short': nrm(ks[10], (DEPTH, 3 * D_HYENA), 0.02),
        'filt_w1': nrm(ks[11], (DEPTH, FILTER_EMB, FILTER_HID), FILTER_EMB ** -0.5),
        'filt_b1': nrm(ks[12], (DEPTH, FILTER_HID), 0.1),
        'filt_freq1': gain(ks[13], (DEPTH, FILTER_HID)),
        'filt_w2': nrm(ks[14], (DEPTH, FILTER_HID, FILTER_HID), FILTER_HID ** -0.5),
        'filt_b2': nrm(ks[15], (DEPTH, FILTER_HID), 0.1),
        'filt_freq2': gain(ks[16], (DEPTH, FILTER_HID)),
        'filt_w3': nrm(ks[17], (DEPTH, FILTER_HID, 2 * D_HYENA), FILTER_HID ** -0.5),
        'hyena_bias': nrm(ks[18], (DEPTH, D_HYENA), 0.5),
        'norm_attn_out': gain(ks[19], (DEPTH, D_ATTN)),
        'norm_hyena_out': gain(ks[20], (DEPTH, D_HYENA)),
        'w_out': nrm(ks[21], (DEPTH, D_MIX, D_MODEL), D_MIX ** -0.5),
        'rms_ffn': gain(ks[22], (DEPTH, D_MODEL)),
        'w_up': nrm(ks[23], (DEPTH, D_MODEL, 2 * D_FF), D_MODEL ** -0.5),
        'w_ffconv': nrm(ks[24], (DEPTH, FFN_CONV, 2 * D_FF), FFN_CONV ** -0.5),
        'b_ffconv': nrm(ks[25], (DEPTH, 2 * D_FF), 0.02),
        'w_down': nrm(ks[26], (DEPTH, D_FF, D_MODEL), D_FF ** -0.5),
        'w_ple_gate': nrm(ks[27], (DEPTH, D_MODEL, D_MODEL), D_MODEL ** -0.5),
        'w_ple_proj': nrm(ks[28], (DEPTH, D_PLE, D_MODEL), D_PLE ** -0.5),
    }


def reference(x_prompt, x_sample, p_prompt, p_sample, rms_mix, w_in, q_norm, k_norm, sink,
              w_short, b_short, filt_w1, filt_b1, filt_freq1, filt_w2, filt_b2, filt_freq2,
              filt_w3, hyena_bias, norm_attn_out, norm_hyena_out, w_out, rms_ffn, w_up,
              w_ffconv, b_ffconv, w_down, w_ple_gate, w_ple_proj):
    weights = (rms_mix, w_in, q_norm, k_norm, sink, w_short, b_short,
               filt_w1, filt_b1, filt_freq1, filt_w2, filt_b2, filt_freq2, filt_w3, hyena_bias,
               norm_attn_out, norm_hyena_out, w_out, rms_ffn, w_up, w_ffconv, b_ffconv, w_down,
               w_ple_gate, w_ple_proj)
    y_prompt = _trunk(x_prompt, p_prompt, weights)
    y_sample = _trunk(x_sample, p_sample, weights)
    return (y_prompt, y_sample)
```

```python
import os
import math
import contextlib
import numpy as np
import ml_dtypes
import concourse.bass as bass
import concourse.mybir as mybir
from concourse.bass_utils import run_bass_kernel_spmd

F32 = mybir.dt.float32
BF16 = mybir.dt.bfloat16
AF = mybir.ActivationFunctionType
ALU = mybir.AluOpType
BF = ml_dtypes.bfloat16

D = 1024
DEPTH = 4
T = 8192
NT = 16
HALO = 128
NW = T + 2 * HALO
DQ = 512
DH = 512
DFF = 2816
DPLE = 256
EPS = 1e-6
NF = 16384
KA = 65
CG = 64
NG = DH // CG
ROPE_THETA = 500000.0
N_CORES = 8


class Prog:
    def __init__(self, nc):
        self.nc = nc
        self.ops = []
        self.lastw = {}
        self.readers = {}
        self.lastdma = {}
        self.eng = {'pe': nc.tensor, 'act': nc.scalar, 'dve': nc.vector, 'pool': nc.gpsimd, 'sp': nc.sync}
        self.last_on = {}
        self.n_cc = 0

    def op(self, eng, fn, r=(), w=(), dma=None, cc=False):
        idx = len(self.ops)
        deps = set()
        for x in r:
            p = self.lastw.get(x)
            if p is not None:
                deps.add(p)
        for x in w:
            p = self.lastw.get(x)
            if p is not None:
                deps.add(p)
            deps.update(self.readers.get(x, ()))
        if dma is not None:
            p = self.lastdma.get(dma)
            if p is not None:
                deps.add(p)
            self.lastdma[dma] = idx
        for x in r:
            self.readers.setdefault(x, []).append(idx)
        for x in w:
            self.lastw[x] = idx
            self.readers[x] = []
        deps.discard(idx)
        self.ops.append(dict(eng=eng, fn=fn, deps=deps, dma=dma, cc=cc, bar=False))
        self.last_on[eng] = idx
        return idx

    def barrier(self):
        deps = set(self.last_on.values()) | set(self.lastdma.values())
        for k, e in enumerate(('pe', 'act', 'dve', 'pool', 'sp')):
            self.ops.append(dict(eng=e, fn=None, deps=set(deps), dma=None, cc=False, bar=True, reset=(k == 0)))
        self.lastw = {}
        self.readers = {}

    def emit(self, es):
        nc = self.nc
        ops = self.ops
        def pe2pe(p, o):
            return (p['dma'] is None and not p['cc'] and p['eng'] == 'pe' and o['eng'] == 'pe'
                    and o['dma'] is None and o['fn'] is not None)
        sig = [False] * len(ops)
        for o in ops:
            latest = {}
            for d in o['deps']:
                p = ops[d]
                if pe2pe(p, o):
                    continue
                if p['dma'] is not None or p['cc']:
                    sig[d] = True
                else:
                    if latest.get(p['eng'], -1) < d:
                        latest[p['eng']] = d
            for d in latest.values():
                sig[d] = True
            o['bind'] = set(latest.values())
        sems = {}

        def getsem(name):
            if name not in sems:
                sems[name] = es.enter_context(nc.semaphore(name))
            return sems[name]

        cnt = {}
        val = [None] * len(ops)
        chan = [None] * len(ops)
        waited = {e: {} for e in self.eng}
        n_wait = 0
        keyslot = {}
        ncc = 0
        for i, o in enumerate(ops):
            e = o['eng']
            eobj = self.eng[e]
            if o.get('reset'):
                keyslot = {}
            need = {}
            for d in o['deps']:
                if val[d] is None:
                    continue
                p = ops[d]
                if p['dma'] is None and not p['cc'] and d not in o['bind']:
                    continue
                c = chan[d]
                if need.get(c, 0) < val[d]:
                    need[c] = val[d]
            for c, v in need.items():
                if waited[e].get(c, 0) >= v:
                    continue
                eobj.wait_ge(getsem(c), v)
                waited[e][c] = v
                n_wait += 1
            if o['fn'] is None:
                continue
            ins = o['fn'](eobj)
            if o['cc']:
                c = 'cc%d' % (ncc % 8)
                ncc += 1
                cnt[c] = cnt.get(c, 0) + 1
                ins.then_inc(getsem(c), 1)
                chan[i] = c
                val[i] = cnt[c]
            elif o['dma'] is not None:
                if o['dma'] not in keyslot:
                    keyslot[o['dma']] = len(keyslot)
                c = 'dslot%d' % keyslot[o['dma']]
                cnt[c] = cnt.get(c, 0) + 16
                ins.then_inc(getsem(c), 16)
                chan[i] = c
                val[i] = cnt[c]
            elif sig[i]:
                c = 'e_' + e
                cnt[c] = cnt.get(c, 0) + 1
                ins.then_inc(getsem(c), 1)
                chan[i] = c
                val[i] = cnt[c]
        for e in ('sp',):
            eobj = self.eng[e]
            for c, v in cnt.items():
                if waited[e].get(c, 0) < v:
                    eobj.wait_ge(getsem(c), v)
        self.stats = dict(n_ops=len(ops), n_wait=n_wait, n_sems=len(sems))


def _unit_of_rank(rank):
    return [('p', 0), ('p', 1), ('s', 0), ('s', 1), ('s', 2), ('s', 3), ('s', 2), ('s', 3)][rank]


_CONST_CACHE = {}


def _shared_consts():
    if 'shared' in _CONST_CACHE:
        return _CONST_CACHE['shared']
    c = {}
    c['ident_f'] = np.eye(128, dtype=np.float32)
    ones = np.zeros((128, 4, 128), np.float32)
    ones[:, 0, :] = 1.0 / 1024.0
    ones[:, 1, :] = 1.0 / 512.0
    ones[:, 2, :] = 1.0
    blk = np.zeros((128, 128), np.float32)
    blk[:64, :64] = 1.0 / 64.0
    blk[64:, 64:] = 1.0 / 64.0
    ones[:, 3, :] = blk
    c['ones_b'] = ones.astype(BF)
    prot = np.zeros((128, 128), np.float32)
    for p in range(128):
        d = p % 64
        if d < 8:
            prot[p + 8, p] = -1.0
        elif d < 16:
            prot[p - 8, p] = 1.0
    c['prot_b'] = prot.astype(BF)
    j = np.arange(128)[:, None]
    i = np.arange(128)[None, :]
    c['mprev'] = (j >= i).astype(np.float32)
    c['mnext'] = (j <= i).astype(np.float32)
    a = np.arange(128, dtype=np.float64)[:, None]
    ka = np.arange(KA, dtype=np.float64)[None, :]
    th = 2 * np.pi * a * ka / 128.0
    c['f1m'] = np.concatenate([np.cos(th), -np.sin(th)], axis=1).astype(BF)
    b = np.arange(128, dtype=np.float64)[:, None, None]
    kav = np.arange(KA, dtype=np.float64)[None, :, None]
    kb = np.arange(128, dtype=np.float64)[None, None, :]
    th = 2 * np.pi * b * (kav + 128.0 * kb) / NF
    gall = np.stack([np.cos(th), -np.sin(th), np.sin(th)], axis=2)
    c['gall'] = gall.astype(BF)
    kbv = np.arange(128, dtype=np.float64)[:, None]
    bp = np.arange(128, dtype=np.float64)[None, :]
    th = 2 * np.pi * kbv * bp / 128.0
    e1 = np.concatenate([np.cos(th), np.sin(th)], axis=1)
    e2 = np.concatenate([-np.sin(th), np.cos(th)], axis=1)
    c['e12'] = np.stack([e1, e2], axis=1).astype(BF)
    kav = np.arange(KA, dtype=np.float64)[:, None, None]
    bpv = np.arange(128, dtype=np.float64)[None, :, None]
    ap = np.arange(64, dtype=np.float64)[None, None, :]
    ph = 2 * np.pi * kav * (bpv + 128.0 * ap) / NF
    wt = np.full((KA, 1, 1), 2.0)
    wt[0] = 1.0
    wt[64] = 1.0
    hall = np.stack([wt / NF * np.cos(ph), -wt / NF * np.sin(ph)], axis=2)
    c['hall'] = hall.astype(BF)
    deltas = np.linspace(math.log(1e-2) / 1.5, math.log(1e-2) / 0.3, DH).astype(np.float32)
    c['negd'] = np.tile(-np.abs(deltas)[None, :], (128, 1)).astype(np.float32)
    _CONST_CACHE['shared'] = c
    return c


def _zfeat(L):
    key = ('z', L)
    if key in _CONST_CACHE:
        return _CONST_CACHE[key]
    t = np.linspace(0.0, 1.0, L).astype(np.float32)
    w = (2.0 * np.pi * np.arange(L, dtype=np.float64) / L)
    f = np.linspace(1e-4, 15.0, 16)
    fw = f[None, :] * w[:, None]
    z = np.concatenate([t[:, None].astype(np.float64), np.cos(fw), -np.sin(fw)], axis=1).astype(np.float32)
    _CONST_CACHE[key] = (t, z)
    return t, z


def _rank_consts(rank):
    key = ('rank', rank)
    if key in _CONST_CACHE:
        return _CONST_CACHE[key]
    kind, idx = _unit_of_rank(rank)
    sh = _shared_consts()
    c = {}
    pos0 = 8192 if (kind == 'p' and idx == 1) else 0
    mL = 1.0 if (kind == 'p' and idx == 1) else 0.0
    mR = 1.0 if (kind == 'p' and idx == 0) else 0.0
    c['edge'] = np.tile(np.array([[mL, mR]], np.float32), (128, 1))
    masks = np.stack([sh['mprev'], sh['mnext'], sh['mprev'] * mL, sh['mnext'] * mR], axis=1)
    c['masks'] = masks.astype(BF)
    pos = (pos0 - HALO + np.arange(NW)).astype(np.float32)
    inv = (ROPE_THETA ** (-np.arange(0, 16, 2, dtype=np.float32) / 16.0)).astype(np.float32)
    ang = pos[None, :] * inv[:, None]
    cs = np.zeros((128, 2, NW), np.float32)
    cs[:, 0, :] = 1.0
    for p in range(128):
        d = p % 64
        if d < 16:
            cs[p, 0, :] = np.cos(ang[d % 8])
            cs[p, 1, :] = np.sin(ang[d % 8])
    c['cstab'] = cs
    L = 16384 if kind == 'p' else 8192
    t_all, z_all = _zfeat(L)
    n = np.arange(NF)
    zf = np.zeros((2, 33, NF), np.float32)
    mm = np.zeros((2, 128, NF), np.float32)
    td = np.zeros((2, NF), np.float32)
    e0 = np.zeros((2,), np.float32)
    own = idx % 2 if kind == 's' else idx
    for s in range(2):
        lag = np.zeros(NF, np.int64)
        dr = np.zeros(NF, np.int64)
        if s == own:
            lo = n < 8192
            hi = n > 8192
            lag[lo] = n[lo]
            dr[lo] = 1
            lag[hi] = NF - n[hi]
            dr[hi] = 2
            e0[s] = 1.0
        elif kind == 'p':
            lo = n < 8192
            hi = n > 8192
            if idx == 0:
                lag[lo] = 8192 - n[lo]
                dr[lo] = 2
                lag[hi] = 24576 - n[hi]
                dr[hi] = 2
            else:
                lag[lo] = n[lo] + 8192
                dr[lo] = 1
                lag[hi] = n[hi] - 8192
                dr[hi] = 1
        valid = dr > 0
        zf[s][:, valid] = z_all[lag[valid]].T
        td[s][valid] = t_all[lag[valid]]
        mm[s][:64, :] = (dr == 1).astype(np.float32)[None, :]
        mm[s][64:, :] = (dr == 2).astype(np.float32)[None, :]
    c['zf'] = zf
    c['mm'] = mm.astype(BF)
    c['tdec'] = np.ascontiguousarray(td.reshape(2, 128, 128).transpose(1, 0, 2))
    c['e0'] = np.tile(e0[None, :], (128, 1)).astype(np.float32)
    _CONST_CACHE[key] = c
    return c


def _bc(ap, shape, axis):
    return ap.unsqueeze(axis).to_broadcast(shape)


class Builder:
    def __init__(self, depth=DEPTH, debug=False, stop=None):
        self.depth = depth
        self.debug = debug
        self.stop = stop
        self.nc = bass.Bass("TRN2", target_bir_lowering=False)
        self.P = Prog(self.nc)
        self.ges = contextlib.ExitStack()
        self.scope = None
        self.bank = 0
        self.rr = 0
        self.inputs = {}
        self.dbg_out = []
        self.sub = float(os.environ.get('MK_SUB', '99'))
        self.ntr = int(os.environ.get('MK_NT', str(NT)))

    def din(self, name, shape, dt=F32):
        t = self.nc.dram_tensor(name, list(shape), dt, kind="ExternalInput")
        self.inputs[name] = (tuple(shape), dt)
        return t.ap()

    def L(self, name):
        if name not in self.lazy_ap:
            shp, dt = self.lazy[name]
            self.lazy_ap[name] = self.din(name, shp, dt)
        return self.lazy_ap[name]

    def dscr(self, name, shape, dt, dbg=True):
        if self.debug and dbg and name in self.debug:
            self.dbg_out.append(name)
            return self.nc.dram_tensor(name, list(shape), dt, kind="ExternalOutput").ap()
        return self.nc.dram_tensor(name, list(shape), dt).ap()

    def gtile(self, name, shape, dt):
        return self.ges.enter_context(self.nc.sbuf_tensor('sbg_' + name, list(shape), dt))

    def tile(self, name, shape, dt):
        self.uid = getattr(self, 'uid', 0) + 1
        return self.scope.enter_context(self.nc.sbuf_tensor('sb%d_%s' % (self.uid, name), list(shape), dt))

    def begin(self):
        self.scope = contextlib.ExitStack()
        import inspect
        nm = inspect.stack()[1].function
        self.marks = getattr(self, 'marks', [])
        self.marks.append((nm, sum(1 for o in self.P.ops if o['eng'] == 'pe' and o['fn'] is not None)))

    def end(self):
        self.P.barrier()
        self.scope.close()
        self.scope = None

    def nb(self, k=1):
        if self.bank + k > 8:
            self.bank = 0
        b = self.bank
        self.bank = (self.bank + k) % 8
        return b

    def pr(self, b, k=1):
        return [('ps', b + j) for j in range(k)]

    def alt(self, engs=('act', 'dve')):
        self.rr += 1
        return engs[self.rr % len(engs)]

    def mm(self, out, lhsT, rhs, start, stop, r, w):
        self.P.op('pe', lambda e, o=out, l=lhsT, x=rhs, s=start, t=stop: e.matmul(o, l, x, start=s, stop=t), r, w)

    def tp(self, out, in_, ident, r, w):
        self.P.op('pe', lambda e, o=out, i=in_, d=ident: e.transpose(o, i, d), r, w)

    def act(self, out, in_, func, r, w, scale=None, bias=None):
        kw = {}
        if scale is not None:
            kw['scale'] = scale
        if bias is not None:
            kw['bias'] = bias
        self.P.op('act', lambda e, o=out, i=in_, f=func, k=kw: e.activation(o, i, f, **k), r, w)

    def cp(self, eng, out, in_, r, w):
        if eng == 'act':
            self.act(out, in_, AF.Copy, r, w)
        else:
            self.P.op(eng, lambda e, o=out, i=in_: e.tensor_copy(o, i), r, w)

    def tt(self, eng, out, in0, in1, op, r, w):
        self.P.op(eng, lambda e, o=out, a=in0, b=in1, p=op: e.tensor_tensor(o, a, b, p), r, w)

    def ts(self, eng, out, in0, s1, s2, op0, op1, r, w):
        if op1 is None:
            self.P.op(eng, lambda e, o=out, a=in0, x=s1, p=op0: e.tensor_scalar(o, a, x, None, p), r, w)
        else:
            self.P.op(eng, lambda e, o=out, a=in0, x=s1, y=s2, p=op0, q=op1: e.tensor_scalar(o, a, x, y, p, q), r, w)

    def stt(self, eng, out, in0, scalar, in1, op0, op1, r, w):
        self.P.op(eng, lambda e, o=out, a=in0, s=scalar, b=in1, p=op0, q=op1:
                  e.scalar_tensor_tensor(o, a, s, b, p, q), r, w)

    def rstd(self, out, in_, r, w, eng='dve'):
        np_ = out.shape[0]
        self.act(out, in_, AF.Ln, list(r) + ['g_epsc'], list(w), bias=self.epsc[0:np_, 0:1], scale=1.0)
        self.act(out, out, AF.Exp, list(w), list(w), scale=-0.5)

    def dma(self, q, out, in_, r, w, key, slow=False):
        if slow:
            self.P.op(q, lambda e, o=out, i=in_: e.dma_start(out=o, in_=i, allow_slow_non_contiguous=True), r, w, dma=key)
        else:
            self.P.op(q, lambda e, o=out, i=in_: e.dma_start(out=o, in_=i), r, w, dma=key)

    def allgather(self, in2d, out2d, r, w):
        self.P.op('pool', lambda e, i=in2d, o=out2d: e.collective_compute(
            "AllGather", ALU.bypass, replica_groups=[[0, 1], [2, 3], [4, 5], [6, 7]],
            ins=[i.opt()], outs=[o.opt()]), r, w, cc=True)

    def declare(self):
        L = DEPTH
        d = self.din
        self.lazy = {'x': ([T, D], F32), 'p': ([L, T, DPLE], F32), 'w_in': ([L, D, 2304], F32),
                     'w_out': ([L, D, D], F32), 'w_up': ([L, D, 2 * DFF], F32), 'w_down': ([L, DFF, D], F32),
                     'w_ple_gate': ([L, D, D], F32), 'w_ple_proj': ([L, DPLE, D], F32),
                     'filt_w1': ([L, 33, 64], F32), 'filt_w2': ([L, 64, 64], F32), 'filt_w3': ([L, 64, 1024], F32),
                     'gall': ([128, KA, 3, 128], BF16), 'hall': ([KA, 128, 2, 64], BF16),
                     'zf': ([2, 33, NF], F32), 'mm': ([2, 128, NF], BF16), 'cstab': ([128, 2, NW], F32)}
        self.lazy_ap = {}
        self.colspec = {}
        off = 0
        for name, n in [('g_mix', L * 8), ('g_ffn', L * 8), ('gq', L), ('gk', L), ('sink', L * 8),
                        ('w_short', L * 36), ('b_short', L * 12), ('g_ao', L * 4), ('g_ho', L * 4),
                        ('w_ffc', L * 132), ('b_ffc', L * 44), ('fb1', L), ('ffr1', L), ('fb2', L), ('ffr2', L)]:
            self.colspec[name] = (off, n)
            off += n
        self.ncol = off
        self.cols_d = d('cols', [128, self.ncol])
        self.hbias_d = d('hbias', [1, L * DH])
        self.c_ident = d('ident_f', [128, 128])
        self.c_ones = d('ones_b', [128, 4, 128], BF16)
        self.c_prot = d('prot_b', [128, 128], BF16)
        self.c_masks = d('masks', [128, 4, 128], BF16)
        self.c_edge = d('edge', [128, 2])
        self.c_f1m = d('f1m', [128, 130], BF16)
        self.c_e12 = d('e12', [128, 2, 256], BF16)
        self.c_negd = d('negd', [128, DH])
        self.c_tdec = d('tdec', [128, 2, 128])
        self.c_e0 = d('e0', [128, 2])
        self.y = self.nc.dram_tensor('y', [T, D], F32, kind="ExternalOutput").ap()
        s = self.dscr
        self.hres = s('hres', [D, T], F32)
        self.nrm = s('nrm', [D, NW], BF16)
        self.xh_in = s('xh_in', [2 * D, 128], BF16, dbg=False)
        self.xh_out = s('xh_out', [4 * D, 128], BF16, dbg=False)
        self.zhy = s('zhy', [3 * DH, T + 2], BF16)
        self.attn_n = s('attn_n', [DQ, T], BF16)
        self.x0s = s('x0s', [DH, T], BF16)
        self.vfft2 = [s('vfft%d' % g_, [64 * 8, 1024], BF16, dbg=False) for g_ in range(NG)]
        self.vall2 = [s('vall%d' % g_, [128 * 8, 1024], BF16, dbg=False) for g_ in range(NG)]
        self.vfft = [t_.rearrange('(a x) y -> a (x y)', x=8) for t_ in self.vfft2]
        self.vall = [t_.rearrange('(a x) y -> a (x y)', x=8) for t_ in self.vall2]
        self.hfs = s('hfs', [NG, 128, KA * 2 * 2 * CG], BF16)
        self.hdn = s('hdn', [2, 128, NF], BF16)
        self.yconv = s('yconv', [DH, T], BF16)
        self.n2s = s('n2s', [D, T + 2], BF16)
        self.xn_in = s('xn_in', [2, D], BF16, dbg=False)
        self.xn_out = s('xn_out', [4, D], BF16, dbg=False)
        self.acts = s('acts', [DFF, T], BF16)

    def col(self, name, l=None, k=1, j=0):
        off, n = self.colspec[name]
        per = n // DEPTH
        if l is None:
            return self.cols[:, off:off + n]
        a = off + l * per + j
        return self.cols[:, a:a + k]

    def setup_globals(self):
        g = self.gtile
        self.ps = self.ges.enter_context(self.nc.psum_tensor('ps', [128, 8, 512], F32))
        self.ident = g('ident', [128, 128], F32)
        self.ones = g('ones', [128, 4, 128], BF16)
        self.prot = g('prot', [128, 128], BF16)
        self.masks = g('masks', [128, 4, 128], BF16)
        self.edge = g('edge', [128, 2], F32)
        self.cols = g('cols', [128, self.ncol], F32)
        self.esk = g('esk', [128, DEPTH * 8], F32)
        self.fsc = g('fsc', [128, DEPTH * 4], F32)
        self.e0 = g('e0', [128, 2], F32)
        self.epsc = g('epsc', [128, 1], F32)
        self.P.op('dve', lambda e: e.memset(self.epsc[:], EPS), [], ['g_epsc'])
        for t, dsrc, nm in [(self.ident, self.c_ident, 'ident'), (self.ones, self.c_ones, 'ones'),
                            (self.prot, self.c_prot, 'prot'), (self.masks, self.c_masks, 'masks'),
                            (self.edge, self.c_edge, 'edge'), (self.cols, self.cols_d, 'cols'),
                            (self.e0, self.c_e0, 'e0')]:
            self.dma('sp', t[:], dsrc, [], ['g_' + nm], 'g_' + nm)
        o, n = self.colspec['sink']
        self.act(self.esk[:], self.cols[:, o:o + n], AF.Exp, ['g_cols'], ['g_esk'])
        for l in range(self.depth):
            for k, (fr, fb) in enumerate([('ffr1', 'fb1'), ('ffr2', 'fb2')]):
                self.ts('dve', self.fsc[:, l * 4 + 2 * k:l * 4 + 2 * k + 1], self.col(fr, l), 1.0 / 3.0, None, ALU.mult, None,
                        ['g_cols'], [('fsc', l, 2 * k)])
                self.tt('dve', self.fsc[:, l * 4 + 2 * k + 1:l * 4 + 2 * k + 2], self.fsc[:, l * 4 + 2 * k:l * 4 + 2 * k + 1],
                        self.col(fb, l), ALU.mult, [('fsc', l, 2 * k), 'g_cols'], [('fsc', l, 2 * k + 1)])
        self.P.barrier()

    def load_w(self, src_rows, ncols, scale, stg, sname, pieces):
        self.wslot = getattr(self, 'wslot', 0) + 1
        s = self.wslot % len(stg)
        st = stg[s]
        rn = (sname, s)
        self.dma('sp', st[:, 0:ncols], src_rows, [], [rn], '%s%d' % (sname, s))
        for (d_ap, c0, c1, vf, wn) in pieces:
            src = st[:, c0:c1]
            if vf is not None:
                src = vf(src)
            eng = self.alt(tuple(os.environ.get('MK_WENG', 'act,dve,pool').split(',')))
            if scale is None:
                self.cp(eng, d_ap, src, [rn], [wn])
            elif eng == 'act':
                self.act(d_ap, src, AF.Copy, [rn, 'g_cols'], [wn], scale=scale)
            else:
                self.ts(eng, d_ap, src, scale, None, ALU.mult, None, [rn, 'g_cols'], [wn])

    def phase_p0(self):
        self.begin()
        xt = [self.tile('p0_xt%d' % s, [128, 4, D], F32) for s in range(2)]
        ht = [self.tile('p0_ht%d' % s, [128, 8, 512], F32) for s in range(2)]
        for i in range(NT):
            s = i % 2
            self.dma('sp', xt[s][:], self.L('x')[i * 512:(i + 1) * 512, :].rearrange('(b p) f -> p b f', p=128),
                     [], [('xt', s)], 'xt%d' % s)
            for fc in range(8):
                bk = self.nb()
                for blk in range(4):
                    self.tp(self.ps[:, bk, blk * 128:(blk + 1) * 128], xt[s][:, blk, fc * 128:(fc + 1) * 128],
                            self.ident[:], [('xt', s), 'g_ident'], self.pr(bk))
                self.cp(self.alt(), ht[s][:, fc, :], self.ps[:, bk, :], self.pr(bk), [('ht', s, fc)])
            self.dma('sp', self.hres[:, i * 512:(i + 1) * 512].rearrange('(c p) t -> p c t', p=128), ht[s][:],
                     [('ht', s, fc) for fc in range(8)], [('hres', i)], 'ht%d_st' % s)
        self.end()

    def phase_a0(self, l):
        self.begin()
        ht = [self.tile('a0_ht%d' % s, [128, 8, 512], F32) for s in range(2)]
        sq = [self.tile('a0_sq%d' % s, [128, 8, 512], BF16) for s in range(2)]
        rs = [self.tile('a0_rs%d' % s, [128, 512], F32) for s in range(2)]
        nt = [self.tile('a0_nt%d' % s, [128, 8, 512], BF16) for s in range(2)]
        hl = self.tile('a0_hl', [128, 8, 128], BF16)
        hr = self.tile('a0_hr', [128, 8, 128], BF16)
        def loads(i):
            s = i % 2
            self.dma('sp', ht[s][:], self.hres[:, i * 512:(i + 1) * 512].rearrange('(c p) t -> p c t', p=128),
                     [('hres', i)], [('ht', s)], 'a0ht%d' % s)
        loads(0)
        for i in range(NT):
            s = i % 2
            if i + 1 < NT:
                loads(i + 1)
            self.act(sq[s][:], ht[s][:], AF.Square, [('ht', s)], [('sq', s)])
            bk = self.nb()
            for fc in range(8):
                self.mm(self.ps[:, bk, :], self.ones[:, 0, :], sq[s][:, fc, :], fc == 0, fc == 7,
                        [('sq', s), 'g_ones'], self.pr(bk))
            self.rstd(rs[s][:], self.ps[:, bk, :], self.pr(bk), [('rs', s)])
            for hf in range(2):
                eng = 'dve' if hf == 0 else 'pool'
                self.tt(eng, nt[s][:, hf * 4:(hf + 1) * 4, :], ht[s][:, hf * 4:(hf + 1) * 4, :],
                        _bc(rs[s][:], [128, 4, 512], 1), ALU.mult, [('ht', s), ('rs', s)], [('nt', s, hf)])
            rd = [('nt', s, 0), ('nt', s, 1)]
            self.dma('sp', self.nrm[:, HALO + i * 512:HALO + (i + 1) * 512].rearrange('(c p) t -> p c t', p=128),
                     nt[s][:], rd, [('nrm', i)], 'a0nt%d_st' % s)
            if i == 0:
                self.dma('sp', self.xh_in[0:D, :].rearrange('(c p) t -> p c t', p=128), nt[s][:, :, 0:128],
                         rd, ['xh_in0'], 'a0x0')
            if i == NT - 1:
                self.dma('sp', self.xh_in[D:2 * D, :].rearrange('(c p) t -> p c t', p=128), nt[s][:, :, 384:512],
                         rd, ['xh_in1'], 'a0x1')
        self.allgather(self.xh_in, self.xh_out, ['xh_in0', 'xh_in1'], ['xh_out'])
        for k, (tl, r0, col) in enumerate([(hl, D, 0), (hr, 2 * D, NW - HALO)]):
            self.dma('sp', tl[:], self.xh_out[r0:r0 + D, :].rearrange('(c p) t -> p c t', p=128),
                     ['xh_out'], [('hal', k)], 'a0h%d' % k)
            self.ts('dve', tl[:], tl[:], self.edge[:, k:k + 1], None, ALU.mult, None, [('hal', k), 'g_edge'], [('hal', k)])
            self.dma('sp', self.nrm[:, col:col + HALO].rearrange('(c p) t -> p c t', p=128), tl[:],
                     [('hal', k)], [('nrmh', k)], 'a0h%d_st' % k)
        self.end()


    def capture(self, fn):
        rec = []
        real = self.P.op
        self.P.op = lambda *a, **k: rec.append((a, k))
        try:
            fn()
        finally:
            self.P.op = real
        return rec

    def replay(self, recs):
        for a, k in recs:
            self.P.op(*a, **k)

    @staticmethod
    def merge(a, b):
        def split(x):
            segs = [[]]
            for r in x:
                if r[0][0] == 'MARK':
                    segs.append([])
                else:
                    segs[-1].append(r)
            return segs
        sa = split(a)
        sb = [g_ for g_ in split(b) if g_]
        ncut = max(1, len(sa) - 1)
        out = []
        ib = 0
        for k, sg in enumerate(sa):
            out.extend(sg)
            if k < len(sa) - 1:
                tgt = (k + 1) * len(sb) // ncut
                while ib < min(tgt, len(sb)):
                    out.extend(sb[ib])
                    ib += 1
        while ib < len(sb):
            out.extend(sb[ib])
            ib += 1
        return out

    def phase_a(self, l):
        self.begin()
        w_in = self.L('w_in')
        cstab = self.L('cstab')
        wq = self.tile('wq', [128, 8, 512], BF16)
        wk = self.tile('wk', [128, 8, 256], BF16)
        wv = self.tile('wv', [128, 8, 256], BF16)
        why = self.tile('why', [128, 8, 1536], BF16)
        stg = [self.tile('a_stg', [128, 1152], F32)]
        dup = lambda a: _bc(a.rearrange('p (k d) -> p k d', k=2), [128, 2, 2, 64], 2)
        for kc in range(8):
            self.load_w(w_in[l, kc * 128:(kc + 1) * 128, 0:1152], 1152, self.col('g_mix', l, 1, kc), stg, 'astg', [
                (wq[:, kc, :], 0, 512, None, ('wq', kc)),
                (wk[:, kc, :].rearrange('p (k u d) -> p k u d', k=2, u=2), 512, 640, dup, ('wk', kc)),
                (wv[:, kc, :].rearrange('p (k u d) -> p k u d', k=2, u=2), 640, 768, dup, ('wv', kc)),
                (why[:, kc, 0:384], 768, 1152, None, ('why', kc, 0))])
            self.load_w(w_in[l, kc * 128:(kc + 1) * 128, 1152:2304], 1152, self.col('g_mix', l, 1, kc), stg, 'astg', [
                (why[:, kc, 384:1536], 0, 1152, None, ('why', kc, 1))])
        nt = [self.tile('nt%d' % s, [128, 8, 768], BF16) for s in range(2)]
        ct = [self.tile('ct%d' % s, [128, 2, 768], F32) for s in range(2)]
        qraw = self.tile('qraw', [128, 4, 512], F32)
        kraw = self.tile('kraw', [128, 2, 768], F32)
        sqq = self.tile('sqq', [128, 4, 512], BF16)
        sqk = self.tile('sqk', [128, 2, 768], BF16)
        rq = self.tile('rq', [128, 4, 512], F32)
        rk = self.tile('rk', [128, 2, 768], F32)
        qn = self.tile('qn', [128, 4, 512], BF16)
        kn = self.tile('kn', [128, 2, 768], BF16)
        qr = [self.tile('qr%d' % s, [128, 4, 512], BF16) for s in range(2)]
        krl = [self.tile('krl%d' % s, [128, 2, 768], BF16) for s in range(2)]
        krh = [self.tile('krh%d' % s, [128, 2, 768], BF16) for s in range(2)]
        for s in range(2):
            self.P.op('pool', lambda e, t_=krl[s]: e.memset(t_[64:128], 0.0), [], [('krl0', s)])
            self.P.op('pool', lambda e, t_=krh[s]: e.memset(t_[0:64], 0.0), [], [('krh0', s)])
        vd = [self.tile('vd%d' % s, [128, 6, 256], BF16) for s in range(2)]
        zt = self.tile('zt', [128, 12, 512], BF16)
        zh = self.tile('zh', [128, 12, 2], BF16)
        E = [self.tile('E%d' % s, [128, 3, 512], BF16) for s in range(2)]
        rec = [self.tile('rec%d' % s, [128, 512], F32) for s in range(2)]
        attn = self.tile('attn', [128, 4, 512], F32)
        sqa = self.tile('sqa', [128, 4, 512], BF16)
        rsa = self.tile('rsa', [128, 512], F32)
        an = [self.tile('an%d' % s, [128, 4, 512], BF16) for s in range(2)]
        gq = self.col('gq', l)
        gk = self.col('gk', l)
        st_ = {'ei': 0}
        kseg = [(0, 0, 512, 0, 0), (0, 512, 768, 1, 0), (1, 0, 256, 1, 256), (1, 256, 768, 2, 0)]
        whyr = lambda kc: [('why', kc, 0), ('why', kc, 1)]

        def loads(i):
            s = i % 2
            t0 = i * 512
            self.dma('sp', nt[s][:], self.nrm[:, t0:t0 + 768].rearrange('(c p) t -> p c t', p=128),
                     [], [('nt', s)], 'a_nt%d' % s)
            self.dma('sp', ct[s][:], cstab[:, :, t0:t0 + 768], [], [('ct', s)], 'a_ct%d' % s)

        def X(i):
            s = i % 2
            t0 = i * 512
            rw = [('nt', s)]
            bq = self.nb(4)
            for mc in range(4):
                for kc in range(8):
                    self.mm(self.ps[:, bq + mc, :], wq[:, kc, mc * 128:(mc + 1) * 128], nt[s][:, kc, 128:640],
                            kc == 0, kc == 7, rw + [('wq', kc)], self.pr(bq + mc))
            self.cp('act', qraw[:], self.ps[:, bq:bq + 4, :], self.pr(bq, 4), ['qraw'])
            self.P.op('MARK', None)
            bkk = self.nb(3)
            for (c2, c0, c1, bo, pc0) in kseg:
                for kc in range(8):
                    self.mm(self.ps[:, bkk + bo, pc0:pc0 + (c1 - c0)], wk[:, kc, c2 * 128:(c2 + 1) * 128],
                            nt[s][:, kc, c0:c1], kc == 0, kc == 7, rw + [('wk', kc)], self.pr(bkk + bo))
            psk = self.ps[:, bkk:bkk + 3, :].rearrange('p b t -> p (b t)').rearrange('p (c t) -> p c t', c=2)
            self.cp('dve', kraw[:], psk, self.pr(bkk, 3), ['kraw'])
            self.P.op('MARK', None)
            bv = self.nb(3)
            for kb in range(6):
                for kc in range(8):
                    self.mm(self.ps[:, bv + kb // 2, (kb % 2) * 256:(kb % 2 + 1) * 256], nt[s][:, kc, kb * 128:(kb + 1) * 128],
                            wv[:, kc, :], kc == 0, kc == 7, rw + [('wv', kc)], self.pr(bv + kb // 2))
            self.cp('act', vd[s][:], self.ps[:, bv:bv + 3, :].rearrange('p b (u t) -> p (b u) t', u=2), self.pr(bv, 3), [('vd', s)])
            for g3 in range(3):
                self.P.op('MARK', None)
                bh = self.nb(4)
                for j in range(4):
                    mc = g3 * 4 + j
                    for kc in range(8):
                        self.mm(self.ps[:, bh + j, :], why[:, kc, mc * 128:(mc + 1) * 128], nt[s][:, kc, 128:640],
                                kc == 0, kc == 7, rw + whyr(kc), self.pr(bh + j))
                self.cp(self.alt(), zt[:, g3 * 4:(g3 + 1) * 4, :], self.ps[:, bh:bh + 4, :], self.pr(bh, 4), [('zt', g3)])
            self.dma('sp', self.zhy[:, 1 + t0:1 + t0 + 512].rearrange('(c p) t -> p c t', p=128), zt[:],
                     [('zt', g3) for g3 in range(3)], [('zhy', i)], 'a_zt_st')
            if i == 0 or i == NT - 1:
                self.P.op('MARK', None)
                colh = 127 if i == 0 else 640
                hi = 0 if i == 0 else 1
                bh = self.nb(1)
                for mc in range(12):
                    for kc in range(8):
                        self.mm(self.ps[:, bh, mc:mc + 1], why[:, kc, mc * 128:(mc + 1) * 128], nt[s][:, kc, colh:colh + 1],
                                kc == 0, kc == 7, rw + whyr(kc), self.pr(bh))
                self.cp('dve', zh[:, :, hi], self.ps[:, bh, 0:12], self.pr(bh), [('zh', hi)])
                dcol = 0 if i == 0 else T + 1
                self.dma('sp', self.zhy[:, dcol:dcol + 1].rearrange('(c p) t -> p c t', p=128), zh[:, :, hi:hi + 1],
                         [('zh', hi)], [('zhyh', hi)], 'a_zh%d_st' % hi, slow=True)

        def Y(i):
            s = i % 2
            self.act(sqq[:], qraw[:], AF.Square, ['qraw'], ['sqq'])
            self.act(sqk[:], kraw[:], AF.Square, ['kraw'], ['sqk'])
            self.P.op('MARK', None)
            bs = self.nb(4)
            for mc in range(4):
                self.mm(self.ps[:, bs + mc, :], self.ones[:, 3, :], sqq[:, mc, :], True, True, ['sqq', 'g_ones'], self.pr(bs + mc))
            self.rstd(rq[:], self.ps[:, bs:bs + 4, :], self.pr(bs, 4), ['rq'])
            self.P.op('MARK', None)
            bs2 = self.nb(3)
            for (c2, c0, c1, bo, pc0) in kseg:
                self.mm(self.ps[:, bs2 + bo, pc0:pc0 + (c1 - c0)], self.ones[:, 3, :], sqk[:, c2, c0:c1], True, True,
                        ['sqk', 'g_ones'], self.pr(bs2 + bo))
            psk = self.ps[:, bs2:bs2 + 3, :].rearrange('p b t -> p (b t)').rearrange('p (c t) -> p c t', c=2)
            self.rstd(rk[:], psk, self.pr(bs2, 3), ['rk'])
            self.P.op('MARK', None)
            self.stt('dve', qn[:], qraw[:], gq, rq[:], ALU.mult, ALU.mult, ['qraw', 'rq', 'g_cols'], ['qn'])
            self.stt('dve', kn[:], kraw[:], gk, rk[:], ALU.mult, ALU.mult, ['kraw', 'rk', 'g_cols'], ['kn'])
            self.P.op('MARK', None)
            br = self.nb(4)
            for mc in range(4):
                self.mm(self.ps[:, br + mc, :], self.prot[:], qn[:, mc, :], True, True, ['qn', 'g_prot'], self.pr(br + mc))
            self.tt('pool', qraw[:], qn[:], _bc(ct[s][:, 0, 128:640], [128, 4, 512], 1), ALU.mult, ['qn', ('ct', s)], ['qraw'])
            self.tt('dve', rq[:], self.ps[:, br:br + 4, :], _bc(ct[s][:, 1, 128:640], [128, 4, 512], 1), ALU.mult,
                    self.pr(br, 4) + [('ct', s)], ['rq'])
            self.tt('pool', qr[s][:], qraw[:], rq[:], ALU.add, ['qraw', 'rq'], [('qr', s)])
            self.P.op('MARK', None)
            br2 = self.nb(3)
            for (c2, c0, c1, bo, pc0) in kseg:
                self.mm(self.ps[:, br2 + bo, pc0:pc0 + (c1 - c0)], self.prot[:], kn[:, c2, c0:c1], True, True,
                        ['kn', 'g_prot'], self.pr(br2 + bo))
            psk = self.ps[:, br2:br2 + 3, :].rearrange('p b t -> p (b t)').rearrange('p (c t) -> p c t', c=2)
            self.tt('pool', kraw[:], kn[:], _bc(ct[s][:, 0, :], [128, 2, 768], 1), ALU.mult, ['kn', ('ct', s)], ['kraw'])
            self.tt('dve', rk[:], psk, _bc(ct[s][:, 1, :], [128, 2, 768], 1), ALU.mult, self.pr(br2, 3) + [('ct', s)], ['rk'])
            self.tt('pool', krl[s][0:64], kraw[0:64], rk[0:64], ALU.add, ['kraw', 'rk'], [('krl', s)])
            self.tt('dve', krh[s][64:128], kraw[64:128], rk[64:128], ALU.add, ['kraw', 'rk'], [('krh', s)])

        def Z(i):
            s = i % 2
            t0 = i * 512
            kread = [('krl', s), ('krh', s), ('krl0', s), ('krh0', s), ('qr', s)]
            for qb in range(4):
                for kvh in range(2):
                    es_ = st_['ei'] % 2
                    st_['ei'] += 1
                    b0 = self.nb(3)
                    for kk in range(3):
                        kbw = qb + kk
                        for half in range(2):
                            kt_ = krl[s] if half == 0 else krh[s]
                            self.mm(self.ps[:, b0 + kk, half * 256:(half + 1) * 256],
                                    kt_[:, kvh, kbw * 128:(kbw + 1) * 128],
                                    qr[s][:, 2 * kvh:2 * kvh + 2, qb * 128:(qb + 1) * 128],
                                    True, True, kread, self.pr(b0 + kk))
                    self.act(E[es_][:], self.ps[:, b0:b0 + 3, :], AF.Exp, self.pr(b0, 3), [('E', es_)], scale=0.125)
                    first = (i == 0 and qb == 0)
                    lastb = (i == NT - 1 and qb == 3)
                    for (kk, mi) in [(0, 2 if first else 0), (2, 3 if lastb else 1)]:
                        ev = E[es_][:, kk, :].rearrange('p (g q) -> p g q', g=4)
                        self.tt('dve', ev, ev, _bc(self.masks[:, mi, :], [128, 4, 128], 1), ALU.mult,
                                [('E', es_), 'g_masks'], [('E', es_)])
                    bpv = self.nb()
                    bdn = self.nb()
                    for kk in range(3):
                        self.mm(self.ps[:, bpv, :], vd[s][:, qb + kk, kvh * 128:(kvh + 1) * 128], E[es_][:, kk, :],
                                kk == 0, kk == 2, [('vd', s), ('E', es_)], self.pr(bpv))
                    for kk in range(3):
                        self.mm(self.ps[:, bdn, :], self.ones[:, 2, :], E[es_][:, kk, :], kk == 0, kk == 2,
                                [('E', es_), 'g_ones'], self.pr(bdn))
                    rv = rec[es_][:].rearrange('p (g q) -> p g q', g=4)
                    sk = self.esk[:, l * 8 + kvh * 4:l * 8 + kvh * 4 + 4]
                    self.tt('dve', rv, self.ps[:, bdn, :].rearrange('p (g q) -> p g q', g=4), _bc(sk, [128, 4, 128], 2),
                            ALU.add, self.pr(bdn) + ['g_esk'], [('rec', es_)])
                    self.act(rec[es_][:], rec[es_][:], AF.Ln, [('rec', es_)], [('rec', es_)])
                    self.act(rec[es_][:], rec[es_][:], AF.Exp, [('rec', es_)], [('rec', es_)], scale=-1.0)
                    for half in range(2):
                        pp = slice(half * 64, (half + 1) * 64)
                        self.tt('dve', attn[pp, 2 * kvh:2 * kvh + 2, qb * 128:(qb + 1) * 128],
                                self.ps[pp, bpv, half * 256:(half + 1) * 256].rearrange('p (g q) -> p g q', g=2),
                                rec[es_][pp, half * 256:(half + 1) * 256].rearrange('p (g q) -> p g q', g=2), ALU.mult,
                                self.pr(bpv) + [('rec', es_)], [('attn', qb, kvh, half)])
                    self.P.op('MARK', None)
            ar = [('attn', qb, kvh, half) for qb in range(4) for kvh in range(2) for half in range(2)]
            self.act(sqa[:], attn[:], AF.Square, ar, ['sqa'])
            bs = self.nb()
            for mc in range(4):
                self.mm(self.ps[:, bs, :], self.ones[:, 1, :], sqa[:, mc, :], mc == 0, mc == 3, ['sqa', 'g_ones'], self.pr(bs))
            self.rstd(rsa[:], self.ps[:, bs, :], self.pr(bs), ['rsa'])
            self.tt('pool', an[s][:], attn[:], _bc(rsa[:], [128, 4, 512], 1), ALU.mult, ar + ['rsa'], [('an', s)])
            self.dma('sp', self.attn_n[:, t0:t0 + 512].rearrange('(c p) t -> p c t', p=128), an[s][:],
                     [('an', s)], [('attn_n', i)], 'a_an%d_st' % s)

        loads(0)
        loads(1)
        self.replay([r_ for r_ in self.capture(lambda: (X(0), Y(0))) if r_[0][0] != 'MARK'])
        for i in range(NT):
            zs = self.capture(lambda: Z(i))
            if i + 1 < NT:
                xs = self.capture(lambda: X(i + 1))
                ys = self.capture(lambda: Y(i + 1))
                self.replay(self.merge(zs, xs + [(('MARK', None), {})] + ys))
            else:
                self.replay([r_ for r_ in zs if r_[0][0] != 'MARK'])
            if i + 2 < NT:
                loads(i + 2)
        self.end()

    def phase_a2(self, l):
        self.begin()
        zw = [self.tile('zw%d' % s, [128, 12, 514], BF16) for s in range(2)]
        u = self.tile('u', [128, 12, 512], F32)
        xo = [self.tile('xo%d' % s, [128, 4, 512], BF16) for s in range(2)]
        vt = [self.tile('vt%d' % s, [128, 4, 512], BF16) for s in range(2)]
        def loads(i):
            s = i % 2
            self.dma('sp', zw[s][:], self.zhy[:, i * 512:i * 512 + 514].rearrange('(c p) t -> p c t', p=128), [], [('zw', s)], 'a2zw%d' % s)
        loads(0)
        for i in range(NT):
            s = i % 2
            t0 = i * 512
            if i + 1 < NT:
                loads(i + 1)
            for j in range(12):
                wc = lambda k, j=j: self.col('w_short', l, 1, k * 12 + j)
                self.act(u[:, j, :], zw[s][:, j, 1:513], AF.Identity, [('zw', s), 'g_cols'], [('u', j)],
                         scale=wc(1), bias=self.col('b_short', l, 1, j))
                e1 = 'dve'
                self.stt(e1, u[:, j, :], zw[s][:, j, 0:512], wc(0), u[:, j, :], ALU.mult, ALU.add, [('zw', s), ('u', j), 'g_cols'], [('u', j)])
                self.stt(e1, u[:, j, :], zw[s][:, j, 2:514], wc(2), u[:, j, :], ALU.mult, ALU.add, [('zw', s), ('u', j), 'g_cols'], [('u', j)])
            self.cp('act', xo[s][:], u[:, 0:4, :], [('u', j) for j in range(4)], [('xo', s)])
            self.tt('dve', vt[s][:], u[:, 4:8, :], u[:, 8:12, :], ALU.mult, [('u', j) for j in range(4, 12)], [('vt', s)])
            self.dma('sp', self.x0s[:, t0:t0 + 512].rearrange('(c p) t -> p c t', p=128), xo[s][:], [('xo', s)], [('x0s', i)], 'a2xo%d_st' % s)
            for j in range(4):
                for h2 in range(2):
                    gg = 2 * j + h2
                    self.dma('sp', self.vfft[gg][4 * i:4 * i + 4, :].rearrange('a (p b) -> p a b', p=64),
                             vt[s][h2 * 64:(h2 + 1) * 64, j, :].rearrange('p (a b) -> p a b', a=4), [('vt', s)], [('vfft', gg, i)],
                             'a2vt%d_%d_st' % (s, gg))
        if os.environ.get('MK_NOAG') is None:
            for gg in range(NG):
                self.allgather(self.vfft2[gg], self.vall2[gg], [('vfft', gg, i) for i in range(NT)], [('vall', gg)])
        self.end()

    def phase_f1(self, l):
        self.begin()
        zf = self.L('zf')
        mmk = self.L('mm')
        w1t = self.tile('w1t', [33, 64], F32)
        w2d = self.tile('w2d', [64, 128], F32)
        self.dma('sp', w1t[:], self.L('filt_w1')[l], [], ['w1t'], 'f1w1')
        for k in range(2):
            self.dma('sp', w2d[:, k * 64:(k + 1) * 64], self.L('filt_w2')[l], [], [('w2d', k)], 'f1w2%d' % k)
        zt_ = [self.tile('zt_%d' % s, [33, 2048], F32) for s in range(2)]
        mk = [self.tile('mk%d' % s, [128, 2048], BF16) for s in range(2)]
        s1 = self.tile('s1', [64, 2048], F32)
        t1 = self.tile('t1', [64, 2048], F32)
        s2 = self.tile('s2', [128, 2048], F32)
        t2 = self.tile('t2', [128, 2048], F32)
        hd = [self.tile('hd%d' % s, [128, 2048], BF16) for s in range(2)]
        f = lambda k: self.fsc[:, l * 4 + k:l * 4 + k + 1]
        it = 0
        for sl in range(2):
            for cch in range(8):
                s = it % 2
                it += 1
                c0 = cch * 2048
                self.dma('sp', zt_[s][:], zf[sl, :, c0:c0 + 2048], [], [('zt_', s)], 'f1zt%d' % s)
                self.dma('sp', mk[s][:], mmk[sl, :, c0:c0 + 2048], [], [('mk', s)], 'f1mk%d' % s)
                b1 = self.nb(4)
                for q4 in range(4):
                    self.mm(self.ps[0:64, b1 + q4, :], w1t[:], zt_[s][:, q4 * 512:(q4 + 1) * 512], True, True,
                            ['w1t', ('zt_', s)], self.pr(b1 + q4))
                self.act(s1[:], self.ps[0:64, b1:b1 + 4, :], AF.Sin, self.pr(b1, 4) + [('fsc', l, 0), ('fsc', l, 1)], ['s1'],
                         scale=f(0)[0:64], bias=f(1)[0:64])
                self.tt('dve', t1[:], s1[:], s1[:], ALU.mult, ['s1'], ['t1'])
                self.ts('dve', t1[:], t1[:], -4.0, 3.0, ALU.mult, ALU.add, ['t1'], ['t1'])
                self.tt('pool', t1[:], t1[:], s1[:], ALU.mult, ['t1', 's1'], ['t1'])
                b2 = self.nb(4)
                for q4 in range(4):
                    self.mm(self.ps[:, b2 + q4, :], w2d[:], t1[:, q4 * 512:(q4 + 1) * 512], True, True,
                            [('w2d', 0), ('w2d', 1), 't1'], self.pr(b2 + q4))
                self.act(s2[:], self.ps[:, b2:b2 + 4, :], AF.Sin, self.pr(b2, 4) + [('fsc', l, 2), ('fsc', l, 3)], ['s2'],
                         scale=f(2), bias=f(3))
                self.tt('dve', t2[:], s2[:], s2[:], ALU.mult, ['s2'], ['t2'])
                self.ts('dve', t2[:], t2[:], -4.0, 3.0, ALU.mult, ALU.add, ['t2'], ['t2'])
                self.tt('pool', t2[:], t2[:], s2[:], ALU.mult, ['t2', 's2'], ['t2'])
                self.tt('pool', hd[s][:], t2[:], mk[s][:], ALU.mult, ['t2', ('mk', s)], [('hd', s)])
                self.dma('sp', self.hdn[sl, :, c0:c0 + 2048], hd[s][:], [('hd', s)], [('hdn', sl, cch)], 'f1hd%d_st' % s)
        self.end()

    def g_stream(self, Gt, kab, ka0, gi_):
        s = gi_ % 2
        n = min(5, KA - ka0)
        self.dma('sp', Gt[s][:, 0:n], self.L('gall')[:, ka0:ka0 + n], [], [('Gt', s)], 'Gt%d' % s)
        return s

    def phase_f2(self, l):
        self.begin()
        hdn = self.tile('hdn', [128, NF], BF16)
        g = self.tile('g', [128, 128, 256], BF16)
        Ysb = self.tile('Ysb', [128, 2, KA, 256], BF16)
        HfT = [self.tile('HfT%d' % s, [128, 5, 2, 256], BF16) for s in range(2)]
        Gt = [self.tile('Gt%d' % s, [128, 5, 3, 128], BF16) for s in range(2)]
        dc = [self.tile('dc%d' % s, [128, 2, 256], F32) for s in range(2)]
        w3f = self.tile('w3f', [128, 256], F32)
        w3s = self.tile('w3s', [128, 256], BF16)
        f1m = self.tile('f1m', [128, 130], BF16)
        negd = self.tile('negd', [128, DH], F32)
        tdec = self.tile('tdec', [128, 2, 128], F32)
        hbt = self.tile('hbt', [1, 256], F32)
        self.dma('sp', f1m[:], self.c_f1m, [], ['f1m'], 'f2f1m')
        self.dma('sp', negd[:], self.c_negd, [], ['negd'], 'f2negd')
        self.dma('sp', tdec[:], self.c_tdec, [], ['tdec'], 'f2tdec')
        w3 = self.L('filt_w3')
        gi_ = 0
        hi_ = 0
        for sl in range(2):
            self.dma('sp', hdn[:], self.hdn[sl], [], ['hdn'], 'f2hdn')
            for hh in range(2):
                for k in range(2):
                    self.dma('sp', w3f[k * 64:(k + 1) * 64, :], w3[l, :, k * 512 + hh * 256:k * 512 + (hh + 1) * 256], [], [('w3f', k)], 'f2w3%d' % k)
                self.cp('dve', w3s[:], w3f[:], [('w3f', 0), ('w3f', 1)], ['w3s'])
                self.dma('sp', hbt[:], self.hbias_d[0:1, l * DH + hh * 256:l * DH + (hh + 1) * 256], [], ['hbt'], 'f2hbt')
                for b2 in range(64):
                    bk = self.nb()
                    ds_ = b2 % 2
                    for u2 in range(2):
                        b = b2 * 2 + u2
                        self.mm(self.ps[:, bk, u2 * 256:(u2 + 1) * 256], hdn[:].rearrange('p (a b) -> p b a', b=128)[:, b, :],
                                w3s[:], True, True, ['hdn', 'w3s'], self.pr(bk))
                        self.act(dc[ds_][:, u2, :], negd[:, hh * 256:(hh + 1) * 256], AF.Exp, ['negd', 'tdec'], [('dc', ds_, u2)],
                                 scale=tdec[:, sl, b:b + 1])
                    self.tt('dve', g[:, 2 * b2:2 * b2 + 2, :], self.ps[:, bk, :].rearrange('p (u c) -> p u c', u=2), dc[ds_][:],
                            ALU.mult, self.pr(bk) + [('dc', ds_, 0), ('dc', ds_, 1)], [('g', b2)])
                    if b2 == 0:
                        self.stt('dve', g[0:1, 0, :], hbt[:], self.e0[0:1, sl:sl + 1],
                                 g[0:1, 0, :], ALU.mult, ALU.add, [('g', 0), 'hbt', 'g_e0'], [('g', 0)])
                gr = [('g', b2) for b2 in range(64)]
                c = 0
                while c < 256:
                    n = min(3, 256 - c)
                    bk = self.nb()
                    for u3 in range(n):
                        self.mm(self.ps[:, bk, u3 * 130:(u3 + 1) * 130], g[:, :, c + u3], f1m[:], True, True, gr + ['f1m'], self.pr(bk))
                    self.cp(self.alt(), Ysb[:, :, :, c:c + n], self.ps[:, bk, 0:n * 130].rearrange('p (c r k) -> p r k c', c=n, r=2),
                            self.pr(bk), [('Ysb', c)])
                    c += n
                yr = [('Ysb', c) for c in range(0, 256, 3)]
                for ka0 in range(0, KA, 5):
                    gs = self.g_stream(Gt, None, ka0, gi_)
                    gi_ += 1
                    hs = hi_ % 2
                    hi_ += 1
                    nk = min(5, KA - ka0)
                    for kq in range(nk):
                        ka = ka0 + kq
                        bk = self.nb()
                        zr = self.ps[:, bk, 0:256]
                        zi = self.ps[:, bk, 256:512]
                        rr = yr + [('Gt', gs)]
                        self.mm(zr, Gt[gs][:, kq, 0, :], Ysb[:, 0, ka, :], True, False, rr, self.pr(bk))
                        self.mm(zr, Gt[gs][:, kq, 2, :], Ysb[:, 1, ka, :], False, True, rr, self.pr(bk))
                        self.mm(zi, Gt[gs][:, kq, 0, :], Ysb[:, 1, ka, :], True, False, rr, self.pr(bk))
                        self.mm(zi, Gt[gs][:, kq, 1, :], Ysb[:, 0, ka, :], False, True, rr, self.pr(bk))
                        self.cp(self.alt(), HfT[hs][:, kq, :, :], self.ps[:, bk, :].rearrange('p (r c) -> p r c', r=2), self.pr(bk), [('HfT', hs, kq)])
                    for g4 in range(4):
                        gg = hh * 4 + g4
                        dst = self.hfs[gg].rearrange('p (k r s c) -> p k r s c', k=KA, r=2, s=2)[:, ka0:ka0 + nk, :, sl, :]
                        self.dma('sp', dst, HfT[hs][:, 0:nk, :, g4 * 64:(g4 + 1) * 64], [('HfT', hs, kq) for kq in range(nk)],
                                 [('hfs', gg, sl, ka0)], 'f2hf%d_%d_st' % (hs, g4))
        self.end()

    def phase_b(self, l):
        self.begin()
        xa = [self.tile('xa%d' % s, [64, CG, 128], BF16) for s in range(2)]
        hft = self.tile('hft', [128, KA, 2, 2, CG], BF16)
        Ysb = self.tile('Ysb', [128, 2, KA, 2, CG], BF16)
        Wt = self.tile('Wt', [128, 2, CG, KA], BF16)
        U = self.tile('U', [KA, 2, 128, CG], BF16)
        Yo = self.tile('Yo', [64, CG, 128], BF16)
        A = self.tile('A', [128, 4, 2, 2 * CG], F32)
        B1 = self.tile('B1', [128, 4, 2 * CG], F32)
        B2 = self.tile('B2', [128, 4, 2 * CG], F32)
        Dr = self.tile('Dr', [128, 4, 2 * CG], F32)
        Di = self.tile('Di', [128, 4, 2 * CG], F32)
        Gt = [self.tile('Gt%d' % s, [128, 5, 3, 128], BF16) for s in range(2)]
        Ht = [self.tile('Ht%d' % s, [KA, 16, 2, 64], BF16) for s in range(2)]
        f1m = self.tile('f1m', [128, 130], BF16)
        e12 = self.tile('e12', [128, 2, 256], BF16)
        self.dma('sp', f1m[:], self.c_f1m, [], ['f1m'], 'bf1m')
        self.dma('sp', e12[:], self.c_e12, [], ['e12'], 'be12')
        hall = self.L('hall')
        gi_ = 0
        hi_ = 0
        for gg in range(NG):
            c0 = gg * CG
            for sl in range(2):
                self.dma('sp', xa[sl][:], self.vall[gg][sl * 64:(sl + 1) * 64, :].rearrange('a (c b) -> a c b', c=CG),
                         [], [('xa', sl)], 'bxa%d' % sl)
            self.dma('sp', hft[:], self.hfs[gg].rearrange('p (k r s c) -> p k r s c', k=KA, r=2, s=2), [], ['hft'], 'bhft')
            for sl in range(2):
                c = 0
                while c < CG:
                    n = min(3, CG - c)
                    bk = self.nb()
                    for u3 in range(n):
                        self.mm(self.ps[:, bk, u3 * 130:(u3 + 1) * 130], xa[sl][:, c + u3, :], f1m[0:64, :], True, True,
                                [('xa', sl), 'f1m'], self.pr(bk))
                    self.cp(self.alt(), Ysb[:, :, :, sl, c:c + n], self.ps[:, bk, 0:n * 130].rearrange('p (c r k) -> p r k c', c=n, r=2),
                            self.pr(bk), [('Ysb', sl, c)])
                    c += n
            yr = [('Ysb', sl, c) for sl in range(2) for c in range(0, CG, 3)]
            for ka0 in range(0, KA, 4):
                nk = min(4, KA - ka0)
                bz = self.nb(2)
                for kq in range(nk):
                    ka = ka0 + kq
                    if ka % 5 == 0:
                        gs = self.g_stream(Gt, None, ka, gi_)
                        gi_ += 1
                    gq_ = ka % 5
                    zr = self.ps[:, bz + kq // 2, (kq % 2) * 256:(kq % 2) * 256 + 128]
                    zi = self.ps[:, bz + kq // 2, (kq % 2) * 256 + 128:(kq % 2) * 256 + 256]
                    rr = yr + [('Gt', gs)]
                    yre = Ysb[:, 0, ka, :, :].rearrange('p s c -> p (s c)')
                    yim = Ysb[:, 1, ka, :, :].rearrange('p s c -> p (s c)')
                    w_ = self.pr(bz + kq // 2)
                    self.mm(zr, Gt[gs][:, gq_, 0, :], yre, True, False, rr, w_)
                    self.mm(zr, Gt[gs][:, gq_, 2, :], yim, False, True, rr, w_)
                    self.mm(zi, Gt[gs][:, gq_, 0, :], yim, True, False, rr, w_)
                    self.mm(zi, Gt[gs][:, gq_, 1, :], yre, False, True, rr, w_)
                zps = self.ps[:, bz:bz + 2, :].rearrange('p b (k r n) -> p (b k) r n', k=2, r=2)[:, 0:nk]
                hf_ = hft[:, ka0:ka0 + nk].rearrange('p k r s c -> p k r (s c)')
                pz = self.pr(bz, 2)
                self.tt('dve', A[:, 0:nk], zps, hf_, ALU.mult, pz + ['hft'], ['A'])
                self.tt('dve', B1[:, 0:nk], zps[:, :, 0, :], hf_[:, :, 1, :], ALU.mult, pz + ['hft'], ['B1'])
                self.tt('dve', B2[:, 0:nk], zps[:, :, 1, :], hf_[:, :, 0, :], ALU.mult, pz + ['hft'], ['B2'])
                self.tt('dve', Dr[:, 0:nk], A[:, 0:nk, 0, :], A[:, 0:nk, 1, :], ALU.subtract, ['A'], ['Dr'])
                self.tt('pool', Di[:, 0:nk], B1[:, 0:nk], B2[:, 0:nk], ALU.add, ['B1', 'B2'], ['Di'])
                for ri, Dx in ((0, Dr), (1, Di)):
                    self.tt('pool' if ri == 0 else 'dve', Wt[:, ri, :, ka0:ka0 + nk], Dx[:, 0:nk, 0:CG].rearrange('p k c -> p c k'),
                            Dx[:, 0:nk, CG:2 * CG].rearrange('p k c -> p c k'), ALU.add, ['Dr' if ri == 0 else 'Di'], [('Wt', ka0, ri)])
            wr = [('Wt', ka0, ri) for ka0 in range(0, KA, 4) for ri in range(2)]
            for c in range(0, CG, 2):
                bk = self.nb()
                for u2 in range(2):
                    o_ = self.ps[0:KA, bk, u2 * 256:(u2 + 1) * 256]
                    self.mm(o_, Wt[:, 0, c + u2, :], e12[:, 0, :], True, False, wr + ['e12'], self.pr(bk))
                    self.mm(o_, Wt[:, 1, c + u2, :], e12[:, 1, :], False, True, wr + ['e12'], self.pr(bk))
                self.cp(self.alt(), U[:, :, :, c:c + 2], self.ps[0:KA, bk, :].rearrange('p (c r b) -> p r b c', c=2, r=2),
                        self.pr(bk), [('U', c)])
            ur = [('U', c) for c in range(0, CG, 2)]
            for b0 in range(0, 128, 8):
                if b0 % 16 == 0:
                    hs = hi_ % 2
                    hi_ += 1
                    self.dma('sp', Ht[hs][:], hall[:, b0:b0 + 16], [], [('Ht', hs)], 'bHt%d' % hs)
                bk = self.nb()
                for q8 in range(8):
                    bp = b0 + q8
                    o_ = self.ps[0:64, bk, q8 * 64:(q8 + 1) * 64]
                    self.mm(o_, Ht[hs][:, bp % 16, 0, :], U[:, 0, bp, :], True, False, ur + [('Ht', hs)], self.pr(bk))
                    self.mm(o_, Ht[hs][:, bp % 16, 1, :], U[:, 1, bp, :], False, True, ur + [('Ht', hs)], self.pr(bk))
                self.cp(self.alt(), Yo[:, :, b0:b0 + 8], self.ps[0:64, bk, :].rearrange('p (b c) -> p c b', b=8), self.pr(bk), [('Yo', b0)])
            self.dma('sp', self.yconv[c0:c0 + CG, :].rearrange('c (a b) -> a c b', a=64), Yo[:],
                     [('Yo', b0) for b0 in range(0, 128, 8)], [('yconv', gg)], 'bYo_st')
        self.end()

    def phase_c1a(self, l):
        self.begin()
        w_out = self.L('w_out')
        wo = self.tile('wo', [128, 8, D], BF16)
        stg = [self.tile('c1a_stg%d' % s, [128, D], F32) for s in range(2)]
        for kc in range(8):
            sc = self.col('g_ao', l, 1, kc) if kc < 4 else self.col('g_ho', l, 1, kc - 4)
            self.load_w(w_out[l, kc * 128:(kc + 1) * 128, :], D, sc, stg, 'c1astg', [(wo[:, kc, :], 0, D, None, ('wo', kc))])
        mix = [self.tile('mix%d' % s, [128, 8, 512], BF16) for s in range(2)]
        xy = [self.tile('xy%d' % s, [128, 2, 4, 512], BF16) for s in range(2)]
        ht = [self.tile('ht%d' % s, [128, 8, 512], F32) for s in range(2)]
        hy = self.tile('hy', [128, 4, 512], F32)
        sqh = self.tile('sqh', [128, 4, 512], BF16)
        rsh = self.tile('rsh', [128, 512], F32)
        sq2 = self.tile('sq2', [128, 8, 512], BF16)
        rs2 = self.tile('rs2', [128, 512], F32)
        n2 = [self.tile('n2_%d' % s, [128, 8, 512], BF16) for s in range(2)]
        ed = self.tile('ed', [128, 8, 2], BF16)
        def loads(i):
            s = i % 2
            cs = slice(i * 512, i * 512 + 512)
            self.dma('sp', mix[s][:, 0:4, :], self.attn_n[:, cs].rearrange('(c p) t -> p c t', p=128), [], [('mixa', s)], 'c1a_ma%d' % s)
            self.dma('sp', xy[s][:, 0], self.x0s[:, cs].rearrange('(c p) t -> p c t', p=128), [], [('xy', s, 0)], 'c1a_x%d' % s)
            self.dma('sp', xy[s][:, 1], self.yconv[:, cs].rearrange('(c p) t -> p c t', p=128), [], [('xy', s, 1)], 'c1a_y%d' % s)
            self.dma('sp', ht[s][:], self.hres[:, cs].rearrange('(c p) t -> p c t', p=128), [], [('ht', s)], 'c1a_h%d' % s)
        def P1(i):
            s = i % 2
            self.tt('dve', hy[:], xy[s][:, 0], xy[s][:, 1], ALU.mult, [('xy', s, 0), ('xy', s, 1)], ['hy'])
            self.act(sqh[:], hy[:], AF.Square, ['hy'], ['sqh'])
            bk = self.nb()
            for mc in range(4):
                self.mm(self.ps[:, bk, :], self.ones[:, 1, :], sqh[:, mc, :], mc == 0, mc == 3, ['sqh', 'g_ones'], self.pr(bk))
            self.rstd(rsh[:], self.ps[:, bk, :], self.pr(bk), ['rsh'])
            self.tt('pool', mix[s][:, 4:8, :], hy[:], _bc(rsh[:], [128, 4, 512], 1), ALU.mult, ['hy', 'rsh'], [('mixh', s)])

        def P2(i):
            s = i % 2
            cs = slice(i * 512, i * 512 + 512)
            for g2 in range(2):
                bo = self.nb(4)
                for j in range(4):
                    mc = g2 * 4 + j
                    for kc in range(8):
                        self.mm(self.ps[:, bo + j, :], wo[:, kc, mc * 128:(mc + 1) * 128], mix[s][:, kc, :], kc == 0, kc == 7,
                                [('mixa', s), ('mixh', s), ('wo', kc)], self.pr(bo + j))
                self.tt('dve', ht[s][:, g2 * 4:(g2 + 1) * 4, :], ht[s][:, g2 * 4:(g2 + 1) * 4, :], self.ps[:, bo:bo + 4, :], ALU.add,
                        [('ht', s)] + self.pr(bo, 4), [('ht', s)])
            self.dma('sp', self.hres[:, cs].rearrange('(c p) t -> p c t', p=128), ht[s][:], [('ht', s)], [('hres', i)], 'c1a_h%d_st' % s)

        def P3(i):
            s = i % 2
            t0 = i * 512
            self.act(sq2[:], ht[s][:], AF.Square, [('ht', s)], ['sq2'])
            bk = self.nb()
            for kc in range(8):
                self.mm(self.ps[:, bk, :], self.ones[:, 0, :], sq2[:, kc, :], kc == 0, kc == 7, ['sq2', 'g_ones'], self.pr(bk))
            self.rstd(rs2[:], self.ps[:, bk, :], self.pr(bk), ['rs2'])
            for hf in range(2):
                eng = 'dve' if hf == 0 else 'pool'
                self.tt(eng, n2[s][:, hf * 4:(hf + 1) * 4, :], ht[s][:, hf * 4:(hf + 1) * 4, :], _bc(rs2[:], [128, 4, 512], 1), ALU.mult,
                        [('ht', s), 'rs2'], [('n2', s, hf)])
            rd = [('n2', s, 0), ('n2', s, 1)]
            self.dma('sp', self.n2s[:, 1 + t0:1 + t0 + 512].rearrange('(c p) t -> p c t', p=128), n2[s][:], rd, [('n2s', i)], 'c1a_n%d_st' % s)
            if i == 0:
                self.dma('sp', self.xn_in[0:1, :].rearrange('o (c p) -> p c o', p=128), n2[s][:, :, 0:1], rd, ['xn0'], 'c1a_e0', slow=True)
            if i == NT - 1:
                self.dma('sp', self.xn_in[1:2, :].rearrange('o (c p) -> p c o', p=128), n2[s][:, :, 511:512], rd, ['xn1'], 'c1a_e1', slow=True)

        loads(0)
        loads(1)
        P1(0)
        for i in range(NT):
            if i + 1 < NT:
                P1(i + 1)
            P2(i)
            P3(i)
            if i + 2 < NT:
                loads(i + 2)
        self.allgather(self.xn_in, self.xn_out, ['xn0', 'xn1'], ['xn_out'])
        for k, (row, col) in enumerate([(1, 0), (2, T + 1)]):
            self.dma('sp', ed[:, :, k:k + 1], self.xn_out[row:row + 1, :].rearrange('o (c p) -> p c o', p=128), ['xn_out'], [('ed', k)], 'c1a_ed%d' % k, slow=True)
            self.ts('dve', ed[:, :, k:k + 1], ed[:, :, k:k + 1], self.edge[:, k:k + 1], None, ALU.mult, None, [('ed', k), 'g_edge'], [('ed', k)])
            self.dma('sp', self.n2s[:, col:col + 1].rearrange('(c p) t -> p c t', p=128), ed[:, :, k:k + 1], [('ed', k)], [('n2sh', k)], 'c1a_ed%d_st' % k, slow=True)
        self.end()

    def phase_c1b(self, l):
        self.begin()
        w_up = self.L('w_up')
        wu = self.tile('wu', [128, 8, 2 * DFF], BF16)
        stg = [self.tile('c1b_stg%d' % s, [128, 2816], F32) for s in range(2)]
        for kc in range(8):
            for hh in range(2):
                self.load_w(w_up[l, kc * 128:(kc + 1) * 128, hh * DFF:(hh + 1) * DFF], DFF, self.col('g_ffn', l, 1, kc), stg, 'c1bstg',
                            [(wu[:, kc, hh * DFF:(hh + 1) * DFF], 0, DFF, None, ('wu', kc, hh))])
        n2t = [self.tile('n2t%d' % s, [128, 8, 512], BF16) for s in range(2)]
        at = [self.tile('at%d' % s, [128, 22, 510], BF16) for s in range(2)]
        ntl = (T + 509) // 510
        NQ = 3
        ua = [self.tile('ua%d' % q, [128, 510], F32) for q in range(NQ)]
        ug = [self.tile('ug%d' % q, [128, 510], F32) for q in range(NQ)]
        sg = [self.tile('sg%d' % q, [128, 510], F32) for q in range(NQ)]

        def loads(i):
            s = i % 2
            T0 = 510 * i
            nin = min(510, T - T0) + 2
            self.dma('sp', n2t[s][:, :, 0:nin], self.n2s[:, T0:T0 + nin].rearrange('(c p) t -> p c t', p=128), [], [('n2t', s)], 'c1b_n%d' % s)

        def front(i, j, q):
            s = i % 2
            nout = min(510, T - 510 * i)
            nin = nout + 2
            bk = self.nb(2)
            for hh in range(2):
                for kc in range(8):
                    self.mm(self.ps[:, bk + hh, 0:nin], wu[:, kc, hh * DFF + j * 128:hh * DFF + (j + 1) * 128], n2t[s][:, kc, 0:nin],
                            kc == 0, kc == 7, [('n2t', s), ('wu', kc, hh)], self.pr(bk + hh))
            for hh, ut in ((0, ua[q]), (1, ug[q])):
                wc = lambda k, hh=hh, j=j: self.col('w_ffc', l, 1, k * 44 + hh * 22 + j)
                pb = self.ps[:, bk + hh, :]
                rn = ('u', hh, q)
                self.act(ut[:, 0:nout], pb[:, 1:1 + nout], AF.Identity, self.pr(bk + hh) + ['g_cols'], [rn],
                         scale=wc(1), bias=self.col('b_ffc', l, 1, hh * 22 + j))
                self.stt('dve', ut[:, 0:nout], pb[:, 0:nout], wc(0), ut[:, 0:nout], ALU.mult, ALU.add, self.pr(bk + hh) + [rn, 'g_cols'], [rn])
                self.stt('dve', ut[:, 0:nout], pb[:, 2:2 + nout], wc(2), ut[:, 0:nout], ALU.mult, ALU.add, self.pr(bk + hh) + [rn, 'g_cols'], [rn])

        def back(i, j, q):
            s = i % 2
            nout = min(510, T - 510 * i)
            self.act(sg[q][:, 0:nout], ug[q][:, 0:nout], AF.Silu, [('u', 1, q)], [('sg', q)])
            self.tt('pool', at[s][:, j, 0:nout], sg[q][:, 0:nout], ua[q][:, 0:nout], ALU.mult, [('sg', q), ('u', 0, q)], [('at', s, j)])
            if j == 21:
                T0 = 510 * i
                self.dma('sp', self.acts[:, T0:T0 + nout].rearrange('(c p) t -> p c t', p=128), at[s][:, :, 0:nout],
                         [('at', s, jx) for jx in range(22)], [('acts', i)], 'c1b_a%d_st' % s)

        loads(0)
        seq = [(i, j) for i in range(ntl) for j in range(22)]
        for n_, (i, j) in enumerate(seq):
            if j == 0 and i + 1 < ntl:
                loads(i + 1)
            front(i, j, n_ % NQ)
            if n_ >= 1:
                pi, pj = seq[n_ - 1]
                back(pi, pj, (n_ - 1) % NQ)
        pi, pj = seq[-1]
        back(pi, pj, (len(seq) - 1) % NQ)
        self.end()

    def phase_c2(self, l, last):
        self.begin()
        wd = self.tile('wd', [128, 22, D], BF16)
        wg = self.tile('wg', [128, 8, D], BF16)
        wp = self.tile('wp', [128, 2, D], BF16)
        stg = [self.tile('c2_stg%d' % s, [128, D], F32) for s in range(2)]
        for kc in range(22):
            self.load_w(self.L('w_down')[l, kc * 128:(kc + 1) * 128, :], D, None, stg, 'c2stg', [(wd[:, kc, :], 0, D, None, ('wd', kc))])
        for kc in range(8):
            self.load_w(self.L('w_ple_gate')[l, kc * 128:(kc + 1) * 128, :], D, None, stg, 'c2stg', [(wg[:, kc, :], 0, D, None, ('wg', kc))])
        for kc in range(2):
            self.load_w(self.L('w_ple_proj')[l, kc * 128:(kc + 1) * 128, :], D, None, stg, 'c2stg', [(wp[:, kc, :], 0, D, None, ('wp', kc))])
        at = [self.tile('at%d' % s, [128, 22, 512], BF16) for s in range(2)]
        ht = [self.tile('ht%d' % s, [128, 8, 512], F32) for s in range(2)]
        pt = [self.tile('pt%d' % s, [128, 4, DPLE], F32) for s in range(2)]
        pT = self.tile('pT', [128, 2, 512], BF16)
        hb = self.tile('hb', [128, 8, 512], BF16)
        sgm = self.tile('sgm', [128, 4, 512], F32)
        yo = self.tile('yo', [128, 4, D], F32) if last else None
        p_d = self.L('p')
        def loads(i):
            s = i % 2
            cs = slice(i * 512, i * 512 + 512)
            self.dma('sp', at[s][:], self.acts[:, cs].rearrange('(c p) t -> p c t', p=128), [], [('at', s)], 'c2_a%d' % s)
            self.dma('sp', ht[s][:], self.hres[:, cs].rearrange('(c p) t -> p c t', p=128), [], [('ht', s, 0), ('ht', s, 1)], 'c2_h%d' % s)
            self.dma('sp', pt[s][:], p_d[l, cs, :].rearrange('(b p) f -> p b f', p=128), [], [('pt', s)], 'c2_p%d' % s)
        loads(0)
        for i in range(NT):
            s = i % 2
            t0 = i * 512
            cs = slice(t0, t0 + 512)
            if i + 1 < NT:
                loads(i + 1)
            for pc in range(2):
                bk = self.nb()
                for blk in range(4):
                    self.tp(self.ps[:, bk, blk * 128:(blk + 1) * 128], pt[s][:, blk, pc * 128:(pc + 1) * 128], self.ident[:],
                            [('pt', s), 'g_ident'], self.pr(bk))
                self.cp(self.alt(), pT[:, pc, :], self.ps[:, bk, :], self.pr(bk), [('pT', pc)])
            for g2 in range(2):
                bo = self.nb(4)
                for j in range(4):
                    mc = g2 * 4 + j
                    for kc in range(22):
                        self.mm(self.ps[:, bo + j, :], wd[:, kc, mc * 128:(mc + 1) * 128], at[s][:, kc, :], kc == 0, kc == 21,
                                [('at', s), ('wd', kc)], self.pr(bo + j))
                hs_ = ht[s][:, g2 * 4:(g2 + 1) * 4, :]
                self.tt('dve', hs_, hs_, self.ps[:, bo:bo + 4, :], ALU.add, [('ht', s, g2)] + self.pr(bo, 4), [('ht', s, g2)])
                self.cp('act', hb[:, g2 * 4:(g2 + 1) * 4, :], hs_, [('ht', s, g2)], [('hb', g2)])
            for g2 in range(2):
                bo = self.nb(4)
                for j in range(4):
                    mc = g2 * 4 + j
                    for kc in range(8):
                        self.mm(self.ps[:, bo + j, :], wg[:, kc, mc * 128:(mc + 1) * 128], hb[:, kc, :], kc == 0, kc == 7,
                                [('hb', 0), ('hb', 1), ('wg', kc)], self.pr(bo + j))
                self.act(sgm[:], self.ps[:, bo:bo + 4, :], AF.Sigmoid, self.pr(bo, 4), ['sgm'])
                bp = self.nb(4)
                for j in range(4):
                    mc = g2 * 4 + j
                    for kc in range(2):
                        self.mm(self.ps[:, bp + j, :], wp[:, kc, mc * 128:(mc + 1) * 128], pT[:, kc, :], kc == 0, kc == 1,
                                [('pT', 0), ('pT', 1), ('wp', kc)], self.pr(bp + j))
                self.tt('dve', sgm[:], sgm[:], self.ps[:, bp:bp + 4, :], ALU.mult, ['sgm'] + self.pr(bp, 4), ['sgm'])
                hs_ = ht[s][:, g2 * 4:(g2 + 1) * 4, :]
                self.tt('pool', hs_, hs_, sgm[:], ALU.add, [('ht', s, g2), 'sgm'], [('ht', s, g2)])
            hr_ = [('ht', s, 0), ('ht', s, 1)]
            if not last:
                self.dma('sp', self.hres[:, cs].rearrange('(c p) t -> p c t', p=128), ht[s][:], hr_, [('hres', i)], 'c2_h%d_st' % s)
            else:
                for blk in range(4):
                    for g2 in range(2):
                        bk = self.nb()
                        for j in range(4):
                            mc = g2 * 4 + j
                            self.tp(self.ps[:, bk, j * 128:(j + 1) * 128], ht[s][:, mc, blk * 128:(blk + 1) * 128], self.ident[:],
                                    hr_ + ['g_ident'], self.pr(bk))
                        self.cp(self.alt(), yo[:, blk, g2 * 512:(g2 + 1) * 512], self.ps[:, bk, :], self.pr(bk), [('yo', blk, g2)])
                self.dma('sp', self.y[cs, :].rearrange('(b p) f -> p b f', p=128), yo[:],
                         [('yo', blk, g2) for blk in range(4) for g2 in range(2)], [('y', i)], 'c2_y_st')
        self.end()

    def build(self):
        self.phase_p0()
        for l in range(self.depth):
            last = (l == self.depth - 1)
            for name in ['a0', 'a', 'a2', 'f1', 'f2', 'b', 'c1a', 'c1b', 'c2']:
                fn = getattr(self, 'phase_' + name, None)
                if fn is None:
                    return
                if name == 'c2':
                    fn(l, last)
                else:
                    fn(l)
                if self.stop == (name, l):
                    return


def _cols_table(b, W):
    L = DEPTH
    tab = np.zeros((128, b.ncol), np.float32)

    def put(name, arr):
        o, n = b.colspec[name]
        assert arr.shape == (128, n), (name, arr.shape, n)
        tab[:, o:o + n] = arr

    def chunks(v, nch):
        return v.reshape(L, nch, 128).transpose(2, 0, 1).reshape(128, L * nch)

    put('g_mix', chunks(W['rms_mix'], 8))
    put('g_ffn', chunks(W['rms_ffn'], 8))
    put('gq', np.tile(W['q_norm'].T, (2, 1)))
    put('gk', np.tile(W['k_norm'].T, (2, 1)))
    sk = W['sink'].reshape(L, 2, 4)[:, :, [0, 2, 1, 3]].reshape(1, L * 8)
    put('sink', np.tile(sk, (128, 1)))
    put('w_short', W['w_short'].reshape(L, 3, 12, 128).transpose(3, 0, 1, 2).reshape(128, L * 36))
    put('b_short', chunks(W['b_short'], 12))
    put('g_ao', chunks(W['norm_attn_out'], 4))
    put('g_ho', chunks(W['norm_hyena_out'], 4))
    put('w_ffc', W['w_ffconv'].reshape(L, 3, 44, 128).transpose(3, 0, 1, 2).reshape(128, L * 132))
    put('b_ffc', chunks(W['b_ffconv'], 44))
    for nm, key in [('fb1', 'filt_b1'), ('ffr1', 'filt_freq1'), ('fb2', 'filt_b2'), ('ffr2', 'filt_freq2')]:
        put(nm, np.tile(W[key].T, (2, 1)))
    return tab


_BUILD_CACHE = {}


def _get_builder(depth, debug, stop):
    key = (depth, debug, stop)
    if key not in _BUILD_CACHE:
        b = Builder(depth, debug, stop)
        b.declare()
        b.setup_globals()
        b.build()
        es = contextlib.ExitStack()
        b.P.emit(es)
        b._es = es
        _BUILD_CACHE[key] = b
    return _BUILD_CACHE[key]


def _run(inputs, depth=DEPTH, debug=False, stop=None):
    W = {k: np.asarray(v, dtype=np.float32) for k, v in inputs.items()}
    b = _get_builder(depth, debug, stop)
    sh = _shared_consts()
    cols = _cols_table(b, W)
    xp = W['x_prompt'][0]
    xs = W['x_sample']
    pp = W['p_prompt'][:, 0]
    psm = W['p_sample']
    in_maps = []
    for rank in range(N_CORES):
        kind, idx = _unit_of_rank(rank)
        rc = _rank_consts(rank)
        if kind == 'p':
            x = xp[idx * T:(idx + 1) * T]
            p = pp[:, idx * T:(idx + 1) * T]
        else:
            x = xs[idx]
            p = psm[:, idx]
        m = {
            'x': np.ascontiguousarray(x), 'p': np.ascontiguousarray(p),
            'w_in': W['w_in'], 'w_out': W['w_out'], 'w_up': W['w_up'], 'w_down': W['w_down'],
            'w_ple_gate': W['w_ple_gate'], 'w_ple_proj': W['w_ple_proj'],
            'filt_w1': W['filt_w1'], 'filt_w2': W['filt_w2'], 'filt_w3': W['filt_w3'],
            'cols': cols, 'hbias': W['hyena_bias'].reshape(1, -1),
            'ident_f': sh['ident_f'], 'ones_b': sh['ones_b'], 'prot_b': sh['prot_b'],
            'masks': rc['masks'], 'edge': rc['edge'], 'cstab': rc['cstab'],
            'f1m': sh['f1m'], 'gall': sh['gall'], 'e12': sh['e12'], 'hall': sh['hall'], 'negd': sh['negd'],
            'zf': rc['zf'], 'mm': rc['mm'], 'tdec': rc['tdec'], 'e0': rc['e0'],
        }
        in_maps.append({k: v for k, v in m.items() if k in b.inputs})
    res = run_bass_kernel_spmd(b.nc, in_maps, core_ids=list(range(N_CORES)))
    return b, res


def kernel(**inputs):
    b, res = _run(inputs)
    ys = [np.asarray(res.results[r]['y'], dtype=np.float32) for r in range(6)]
    y_prompt = np.concatenate([ys[0], ys[1]], axis=0)[None]
    y_sample = np.stack(ys[2:6], axis=0)
    return (y_prompt, y_sample)
```

```python
import os
import math
import contextlib
import numpy as np
import ml_dtypes
import concourse.bass as bass
import concourse.mybir as mybir
from concourse.bass_utils import run_bass_kernel_spmd

F32 = mybir.dt.float32
BF16 = mybir.dt.bfloat16
AF = mybir.ActivationFunctionType
ALU = mybir.AluOpType
BF = ml_dtypes.bfloat16

D = 1024
DEPTH = 4
T = 8192
NT = 16
HALO = 128
NW = T + 2 * HALO
DQ = 512
DH = 512
DFF = 2816
DPLE = 256
EPS = 1e-6
NF = 16384
KA = 65
CG = 64
NG = DH // CG
ROPE_THETA = 500000.0
N_CORES = 8


class Prog:
    def __init__(self, nc):
        self.nc = nc
        self.ops = []
        self.lastw = {}
        self.readers = {}
        self.lastdma = {}
        self.eng = {'pe': nc.tensor, 'act': nc.scalar, 'dve': nc.vector, 'pool': nc.gpsimd, 'sp': nc.sync}
        self.last_on = {}
        self.n_cc = 0

    def op(self, eng, fn, r=(), w=(), dma=None, cc=False):
        idx = len(self.ops)
        deps = set()
        for x in r:
            p = self.lastw.get(x)
            if p is not None:
                deps.add(p)
        for x in w:
            p = self.lastw.get(x)
            if p is not None:
                deps.add(p)
            deps.update(self.readers.get(x, ()))
        if dma is not None:
            p = self.lastdma.get(dma)
            if p is not None:
                deps.add(p)
            self.lastdma[dma] = idx
        for x in r:
            self.readers.setdefault(x, []).append(idx)
        for x in w:
            self.lastw[x] = idx
            self.readers[x] = []
        deps.discard(idx)
        self.ops.append(dict(eng=eng, fn=fn, deps=deps, dma=dma, cc=cc, bar=False))
        if not cc:
            self.last_on[eng] = idx
        else:
            self.sticky = getattr(self, 'sticky', {})
            for x in w:
                self.sticky[x] = idx
        return idx

    def barrier(self):
        deps = set(self.last_on.values()) | set(self.lastdma.values())
        for k, e in enumerate(('pe', 'act', 'dve', 'pool', 'sp')):
            self.ops.append(dict(eng=e, fn=None, deps=set(deps), dma=None, cc=False, bar=True, reset=(k == 0)))
        self.lastw = dict(getattr(self, 'sticky', {}))
        self.readers = {}

    def emit(self, es):
        nc = self.nc
        ops = self.ops
        def pe2pe(p, o):
            return (p['dma'] is None and not p['cc'] and p['eng'] == 'pe' and o['eng'] == 'pe'
                    and o['dma'] is None and o['fn'] is not None)
        sig = [False] * len(ops)
        for o in ops:
            latest = {}
            for d in o['deps']:
                p = ops[d]
                if pe2pe(p, o):
                    continue
                if p['dma'] is not None or p['cc']:
                    sig[d] = True
                else:
                    if latest.get(p['eng'], -1) < d:
                        latest[p['eng']] = d
            for d in latest.values():
                sig[d] = True
            o['bind'] = set(latest.values())
        sems = {}

        def getsem(name):
            if name not in sems:
                sems[name] = es.enter_context(nc.semaphore(name))
            return sems[name]

        cnt = {}
        val = [None] * len(ops)
        chan = [None] * len(ops)
        waited = {e: {} for e in self.eng}
        n_wait = 0
        keyslot = {}
        ncc = 0
        for i, o in enumerate(ops):
            e = o['eng']
            eobj = self.eng[e]
            if o.get('reset'):
                keyslot = {}
            need = {}
            for d in o['deps']:
                if val[d] is None:
                    continue
                p = ops[d]
                if p['dma'] is None and not p['cc'] and d not in o['bind']:
                    continue
                c = chan[d]
                if need.get(c, 0) < val[d]:
                    need[c] = val[d]
            for c, v in need.items():
                if waited[e].get(c, 0) >= v:
                    continue
                eobj.wait_ge(getsem(c), v)
                waited[e][c] = v
                n_wait += 1
            if o['fn'] is None:
                continue
            ins = o['fn'](eobj)
            if o['cc']:
                c = 'cc%d' % (ncc % 8)
                ncc += 1
                cnt[c] = cnt.get(c, 0) + 1
                ins.then_inc(getsem(c), 1)
                chan[i] = c
                val[i] = cnt[c]
            elif o['dma'] is not None:
                if o['dma'] not in keyslot:
                    keyslot[o['dma']] = len(keyslot)
                c = 'dslot%d' % keyslot[o['dma']]
                cnt[c] = cnt.get(c, 0) + 16
                ins.then_inc(getsem(c), 16)
                chan[i] = c
                val[i] = cnt[c]
            elif sig[i]:
                c = 'e_' + e
                cnt[c] = cnt.get(c, 0) + 1
                ins.then_inc(getsem(c), 1)
                chan[i] = c
                val[i] = cnt[c]
        for e in ('sp',):
            eobj = self.eng[e]
            for c, v in cnt.items():
                if waited[e].get(c, 0) < v:
                    eobj.wait_ge(getsem(c), v)
        self.stats = dict(n_ops=len(ops), n_wait=n_wait, n_sems=len(sems))


def _unit_of_rank(rank):
    return [('p', 0), ('p', 1), ('s', 0), ('s', 1), ('s', 2), ('s', 3), ('s', 2), ('s', 3)][rank]


_CONST_CACHE = {}


def _shared_consts():
    if 'shared' in _CONST_CACHE:
        return _CONST_CACHE['shared']
    c = {}
    c['ident_f'] = np.eye(128, dtype=np.float32)
    ones = np.zeros((128, 4, 128), np.float32)
    ones[:, 0, :] = 1.0 / 1024.0
    ones[:, 1, :] = 1.0 / 512.0
    ones[:, 2, :] = 1.0
    blk = np.zeros((128, 128), np.float32)
    blk[:64, :64] = 1.0 / 64.0
    blk[64:, 64:] = 1.0 / 64.0
    ones[:, 3, :] = blk
    c['ones_b'] = ones.astype(BF)
    prot = np.zeros((128, 128), np.float32)
    for p in range(128):
        d = p % 64
        if d < 8:
            prot[p + 8, p] = -1.0
        elif d < 16:
            prot[p - 8, p] = 1.0
    c['prot_b'] = prot.astype(BF)
    j = np.arange(128)[:, None]
    i = np.arange(128)[None, :]
    c['mprev'] = (j >= i).astype(np.float32)
    c['mnext'] = (j <= i).astype(np.float32)
    a = np.arange(128, dtype=np.float64)[:, None]
    ka = np.arange(KA, dtype=np.float64)[None, :]
    th = 2 * np.pi * a * ka / 128.0
    c['f1m'] = np.concatenate([np.cos(th), -np.sin(th)], axis=1).astype(BF)
    b = np.arange(128, dtype=np.float64)[:, None, None]
    kav = np.arange(KA, dtype=np.float64)[None, :, None]
    kb = np.arange(128, dtype=np.float64)[None, None, :]
    th = 2 * np.pi * b * (kav + 128.0 * kb) / NF
    gall = np.stack([np.cos(th), -np.sin(th), np.sin(th)], axis=2)
    c['gall'] = gall.astype(BF)
    kbv = np.arange(128, dtype=np.float64)[:, None]
    bp = np.arange(128, dtype=np.float64)[None, :]
    th = 2 * np.pi * kbv * bp / 128.0
    e1 = np.concatenate([np.cos(th), np.sin(th)], axis=1)
    e2 = np.concatenate([-np.sin(th), np.cos(th)], axis=1)
    c['e12'] = np.stack([e1, e2], axis=1).astype(BF)
    kav = np.arange(KA, dtype=np.float64)[:, None, None]
    bpv = np.arange(128, dtype=np.float64)[None, :, None]
    ap = np.arange(64, dtype=np.float64)[None, None, :]
    ph = 2 * np.pi * kav * (bpv + 128.0 * ap) / NF
    wt = np.full((KA, 1, 1), 2.0)
    wt[0] = 1.0
    wt[64] = 1.0
    hall = np.stack([wt / NF * np.cos(ph), -wt / NF * np.sin(ph)], axis=2)
    c['hall'] = hall.astype(BF)
    deltas = np.linspace(math.log(1e-2) / 1.5, math.log(1e-2) / 0.3, DH).astype(np.float32)
    c['negd'] = np.tile(-np.abs(deltas)[None, :], (128, 1)).astype(np.float32)
    _CONST_CACHE['shared'] = c
    return c


def _zfeat(L):
    key = ('z', L)
    if key in _CONST_CACHE:
        return _CONST_CACHE[key]
    t = np.linspace(0.0, 1.0, L).astype(np.float32)
    w = (2.0 * np.pi * np.arange(L, dtype=np.float64) / L)
    f = np.linspace(1e-4, 15.0, 16)
    fw = f[None, :] * w[:, None]
    z = np.concatenate([t[:, None].astype(np.float64), np.cos(fw), -np.sin(fw)], axis=1).astype(np.float32)
    _CONST_CACHE[key] = (t, z)
    return t, z


def _rank_consts(rank):
    key = ('rank', rank)
    if key in _CONST_CACHE:
        return _CONST_CACHE[key]
    kind, idx = _unit_of_rank(rank)
    sh = _shared_consts()
    c = {}
    pos0 = 8192 if (kind == 'p' and idx == 1) else 0
    mL = 1.0 if (kind == 'p' and idx == 1) else 0.0
    mR = 1.0 if (kind == 'p' and idx == 0) else 0.0
    c['edge'] = np.tile(np.array([[mL, mR]], np.float32), (128, 1))
    masks = np.stack([sh['mprev'], sh['mnext'], sh['mprev'] * mL, sh['mnext'] * mR], axis=1)
    c['masks'] = masks.astype(BF)
    pos = (pos0 - HALO + np.arange(NW)).astype(np.float32)
    inv = (ROPE_THETA ** (-np.arange(0, 16, 2, dtype=np.float32) / 16.0)).astype(np.float32)
    ang = pos[None, :] * inv[:, None]
    cs = np.zeros((128, 2, NW), np.float32)
    cs[:, 0, :] = 1.0
    for p in range(128):
        d = p % 64
        if d < 16:
            cs[p, 0, :] = np.cos(ang[d % 8])
            cs[p, 1, :] = np.sin(ang[d % 8])
    c['cstab'] = cs
    L = 16384 if kind == 'p' else 8192
    t_all, z_all = _zfeat(L)
    n = np.arange(NF)
    zf = np.zeros((2, 33, NF), np.float32)
    mm = np.zeros((2, 128, NF), np.float32)
    td = np.zeros((2, NF), np.float32)
    e0 = np.zeros((2,), np.float32)
    own = idx % 2 if kind == 's' else idx
    for s in range(2):
        lag = np.zeros(NF, np.int64)
        dr = np.zeros(NF, np.int64)
        if s == own:
            lo = n < 8192
            hi = n > 8192
            lag[lo] = n[lo]
            dr[lo] = 1
            lag[hi] = NF - n[hi]
            dr[hi] = 2
            e0[s] = 1.0
        elif kind == 'p':
            lo = n < 8192
            hi = n > 8192
            if idx == 0:
                lag[lo] = 8192 - n[lo]
                dr[lo] = 2
                lag[hi] = 24576 - n[hi]
                dr[hi] = 2
            else:
                lag[lo] = n[lo] + 8192
                dr[lo] = 1
                lag[hi] = n[hi] - 8192
                dr[hi] = 1
        valid = dr > 0
        zf[s][:, valid] = z_all[lag[valid]].T
        td[s][valid] = t_all[lag[valid]]
        mm[s][:64, :] = (dr == 1).astype(np.float32)[None, :]
        mm[s][64:, :] = (dr == 2).astype(np.float32)[None, :]
    c['zf'] = zf
    c['mm'] = mm.astype(BF)
    c['tdec'] = np.ascontiguousarray(td.reshape(2, 128, 128).transpose(1, 0, 2))
    c['e0'] = np.tile(e0[None, :], (128, 1)).astype(np.float32)
    _CONST_CACHE[key] = c
    return c


def _bc(ap, shape, axis):
    return ap.unsqueeze(axis).to_broadcast(shape)


class Builder:
    def __init__(self, depth=DEPTH, debug=False, stop=None):
        self.depth = depth
        self.debug = debug
        self.stop = stop
        self.nc = bass.Bass("TRN2", target_bir_lowering=False)
        self.P = Prog(self.nc)
        self.ges = contextlib.ExitStack()
        self.scope = None
        self.bank = 0
        self.rr = 0
        self.inputs = {}
        self.dbg_out = []
        self.sub = float(os.environ.get('MK_SUB', '99'))
        self.ntr = int(os.environ.get('MK_NT', str(NT)))

    def din(self, name, shape, dt=F32):
        t = self.nc.dram_tensor(name, list(shape), dt, kind="ExternalInput")
        self.inputs[name] = (tuple(shape), dt)
        return t.ap()

    def L(self, name):
        if name not in self.lazy_ap:
            shp, dt = self.lazy[name]
            self.lazy_ap[name] = self.din(name, shp, dt)
        return self.lazy_ap[name]

    def dscr(self, name, shape, dt, dbg=True):
        if self.debug and dbg and name in self.debug:
            self.dbg_out.append(name)
            return self.nc.dram_tensor(name, list(shape), dt, kind="ExternalOutput").ap()
        return self.nc.dram_tensor(name, list(shape), dt).ap()

    def gtile(self, name, shape, dt):
        return self.ges.enter_context(self.nc.sbuf_tensor('sbg_' + name, list(shape), dt))

    def tile(self, name, shape, dt):
        self.uid = getattr(self, 'uid', 0) + 1
        return self.scope.enter_context(self.nc.sbuf_tensor('sb%d_%s' % (self.uid, name), list(shape), dt))

    def begin(self):
        self.scope = contextlib.ExitStack()
        import inspect
        nm = inspect.stack()[1].function
        self.marks = getattr(self, 'marks', [])
        self.marks.append((nm, sum(1 for o in self.P.ops if o['eng'] == 'pe' and o['fn'] is not None)))

    def end(self):
        self.P.barrier()
        self.scope.close()
        self.scope = None

    def nb(self, k=1):
        if self.bank + k > 8:
            self.bank = 0
        b = self.bank
        self.bank = (self.bank + k) % 8
        return b

    def pr(self, b, k=1):
        return [('ps', b + j) for j in range(k)]

    def alt(self, engs=('act', 'dve')):
        self.rr += 1
        return engs[self.rr % len(engs)]

    def mm(self, out, lhsT, rhs, start, stop, r, w):
        self.P.op('pe', lambda e, o=out, l=lhsT, x=rhs, s=start, t=stop: e.matmul(o, l, x, start=s, stop=t), r, w)

    def tp(self, out, in_, ident, r, w):
        self.P.op('pe', lambda e, o=out, i=in_, d=ident: e.transpose(o, i, d), r, w)

    def act(self, out, in_, func, r, w, scale=None, bias=None):
        kw = {}
        if scale is not None:
            kw['scale'] = scale
        if bias is not None:
            kw['bias'] = bias
        self.P.op('act', lambda e, o=out, i=in_, f=func, k=kw: e.activation(o, i, f, **k), r, w)

    def cp(self, eng, out, in_, r, w):
        if eng == 'act':
            self.act(out, in_, AF.Copy, r, w)
        else:
            self.P.op(eng, lambda e, o=out, i=in_: e.tensor_copy(o, i), r, w)

    def tt(self, eng, out, in0, in1, op, r, w):
        self.P.op(eng, lambda e, o=out, a=in0, b=in1, p=op: e.tensor_tensor(o, a, b, p), r, w)

    def ts(self, eng, out, in0, s1, s2, op0, op1, r, w):
        if op1 is None:
            self.P.op(eng, lambda e, o=out, a=in0, x=s1, p=op0: e.tensor_scalar(o, a, x, None, p), r, w)
        else:
            self.P.op(eng, lambda e, o=out, a=in0, x=s1, y=s2, p=op0, q=op1: e.tensor_scalar(o, a, x, y, p, q), r, w)

    def stt(self, eng, out, in0, scalar, in1, op0, op1, r, w):
        self.P.op(eng, lambda e, o=out, a=in0, s=scalar, b=in1, p=op0, q=op1:
                  e.scalar_tensor_tensor(o, a, s, b, p, q), r, w)

    def rstd(self, out, in_, r, w, eng='dve'):
        np_ = out.shape[0]
        self.act(out, in_, AF.Ln, list(r) + ['g_epsc'], list(w), bias=self.epsc[0:np_, 0:1], scale=1.0)
        self.act(out, out, AF.Exp, list(w), list(w), scale=-0.5)

    def dma(self, q, out, in_, r, w, key, slow=False):
        if slow:
            self.P.op(q, lambda e, o=out, i=in_: e.dma_start(out=o, in_=i, allow_slow_non_contiguous=True), r, w, dma=key)
        else:
            self.P.op(q, lambda e, o=out, i=in_: e.dma_start(out=o, in_=i), r, w, dma=key)

    def allgather(self, in2d, out2d, r, w):
        self.P.op('pool', lambda e, i=in2d, o=out2d: e.collective_compute(
            "AllGather", ALU.bypass, replica_groups=[[0, 1], [2, 3], [4, 5], [6, 7]],
            ins=[i.opt()], outs=[o.opt()]), r, w, cc=True)

    def declare(self):
        L = DEPTH
        d = self.din
        self.lazy = {'x': ([T, D], F32), 'p': ([L, T, DPLE], F32), 'w_in': ([L, D, 2304], F32),
                     'w_out': ([L, D, D], F32), 'w_up': ([L, D, 2 * DFF], F32), 'w_down': ([L, DFF, D], F32),
                     'w_ple_gate': ([L, D, D], F32), 'w_ple_proj': ([L, DPLE, D], F32),
                     'filt_w1': ([L, 33, 64], F32), 'filt_w2': ([L, 64, 64], F32), 'filt_w3': ([L, 64, 1024], F32),
                     'gall': ([128, KA, 3, 128], BF16), 'hall': ([KA, 128, 2, 64], BF16),
                     'zf': ([2, 33, NF], F32), 'mm': ([2, 128, NF], BF16), 'cstab': ([128, 2, NW], F32)}
        self.lazy_ap = {}
        self.colspec = {}
        off = 0
        for name, n in [('g_mix', L * 8), ('g_ffn', L * 8), ('gq', L), ('gk', L), ('sink', L * 8),
                        ('w_short', L * 36), ('b_short', L * 12), ('g_ao', L * 4), ('g_ho', L * 4),
                        ('w_ffc', L * 132), ('b_ffc', L * 44), ('fb1', L), ('ffr1', L), ('fb2', L), ('ffr2', L)]:
            self.colspec[name] = (off, n)
            off += n
        self.ncol = off
        self.cols_d = d('cols', [128, self.ncol])
        self.hbias_d = d('hbias', [1, L * DH])
        self.c_ident = d('ident_f', [128, 128])
        self.c_ones = d('ones_b', [128, 4, 128], BF16)
        self.c_prot = d('prot_b', [128, 128], BF16)
        self.c_masks = d('masks', [128, 4, 128], BF16)
        self.c_edge = d('edge', [128, 2])
        self.c_f1m = d('f1m', [128, 130], BF16)
        self.c_e12 = d('e12', [128, 2, 256], BF16)
        self.c_negd = d('negd', [128, DH])
        self.c_tdec = d('tdec', [128, 2, 128])
        self.c_e0 = d('e0', [128, 2])
        self.y = self.nc.dram_tensor('y', [T, D], F32, kind="ExternalOutput").ap()
        s = self.dscr
        self.hres = s('hres', [D, T], F32)
        self.nrm = s('nrm', [D, NW], BF16)
        self.xh_in = s('xh_in', [2 * D, 128], BF16, dbg=False)
        self.xh_out = s('xh_out', [4 * D, 128], BF16, dbg=False)
        self.zhy = s('zhy', [3 * DH, T + 2], BF16)
        self.attn_n = s('attn_n', [DQ, T], BF16)
        self.x0s = s('x0s', [DH, T], BF16)
        self.vfft2 = [s('vfft%d' % g_, [64 * 8, 1024], BF16, dbg=False) for g_ in range(NG)]
        self.vall2 = [s('vall%d' % g_, [128 * 8, 1024], BF16, dbg=False) for g_ in range(NG)]
        self.vfft = [t_.rearrange('(a x) y -> a (x y)', x=8) for t_ in self.vfft2]
        self.vall = [t_.rearrange('(a x) y -> a (x y)', x=8) for t_ in self.vall2]
        self.hfs = s('hfs', [NG, 128, KA * 2 * 2 * CG], BF16)
        self.hdn = s('hdn', [2, 128, NF], BF16)
        self.yconv = s('yconv', [DH, T], BF16)
        self.n2s = s('n2s', [D, T + 2], BF16)
        self.xn_in = s('xn_in', [2, D], BF16, dbg=False)
        self.xn_out = s('xn_out', [4, D], BF16, dbg=False)
        self.acts = s('acts', [DFF, T], BF16)

    def col(self, name, l=None, k=1, j=0):
        off, n = self.colspec[name]
        per = n // DEPTH
        if l is None:
            return self.cols[:, off:off + n]
        a = off + l * per + j
        return self.cols[:, a:a + k]

    def setup_globals(self):
        g = self.gtile
        self.ps = self.ges.enter_context(self.nc.psum_tensor('ps', [128, 8, 512], F32))
        self.ident = g('ident', [128, 128], F32)
        self.ones = g('ones', [128, 4, 128], BF16)
        self.prot = g('prot', [128, 128], BF16)
        self.masks = g('masks', [128, 4, 128], BF16)
        self.edge = g('edge', [128, 2], F32)
        self.cols = g('cols', [128, self.ncol], F32)
        self.esk = g('esk', [128, DEPTH * 8], F32)
        self.fsc = g('fsc', [128, DEPTH * 4], F32)
        self.e0 = g('e0', [128, 2], F32)
        self.epsc = g('epsc', [128, 1], F32)
        self.P.op('dve', lambda e: e.memset(self.epsc[:], EPS), [], ['g_epsc'])
        for t, dsrc, nm in [(self.ident, self.c_ident, 'ident'), (self.ones, self.c_ones, 'ones'),
                            (self.prot, self.c_prot, 'prot'), (self.masks, self.c_masks, 'masks'),
                            (self.edge, self.c_edge, 'edge'), (self.cols, self.cols_d, 'cols'),
                            (self.e0, self.c_e0, 'e0')]:
            self.dma('sp', t[:], dsrc, [], ['g_' + nm], 'g_' + nm)
        o, n = self.colspec['sink']
        self.act(self.esk[:], self.cols[:, o:o + n], AF.Exp, ['g_cols'], ['g_esk'])
        for l in range(self.depth):
            for k, (fr, fb) in enumerate([('ffr1', 'fb1'), ('ffr2', 'fb2')]):
                self.ts('dve', self.fsc[:, l * 4 + 2 * k:l * 4 + 2 * k + 1], self.col(fr, l), 1.0 / 3.0, None, ALU.mult, None,
                        ['g_cols'], [('fsc', l, 2 * k)])
                self.tt('dve', self.fsc[:, l * 4 + 2 * k + 1:l * 4 + 2 * k + 2], self.fsc[:, l * 4 + 2 * k:l * 4 + 2 * k + 1],
                        self.col(fb, l), ALU.mult, [('fsc', l, 2 * k), 'g_cols'], [('fsc', l, 2 * k + 1)])
        self.P.barrier()

    def load_w(self, src_rows, ncols, scale, stg, sname, pieces):
        self.wslot = getattr(self, 'wslot', 0) + 1
        s = self.wslot % len(stg)
        st = stg[s]
        rn = (sname, s)
        self.dma('sp', st[:, 0:ncols], src_rows, [], [rn], '%s%d' % (sname, s))
        for (d_ap, c0, c1, vf, wn) in pieces:
            src = st[:, c0:c1]
            if vf is not None:
                src = vf(src)
            eng = self.alt(('act', 'dve'))
            if scale is None:
                self.cp(eng, d_ap, src, [rn], [wn])
            elif eng == 'act':
                self.act(d_ap, src, AF.Copy, [rn, 'g_cols'], [wn], scale=scale)
            else:
                self.ts(eng, d_ap, src, scale, None, ALU.mult, None, [rn, 'g_cols'], [wn])

    def phase_p0(self):
        self.begin()
        xt = [self.tile('p0_xt%d' % s, [128, 4, D], F32) for s in range(2)]
        ht = [self.tile('p0_ht%d' % s, [128, 8, 512], F32) for s in range(2)]
        for i in range(NT):
            s = i % 2
            self.dma('sp', xt[s][:], self.L('x')[i * 512:(i + 1) * 512, :].rearrange('(b p) f -> p b f', p=128),
                     [], [('xt', s)], 'xt%d' % s)
            for fc in range(8):
                bk = self.nb()
                for blk in range(4):
                    self.tp(self.ps[:, bk, blk * 128:(blk + 1) * 128], xt[s][:, blk, fc * 128:(fc + 1) * 128],
                            self.ident[:], [('xt', s), 'g_ident'], self.pr(bk))
                self.cp(self.alt(), ht[s][:, fc, :], self.ps[:, bk, :], self.pr(bk), [('ht', s, fc)])
            self.dma('sp', self.hres[:, i * 512:(i + 1) * 512].rearrange('(c p) t -> p c t', p=128), ht[s][:],
                     [('ht', s, fc) for fc in range(8)], [('hres', i)], 'ht%d_st' % s)
        self.end()

    def phase_a0(self, l):
        self.begin()
        ht = [self.tile('a0_ht%d' % s, [128, 8, 512], F32) for s in range(2)]
        sq = [self.tile('a0_sq%d' % s, [128, 8, 512], BF16) for s in range(2)]
        rs = [self.tile('a0_rs%d' % s, [128, 512], F32) for s in range(2)]
        nt = [self.tile('a0_nt%d' % s, [128, 8, 512], BF16) for s in range(2)]
        hl = self.tile('a0_hl', [128, 8, 128], BF16)
        hr = self.tile('a0_hr', [128, 8, 128], BF16)
        def loads(i):
            s = i % 2
            self.dma('sp', ht[s][:], self.hres[:, i * 512:(i + 1) * 512].rearrange('(c p) t -> p c t', p=128),
                     [('hres', i)], [('ht', s)], 'a0ht%d' % s)
        loads(0)
        for i in range(NT):
            s = i % 2
            if i + 1 < NT:
                loads(i + 1)
            self.act(sq[s][:], ht[s][:], AF.Square, [('ht', s)], [('sq', s)])
            bk = self.nb()
            for fc in range(8):
                self.mm(self.ps[:, bk, :], self.ones[:, 0, :], sq[s][:, fc, :], fc == 0, fc == 7,
                        [('sq', s), 'g_ones'], self.pr(bk))
            self.rstd(rs[s][:], self.ps[:, bk, :], self.pr(bk), [('rs', s)])
            for hf in range(2):
                eng = 'dve' if hf == 0 else 'pool'
                self.tt(eng, nt[s][:, hf * 4:(hf + 1) * 4, :], ht[s][:, hf * 4:(hf + 1) * 4, :],
                        _bc(rs[s][:], [128, 4, 512], 1), ALU.mult, [('ht', s), ('rs', s)], [('nt', s, hf)])
            rd = [('nt', s, 0), ('nt', s, 1)]
            self.dma('sp', self.nrm[:, HALO + i * 512:HALO + (i + 1) * 512].rearrange('(c p) t -> p c t', p=128),
                     nt[s][:], rd, [('nrm', i)], 'a0nt%d_st' % s)
            if i == 0:
                self.dma('sp', self.xh_in[0:D, :].rearrange('(c p) t -> p c t', p=128), nt[s][:, :, 0:128],
                         rd, ['xh_in0'], 'a0x0')
            if i == NT - 1:
                self.dma('sp', self.xh_in[D:2 * D, :].rearrange('(c p) t -> p c t', p=128), nt[s][:, :, 384:512],
                         rd, ['xh_in1'], 'a0x1')
        self.allgather(self.xh_in, self.xh_out, ['xh_in0', 'xh_in1'], ['xh_out'])
        for k, (tl, r0, col) in enumerate([(hl, D, 0), (hr, 2 * D, NW - HALO)]):
            self.dma('sp', tl[:], self.xh_out[r0:r0 + D, :].rearrange('(c p) t -> p c t', p=128),
                     ['xh_out'], [('hal', k)], 'a0h%d' % k)
            self.ts('dve', tl[:], tl[:], self.edge[:, k:k + 1], None, ALU.mult, None, [('hal', k), 'g_edge'], [('hal', k)])
            self.dma('sp', self.nrm[:, col:col + HALO].rearrange('(c p) t -> p c t', p=128), tl[:],
                     [('hal', k)], [('nrmh', k)], 'a0h%d_st' % k)
        self.end()


    def capture(self, fn):
        rec = []
        real = self.P.op
        self.P.op = lambda *a, **k: rec.append((a, k))
        try:
            fn()
        finally:
            self.P.op = real
        return rec

    def replay(self, recs):
        for a, k in recs:
            self.P.op(*a, **k)

    @staticmethod
    def merge(a, b):
        def split(x):
            segs = [[]]
            for r in x:
                if r[0][0] == 'MARK':
                    segs.append([])
                else:
                    segs[-1].append(r)
            return segs
        sa = split(a)
        sb = [g_ for g_ in split(b) if g_]
        ncut = max(1, len(sa) - 1)
        out = []
        ib = 0
        for k, sg in enumerate(sa):
            out.extend(sg)
            if k < len(sa) - 1:
                tgt = (k + 1) * len(sb) // ncut
                while ib < min(tgt, len(sb)):
                    out.extend(sb[ib])
                    ib += 1
        while ib < len(sb):
            out.extend(sb[ib])
            ib += 1
        return out

    def phase_a(self, l):
        self.begin()
        w_in = self.L('w_in')
        cstab = self.L('cstab')
        wq = self.tile('wq', [128, 8, 512], BF16)
        wk = self.tile('wk', [128, 8, 256], BF16)
        wv = self.tile('wv', [128, 8, 256], BF16)
        why = self.tile('why', [128, 8, 1536], BF16)
        stg = [self.tile('a_stg%d' % s_, [128, 1152], F32) for s_ in range(2)]
        dup = lambda a: _bc(a.rearrange('p (k d) -> p k d', k=2), [128, 2, 2, 64], 2)
        for kc in range(8):
            self.load_w(w_in[l, kc * 128:(kc + 1) * 128, 0:1152], 1152, self.col('g_mix', l, 1, kc), stg, 'astg', [
                (wq[:, kc, :], 0, 512, None, ('wq', kc)),
                (wk[:, kc, :].rearrange('p (k u d) -> p k u d', k=2, u=2), 512, 640, dup, ('wk', kc)),
                (wv[:, kc, :].rearrange('p (k u d) -> p k u d', k=2, u=2), 640, 768, dup, ('wv', kc)),
                (why[:, kc, 0:384], 768, 1152, None, ('why', kc, 0))])
            self.load_w(w_in[l, kc * 128:(kc + 1) * 128, 1152:2304], 1152, self.col('g_mix', l, 1, kc), stg, 'astg', [
                (why[:, kc, 384:1536], 0, 1152, None, ('why', kc, 1))])
        nt = [self.tile('nt%d' % s, [128, 8, 768], BF16) for s in range(2)]
        ct = [self.tile('ct%d' % s, [128, 2, 768], F32) for s in range(2)]
        qraw = self.tile('qraw', [128, 4, 512], F32)
        kraw = self.tile('kraw', [128, 2, 768], F32)
        sqq = self.tile('sqq', [128, 4, 512], BF16)
        sqk = self.tile('sqk', [128, 2, 768], BF16)
        rq = self.tile('rq', [128, 4, 512], F32)
        rk = self.tile('rk', [128, 2, 768], F32)
        qn = self.tile('qn', [128, 4, 512], BF16)
        kn = self.tile('kn', [128, 2, 768], BF16)
        qr = [self.tile('qr%d' % s, [128, 4, 512], BF16) for s in range(2)]
        krl = [self.tile('krl%d' % s, [128, 2, 768], BF16) for s in range(2)]
        krh = [self.tile('krh%d' % s, [128, 2, 768], BF16) for s in range(2)]
        for s in range(2):
            self.P.op('pool', lambda e, t_=krl[s]: e.memset(t_[64:128], 0.0), [], [('krl0', s)])
            self.P.op('pool', lambda e, t_=krh[s]: e.memset(t_[0:64], 0.0), [], [('krh0', s)])
        vd = [self.tile('vd%d' % s, [128, 6, 256], BF16) for s in range(2)]
        zt = self.tile('zt', [128, 12, 512], BF16)
        zh = self.tile('zh', [128, 12, 2], BF16)
        E = [self.tile('E%d' % s, [128, 3, 512], BF16) for s in range(2)]
        rec = [self.tile('rec%d' % s, [128, 512], F32) for s in range(2)]
        attn = self.tile('attn', [128, 4, 512], F32)
        sqa = self.tile('sqa', [128, 4, 512], BF16)
        rsa = self.tile('rsa', [128, 512], F32)
        an = [self.tile('an%d' % s, [128, 4, 512], BF16) for s in range(2)]
        gq = self.col('gq', l)
        gk = self.col('gk', l)
        st_ = {'ei': 0}
        kseg = [(0, 0, 512, 0, 0), (0, 512, 768, 1, 0), (1, 0, 256, 1, 256), (1, 256, 768, 2, 0)]
        whyr = lambda kc: [('why', kc, 0), ('why', kc, 1)]

        def loads(i):
            s = i % 2
            t0 = i * 512
            self.dma('sp', nt[s][:], self.nrm[:, t0:t0 + 768].rearrange('(c p) t -> p c t', p=128),
                     [], [('nt', s)], 'a_nt%d' % s)
            self.dma('sp', ct[s][:], cstab[:, :, t0:t0 + 768], [], [('ct', s)], 'a_ct%d' % s)

        def X(i):
            s = i % 2
            t0 = i * 512
            rw = [('nt', s)]
            bq = self.nb(4)
            for mc in range(4):
                for kc in range(8):
                    self.mm(self.ps[:, bq + mc, :], wq[:, kc, mc * 128:(mc + 1) * 128], nt[s][:, kc, 128:640],
                            kc == 0, kc == 7, rw + [('wq', kc)], self.pr(bq + mc))
            self.cp('act', qraw[:], self.ps[:, bq:bq + 4, :], self.pr(bq, 4), ['qraw'])
            self.P.op('MARK', None)
            bkk = self.nb(3)
            for (c2, c0, c1, bo, pc0) in kseg:
                for kc in range(8):
                    self.mm(self.ps[:, bkk + bo, pc0:pc0 + (c1 - c0)], wk[:, kc, c2 * 128:(c2 + 1) * 128],
                            nt[s][:, kc, c0:c1], kc == 0, kc == 7, rw + [('wk', kc)], self.pr(bkk + bo))
            psk = self.ps[:, bkk:bkk + 3, :].rearrange('p b t -> p (b t)').rearrange('p (c t) -> p c t', c=2)
            self.cp('dve', kraw[:], psk, self.pr(bkk, 3), ['kraw'])
            self.P.op('MARK', None)
            bv = self.nb(3)
            for kb in range(6):
                for kc in range(8):
                    self.mm(self.ps[:, bv + kb // 2, (kb % 2) * 256:(kb % 2 + 1) * 256], nt[s][:, kc, kb * 128:(kb + 1) * 128],
                            wv[:, kc, :], kc == 0, kc == 7, rw + [('wv', kc)], self.pr(bv + kb // 2))
            self.cp('act', vd[s][:], self.ps[:, bv:bv + 3, :].rearrange('p b (u t) -> p (b u) t', u=2), self.pr(bv, 3), [('vd', s)])
            for g3 in range(3):
                self.P.op('MARK', None)
                bh = self.nb(4)
                for j in range(4):
                    mc = g3 * 4 + j
                    for kc in range(8):
                        self.mm(self.ps[:, bh + j, :], why[:, kc, mc * 128:(mc + 1) * 128], nt[s][:, kc, 128:640],
                                kc == 0, kc == 7, rw + whyr(kc), self.pr(bh + j))
                self.cp(self.alt(), zt[:, g3 * 4:(g3 + 1) * 4, :], self.ps[:, bh:bh + 4, :], self.pr(bh, 4), [('zt', g3)])
            self.dma('sp', self.zhy[:, 1 + t0:1 + t0 + 512].rearrange('(c p) t -> p c t', p=128), zt[:],
                     [('zt', g3) for g3 in range(3)], [('zhy', i)], 'a_zt_st')
            if i == 0 or i == NT - 1:
                self.P.op('MARK', None)
                colh = 127 if i == 0 else 640
                hi = 0 if i == 0 else 1
                bh = self.nb(1)
                for mc in range(12):
                    for kc in range(8):
                        self.mm(self.ps[:, bh, mc:mc + 1], why[:, kc, mc * 128:(mc + 1) * 128], nt[s][:, kc, colh:colh + 1],
                                kc == 0, kc == 7, rw + whyr(kc), self.pr(bh))
                self.cp('dve', zh[:, :, hi], self.ps[:, bh, 0:12], self.pr(bh), [('zh', hi)])
                dcol = 0 if i == 0 else T + 1
                self.dma('sp', self.zhy[:, dcol:dcol + 1].rearrange('(c p) t -> p c t', p=128), zh[:, :, hi:hi + 1],
                         [('zh', hi)], [('zhyh', hi)], 'a_zh%d_st' % hi, slow=True)

        def Y(i):
            s = i % 2
            self.act(sqq[:], qraw[:], AF.Square, ['qraw'], ['sqq'])
            self.act(sqk[:], kraw[:], AF.Square, ['kraw'], ['sqk'])
            self.P.op('MARK', None)
            bs = self.nb(4)
            for mc in range(4):
                self.mm(self.ps[:, bs + mc, :], self.ones[:, 3, :], sqq[:, mc, :], True, True, ['sqq', 'g_ones'], self.pr(bs + mc))
            self.rstd(rq[:], self.ps[:, bs:bs + 4, :], self.pr(bs, 4), ['rq'])
            self.P.op('MARK', None)
            bs2 = self.nb(3)
            for (c2, c0, c1, bo, pc0) in kseg:
                self.mm(self.ps[:, bs2 + bo, pc0:pc0 + (c1 - c0)], self.ones[:, 3, :], sqk[:, c2, c0:c1], True, True,
                        ['sqk', 'g_ones'], self.pr(bs2 + bo))
            psk = self.ps[:, bs2:bs2 + 3, :].rearrange('p b t -> p (b t)').rearrange('p (c t) -> p c t', c=2)
            self.rstd(rk[:], psk, self.pr(bs2, 3), ['rk'])
            self.P.op('MARK', None)
            self.stt('dve', qn[:], qraw[:], gq, rq[:], ALU.mult, ALU.mult, ['qraw', 'rq', 'g_cols'], ['qn'])
            self.stt('dve', kn[:], kraw[:], gk, rk[:], ALU.mult, ALU.mult, ['kraw', 'rk', 'g_cols'], ['kn'])
            self.P.op('MARK', None)
            br = self.nb(4)
            for mc in range(4):
                self.mm(self.ps[:, br + mc, :], self.prot[:], qn[:, mc, :], True, True, ['qn', 'g_prot'], self.pr(br + mc))
            self.tt('pool', qraw[:], qn[:], _bc(ct[s][:, 0, 128:640], [128, 4, 512], 1), ALU.mult, ['qn', ('ct', s)], ['qraw'])
            self.tt('dve', rq[:], self.ps[:, br:br + 4, :], _bc(ct[s][:, 1, 128:640], [128, 4, 512], 1), ALU.mult,
                    self.pr(br, 4) + [('ct', s)], ['rq'])
            self.tt('pool', qr[s][:], qraw[:], rq[:], ALU.add, ['qraw', 'rq'], [('qr', s)])
            self.P.op('MARK', None)
            br2 = self.nb(3)
            for (c2, c0, c1, bo, pc0) in kseg:
                self.mm(self.ps[:, br2 + bo, pc0:pc0 + (c1 - c0)], self.prot[:], kn[:, c2, c0:c1], True, True,
                        ['kn', 'g_prot'], self.pr(br2 + bo))
            psk = self.ps[:, br2:br2 + 3, :].rearrange('p b t -> p (b t)').rearrange('p (c t) -> p c t', c=2)
            self.tt('pool', kraw[:], kn[:], _bc(ct[s][:, 0, :], [128, 2, 768], 1), ALU.mult, ['kn', ('ct', s)], ['kraw'])
            self.tt('dve', rk[:], psk, _bc(ct[s][:, 1, :], [128, 2, 768], 1), ALU.mult, self.pr(br2, 3) + [('ct', s)], ['rk'])
            self.tt('pool', krl[s][0:64], kraw[0:64], rk[0:64], ALU.add, ['kraw', 'rk'], [('krl', s)])
            self.tt('dve', krh[s][64:128], kraw[64:128], rk[64:128], ALU.add, ['kraw', 'rk'], [('krh', s)])

        def Z(i):
            s = i % 2
            t0 = i * 512
            kread = [('krl', s), ('krh', s), ('krl0', s), ('krh0', s), ('qr', s)]
            for qb in range(4):
                for kvh in range(2):
                    es_ = st_['ei'] % 2
                    st_['ei'] += 1
                    b0 = self.nb(3)
                    for kk in range(3):
                        kbw = qb + kk
                        for half in range(2):
                            kt_ = krl[s] if half == 0 else krh[s]
                            self.mm(self.ps[:, b0 + kk, half * 256:(half + 1) * 256],
                                    kt_[:, kvh, kbw * 128:(kbw + 1) * 128],
                                    qr[s][:, 2 * kvh:2 * kvh + 2, qb * 128:(qb + 1) * 128],
                                    True, True, kread, self.pr(b0 + kk))
                    self.act(E[es_][:], self.ps[:, b0:b0 + 3, :], AF.Exp, self.pr(b0, 3), [('E', es_)], scale=0.125)
                    first = (i == 0 and qb == 0)
                    lastb = (i == NT - 1 and qb == 3)
                    for (kk, mi) in [(0, 2 if first else 0), (2, 3 if lastb else 1)]:
                        ev = E[es_][:, kk, :].rearrange('p (g q) -> p g q', g=4)
                        self.tt('dve', ev, ev, _bc(self.masks[:, mi, :], [128, 4, 128], 1), ALU.mult,
                                [('E', es_), 'g_masks'], [('E', es_)])
                    bpv = self.nb()
                    bdn = self.nb()
                    for kk in range(3):
                        self.mm(self.ps[:, bpv, :], vd[s][:, qb + kk, kvh * 128:(kvh + 1) * 128], E[es_][:, kk, :],
                                kk == 0, kk == 2, [('vd', s), ('E', es_)], self.pr(bpv))
                    for kk in range(3):
                        self.mm(self.ps[:, bdn, :], self.ones[:, 2, :], E[es_][:, kk, :], kk == 0, kk == 2,
                                [('E', es_), 'g_ones'], self.pr(bdn))
                    rv = rec[es_][:].rearrange('p (g q) -> p g q', g=4)
                    sk = self.esk[:, l * 8 + kvh * 4:l * 8 + kvh * 4 + 4]
                    self.tt('dve', rv, self.ps[:, bdn, :].rearrange('p (g q) -> p g q', g=4), _bc(sk, [128, 4, 128], 2),
                            ALU.add, self.pr(bdn) + ['g_esk'], [('rec', es_)])
                    self.act(rec[es_][:], rec[es_][:], AF.Ln, [('rec', es_)], [('rec', es_)])
                    self.act(rec[es_][:], rec[es_][:], AF.Exp, [('rec', es_)], [('rec', es_)], scale=-1.0)
                    for half in range(2):
                        pp = slice(half * 64, (half + 1) * 64)
                        self.tt('dve', attn[pp, 2 * kvh:2 * kvh + 2, qb * 128:(qb + 1) * 128],
                                self.ps[pp, bpv, half * 256:(half + 1) * 256].rearrange('p (g q) -> p g q', g=2),
                                rec[es_][pp, half * 256:(half + 1) * 256].rearrange('p (g q) -> p g q', g=2), ALU.mult,
                                self.pr(bpv) + [('rec', es_)], [('attn', qb, kvh, half)])
                    self.P.op('MARK', None)
            ar = [('attn', qb, kvh, half) for qb in range(4) for kvh in range(2) for half in range(2)]
            self.act(sqa[:], attn[:], AF.Square, ar, ['sqa'])
            bs = self.nb()
            for mc in range(4):
                self.mm(self.ps[:, bs, :], self.ones[:, 1, :], sqa[:, mc, :], mc == 0, mc == 3, ['sqa', 'g_ones'], self.pr(bs))
            self.rstd(rsa[:], self.ps[:, bs, :], self.pr(bs), ['rsa'])
            self.tt('pool', an[s][:], attn[:], _bc(rsa[:], [128, 4, 512], 1), ALU.mult, ar + ['rsa'], [('an', s)])
            self.dma('sp', self.attn_n[:, t0:t0 + 512].rearrange('(c p) t -> p c t', p=128), an[s][:],
                     [('an', s)], [('attn_n', i)], 'a_an%d_st' % s)

        loads(0)
        loads(1)
        self.replay([r_ for r_ in self.capture(lambda: (X(0), Y(0))) if r_[0][0] != 'MARK'])
        for i in range(NT):
            zs = self.capture(lambda: Z(i))
            if i + 1 < NT:
                xs = self.capture(lambda: X(i + 1))
                ys = self.capture(lambda: Y(i + 1))
                self.replay(self.merge(zs, xs + [(('MARK', None), {})] + ys))
            else:
                self.replay([r_ for r_ in zs if r_[0][0] != 'MARK'])
            if i + 2 < NT:
                loads(i + 2)
        self.end()

    def phase_a2(self, l):
        self.begin()
        zw = [self.tile('zw%d' % s, [128, 12, 514], BF16) for s in range(2)]
        u = self.tile('u', [128, 12, 512], F32)
        xo = [self.tile('xo%d' % s, [128, 4, 512], BF16) for s in range(2)]
        vt = [self.tile('vt%d' % s, [128, 4, 512], BF16) for s in range(2)]
        def loads(i):
            s = i % 2
            self.dma('sp', zw[s][:], self.zhy[:, i * 512:i * 512 + 514].rearrange('(c p) t -> p c t', p=128), [], [('zw', s)], 'a2zw%d' % s)
        loads(0)
        for i in range(NT):
            s = i % 2
            t0 = i * 512
            if i + 1 < NT:
                loads(i + 1)
            for j in range(12):
                wc = lambda k, j=j: self.col('w_short', l, 1, k * 12 + j)
                self.act(u[:, j, :], zw[s][:, j, 1:513], AF.Identity, [('zw', s), 'g_cols'], [('u', j)],
                         scale=wc(1), bias=self.col('b_short', l, 1, j))
                e1 = 'dve'
                self.stt(e1, u[:, j, :], zw[s][:, j, 0:512], wc(0), u[:, j, :], ALU.mult, ALU.add, [('zw', s), ('u', j), 'g_cols'], [('u', j)])
                self.stt(e1, u[:, j, :], zw[s][:, j, 2:514], wc(2), u[:, j, :], ALU.mult, ALU.add, [('zw', s), ('u', j), 'g_cols'], [('u', j)])
            self.cp('act', xo[s][:], u[:, 0:4, :], [('u', j) for j in range(4)], [('xo', s)])
            self.tt('dve', vt[s][:], u[:, 4:8, :], u[:, 8:12, :], ALU.mult, [('u', j) for j in range(4, 12)], [('vt', s)])
            self.dma('sp', self.x0s[:, t0:t0 + 512].rearrange('(c p) t -> p c t', p=128), xo[s][:], [('xo', s)], [('x0s', i)], 'a2xo%d_st' % s)
            for j in range(4):
                for h2 in range(2):
                    gg = 2 * j + h2
                    self.dma('sp', self.vfft[gg][4 * i:4 * i + 4, :].rearrange('a (p b) -> p a b', p=64),
                             vt[s][h2 * 64:(h2 + 1) * 64, j, :].rearrange('p (a b) -> p a b', a=4), [('vt', s)], [('vfft', gg, i)],
                             'a2vt%d_%d_st' % (s, gg))
        if os.environ.get('MK_NOAG') is None:
            for gg in range(NG):
                self.allgather(self.vfft2[gg], self.vall2[gg], [('vfft', gg, i) for i in range(NT)], [('vall', gg)])
        self.end()

    def phase_f1(self, l):
        self.begin()
        zf = self.L('zf')
        mmk = self.L('mm')
        w1t = self.tile('w1t', [33, 64], F32)
        w2d = self.tile('w2d', [64, 128], F32)
        self.dma('sp', w1t[:], self.L('filt_w1')[l], [], ['w1t'], 'f1w1')
        for k in range(2):
            self.dma('sp', w2d[:, k * 64:(k + 1) * 64], self.L('filt_w2')[l], [], [('w2d', k)], 'f1w2%d' % k)
        zt_ = [self.tile('zt_%d' % s, [33, 2048], F32) for s in range(2)]
        mk = [self.tile('mk%d' % s, [128, 2048], BF16) for s in range(2)]
        s1 = self.tile('s1', [64, 2048], F32)
        t1 = self.tile('t1', [64, 2048], F32)
        s2 = self.tile('s2', [128, 2048], F32)
        t2 = self.tile('t2', [128, 2048], F32)
        hd = [self.tile('hd%d' % s, [128, 2048], BF16) for s in range(2)]
        f = lambda k: self.fsc[:, l * 4 + k:l * 4 + k + 1]
        it = 0
        for sl in range(2):
            for cch in range(8):
                s = it % 2
                it += 1
                c0 = cch * 2048
                self.dma('sp', zt_[s][:], zf[sl, :, c0:c0 + 2048], [], [('zt_', s)], 'f1zt%d' % s)
                self.dma('sp', mk[s][:], mmk[sl, :, c0:c0 + 2048], [], [('mk', s)], 'f1mk%d' % s)
                b1 = self.nb(4)
                for q4 in range(4):
                    self.mm(self.ps[0:64, b1 + q4, :], w1t[:], zt_[s][:, q4 * 512:(q4 + 1) * 512], True, True,
                            ['w1t', ('zt_', s)], self.pr(b1 + q4))
                self.act(s1[:], self.ps[0:64, b1:b1 + 4, :], AF.Sin, self.pr(b1, 4) + [('fsc', l, 0), ('fsc', l, 1)], ['s1'],
                         scale=f(0)[0:64], bias=f(1)[0:64])
                self.tt('dve', t1[:], s1[:], s1[:], ALU.mult, ['s1'], ['t1'])
                self.ts('dve', t1[:], t1[:], -4.0, 3.0, ALU.mult, ALU.add, ['t1'], ['t1'])
                self.tt('pool', t1[:], t1[:], s1[:], ALU.mult, ['t1', 's1'], ['t1'])
                b2 = self.nb(4)
                for q4 in range(4):
                    self.mm(self.ps[:, b2 + q4, :], w2d[:], t1[:, q4 * 512:(q4 + 1) * 512], True, True,
                            [('w2d', 0), ('w2d', 1), 't1'], self.pr(b2 + q4))
                self.act(s2[:], self.ps[:, b2:b2 + 4, :], AF.Sin, self.pr(b2, 4) + [('fsc', l, 2), ('fsc', l, 3)], ['s2'],
                         scale=f(2), bias=f(3))
                self.tt('dve', t2[:], s2[:], s2[:], ALU.mult, ['s2'], ['t2'])
                self.ts('dve', t2[:], t2[:], -4.0, 3.0, ALU.mult, ALU.add, ['t2'], ['t2'])
                self.tt('pool', t2[:], t2[:], s2[:], ALU.mult, ['t2', 's2'], ['t2'])
                self.tt('pool', hd[s][:], t2[:], mk[s][:], ALU.mult, ['t2', ('mk', s)], [('hd', s)])
                self.dma('sp', self.hdn[sl, :, c0:c0 + 2048], hd[s][:], [('hd', s)], [('hdn', sl, cch)], 'f1hd%d_st' % s)
        self.end()

    def g_stream(self, Gt, kab, ka0, gi_):
        s = gi_ % 2
        n = min(5, KA - ka0)
        self.dma('sp', Gt[s][:, 0:n], self.L('gall')[:, ka0:ka0 + n], [], [('Gt', s)], 'Gt%d' % s)
        return s

    def phase_f2(self, l):
        self.begin()
        hdn = self.tile('hdn', [128, NF], BF16)
        g = self.tile('g', [128, 128, 256], BF16)
        Ysb = self.tile('Ysb', [128, 2, KA, 256], BF16)
        HfT = [self.tile('HfT%d' % s, [128, 5, 2, 256], BF16) for s in range(2)]
        Gt = [self.tile('Gt%d' % s, [128, 5, 3, 128], BF16) for s in range(2)]
        dc = [self.tile('dc%d' % s, [128, 2, 256], F32) for s in range(2)]
        w3f = self.tile('w3f', [128, 256], F32)
        w3s = self.tile('w3s', [128, 256], BF16)
        f1m = self.tile('f1m', [128, 130], BF16)
        negd = self.tile('negd', [128, DH], F32)
        tdec = self.tile('tdec', [128, 2, 128], F32)
        hbt = self.tile('hbt', [1, 256], F32)
        self.dma('sp', f1m[:], self.c_f1m, [], ['f1m'], 'f2f1m')
        self.dma('sp', negd[:], self.c_negd, [], ['negd'], 'f2negd')
        self.dma('sp', tdec[:], self.c_tdec, [], ['tdec'], 'f2tdec')
        w3 = self.L('filt_w3')
        gi_ = 0
        hi_ = 0
        for sl in range(2):
            self.dma('sp', hdn[:], self.hdn[sl], [], ['hdn'], 'f2hdn')
            for hh in range(2):
                for k in range(2):
                    self.dma('sp', w3f[k * 64:(k + 1) * 64, :], w3[l, :, k * 512 + hh * 256:k * 512 + (hh + 1) * 256], [], [('w3f', k)], 'f2w3%d' % k)
                self.cp('dve', w3s[:], w3f[:], [('w3f', 0), ('w3f', 1)], ['w3s'])
                self.dma('sp', hbt[:], self.hbias_d[0:1, l * DH + hh * 256:l * DH + (hh + 1) * 256], [], ['hbt'], 'f2hbt')
                for b2 in range(64):
                    bk = self.nb()
                    ds_ = b2 % 2
                    for u2 in range(2):
                        b = b2 * 2 + u2
                        self.mm(self.ps[:, bk, u2 * 256:(u2 + 1) * 256], hdn[:].rearrange('p (a b) -> p b a', b=128)[:, b, :],
                                w3s[:], True, True, ['hdn', 'w3s'], self.pr(bk))
                        self.act(dc[ds_][:, u2, :], negd[:, hh * 256:(hh + 1) * 256], AF.Exp, ['negd', 'tdec'], [('dc', ds_, u2)],
                                 scale=tdec[:, sl, b:b + 1])
                    self.tt('dve', g[:, 2 * b2:2 * b2 + 2, :], self.ps[:, bk, :].rearrange('p (u c) -> p u c', u=2), dc[ds_][:],
                            ALU.mult, self.pr(bk) + [('dc', ds_, 0), ('dc', ds_, 1)], [('g', b2)])
                    if b2 == 0:
                        self.stt('dve', g[0:1, 0, :], hbt[:], self.e0[0:1, sl:sl + 1],
                                 g[0:1, 0, :], ALU.mult, ALU.add, [('g', 0), 'hbt', 'g_e0'], [('g', 0)])
                gr = [('g', b2) for b2 in range(64)]
                c = 0
                while c < 256:
                    n = min(3, 256 - c)
                    bk = self.nb()
                    for u3 in range(n):
                        self.mm(self.ps[:, bk, u3 * 130:(u3 + 1) * 130], g[:, :, c + u3], f1m[:], True, True, gr + ['f1m'], self.pr(bk))
                    self.cp(self.alt(), Ysb[:, :, :, c:c + n], self.ps[:, bk, 0:n * 130].rearrange('p (c r k) -> p r k c', c=n, r=2),
                            self.pr(bk), [('Ysb', c)])
                    c += n
                yr = [('Ysb', c) for c in range(0, 256, 3)]
                for ka0 in range(0, KA, 5):
                    gs = self.g_stream(Gt, None, ka0, gi_)
                    gi_ += 1
                    hs = hi_ % 2
                    hi_ += 1
                    nk = min(5, KA - ka0)
                    for kq in range(nk):
                        ka = ka0 + kq
                        bk = self.nb()
                        zr = self.ps[:, bk, 0:256]
                        zi = self.ps[:, bk, 256:512]
                        rr = yr + [('Gt', gs)]
                        self.mm(zr, Gt[gs][:, kq, 0, :], Ysb[:, 0, ka, :], True, False, rr, self.pr(bk))
                        self.mm(zr, Gt[gs][:, kq, 2, :], Ysb[:, 1, ka, :], False, True, rr, self.pr(bk))
                        self.mm(zi, Gt[gs][:, kq, 0, :], Ysb[:, 1, ka, :], True, False, rr, self.pr(bk))
                        self.mm(zi, Gt[gs][:, kq, 1, :], Ysb[:, 0, ka, :], False, True, rr, self.pr(bk))
                        self.cp(self.alt(), HfT[hs][:, kq, :, :], self.ps[:, bk, :].rearrange('p (r c) -> p r c', r=2), self.pr(bk), [('HfT', hs, kq)])
                    for g4 in range(4):
                        gg = hh * 4 + g4
                        dst = self.hfs[gg].rearrange('p (k r s c) -> p k r s c', k=KA, r=2, s=2)[:, ka0:ka0 + nk, :, sl, :]
                        self.dma('sp', dst, HfT[hs][:, 0:nk, :, g4 * 64:(g4 + 1) * 64], [('HfT', hs, kq) for kq in range(nk)],
                                 [('hfs', gg, sl, ka0)], 'f2hf%d_%d_st' % (hs, g4))
        self.end()

    def phase_b(self, l):
        self.begin()
        xa = [self.tile('xa%d' % s, [64, CG, 128], BF16) for s in range(2)]
        hft = self.tile('hft', [128, KA, 2, 2, CG], BF16)
        Ysb = self.tile('Ysb', [128, 2, KA, 2, CG], BF16)
        Wt = self.tile('Wt', [128, 2, CG, KA], BF16)
        U = self.tile('U', [KA, 2, 128, CG], BF16)
        Yo = self.tile('Yo', [64, CG, 128], BF16)
        A = self.tile('A', [128, 4, 2, 2 * CG], F32)
        B1 = self.tile('B1', [128, 4, 2 * CG], F32)
        B2 = self.tile('B2', [128, 4, 2 * CG], F32)
        Dr = self.tile('Dr', [128, 4, 2 * CG], F32)
        Di = self.tile('Di', [128, 4, 2 * CG], F32)
        Gt = [self.tile('Gt%d' % s, [128, 5, 3, 128], BF16) for s in range(2)]
        Ht = [self.tile('Ht%d' % s, [KA, 16, 2, 64], BF16) for s in range(2)]
        f1m = self.tile('f1m', [128, 130], BF16)
        e12 = self.tile('e12', [128, 2, 256], BF16)
        self.dma('sp', f1m[:], self.c_f1m, [], ['f1m'], 'bf1m')
        self.dma('sp', e12[:], self.c_e12, [], ['e12'], 'be12')
        hall = self.L('hall')
        gi_ = 0
        hi_ = 0
        for gg in range(NG):
            c0 = gg * CG
            for sl in range(2):
                self.dma('sp', xa[sl][:], self.vall[gg][sl * 64:(sl + 1) * 64, :].rearrange('a (c b) -> a c b', c=CG),
                         [], [('xa', sl)], 'bxa%d' % sl)
            self.dma('sp', hft[:], self.hfs[gg].rearrange('p (k r s c) -> p k r s c', k=KA, r=2, s=2), [], ['hft'], 'bhft')
            for sl in range(2):
                c = 0
                while c < CG:
                    n = min(3, CG - c)
                    bk = self.nb()
                    for u3 in range(n):
                        self.mm(self.ps[:, bk, u3 * 130:(u3 + 1) * 130], xa[sl][:, c + u3, :], f1m[0:64, :], True, True,
                                [('xa', sl), 'f1m'], self.pr(bk))
                    self.cp(self.alt(), Ysb[:, :, :, sl, c:c + n], self.ps[:, bk, 0:n * 130].rearrange('p (c r k) -> p r k c', c=n, r=2),
                            self.pr(bk), [('Ysb', sl, c)])
                    c += n
            yr = [('Ysb', sl, c) for sl in range(2) for c in range(0, CG, 3)]
            for ka0 in range(0, KA, 4):
                nk = min(4, KA - ka0)
                bz = self.nb(2)
                for kq in range(nk):
                    ka = ka0 + kq
                    if ka % 5 == 0:
                        gs = self.g_stream(Gt, None, ka, gi_)
                        gi_ += 1
                    gq_ = ka % 5
                    zr = self.ps[:, bz + kq // 2, (kq % 2) * 256:(kq % 2) * 256 + 128]
                    zi = self.ps[:, bz + kq // 2, (kq % 2) * 256 + 128:(kq % 2) * 256 + 256]
                    rr = yr + [('Gt', gs)]
                    yre = Ysb[:, 0, ka, :, :].rearrange('p s c -> p (s c)')
                    yim = Ysb[:, 1, ka, :, :].rearrange('p s c -> p (s c)')
                    w_ = self.pr(bz + kq // 2)
                    self.mm(zr, Gt[gs][:, gq_, 0, :], yre, True, False, rr, w_)
                    self.mm(zr, Gt[gs][:, gq_, 2, :], yim, False, True, rr, w_)
                    self.mm(zi, Gt[gs][:, gq_, 0, :], yim, True, False, rr, w_)
                    self.mm(zi, Gt[gs][:, gq_, 1, :], yre, False, True, rr, w_)
                zps = self.ps[:, bz:bz + 2, :].rearrange('p b (k r n) -> p (b k) r n', k=2, r=2)[:, 0:nk]
                hf_ = hft[:, ka0:ka0 + nk].rearrange('p k r s c -> p k r (s c)')
                pz = self.pr(bz, 2)
                self.tt('dve', A[:, 0:nk], zps, hf_, ALU.mult, pz + ['hft'], ['A'])
                self.tt('dve', B1[:, 0:nk], zps[:, :, 0, :], hf_[:, :, 1, :], ALU.mult, pz + ['hft'], ['B1'])
                self.tt('dve', B2[:, 0:nk], zps[:, :, 1, :], hf_[:, :, 0, :], ALU.mult, pz + ['hft'], ['B2'])
                self.tt('dve', Dr[:, 0:nk], A[:, 0:nk, 0, :], A[:, 0:nk, 1, :], ALU.subtract, ['A'], ['Dr'])
                self.tt('pool', Di[:, 0:nk], B1[:, 0:nk], B2[:, 0:nk], ALU.add, ['B1', 'B2'], ['Di'])
                for ri, Dx in ((0, Dr), (1, Di)):
                    self.tt('pool' if ri == 0 else 'dve', Wt[:, ri, :, ka0:ka0 + nk], Dx[:, 0:nk, 0:CG].rearrange('p k c -> p c k'),
                            Dx[:, 0:nk, CG:2 * CG].rearrange('p k c -> p c k'), ALU.add, ['Dr' if ri == 0 else 'Di'], [('Wt', ka0, ri)])
            wr = [('Wt', ka0, ri) for ka0 in range(0, KA, 4) for ri in range(2)]
            for c in range(0, CG, 2):
                bk = self.nb()
                for u2 in range(2):
                    o_ = self.ps[0:KA, bk, u2 * 256:(u2 + 1) * 256]
                    self.mm(o_, Wt[:, 0, c + u2, :], e12[:, 0, :], True, False, wr + ['e12'], self.pr(bk))
                    self.mm(o_, Wt[:, 1, c + u2, :], e12[:, 1, :], False, True, wr + ['e12'], self.pr(bk))
                self.cp(self.alt(), U[:, :, :, c:c + 2], self.ps[0:KA, bk, :].rearrange('p (c r b) -> p r b c', c=2, r=2),
                        self.pr(bk), [('U', c)])
            ur = [('U', c) for c in range(0, CG, 2)]
            for b0 in range(0, 128, 8):
                if b0 % 16 == 0:
                    hs = hi_ % 2
                    hi_ += 1
                    self.dma('sp', Ht[hs][:], hall[:, b0:b0 + 16], [], [('Ht', hs)], 'bHt%d' % hs)
                bk = self.nb()
                for q8 in range(8):
                    bp = b0 + q8
                    o_ = self.ps[0:64, bk, q8 * 64:(q8 + 1) * 64]
                    self.mm(o_, Ht[hs][:, bp % 16, 0, :], U[:, 0, bp, :], True, False, ur + [('Ht', hs)], self.pr(bk))
                    self.mm(o_, Ht[hs][:, bp % 16, 1, :], U[:, 1, bp, :], False, True, ur + [('Ht', hs)], self.pr(bk))
                self.cp(self.alt(), Yo[:, :, b0:b0 + 8], self.ps[0:64, bk, :].rearrange('p (b c) -> p c b', b=8), self.pr(bk), [('Yo', b0)])
            self.dma('sp', self.yconv[c0:c0 + CG, :].rearrange('c (a b) -> a c b', a=64), Yo[:],
                     [('Yo', b0) for b0 in range(0, 128, 8)], [('yconv', gg)], 'bYo_st')
        self.end()

    def phase_c1a(self, l):
        self.begin()
        w_out = self.L('w_out')
        wo = self.tile('wo', [128, 8, D], BF16)
        stg = [self.tile('c1a_stg%d' % s, [128, D], F32) for s in range(4)]
        for kc in range(8):
            sc = self.col('g_ao', l, 1, kc) if kc < 4 else self.col('g_ho', l, 1, kc - 4)
            self.load_w(w_out[l, kc * 128:(kc + 1) * 128, :], D, sc, stg, 'c1astg', [(wo[:, kc, :], 0, D, None, ('wo', kc))])
        mix = [self.tile('mix%d' % s, [128, 8, 512], BF16) for s in range(2)]
        xy = [self.tile('xy%d' % s, [128, 2, 4, 512], BF16) for s in range(2)]
        ht = [self.tile('ht%d' % s, [128, 8, 512], F32) for s in range(2)]
        hy = self.tile('hy', [128, 4, 512], F32)
        sqh = self.tile('sqh', [128, 4, 512], BF16)
        rsh = self.tile('rsh', [128, 512], F32)
        sq2 = self.tile('sq2', [128, 8, 512], BF16)
        rs2 = self.tile('rs2', [128, 512], F32)
        n2 = [self.tile('n2_%d' % s, [128, 8, 512], BF16) for s in range(2)]
        ed = self.tile('ed', [128, 8, 2], BF16)
        def loads(i):
            s = i % 2
            cs = slice(i * 512, i * 512 + 512)
            self.dma('sp', mix[s][:, 0:4, :], self.attn_n[:, cs].rearrange('(c p) t -> p c t', p=128), [], [('mixa', s)], 'c1a_ma%d' % s)
            self.dma('sp', xy[s][:, 0], self.x0s[:, cs].rearrange('(c p) t -> p c t', p=128), [], [('xy', s, 0)], 'c1a_x%d' % s)
            self.dma('sp', xy[s][:, 1], self.yconv[:, cs].rearrange('(c p) t -> p c t', p=128), [], [('xy', s, 1)], 'c1a_y%d' % s)
            self.dma('sp', ht[s][:], self.hres[:, cs].rearrange('(c p) t -> p c t', p=128), [], [('ht', s)], 'c1a_h%d' % s)
        def P1(i):
            s = i % 2
            self.tt('dve', hy[:], xy[s][:, 0], xy[s][:, 1], ALU.mult, [('xy', s, 0), ('xy', s, 1)], ['hy'])
            self.act(sqh[:], hy[:], AF.Square, ['hy'], ['sqh'])
            bk = self.nb()
            for mc in range(4):
                self.mm(self.ps[:, bk, :], self.ones[:, 1, :], sqh[:, mc, :], mc == 0, mc == 3, ['sqh', 'g_ones'], self.pr(bk))
            self.rstd(rsh[:], self.ps[:, bk, :], self.pr(bk), ['rsh'])
            self.tt('pool', mix[s][:, 4:8, :], hy[:], _bc(rsh[:], [128, 4, 512], 1), ALU.mult, ['hy', 'rsh'], [('mixh', s)])

        def P2(i):
            s = i % 2
            cs = slice(i * 512, i * 512 + 512)
            for g2 in range(2):
                bo = self.nb(4)
                for j in range(4):
                    mc = g2 * 4 + j
                    for kc in range(8):
                        self.mm(self.ps[:, bo + j, :], wo[:, kc, mc * 128:(mc + 1) * 128], mix[s][:, kc, :], kc == 0, kc == 7,
                                [('mixa', s), ('mixh', s), ('wo', kc)], self.pr(bo + j))
                self.tt('dve', ht[s][:, g2 * 4:(g2 + 1) * 4, :], ht[s][:, g2 * 4:(g2 + 1) * 4, :], self.ps[:, bo:bo + 4, :], ALU.add,
                        [('ht', s)] + self.pr(bo, 4), [('ht', s)])
            self.dma('sp', self.hres[:, cs].rearrange('(c p) t -> p c t', p=128), ht[s][:], [('ht', s)], [('hres', i)], 'c1a_h%d_st' % s)

        def P3(i):
            s = i % 2
            t0 = i * 512
            self.act(sq2[:], ht[s][:], AF.Square, [('ht', s)], ['sq2'])
            bk = self.nb()
            for kc in range(8):
                self.mm(self.ps[:, bk, :], self.ones[:, 0, :], sq2[:, kc, :], kc == 0, kc == 7, ['sq2', 'g_ones'], self.pr(bk))
            self.rstd(rs2[:], self.ps[:, bk, :], self.pr(bk), ['rs2'])
            for hf in range(2):
                eng = 'dve' if hf == 0 else 'pool'
                self.tt(eng, n2[s][:, hf * 4:(hf + 1) * 4, :], ht[s][:, hf * 4:(hf + 1) * 4, :], _bc(rs2[:], [128, 4, 512], 1), ALU.mult,
                        [('ht', s), 'rs2'], [('n2', s, hf)])
            rd = [('n2', s, 0), ('n2', s, 1)]
            self.dma('sp', self.n2s[:, 1 + t0:1 + t0 + 512].rearrange('(c p) t -> p c t', p=128), n2[s][:], rd, [('n2s', i)], 'c1a_n%d_st' % s)
            if i == 0:
                self.dma('sp', self.xn_in[0:1, :].rearrange('o (c p) -> p c o', p=128), n2[s][:, :, 0:1], rd, ['xn0'], 'c1a_e0', slow=True)
            if i == NT - 1:
                self.dma('sp', self.xn_in[1:2, :].rearrange('o (c p) -> p c o', p=128), n2[s][:, :, 511:512], rd, ['xn1'], 'c1a_e1', slow=True)

        loads(0)
        loads(1)
        P1(0)
        for i in range(NT):
            if i + 1 < NT:
                P1(i + 1)
            P2(i)
            P3(i)
            if i + 2 < NT:
                loads(i + 2)
        self.allgather(self.xn_in, self.xn_out, ['xn0', 'xn1'], ['xn_out'])
        for k, (row, col) in enumerate([(1, 0), (2, T + 1)]):
            self.dma('sp', ed[:, :, k:k + 1], self.xn_out[row:row + 1, :].rearrange('o (c p) -> p c o', p=128), ['xn_out'], [('ed', k)], 'c1a_ed%d' % k, slow=True)
            self.ts('dve', ed[:, :, k:k + 1], ed[:, :, k:k + 1], self.edge[:, k:k + 1], None, ALU.mult, None, [('ed', k), 'g_edge'], [('ed', k)])
            self.dma('sp', self.n2s[:, col:col + 1].rearrange('(c p) t -> p c t', p=128), ed[:, :, k:k + 1], [('ed', k)], [('n2sh', k)], 'c1a_ed%d_st' % k, slow=True)
        self.end()

    def phase_c1b(self, l):
        self.begin()
        w_up = self.L('w_up')
        wu = self.tile('wu', [128, 8, 2 * DFF], BF16)
        stg = [self.tile('c1b_stg%d' % s, [128, 2816], F32) for s in range(3)]
        for kc in range(8):
            for hh in range(2):
                self.load_w(w_up[l, kc * 128:(kc + 1) * 128, hh * DFF:(hh + 1) * DFF], DFF, self.col('g_ffn', l, 1, kc), stg, 'c1bstg',
                            [(wu[:, kc, hh * DFF:(hh + 1) * DFF], 0, DFF, None, ('wu', kc, hh))])
        n2t = [self.tile('n2t%d' % s, [128, 8, 512], BF16) for s in range(2)]
        at = [self.tile('at%d' % s, [128, 22, 510], BF16) for s in range(2)]
        ntl = (T + 509) // 510
        NQ = 3
        ua = [self.tile('ua%d' % q, [128, 510], F32) for q in range(NQ)]
        ug = [self.tile('ug%d' % q, [128, 510], F32) for q in range(NQ)]
        sg = [self.tile('sg%d' % q, [128, 510], F32) for q in range(NQ)]

        def loads(i):
            s = i % 2
            T0 = 510 * i
            nin = min(510, T - T0) + 2
            self.dma('sp', n2t[s][:, :, 0:nin], self.n2s[:, T0:T0 + nin].rearrange('(c p) t -> p c t', p=128), [], [('n2t', s)], 'c1b_n%d' % s)

        def front(i, j, q):
            s = i % 2
            nout = min(510, T - 510 * i)
            nin = nout + 2
            bk = self.nb(2)
            for hh in range(2):
                for kc in range(8):
                    self.mm(self.ps[:, bk + hh, 0:nin], wu[:, kc, hh * DFF + j * 128:hh * DFF + (j + 1) * 128], n2t[s][:, kc, 0:nin],
                            kc == 0, kc == 7, [('n2t', s), ('wu', kc, hh)], self.pr(bk + hh))
            for hh, ut in ((0, ua[q]), (1, ug[q])):
                wc = lambda k, hh=hh, j=j: self.col('w_ffc', l, 1, k * 44 + hh * 22 + j)
                pb = self.ps[:, bk + hh, :]
                rn = ('u', hh, q)
                self.act(ut[:, 0:nout], pb[:, 1:1 + nout], AF.Identity, self.pr(bk + hh) + ['g_cols'], [rn],
                         scale=wc(1), bias=self.col('b_ffc', l, 1, hh * 22 + j))
                self.stt('dve', ut[:, 0:nout], pb[:, 0:nout], wc(0), ut[:, 0:nout], ALU.mult, ALU.add, self.pr(bk + hh) + [rn, 'g_cols'], [rn])
                self.stt('dve', ut[:, 0:nout], pb[:, 2:2 + nout], wc(2), ut[:, 0:nout], ALU.mult, ALU.add, self.pr(bk + hh) + [rn, 'g_cols'], [rn])

        def back(i, j, q):
            s = i % 2
            nout = min(510, T - 510 * i)
            self.act(sg[q][:, 0:nout], ug[q][:, 0:nout], AF.Silu, [('u', 1, q)], [('sg', q)])
            self.tt('pool', at[s][:, j, 0:nout], sg[q][:, 0:nout], ua[q][:, 0:nout], ALU.mult, [('sg', q), ('u', 0, q)], [('at', s, j)])
            if j == 21:
                T0 = 510 * i
                self.dma('sp', self.acts[:, T0:T0 + nout].rearrange('(c p) t -> p c t', p=128), at[s][:, :, 0:nout],
                         [('at', s, jx) for jx in range(22)], [('acts', i)], 'c1b_a%d_st' % s)

        loads(0)
        seq = [(i, j) for i in range(ntl) for j in range(22)]
        for n_, (i, j) in enumerate(seq):
            if j == 0 and i + 1 < ntl:
                loads(i + 1)
            front(i, j, n_ % NQ)
            if n_ >= 1:
                pi, pj = seq[n_ - 1]
                back(pi, pj, (n_ - 1) % NQ)
        pi, pj = seq[-1]
        back(pi, pj, (len(seq) - 1) % NQ)
        self.end()

    def phase_c2(self, l, last):
        self.begin()
        wd = self.tile('wd', [128, 22, D], BF16)
        wg = self.tile('wg', [128, 8, D], BF16)
        wp = self.tile('wp', [128, 2, D], BF16)
        stg = [self.tile('c2_stg%d' % s, [128, D], F32) for s in range(4)]
        for kc in range(22):
            self.load_w(self.L('w_down')[l, kc * 128:(kc + 1) * 128, :], D, None, stg, 'c2stg', [(wd[:, kc, :], 0, D, None, ('wd', kc))])
        for kc in range(8):
            self.load_w(self.L('w_ple_gate')[l, kc * 128:(kc + 1) * 128, :], D, None, stg, 'c2stg', [(wg[:, kc, :], 0, D, None, ('wg', kc))])
        for kc in range(2):
            self.load_w(self.L('w_ple_proj')[l, kc * 128:(kc + 1) * 128, :], D, None, stg, 'c2stg', [(wp[:, kc, :], 0, D, None, ('wp', kc))])
        at = [self.tile('at%d' % s, [128, 22, 512], BF16) for s in range(2)]
        ht = [self.tile('ht%d' % s, [128, 8, 512], F32) for s in range(2)]
        pt = [self.tile('pt%d' % s, [128, 4, DPLE], F32) for s in range(2)]
        pT = self.tile('pT', [128, 2, 512], BF16)
        hb = self.tile('hb', [128, 8, 512], BF16)
        sgm = self.tile('sgm', [128, 4, 512], F32)
        yo = self.tile('yo', [128, 4, D], F32) if last else None
        p_d = self.L('p')
        def loads(i):
            s = i % 2
            cs = slice(i * 512, i * 512 + 512)
            self.dma('sp', at[s][:], self.acts[:, cs].rearrange('(c p) t -> p c t', p=128), [], [('at', s)], 'c2_a%d' % s)
            self.dma('sp', ht[s][:], self.hres[:, cs].rearrange('(c p) t -> p c t', p=128), [], [('ht', s, 0), ('ht', s, 1)], 'c2_h%d' % s)
            self.dma('sp', pt[s][:], p_d[l, cs, :].rearrange('(b p) f -> p b f', p=128), [], [('pt', s)], 'c2_p%d' % s)
        loads(0)
        for i in range(NT):
            s = i % 2
            t0 = i * 512
            cs = slice(t0, t0 + 512)
            if i + 1 < NT:
                loads(i + 1)
            for pc in range(2):
                bk = self.nb()
                for blk in range(4):
                    self.tp(self.ps[:, bk, blk * 128:(blk + 1) * 128], pt[s][:, blk, pc * 128:(pc + 1) * 128], self.ident[:],
                            [('pt', s), 'g_ident'], self.pr(bk))
                self.cp(self.alt(), pT[:, pc, :], self.ps[:, bk, :], self.pr(bk), [('pT', pc)])
            for g2 in range(2):
                bo = self.nb(4)
                for j in range(4):
                    mc = g2 * 4 + j
                    for kc in range(22):
                        self.mm(self.ps[:, bo + j, :], wd[:, kc, mc * 128:(mc + 1) * 128], at[s][:, kc, :], kc == 0, kc == 21,
                                [('at', s), ('wd', kc)], self.pr(bo + j))
                hs_ = ht[s][:, g2 * 4:(g2 + 1) * 4, :]
                self.tt('dve', hs_, hs_, self.ps[:, bo:bo + 4, :], ALU.add, [('ht', s, g2)] + self.pr(bo, 4), [('ht', s, g2)])
                self.cp('act', hb[:, g2 * 4:(g2 + 1) * 4, :], hs_, [('ht', s, g2)], [('hb', g2)])
            for g2 in range(2):
                bo = self.nb(4)
                for j in range(4):
                    mc = g2 * 4 + j
                    for kc in range(8):
                        self.mm(self.ps[:, bo + j, :], wg[:, kc, mc * 128:(mc + 1) * 128], hb[:, kc, :], kc == 0, kc == 7,
                                [('hb', 0), ('hb', 1), ('wg', kc)], self.pr(bo + j))
                self.act(sgm[:], self.ps[:, bo:bo + 4, :], AF.Sigmoid, self.pr(bo, 4), ['sgm'])
                bp = self.nb(4)
                for j in range(4):
                    mc = g2 * 4 + j
                    for kc in range(2):
                        self.mm(self.ps[:, bp + j, :], wp[:, kc, mc * 128:(mc + 1) * 128], pT[:, kc, :], kc == 0, kc == 1,
                                [('pT', 0), ('pT', 1), ('wp', kc)], self.pr(bp + j))
                self.tt('dve', sgm[:], sgm[:], self.ps[:, bp:bp + 4, :], ALU.mult, ['sgm'] + self.pr(bp, 4), ['sgm'])
                hs_ = ht[s][:, g2 * 4:(g2 + 1) * 4, :]
                self.tt('pool', hs_, hs_, sgm[:], ALU.add, [('ht', s, g2), 'sgm'], [('ht', s, g2)])
            hr_ = [('ht', s, 0), ('ht', s, 1)]
            if not last:
                self.dma('sp', self.hres[:, cs].rearrange('(c p) t -> p c t', p=128), ht[s][:], hr_, [('hres', i)], 'c2_h%d_st' % s)
            else:
                for blk in range(4):
                    for g2 in range(2):
                        bk = self.nb()
                        for j in range(4):
                            mc = g2 * 4 + j
                            self.tp(self.ps[:, bk, j * 128:(j + 1) * 128], ht[s][:, mc, blk * 128:(blk + 1) * 128], self.ident[:],
                                    hr_ + ['g_ident'], self.pr(bk))
                        self.cp(self.alt(), yo[:, blk, g2 * 512:(g2 + 1) * 512], self.ps[:, bk, :], self.pr(bk), [('yo', blk, g2)])
                self.dma('sp', self.y[cs, :].rearrange('(b p) f -> p b f', p=128), yo[:],
                         [('yo', blk, g2) for blk in range(4) for g2 in range(2)], [('y', i)], 'c2_y_st')
        self.end()

    def build(self):
        self.phase_p0()
        for l in range(self.depth):
            last = (l == self.depth - 1)
            for name in ['a0', 'a', 'a2', 'f1', 'f2', 'b', 'c1a', 'c1b', 'c2']:
                fn = getattr(self, 'phase_' + name, None)
                if fn is None:
                    return
                if name == 'c2':
                    fn(l, last)
                else:
                    fn(l)
                if self.stop == (name, l):
                    return


def _cols_table(b, W):
    L = DEPTH
    tab = np.zeros((128, b.ncol), np.float32)

    def put(name, arr):
        o, n = b.colspec[name]
        assert arr.shape == (128, n), (name, arr.shape, n)
        tab[:, o:o + n] = arr

    def chunks(v, nch):
        return v.reshape(L, nch, 128).transpose(2, 0, 1).reshape(128, L * nch)

    put('g_mix', chunks(W['rms_mix'], 8))
    put('g_ffn', chunks(W['rms_ffn'], 8))
    put('gq', np.tile(W['q_norm'].T, (2, 1)))
    put('gk', np.tile(W['k_norm'].T, (2, 1)))
    sk = W['sink'].reshape(L, 2, 4)[:, :, [0, 2, 1, 3]].reshape(1, L * 8)
    put('sink', np.tile(sk, (128, 1)))
    put('w_short', W['w_short'].reshape(L, 3, 12, 128).transpose(3, 0, 1, 2).reshape(128, L * 36))
    put('b_short', chunks(W['b_short'], 12))
    put('g_ao', chunks(W['norm_attn_out'], 4))
    put('g_ho', chunks(W['norm_hyena_out'], 4))
    put('w_ffc', W['w_ffconv'].reshape(L, 3, 44, 128).transpose(3, 0, 1, 2).reshape(128, L * 132))
    put('b_ffc', chunks(W['b_ffconv'], 44))
    for nm, key in [('fb1', 'filt_b1'), ('ffr1', 'filt_freq1'), ('fb2', 'filt_b2'), ('ffr2', 'filt_freq2')]:
        put(nm, np.tile(W[key].T, (2, 1)))
    return tab


_BUILD_CACHE = {}


def _get_builder(depth, debug, stop):
    key = (depth, debug, stop)
    if key not in _BUILD_CACHE:
        b = Builder(depth, debug, stop)
        b.declare()
        b.setup_globals()
        b.build()
        es = contextlib.ExitStack()
        b.P.emit(es)
        b._es = es
        _BUILD_CACHE[key] = b
    return _BUILD_CACHE[key]


def _run(inputs, depth=DEPTH, debug=False, stop=None):
    W = {k: np.asarray(v, dtype=np.float32) for k, v in inputs.items()}
    b = _get_builder(depth, debug, stop)
    sh = _shared_consts()
    cols = _cols_table(b, W)
    xp = W['x_prompt'][0]
    xs = W['x_sample']
    pp = W['p_prompt'][:, 0]
    psm = W['p_sample']
    in_maps = []
    for rank in range(N_CORES):
        kind, idx = _unit_of_rank(rank)
        rc = _rank_consts(rank)
        if kind == 'p':
            x = xp[idx * T:(idx + 1) * T]
            p = pp[:, idx * T:(idx + 1) * T]
        else:
            x = xs[idx]
            p = psm[:, idx]
        m = {
            'x': np.ascontiguousarray(x), 'p': np.ascontiguousarray(p),
            'w_in': W['w_in'], 'w_out': W['w_out'], 'w_up': W['w_up'], 'w_down': W['w_down'],
            'w_ple_gate': W['w_ple_gate'], 'w_ple_proj': W['w_ple_proj'],
            'filt_w1': W['filt_w1'], 'filt_w2': W['filt_w2'], 'filt_w3': W['filt_w3'],
            'cols': cols, 'hbias': W['hyena_bias'].reshape(1, -1),
            'ident_f': sh['ident_f'], 'ones_b': sh['ones_b'], 'prot_b': sh['prot_b'],
            'masks': rc['masks'], 'edge': rc['edge'], 'cstab': rc['cstab'],
            'f1m': sh['f1m'], 'gall': sh['gall'], 'e12': sh['e12'], 'hall': sh['hall'], 'negd': sh['negd'],
            'zf': rc['zf'], 'mm': rc['mm'], 'tdec': rc['tdec'], 'e0': rc['e0'],
        }
        in_maps.append({k: v for k, v in m.items() if k in b.inputs})
    res = run_bass_kernel_spmd(b.nc, in_maps, core_ids=list(range(N_CORES)))
    return b, res


def kernel(**inputs):
    b, res = _run(inputs)
    ys = [np.asarray(res.results[r]['y'], dtype=np.float32) for r in range(6)]
    y_prompt = np.concatenate([ys[0], ys[1]], axis=0)[None]
    y_sample = np.stack(ys[2:6], axis=0)
    return (y_prompt, y_sample)
```

```python
import os
import math
import contextlib
import numpy as np
import ml_dtypes
import concourse.bass as bass
import concourse.mybir as mybir
from concourse.bass_utils import run_bass_kernel_spmd

F32 = mybir.dt.float32
BF16 = mybir.dt.bfloat16
AF = mybir.ActivationFunctionType
ALU = mybir.AluOpType
BF = ml_dtypes.bfloat16

D = 1024
DEPTH = 4
T = 8192
NT = 16
HALO = 128
NW = T + 2 * HALO
DQ = 512
DH = 512
DFF = 2816
DPLE = 256
EPS = 1e-6
NF = 16384
KA = 65
CG = 64
NG = DH // CG
ROPE_THETA = 500000.0
N_CORES = 8


class Prog:
    def __init__(self, nc):
        self.nc = nc
        self.ops = []
        self.lastw = {}
        self.readers = {}
        self.lastdma = {}
        self.eng = {'pe': nc.tensor, 'act': nc.scalar, 'dve': nc.vector, 'pool': nc.gpsimd, 'sp': nc.sync}
        self.last_on = {}
        self.n_cc = 0

    def op(self, eng, fn, r=(), w=(), dma=None, cc=False):
        idx = len(self.ops)
        deps = set()
        for x in r:
            p = self.lastw.get(x)
            if p is not None:
                deps.add(p)
        for x in w:
            p = self.lastw.get(x)
            if p is not None:
                deps.add(p)
            deps.update(self.readers.get(x, ()))
        if dma is not None:
            p = self.lastdma.get(dma)
            if p is not None:
                deps.add(p)
            self.lastdma[dma] = idx
        for x in r:
            self.readers.setdefault(x, []).append(idx)
        for x in w:
            self.lastw[x] = idx
            self.readers[x] = []
        deps.discard(idx)
        self.ops.append(dict(eng=eng, fn=fn, deps=deps, dma=dma, cc=cc, bar=False))
        if not cc:
            self.last_on[eng] = idx
        else:
            self.sticky = getattr(self, 'sticky', {})
            for x in w:
                self.sticky[x] = idx
        return idx

    def barrier(self):
        deps = set(self.last_on.values()) | set(self.lastdma.values())
        for k, e in enumerate(('pe', 'act', 'dve', 'pool', 'sp')):
            self.ops.append(dict(eng=e, fn=None, deps=set(deps), dma=None, cc=False, bar=True, reset=(k == 0)))
        self.lastw = dict(getattr(self, 'sticky', {}))
        self.readers = {}

    def emit(self, es):
        nc = self.nc
        ops = self.ops
        def pe2pe(p, o):
            return (p['dma'] is None and not p['cc'] and p['eng'] == 'pe' and o['eng'] == 'pe'
                    and o['dma'] is None and o['fn'] is not None)
        sig = [False] * len(ops)
        for o in ops:
            latest = {}
            for d in o['deps']:
                p = ops[d]
                if pe2pe(p, o):
                    continue
                if p['dma'] is not None or p['cc']:
                    sig[d] = True
                else:
                    if latest.get(p['eng'], -1) < d:
                        latest[p['eng']] = d
            for d in latest.values():
                sig[d] = True
            o['bind'] = set(latest.values())
        sems = {}

        def getsem(name):
            if name not in sems:
                sems[name] = es.enter_context(nc.semaphore(name))
            return sems[name]

        cnt = {}
        val = [None] * len(ops)
        chan = [None] * len(ops)
        waited = {e: {} for e in self.eng}
        n_wait = 0
        keyslot = {}
        ncc = 0
        for i, o in enumerate(ops):
            e = o['eng']
            eobj = self.eng[e]
            if o.get('reset'):
                keyslot = {}
            need = {}
            for d in o['deps']:
                if val[d] is None:
                    continue
                p = ops[d]
                if p['dma'] is None and not p['cc'] and d not in o['bind']:
                    continue
                c = chan[d]
                if need.get(c, 0) < val[d]:
                    need[c] = val[d]
            for c, v in need.items():
                if waited[e].get(c, 0) >= v:
                    continue
                eobj.wait_ge(getsem(c), v)
                waited[e][c] = v
                n_wait += 1
            if o['fn'] is None:
                continue
            ins = o['fn'](eobj)
            if o['cc']:
                c = 'cc%d' % (ncc % 8)
                ncc += 1
                cnt[c] = cnt.get(c, 0) + 1
                ins.then_inc(getsem(c), 1)
                chan[i] = c
                val[i] = cnt[c]
            elif o['dma'] is not None:
                if o['dma'] not in keyslot:
                    keyslot[o['dma']] = len(keyslot)
                c = 'dslot%d' % keyslot[o['dma']]
                cnt[c] = cnt.get(c, 0) + 16
                ins.then_inc(getsem(c), 16)
                chan[i] = c
                val[i] = cnt[c]
            elif sig[i]:
                c = 'e_' + e
                cnt[c] = cnt.get(c, 0) + 1
                ins.then_inc(getsem(c), 1)
                chan[i] = c
                val[i] = cnt[c]
        for e in ('sp',):
            eobj = self.eng[e]
            for c, v in cnt.items():
                if waited[e].get(c, 0) < v:
                    eobj.wait_ge(getsem(c), v)
        self.stats = dict(n_ops=len(ops), n_wait=n_wait, n_sems=len(sems))


def _unit_of_rank(rank):
    return [('p', 0), ('p', 1), ('s', 0), ('s', 1), ('s', 2), ('s', 3), ('s', 2), ('s', 3)][rank]


_CONST_CACHE = {}


def _shared_consts():
    if 'shared' in _CONST_CACHE:
        return _CONST_CACHE['shared']
    c = {}
    c['ident_f'] = np.eye(128, dtype=np.float32)
    ones = np.zeros((128, 4, 128), np.float32)
    ones[:, 0, :] = 1.0 / 1024.0
    ones[:, 1, :] = 1.0 / 512.0
    ones[:, 2, :] = 1.0
    blk = np.zeros((128, 128), np.float32)
    blk[:64, :64] = 1.0 / 64.0
    blk[64:, 64:] = 1.0 / 64.0
    ones[:, 3, :] = blk
    c['ones_b'] = ones.astype(BF)
    prot = np.zeros((128, 128), np.float32)
    for p in range(128):
        d = p % 64
        if d < 8:
            prot[p + 8, p] = -1.0
        elif d < 16:
            prot[p - 8, p] = 1.0
    c['prot_b'] = prot.astype(BF)
    j = np.arange(128)[:, None]
    i = np.arange(128)[None, :]
    c['mprev'] = (j >= i).astype(np.float32)
    c['mnext'] = (j <= i).astype(np.float32)
    a = np.arange(128, dtype=np.float64)[:, None]
    ka = np.arange(KA, dtype=np.float64)[None, :]
    th = 2 * np.pi * a * ka / 128.0
    c['f1m'] = np.concatenate([np.cos(th), -np.sin(th)], axis=1).astype(BF)
    b = np.arange(128, dtype=np.float64)[:, None, None]
    kav = np.arange(KA, dtype=np.float64)[None, :, None]
    kb = np.arange(128, dtype=np.float64)[None, None, :]
    th = 2 * np.pi * b * (kav + 128.0 * kb) / NF
    gall = np.stack([np.cos(th), -np.sin(th), np.sin(th)], axis=2)
    c['gall'] = gall.astype(BF)
    kbv = np.arange(128, dtype=np.float64)[:, None]
    bp = np.arange(128, dtype=np.float64)[None, :]
    th = 2 * np.pi * kbv * bp / 128.0
    e1 = np.concatenate([np.cos(th), np.sin(th)], axis=1)
    e2 = np.concatenate([-np.sin(th), np.cos(th)], axis=1)
    c['e12'] = np.stack([e1, e2], axis=1).astype(BF)
    kav = np.arange(KA, dtype=np.float64)[:, None, None]
    bpv = np.arange(128, dtype=np.float64)[None, :, None]
    ap = np.arange(64, dtype=np.float64)[None, None, :]
    ph = 2 * np.pi * kav * (bpv + 128.0 * ap) / NF
    wt = np.full((KA, 1, 1), 2.0)
    wt[0] = 1.0
    wt[64] = 1.0
    hall = np.stack([wt / NF * np.cos(ph), -wt / NF * np.sin(ph)], axis=2)
    c['hall'] = hall.astype(BF)
    deltas = np.linspace(math.log(1e-2) / 1.5, math.log(1e-2) / 0.3, DH).astype(np.float32)
    c['negd'] = np.tile(-np.abs(deltas)[None, :], (128, 1)).astype(np.float32)
    _CONST_CACHE['shared'] = c
    return c


def _zfeat(L):
    key = ('z', L)
    if key in _CONST_CACHE:
        return _CONST_CACHE[key]
    t = np.linspace(0.0, 1.0, L).astype(np.float32)
    w = (2.0 * np.pi * np.arange(L, dtype=np.float64) / L)
    f = np.linspace(1e-4, 15.0, 16)
    fw = f[None, :] * w[:, None]
    z = np.concatenate([t[:, None].astype(np.float64), np.cos(fw), -np.sin(fw)], axis=1).astype(np.float32)
    _CONST_CACHE[key] = (t, z)
    return t, z


def _rank_consts(rank):
    key = ('rank', rank)
    if key in _CONST_CACHE:
        return _CONST_CACHE[key]
    kind, idx = _unit_of_rank(rank)
    sh = _shared_consts()
    c = {}
    pos0 = 8192 if (kind == 'p' and idx == 1) else 0
    mL = 1.0 if (kind == 'p' and idx == 1) else 0.0
    mR = 1.0 if (kind == 'p' and idx == 0) else 0.0
    c['edge'] = np.tile(np.array([[mL, mR]], np.float32), (128, 1))
    masks = np.stack([sh['mprev'], sh['mnext'], sh['mprev'] * mL, sh['mnext'] * mR], axis=1)
    c['masks'] = masks.astype(BF)
    pos = (pos0 - HALO + np.arange(NW)).astype(np.float32)
    inv = (ROPE_THETA ** (-np.arange(0, 16, 2, dtype=np.float32) / 16.0)).astype(np.float32)
    ang = pos[None, :] * inv[:, None]
    cs = np.zeros((128, 2, NW), np.float32)
    cs[:, 0, :] = 1.0
    for p in range(128):
        d = p % 64
        if d < 16:
            cs[p, 0, :] = np.cos(ang[d % 8])
            cs[p, 1, :] = np.sin(ang[d % 8])
    c['cstab'] = cs
    L = 16384 if kind == 'p' else 8192
    t_all, z_all = _zfeat(L)
    n = np.arange(NF)
    zf = np.zeros((2, 33, NF), np.float32)
    mm = np.zeros((2, 128, NF), np.float32)
    td = np.zeros((2, NF), np.float32)
    e0 = np.zeros((2,), np.float32)
    own = idx % 2 if kind == 's' else idx
    for s in range(2):
        lag = np.zeros(NF, np.int64)
        dr = np.zeros(NF, np.int64)
        if s == own:
            lo = n < 8192
            hi = n > 8192
            lag[lo] = n[lo]
            dr[lo] = 1
            lag[hi] = NF - n[hi]
            dr[hi] = 2
            e0[s] = 1.0
        elif kind == 'p':
            lo = n < 8192
            hi = n > 8192
            if idx == 0:
                lag[lo] = 8192 - n[lo]
                dr[lo] = 2
                lag[hi] = 24576 - n[hi]
                dr[hi] = 2
            else:
                lag[lo] = n[lo] + 8192
                dr[lo] = 1
                lag[hi] = n[hi] - 8192
                dr[hi] = 1
        valid = dr > 0
        zf[s][:, valid] = z_all[lag[valid]].T
        td[s][valid] = t_all[lag[valid]]
        mm[s][:64, :] = (dr == 1).astype(np.float32)[None, :]
        mm[s][64:, :] = (dr == 2).astype(np.float32)[None, :]
    c['zf'] = zf
    c['mm'] = mm.astype(BF)
    c['tdec'] = np.ascontiguousarray(td.reshape(2, 128, 128).transpose(1, 0, 2))
    c['e0'] = np.tile(e0[None, :], (128, 1)).astype(np.float32)
    _CONST_CACHE[key] = c
    return c


def _bc(ap, shape, axis):
    return ap.unsqueeze(axis).to_broadcast(shape)


class Builder:
    def __init__(self, depth=DEPTH, debug=False, stop=None):
        self.depth = depth
        self.debug = debug
        self.stop = stop
        self.nc = bass.Bass("TRN2", target_bir_lowering=False)
        self.P = Prog(self.nc)
        self.ges = contextlib.ExitStack()
        self.scope = None
        self.bank = 0
        self.rr = 0
        self.inputs = {}
        self.dbg_out = []
        self.sub = float(os.environ.get('MK_SUB', '99'))
        self.ntr = int(os.environ.get('MK_NT', str(NT)))

    def din(self, name, shape, dt=F32):
        t = self.nc.dram_tensor(name, list(shape), dt, kind="ExternalInput")
        self.inputs[name] = (tuple(shape), dt)
        return t.ap()

    def L(self, name):
        if name not in self.lazy_ap:
            shp, dt = self.lazy[name]
            self.lazy_ap[name] = self.din(name, shp, dt)
        return self.lazy_ap[name]

    def dscr(self, name, shape, dt, dbg=True):
        if self.debug and dbg and name in self.debug:
            self.dbg_out.append(name)
            return self.nc.dram_tensor(name, list(shape), dt, kind="ExternalOutput").ap()
        return self.nc.dram_tensor(name, list(shape), dt).ap()

    def gtile(self, name, shape, dt):
        return self.ges.enter_context(self.nc.sbuf_tensor('sbg_' + name, list(shape), dt))

    def tile(self, name, shape, dt):
        self.uid = getattr(self, 'uid', 0) + 1
        return self.scope.enter_context(self.nc.sbuf_tensor('sb%d_%s' % (self.uid, name), list(shape), dt))

    def begin(self):
        self.scope = contextlib.ExitStack()
        import inspect
        nm = inspect.stack()[1].function
        self.marks = getattr(self, 'marks', [])
        self.marks.append((nm, sum(1 for o in self.P.ops if o['eng'] == 'pe' and o['fn'] is not None)))

    def end(self):
        self.P.barrier()
        self.scope.close()
        self.scope = None

    def nb(self, k=1):
        if self.bank + k > 8:
            self.bank = 0
        b = self.bank
        self.bank = (self.bank + k) % 8
        return b

    def pr(self, b, k=1):
        return [('ps', b + j) for j in range(k)]

    def alt(self, engs=('act', 'dve')):
        self.rr += 1
        return engs[self.rr % len(engs)]

    def mm(self, out, lhsT, rhs, start, stop, r, w):
        self.P.op('pe', lambda e, o=out, l=lhsT, x=rhs, s=start, t=stop: e.matmul(o, l, x, start=s, stop=t), r, w)

    def tp(self, out, in_, ident, r, w):
        self.P.op('pe', lambda e, o=out, i=in_, d=ident: e.transpose(o, i, d), r, w)

    def act(self, out, in_, func, r, w, scale=None, bias=None):
        kw = {}
        if scale is not None:
            kw['scale'] = scale
        if bias is not None:
            kw['bias'] = bias
        self.P.op('act', lambda e, o=out, i=in_, f=func, k=kw: e.activation(o, i, f, **k), r, w)

    def cp(self, eng, out, in_, r, w):
        if eng == 'act':
            self.act(out, in_, AF.Copy, r, w)
        else:
            self.P.op(eng, lambda e, o=out, i=in_: e.tensor_copy(o, i), r, w)

    def tt(self, eng, out, in0, in1, op, r, w):
        self.P.op(eng, lambda e, o=out, a=in0, b=in1, p=op: e.tensor_tensor(o, a, b, p), r, w)

    def ts(self, eng, out, in0, s1, s2, op0, op1, r, w):
        if op1 is None:
            self.P.op(eng, lambda e, o=out, a=in0, x=s1, p=op0: e.tensor_scalar(o, a, x, None, p), r, w)
        else:
            self.P.op(eng, lambda e, o=out, a=in0, x=s1, y=s2, p=op0, q=op1: e.tensor_scalar(o, a, x, y, p, q), r, w)

    def stt(self, eng, out, in0, scalar, in1, op0, op1, r, w):
        self.P.op(eng, lambda e, o=out, a=in0, s=scalar, b=in1, p=op0, q=op1:
                  e.scalar_tensor_tensor(o, a, s, b, p, q), r, w)

    def rstd(self, out, in_, r, w, eng='dve'):
        np_ = out.shape[0]
        self.act(out, in_, AF.Ln, list(r) + ['g_epsc'], list(w), bias=self.epsc[0:np_, 0:1], scale=1.0)
        self.act(out, out, AF.Exp, list(w), list(w), scale=-0.5)

    def dma(self, q, out, in_, r, w, key, slow=False):
        if slow:
            self.P.op(q, lambda e, o=out, i=in_: e.dma_start(out=o, in_=i, allow_slow_non_contiguous=True), r, w, dma=key)
        else:
            self.P.op(q, lambda e, o=out, i=in_: e.dma_start(out=o, in_=i), r, w, dma=key)

    def allgather(self, in2d, out2d, r, w):
        self.P.op('pool', lambda e, i=in2d, o=out2d: e.collective_compute(
            "AllGather", ALU.bypass, replica_groups=[[0, 1], [2, 3], [4, 5], [6, 7]],
            ins=[i.opt()], outs=[o.opt()]), r, w, cc=True)

    def declare(self):
        L = DEPTH
        d = self.din
        self.lazy = {'x': ([T, D], F32), 'p': ([L, T, DPLE], F32), 'w_in': ([L, D, 2304], F32),
                     'w_out': ([L, D, D], F32), 'w_up': ([L, D, 2 * DFF], F32), 'w_down': ([L, DFF, D], F32),
                     'w_ple_gate': ([L, D, D], F32), 'w_ple_proj': ([L, DPLE, D], F32),
                     'filt_w1': ([L, 33, 64], F32), 'filt_w2': ([L, 64, 64], F32), 'filt_w3': ([L, 64, 1024], F32),
                     'gall': ([128, KA, 3, 128], BF16), 'hall': ([KA, 128, 2, 64], BF16),
                     'zf': ([2, 33, NF], F32), 'mm': ([2, 128, NF], BF16), 'cstab': ([128, 2, NW], F32)}
        self.lazy_ap = {}
        self.colspec = {}
        off = 0
        for name, n in [('g_mix', L * 8), ('g_ffn', L * 8), ('gq', L), ('gk', L), ('sink', L * 8),
                        ('w_short', L * 36), ('b_short', L * 12), ('g_ao', L * 4), ('g_ho', L * 4),
                        ('w_ffc', L * 132), ('b_ffc', L * 44), ('fb1', L), ('ffr1', L), ('fb2', L), ('ffr2', L)]:
            self.colspec[name] = (off, n)
            off += n
        self.ncol = off
        self.cols_d = d('cols', [128, self.ncol])
        self.hbias_d = d('hbias', [1, L * DH])
        self.c_ident = d('ident_f', [128, 128])
        self.c_ones = d('ones_b', [128, 4, 128], BF16)
        self.c_prot = d('prot_b', [128, 128], BF16)
        self.c_masks = d('masks', [128, 4, 128], BF16)
        self.c_edge = d('edge', [128, 2])
        self.c_f1m = d('f1m', [128, 130], BF16)
        self.c_e12 = d('e12', [128, 2, 256], BF16)
        self.c_negd = d('negd', [128, DH])
        self.c_tdec = d('tdec', [128, 2, 128])
        self.c_e0 = d('e0', [128, 2])
        self.y = self.nc.dram_tensor('y', [T, D], F32, kind="ExternalOutput").ap()
        s = self.dscr
        self.hres = s('hres', [D, T], F32)
        self.nrm = s('nrm', [D, NW], BF16)
        self.xh_in = s('xh_in', [2 * D, 128], BF16, dbg=False)
        self.xh_out = s('xh_out', [4 * D, 128], BF16, dbg=False)
        self.zhy = s('zhy', [3 * DH, T + 2], BF16)
        self.attn_n = s('attn_n', [DQ, T], BF16)
        self.x0s = s('x0s', [DH, T], BF16)
        self.vfft2 = [s('vfft%d' % g_, [64 * 8, 1024], BF16, dbg=False) for g_ in range(NG)]
        self.vall2 = [s('vall%d' % g_, [128 * 8, 1024], BF16, dbg=False) for g_ in range(NG)]
        self.vfft = [t_.rearrange('(a x) y -> a (x y)', x=8) for t_ in self.vfft2]
        self.vall = [t_.rearrange('(a x) y -> a (x y)', x=8) for t_ in self.vall2]
        self.hfs = s('hfs', [NG, 128, KA * 2 * 2 * CG], BF16)
        self.hdn = s('hdn', [2, 128, NF], BF16)
        self.yconv = s('yconv', [DH, T], BF16)
        self.n2s = s('n2s', [D, T + 2], BF16)
        self.xn_in = s('xn_in', [2, D], BF16, dbg=False)
        self.xn_out = s('xn_out', [4, D], BF16, dbg=False)
        self.acts = s('acts', [DFF, T], BF16)

    def col(self, name, l=None, k=1, j=0):
        off, n = self.colspec[name]
        per = n // DEPTH
        if l is None:
            return self.cols[:, off:off + n]
        a = off + l * per + j
        return self.cols[:, a:a + k]

    def setup_globals(self):
        g = self.gtile
        self.ps = self.ges.enter_context(self.nc.psum_tensor('ps', [128, 8, 512], F32))
        self.ident = g('ident', [128, 128], F32)
        self.ones = g('ones', [128, 4, 128], BF16)
        self.prot = g('prot', [128, 128], BF16)
        self.masks = g('masks', [128, 4, 128], BF16)
        self.edge = g('edge', [128, 2], F32)
        self.cols = g('cols', [128, self.ncol], F32)
        self.esk = g('esk', [128, DEPTH * 8], F32)
        self.fsc = g('fsc', [128, DEPTH * 4], F32)
        self.e0 = g('e0', [128, 2], F32)
        self.epsc = g('epsc', [128, 1], F32)
        self.P.op('dve', lambda e: e.memset(self.epsc[:], EPS), [], ['g_epsc'])
        for t, dsrc, nm in [(self.ident, self.c_ident, 'ident'), (self.ones, self.c_ones, 'ones'),
                            (self.prot, self.c_prot, 'prot'), (self.masks, self.c_masks, 'masks'),
                            (self.edge, self.c_edge, 'edge'), (self.cols, self.cols_d, 'cols'),
                            (self.e0, self.c_e0, 'e0')]:
            self.dma('sp', t[:], dsrc, [], ['g_' + nm], 'g_' + nm)
        o, n = self.colspec['sink']
        self.act(self.esk[:], self.cols[:, o:o + n], AF.Exp, ['g_cols'], ['g_esk'])
        for l in range(self.depth):
            for k, (fr, fb) in enumerate([('ffr1', 'fb1'), ('ffr2', 'fb2')]):
                self.ts('dve', self.fsc[:, l * 4 + 2 * k:l * 4 + 2 * k + 1], self.col(fr, l), 1.0 / 3.0, None, ALU.mult, None,
                        ['g_cols'], [('fsc', l, 2 * k)])
                self.tt('dve', self.fsc[:, l * 4 + 2 * k + 1:l * 4 + 2 * k + 2], self.fsc[:, l * 4 + 2 * k:l * 4 + 2 * k + 1],
                        self.col(fb, l), ALU.mult, [('fsc', l, 2 * k), 'g_cols'], [('fsc', l, 2 * k + 1)])
        self.P.barrier()

    def load_w(self, src_rows, ncols, scale, stg, sname, pieces):
        self.wslot = getattr(self, 'wslot', 0) + 1
        s = self.wslot % len(stg)
        st = stg[s]
        rn = (sname, s)
        self.dma('sp', st[:, 0:ncols], src_rows, [], [rn], '%s%d' % (sname, s))
        for (d_ap, c0, c1, vf, wn) in pieces:
            src = st[:, c0:c1]
            if vf is not None:
                src = vf(src)
            eng = self.alt(('act', 'dve'))
            if scale is None:
                self.cp(eng, d_ap, src, [rn], [wn])
            elif eng == 'act':
                self.act(d_ap, src, AF.Copy, [rn, 'g_cols'], [wn], scale=scale)
            else:
                self.ts(eng, d_ap, src, scale, None, ALU.mult, None, [rn, 'g_cols'], [wn])

    def phase_p0(self):
        self.begin()
        xt = [self.tile('p0_xt%d' % s, [128, 4, D], F32) for s in range(2)]
        ht = [self.tile('p0_ht%d' % s, [128, 8, 512], F32) for s in range(2)]
        for i in range(NT):
            s = i % 2
            self.dma('sp', xt[s][:], self.L('x')[i * 512:(i + 1) * 512, :].rearrange('(b p) f -> p b f', p=128),
                     [], [('xt', s)], 'xt%d' % s)
            for fc in range(8):
                bk = self.nb()
                for blk in range(4):
                    self.tp(self.ps[:, bk, blk * 128:(blk + 1) * 128], xt[s][:, blk, fc * 128:(fc + 1) * 128],
                            self.ident[:], [('xt', s), 'g_ident'], self.pr(bk))
                self.cp(self.alt(), ht[s][:, fc, :], self.ps[:, bk, :], self.pr(bk), [('ht', s, fc)])
            self.dma('sp', self.hres[:, i * 512:(i + 1) * 512].rearrange('(c p) t -> p c t', p=128), ht[s][:],
                     [('ht', s, fc) for fc in range(8)], [('hres', i)], 'ht%d_st' % s)
        self.end()

    def phase_a0(self, l):
        self.begin()
        ht = [self.tile('a0_ht%d' % s, [128, 8, 512], F32) for s in range(2)]
        sq = [self.tile('a0_sq%d' % s, [128, 8, 512], BF16) for s in range(2)]
        rs = [self.tile('a0_rs%d' % s, [128, 512], F32) for s in range(2)]
        nt = [self.tile('a0_nt%d' % s, [128, 8, 512], BF16) for s in range(2)]
        hl = self.tile('a0_hl', [128, 8, 128], BF16)
        hr = self.tile('a0_hr', [128, 8, 128], BF16)
        def loads(i):
            s = i % 2
            self.dma('sp', ht[s][:], self.hres[:, i * 512:(i + 1) * 512].rearrange('(c p) t -> p c t', p=128),
                     [('hres', i)], [('ht', s)], 'a0ht%d' % s)
        loads(0)
        for i in range(NT):
            s = i % 2
            if i + 1 < NT:
                loads(i + 1)
            self.act(sq[s][:], ht[s][:], AF.Square, [('ht', s)], [('sq', s)])
            bk = self.nb()
            for fc in range(8):
                self.mm(self.ps[:, bk, :], self.ones[:, 0, :], sq[s][:, fc, :], fc == 0, fc == 7,
                        [('sq', s), 'g_ones'], self.pr(bk))
            self.rstd(rs[s][:], self.ps[:, bk, :], self.pr(bk), [('rs', s)])
            for hf in range(2):
                eng = 'dve' if hf == 0 else 'pool'
                self.tt(eng, nt[s][:, hf * 4:(hf + 1) * 4, :], ht[s][:, hf * 4:(hf + 1) * 4, :],
                        _bc(rs[s][:], [128, 4, 512], 1), ALU.mult, [('ht', s), ('rs', s)], [('nt', s, hf)])
            rd = [('nt', s, 0), ('nt', s, 1)]
            self.dma('sp', self.nrm[:, HALO + i * 512:HALO + (i + 1) * 512].rearrange('(c p) t -> p c t', p=128),
                     nt[s][:], rd, [('nrm', i)], 'a0nt%d_st' % s)
            if i == 0:
                self.dma('sp', self.xh_in[0:D, :].rearrange('(c p) t -> p c t', p=128), nt[s][:, :, 0:128],
                         rd, ['xh_in0'], 'a0x0')
            if i == NT - 1:
                self.dma('sp', self.xh_in[D:2 * D, :].rearrange('(c p) t -> p c t', p=128), nt[s][:, :, 384:512],
                         rd, ['xh_in1'], 'a0x1')
        self.allgather(self.xh_in, self.xh_out, ['xh_in0', 'xh_in1'], ['xh_out'])
        for k, (tl, r0, col) in enumerate([(hl, D, 0), (hr, 2 * D, NW - HALO)]):
            self.dma('sp', tl[:], self.xh_out[r0:r0 + D, :].rearrange('(c p) t -> p c t', p=128),
                     ['xh_out'], [('hal', k)], 'a0h%d' % k)
            self.ts('dve', tl[:], tl[:], self.edge[:, k:k + 1], None, ALU.mult, None, [('hal', k), 'g_edge'], [('hal', k)])
            self.dma('sp', self.nrm[:, col:col + HALO].rearrange('(c p) t -> p c t', p=128), tl[:],
                     [('hal', k)], [('nrmh', k)], 'a0h%d_st' % k)
        self.end()


    def capture(self, fn):
        rec = []
        real = self.P.op
        self.P.op = lambda *a, **k: rec.append((a, k))
        try:
            fn()
        finally:
            self.P.op = real
        return rec

    def replay(self, recs):
        for a, k in recs:
            self.P.op(*a, **k)

    @staticmethod
    def merge(a, b):
        def split(x):
            segs = [[]]
            for r in x:
                if r[0][0] == 'MARK':
                    segs.append([])
                else:
                    segs[-1].append(r)
            return segs
        sa = split(a)
        sb = [g_ for g_ in split(b) if g_]
        ncut = max(1, len(sa) - 1)
        out = []
        ib = 0
        for k, sg in enumerate(sa):
            out.extend(sg)
            if k < len(sa) - 1:
                tgt = (k + 1) * len(sb) // ncut
                while ib < min(tgt, len(sb)):
                    out.extend(sb[ib])
                    ib += 1
        while ib < len(sb):
            out.extend(sb[ib])
            ib += 1
        return out

    def phase_a(self, l):
        self.begin()
        w_in = self.L('w_in')
        cstab = self.L('cstab')
        wq = self.tile('wq', [128, 8, 512], BF16)
        wk = self.tile('wk', [128, 8, 256], BF16)
        wv = self.tile('wv', [128, 8, 256], BF16)
        why = self.tile('why', [128, 8, 1536], BF16)
        stg = [self.tile('a_stg%d' % s_, [128, 1152], F32) for s_ in range(2)]
        dup = lambda a: _bc(a.rearrange('p (k d) -> p k d', k=2), [128, 2, 2, 64], 2)
        for kc in range(8):
            self.load_w(w_in[l, kc * 128:(kc + 1) * 128, 0:1152], 1152, self.col('g_mix', l, 1, kc), stg, 'astg', [
                (wq[:, kc, :], 0, 512, None, ('wq', kc)),
                (wk[:, kc, :].rearrange('p (k u d) -> p k u d', k=2, u=2), 512, 640, dup, ('wk', kc)),
                (wv[:, kc, :].rearrange('p (k u d) -> p k u d', k=2, u=2), 640, 768, dup, ('wv', kc)),
                (why[:, kc, 0:384], 768, 1152, None, ('why', kc, 0))])
            self.load_w(w_in[l, kc * 128:(kc + 1) * 128, 1152:2304], 1152, self.col('g_mix', l, 1, kc), stg, 'astg', [
                (why[:, kc, 384:1536], 0, 1152, None, ('why', kc, 1))])
        nt = [self.tile('nt%d' % s, [128, 8, 768], BF16) for s in range(2)]
        ct = [self.tile('ct%d' % s, [128, 2, 768], F32) for s in range(2)]
        qraw = self.tile('qraw', [128, 4, 512], F32)
        kraw = self.tile('kraw', [128, 2, 768], F32)
        sqq = self.tile('sqq', [128, 4, 512], BF16)
        sqk = self.tile('sqk', [128, 2, 768], BF16)
        rq = self.tile('rq', [128, 4, 512], F32)
        rk = self.tile('rk', [128, 2, 768], F32)
        qn = self.tile('qn', [128, 4, 512], BF16)
        kn = self.tile('kn', [128, 2, 768], BF16)
        qr = [self.tile('qr%d' % s, [128, 4, 512], BF16) for s in range(2)]
        krl = [self.tile('krl%d' % s, [128, 2, 768], BF16) for s in range(2)]
        krh = [self.tile('krh%d' % s, [128, 2, 768], BF16) for s in range(2)]
        for s in range(2):
            self.P.op('pool', lambda e, t_=krl[s]: e.memset(t_[64:128], 0.0), [], [('krl0', s)])
            self.P.op('pool', lambda e, t_=krh[s]: e.memset(t_[0:64], 0.0), [], [('krh0', s)])
        vd = [self.tile('vd%d' % s, [128, 6, 256], BF16) for s in range(2)]
        zt = self.tile('zt', [128, 12, 512], BF16)
        zh = self.tile('zh', [128, 12, 2], BF16)
        E = [self.tile('E%d' % s, [128, 3, 512], BF16) for s in range(2)]
        rec = [self.tile('rec%d' % s, [128, 512], F32) for s in range(2)]
        attn = self.tile('attn', [128, 4, 512], F32)
        sqa = self.tile('sqa', [128, 4, 512], BF16)
        rsa = self.tile('rsa', [128, 512], F32)
        an = [self.tile('an%d' % s, [128, 4, 512], BF16) for s in range(2)]
        gq = self.col('gq', l)
        gk = self.col('gk', l)
        st_ = {'ei': 0}
        kseg = [(0, 0, 512, 0, 0), (0, 512, 768, 1, 0), (1, 0, 256, 1, 256), (1, 256, 768, 2, 0)]
        whyr = lambda kc: [('why', kc, 0), ('why', kc, 1)]

        def loads(i):
            s = i % 2
            t0 = i * 512
            self.dma('sp', nt[s][:], self.nrm[:, t0:t0 + 768].rearrange('(c p) t -> p c t', p=128),
                     [], [('nt', s)], 'a_nt%d' % s)
            self.dma('sp', ct[s][:], cstab[:, :, t0:t0 + 768], [], [('ct', s)], 'a_ct%d' % s)

        def X(i):
            s = i % 2
            t0 = i * 512
            rw = [('nt', s)]
            bq = self.nb(4)
            for mc in range(4):
                for kc in range(8):
                    self.mm(self.ps[:, bq + mc, :], wq[:, kc, mc * 128:(mc + 1) * 128], nt[s][:, kc, 128:640],
                            kc == 0, kc == 7, rw + [('wq', kc)], self.pr(bq + mc))
            self.cp('act', qraw[:], self.ps[:, bq:bq + 4, :], self.pr(bq, 4), ['qraw'])
            self.P.op('MARK', None)
            bkk = self.nb(3)
            for (c2, c0, c1, bo, pc0) in kseg:
                for kc in range(8):
                    self.mm(self.ps[:, bkk + bo, pc0:pc0 + (c1 - c0)], wk[:, kc, c2 * 128:(c2 + 1) * 128],
                            nt[s][:, kc, c0:c1], kc == 0, kc == 7, rw + [('wk', kc)], self.pr(bkk + bo))
            psk = self.ps[:, bkk:bkk + 3, :].rearrange('p b t -> p (b t)').rearrange('p (c t) -> p c t', c=2)
            self.cp('dve', kraw[:], psk, self.pr(bkk, 3), ['kraw'])
            self.P.op('MARK', None)
            bv = self.nb(3)
            for kb in range(6):
                for kc in range(8):
                    self.mm(self.ps[:, bv + kb // 2, (kb % 2) * 256:(kb % 2 + 1) * 256], nt[s][:, kc, kb * 128:(kb + 1) * 128],
                            wv[:, kc, :], kc == 0, kc == 7, rw + [('wv', kc)], self.pr(bv + kb // 2))
            self.cp('act', vd[s][:], self.ps[:, bv:bv + 3, :].rearrange('p b (u t) -> p (b u) t', u=2), self.pr(bv, 3), [('vd', s)])
            for g3 in range(3):
                self.P.op('MARK', None)
                bh = self.nb(4)
                for j in range(4):
                    mc = g3 * 4 + j
                    for kc in range(8):
                        self.mm(self.ps[:, bh + j, :], why[:, kc, mc * 128:(mc + 1) * 128], nt[s][:, kc, 128:640],
                                kc == 0, kc == 7, rw + whyr(kc), self.pr(bh + j))
                self.cp(self.alt(), zt[:, g3 * 4:(g3 + 1) * 4, :], self.ps[:, bh:bh + 4, :], self.pr(bh, 4), [('zt', g3)])
            self.dma('sp', self.zhy[:, 1 + t0:1 + t0 + 512].rearrange('(c p) t -> p c t', p=128), zt[:],
                     [('zt', g3) for g3 in range(3)], [('zhy', i)], 'a_zt_st')
            if i == 0 or i == NT - 1:
                self.P.op('MARK', None)
                colh = 127 if i == 0 else 640
                hi = 0 if i == 0 else 1
                bh = self.nb(1)
                for mc in range(12):
                    for kc in range(8):
                        self.mm(self.ps[:, bh, mc:mc + 1], why[:, kc, mc * 128:(mc + 1) * 128], nt[s][:, kc, colh:colh + 1],
                                kc == 0, kc == 7, rw + whyr(kc), self.pr(bh))
                self.cp('dve', zh[:, :, hi], self.ps[:, bh, 0:12], self.pr(bh), [('zh', hi)])
                dcol = 0 if i == 0 else T + 1
                self.dma('sp', self.zhy[:, dcol:dcol + 1].rearrange('(c p) t -> p c t', p=128), zh[:, :, hi:hi + 1],
                         [('zh', hi)], [('zhyh', hi)], 'a_zh%d_st' % hi, slow=True)

        def Y(i):
            s = i % 2
            self.act(sqq[:], qraw[:], AF.Square, ['qraw'], ['sqq'])
            self.act(sqk[:], kraw[:], AF.Square, ['kraw'], ['sqk'])
            self.P.op('MARK', None)
            bs = self.nb(4)
            for mc in range(4):
                self.mm(self.ps[:, bs + mc, :], self.ones[:, 3, :], sqq[:, mc, :], True, True, ['sqq', 'g_ones'], self.pr(bs + mc))
            self.rstd(rq[:], self.ps[:, bs:bs + 4, :], self.pr(bs, 4), ['rq'])
            self.P.op('MARK', None)
            bs2 = self.nb(3)
            for (c2, c0, c1, bo, pc0) in kseg:
                self.mm(self.ps[:, bs2 + bo, pc0:pc0 + (c1 - c0)], self.ones[:, 3, :], sqk[:, c2, c0:c1], True, True,
                        ['sqk', 'g_ones'], self.pr(bs2 + bo))
            psk = self.ps[:, bs2:bs2 + 3, :].rearrange('p b t -> p (b t)').rearrange('p (c t) -> p c t', c=2)
            self.rstd(rk[:], psk, self.pr(bs2, 3), ['rk'])
            self.P.op('MARK', None)
            self.stt('dve', qn[:], qraw[:], gq, rq[:], ALU.mult, ALU.mult, ['qraw', 'rq', 'g_cols'], ['qn'])
            self.stt('dve', kn[:], kraw[:], gk, rk[:], ALU.mult, ALU.mult, ['kraw', 'rk', 'g_cols'], ['kn'])
            self.P.op('MARK', None)
            br = self.nb(4)
            for mc in range(4):
                self.mm(self.ps[:, br + mc, :], self.prot[:], qn[:, mc, :], True, True, ['qn', 'g_prot'], self.pr(br + mc))
            self.tt('pool', qraw[:], qn[:], _bc(ct[s][:, 0, 128:640], [128, 4, 512], 1), ALU.mult, ['qn', ('ct', s)], ['qraw'])
            self.tt('dve', rq[:], self.ps[:, br:br + 4, :], _bc(ct[s][:, 1, 128:640], [128, 4, 512], 1), ALU.mult,
                    self.pr(br, 4) + [('ct', s)], ['rq'])
            self.tt('pool', qr[s][:], qraw[:], rq[:], ALU.add, ['qraw', 'rq'], [('qr', s)])
            self.P.op('MARK', None)
            br2 = self.nb(3)
            for (c2, c0, c1, bo, pc0) in kseg:
                self.mm(self.ps[:, br2 + bo, pc0:pc0 + (c1 - c0)], self.prot[:], kn[:, c2, c0:c1], True, True,
                        ['kn', 'g_prot'], self.pr(br2 + bo))
            psk = self.ps[:, br2:br2 + 3, :].rearrange('p b t -> p (b t)').rearrange('p (c t) -> p c t', c=2)
            self.tt('pool', kraw[:], kn[:], _bc(ct[s][:, 0, :], [128, 2, 768], 1), ALU.mult, ['kn', ('ct', s)], ['kraw'])
            self.tt('dve', rk[:], psk, _bc(ct[s][:, 1, :], [128, 2, 768], 1), ALU.mult, self.pr(br2, 3) + [('ct', s)], ['rk'])
            self.tt('pool', krl[s][0:64], kraw[0:64], rk[0:64], ALU.add, ['kraw', 'rk'], [('krl', s)])
            self.tt('dve', krh[s][64:128], kraw[64:128], rk[64:128], ALU.add, ['kraw', 'rk'], [('krh', s)])

        def Z(i):
            s = i % 2
            t0 = i * 512
            kread = [('krl', s), ('krh', s), ('krl0', s), ('krh0', s), ('qr', s)]
            for qb in range(4):
                for kvh in range(2):
                    es_ = st_['ei'] % 2
                    st_['ei'] += 1
                    b0 = self.nb(3)
                    for kk in range(3):
                        kbw = qb + kk
                        for half in range(2):
                            kt_ = krl[s] if half == 0 else krh[s]
                            self.mm(self.ps[:, b0 + kk, half * 256:(half + 1) * 256],
                                    kt_[:, kvh, kbw * 128:(kbw + 1) * 128],
                                    qr[s][:, 2 * kvh:2 * kvh + 2, qb * 128:(qb + 1) * 128],
                                    True, True, kread, self.pr(b0 + kk))
                    self.act(E[es_][:], self.ps[:, b0:b0 + 3, :], AF.Exp, self.pr(b0, 3), [('E', es_)], scale=0.125)
                    first = (i == 0 and qb == 0)
                    lastb = (i == NT - 1 and qb == 3)
                    for (kk, mi) in [(0, 2 if first else 0), (2, 3 if lastb else 1)]:
                        ev = E[es_][:, kk, :].rearrange('p (g q) -> p g q', g=4)
                        self.tt('dve', ev, ev, _bc(self.masks[:, mi, :], [128, 4, 128], 1), ALU.mult,
                                [('E', es_), 'g_masks'], [('E', es_)])
                    bpv = self.nb()
                    bdn = self.nb()
                    for kk in range(3):
                        self.mm(self.ps[:, bpv, :], vd[s][:, qb + kk, kvh * 128:(kvh + 1) * 128], E[es_][:, kk, :],
                                kk == 0, kk == 2, [('vd', s), ('E', es_)], self.pr(bpv))
                    for kk in range(3):
                        self.mm(self.ps[:, bdn, :], self.ones[:, 2, :], E[es_][:, kk, :], kk == 0, kk == 2,
                                [('E', es_), 'g_ones'], self.pr(bdn))
                    rv = rec[es_][:].rearrange('p (g q) -> p g q', g=4)
                    sk = self.esk[:, l * 8 + kvh * 4:l * 8 + kvh * 4 + 4]
                    self.tt('dve', rv, self.ps[:, bdn, :].rearrange('p (g q) -> p g q', g=4), _bc(sk, [128, 4, 128], 2),
                            ALU.add, self.pr(bdn) + ['g_esk'], [('rec', es_)])
                    self.act(rec[es_][:], rec[es_][:], AF.Ln, [('rec', es_)], [('rec', es_)])
                    self.act(rec[es_][:], rec[es_][:], AF.Exp, [('rec', es_)], [('rec', es_)], scale=-1.0)
                    for half in range(2):
                        pp = slice(half * 64, (half + 1) * 64)
                        self.tt('dve', attn[pp, 2 * kvh:2 * kvh + 2, qb * 128:(qb + 1) * 128],
                                self.ps[pp, bpv, half * 256:(half + 1) * 256].rearrange('p (g q) -> p g q', g=2),
                                rec[es_][pp, half * 256:(half + 1) * 256].rearrange('p (g q) -> p g q', g=2), ALU.mult,
                                self.pr(bpv) + [('rec', es_)], [('attn', qb, kvh, half)])
                    self.P.op('MARK', None)
            ar = [('attn', qb, kvh, half) for qb in range(4) for kvh in range(2) for half in range(2)]
            self.act(sqa[:], attn[:], AF.Square, ar, ['sqa'])
            bs = self.nb()
            for mc in range(4):
                self.mm(self.ps[:, bs, :], self.ones[:, 1, :], sqa[:, mc, :], mc == 0, mc == 3, ['sqa', 'g_ones'], self.pr(bs))
            self.rstd(rsa[:], self.ps[:, bs, :], self.pr(bs), ['rsa'])
            self.tt('pool', an[s][:], attn[:], _bc(rsa[:], [128, 4, 512], 1), ALU.mult, ar + ['rsa'], [('an', s)])
            self.dma('sp', self.attn_n[:, t0:t0 + 512].rearrange('(c p) t -> p c t', p=128), an[s][:],
                     [('an', s)], [('attn_n', i)], 'a_an%d_st' % s)

        loads(0)
        loads(1)
        self.replay([r_ for r_ in self.capture(lambda: (X(0), Y(0))) if r_[0][0] != 'MARK'])
        for i in range(NT):
            zs = self.capture(lambda: Z(i))
            if i + 1 < NT:
                xs = self.capture(lambda: X(i + 1))
                ys = self.capture(lambda: Y(i + 1))
                self.replay(self.merge(zs, xs + [(('MARK', None), {})] + ys))
            else:
                self.replay([r_ for r_ in zs if r_[0][0] != 'MARK'])
            if i + 2 < NT:
                loads(i + 2)
        self.end()

    def phase_a2(self, l):
        self.begin()
        zw = [self.tile('zw%d' % s, [128, 12, 514], BF16) for s in range(2)]
        u = self.tile('u', [128, 12, 512], F32)
        xo = [self.tile('xo%d' % s, [128, 4, 512], BF16) for s in range(2)]
        vt = [self.tile('vt%d' % s, [128, 4, 512], BF16) for s in range(2)]
        def loads(i):
            s = i % 2
            self.dma('sp', zw[s][:], self.zhy[:, i * 512:i * 512 + 514].rearrange('(c p) t -> p c t', p=128), [], [('zw', s)], 'a2zw%d' % s)
        loads(0)
        for i in range(NT):
            s = i % 2
            t0 = i * 512
            if i + 1 < NT:
                loads(i + 1)
            for j in range(12):
                wc = lambda k, j=j: self.col('w_short', l, 1, k * 12 + j)
                self.act(u[:, j, :], zw[s][:, j, 1:513], AF.Identity, [('zw', s), 'g_cols'], [('u', j)],
                         scale=wc(1), bias=self.col('b_short', l, 1, j))
                e1 = 'dve'
                self.stt(e1, u[:, j, :], zw[s][:, j, 0:512], wc(0), u[:, j, :], ALU.mult, ALU.add, [('zw', s), ('u', j), 'g_cols'], [('u', j)])
                self.stt(e1, u[:, j, :], zw[s][:, j, 2:514], wc(2), u[:, j, :], ALU.mult, ALU.add, [('zw', s), ('u', j), 'g_cols'], [('u', j)])
            self.cp('act', xo[s][:], u[:, 0:4, :], [('u', j) for j in range(4)], [('xo', s)])
            self.tt('dve', vt[s][:], u[:, 4:8, :], u[:, 8:12, :], ALU.mult, [('u', j) for j in range(4, 12)], [('vt', s)])
            self.dma('sp', self.x0s[:, t0:t0 + 512].rearrange('(c p) t -> p c t', p=128), xo[s][:], [('xo', s)], [('x0s', i)], 'a2xo%d_st' % s)
            for j in range(4):
                for h2 in range(2):
                    gg = 2 * j + h2
                    self.dma('sp', self.vfft[gg][4 * i:4 * i + 4, :].rearrange('a (p b) -> p a b', p=64),
                             vt[s][h2 * 64:(h2 + 1) * 64, j, :].rearrange('p (a b) -> p a b', a=4), [('vt', s)], [('vfft', gg, i)],
                             'a2vt%d_%d_st' % (s, gg))
        if os.environ.get('MK_NOAG') is None:
            for gg in range(NG):
                self.allgather(self.vfft2[gg], self.vall2[gg], [('vfft', gg, i) for i in range(NT)], [('vall', gg)])
        self.end()

    def phase_f1(self, l):
        self.begin()
        zf = self.L('zf')
        mmk = self.L('mm')
        w1t = self.tile('w1t', [33, 64], F32)
        w2d = self.tile('w2d', [64, 128], F32)
        self.dma('sp', w1t[:], self.L('filt_w1')[l], [], ['w1t'], 'f1w1')
        for k in range(2):
            self.dma('sp', w2d[:, k * 64:(k + 1) * 64], self.L('filt_w2')[l], [], [('w2d', k)], 'f1w2%d' % k)
        zt_ = [self.tile('zt_%d' % s, [33, 2048], F32) for s in range(2)]
        mk = [self.tile('mk%d' % s, [128, 2048], BF16) for s in range(2)]
        s1 = self.tile('s1', [64, 2048], F32)
        t1 = self.tile('t1', [64, 2048], F32)
        s2 = self.tile('s2', [128, 2048], F32)
        t2 = self.tile('t2', [128, 2048], F32)
        hd = [self.tile('hd%d' % s, [128, 2048], BF16) for s in range(2)]
        f = lambda k: self.fsc[:, l * 4 + k:l * 4 + k + 1]
        it = 0
        for sl in range(2):
            for cch in range(8):
                s = it % 2
                it += 1
                c0 = cch * 2048
                self.dma('sp', zt_[s][:], zf[sl, :, c0:c0 + 2048], [], [('zt_', s)], 'f1zt%d' % s)
                self.dma('sp', mk[s][:], mmk[sl, :, c0:c0 + 2048], [], [('mk', s)], 'f1mk%d' % s)
                b1 = self.nb(4)
                for q4 in range(4):
                    self.mm(self.ps[0:64, b1 + q4, :], w1t[:], zt_[s][:, q4 * 512:(q4 + 1) * 512], True, True,
                            ['w1t', ('zt_', s)], self.pr(b1 + q4))
                self.act(s1[:], self.ps[0:64, b1:b1 + 4, :], AF.Sin, self.pr(b1, 4) + [('fsc', l, 0), ('fsc', l, 1)], ['s1'],
                         scale=f(0)[0:64], bias=f(1)[0:64])
                self.tt('dve', t1[:], s1[:], s1[:], ALU.mult, ['s1'], ['t1'])
                self.ts('dve', t1[:], t1[:], -4.0, 3.0, ALU.mult, ALU.add, ['t1'], ['t1'])
                self.tt('pool', t1[:], t1[:], s1[:], ALU.mult, ['t1', 's1'], ['t1'])
                b2 = self.nb(4)
                for q4 in range(4):
                    self.mm(self.ps[:, b2 + q4, :], w2d[:], t1[:, q4 * 512:(q4 + 1) * 512], True, True,
                            [('w2d', 0), ('w2d', 1), 't1'], self.pr(b2 + q4))
                self.act(s2[:], self.ps[:, b2:b2 + 4, :], AF.Sin, self.pr(b2, 4) + [('fsc', l, 2), ('fsc', l, 3)], ['s2'],
                         scale=f(2), bias=f(3))
                self.tt('dve', t2[:], s2[:], s2[:], ALU.mult, ['s2'], ['t2'])
                self.ts('dve', t2[:], t2[:], -4.0, 3.0, ALU.mult, ALU.add, ['t2'], ['t2'])
                self.tt('pool', t2[:], t2[:], s2[:], ALU.mult, ['t2', 's2'], ['t2'])
                self.tt('pool', hd[s][:], t2[:], mk[s][:], ALU.mult, ['t2', ('mk', s)], [('hd', s)])
                self.dma('sp', self.hdn[sl, :, c0:c0 + 2048], hd[s][:], [('hd', s)], [('hdn', sl, cch)], 'f1hd%d_st' % s)
        self.end()

    def g_stream(self, Gt, kab, ka0, gi_):
        s = gi_ % 2
        n = min(5, KA - ka0)
        self.dma('sp', Gt[s][:, 0:n], self.L('gall')[:, ka0:ka0 + n], [], [('Gt', s)], 'Gt%d' % s)
        return s

    def phase_f2(self, l):
        self.begin()
        hdn = self.tile('hdn', [128, NF], BF16)
        g = self.tile('g', [128, 128, 256], BF16)
        Ysb = self.tile('Ysb', [128, 2, KA, 256], BF16)
        HfT = [self.tile('HfT%d' % s, [128, 5, 2, 256], BF16) for s in range(2)]
        Gt = [self.tile('Gt%d' % s, [128, 5, 3, 128], BF16) for s in range(2)]
        dc = [self.tile('dc%d' % s, [128, 2, 256], F32) for s in range(2)]
        w3f = self.tile('w3f', [128, 256], F32)
        w3s = self.tile('w3s', [128, 256], BF16)
        f1m = self.tile('f1m', [128, 130], BF16)
        negd = self.tile('negd', [128, DH], F32)
        tdec = self.tile('tdec', [128, 2, 128], F32)
        hbt = self.tile('hbt', [1, 256], F32)
        self.dma('sp', f1m[:], self.c_f1m, [], ['f1m'], 'f2f1m')
        self.dma('sp', negd[:], self.c_negd, [], ['negd'], 'f2negd')
        self.dma('sp', tdec[:], self.c_tdec, [], ['tdec'], 'f2tdec')
        w3 = self.L('filt_w3')
        gi_ = 0
        hi_ = 0
        for sl in range(2):
            self.dma('sp', hdn[:], self.hdn[sl], [], ['hdn'], 'f2hdn')
            for hh in range(2):
                for k in range(2):
                    self.dma('sp', w3f[k * 64:(k + 1) * 64, :], w3[l, :, k * 512 + hh * 256:k * 512 + (hh + 1) * 256], [], [('w3f', k)], 'f2w3%d' % k)
                self.cp('dve', w3s[:], w3f[:], [('w3f', 0), ('w3f', 1)], ['w3s'])
                self.dma('sp', hbt[:], self.hbias_d[0:1, l * DH + hh * 256:l * DH + (hh + 1) * 256], [], ['hbt'], 'f2hbt')
                for b2 in range(64):
                    bk = self.nb()
                    ds_ = b2 % 2
                    for u2 in range(2):
                        b = b2 * 2 + u2
                        self.mm(self.ps[:, bk, u2 * 256:(u2 + 1) * 256], hdn[:].rearrange('p (a b) -> p b a', b=128)[:, b, :],
                                w3s[:], True, True, ['hdn', 'w3s'], self.pr(bk))
                        self.act(dc[ds_][:, u2, :], negd[:, hh * 256:(hh + 1) * 256], AF.Exp, ['negd', 'tdec'], [('dc', ds_, u2)],
                                 scale=tdec[:, sl, b:b + 1])
                    self.tt('dve', g[:, 2 * b2:2 * b2 + 2, :], self.ps[:, bk, :].rearrange('p (u c) -> p u c', u=2), dc[ds_][:],
                            ALU.mult, self.pr(bk) + [('dc', ds_, 0), ('dc', ds_, 1)], [('g', b2)])
                    if b2 == 0:
                        self.stt('dve', g[0:1, 0, :], hbt[:], self.e0[0:1, sl:sl + 1],
                                 g[0:1, 0, :], ALU.mult, ALU.add, [('g', 0), 'hbt', 'g_e0'], [('g', 0)])
                gr = [('g', b2) for b2 in range(64)]
                c = 0
                while c < 256:
                    n = min(3, 256 - c)
                    bk = self.nb()
                    for u3 in range(n):
                        self.mm(self.ps[:, bk, u3 * 130:(u3 + 1) * 130], g[:, :, c + u3], f1m[:], True, True, gr + ['f1m'], self.pr(bk))
                    self.cp(self.alt(), Ysb[:, :, :, c:c + n], self.ps[:, bk, 0:n * 130].rearrange('p (c r k) -> p r k c', c=n, r=2),
                            self.pr(bk), [('Ysb', c)])
                    c += n
                yr = [('Ysb', c) for c in range(0, 256, 3)]
                for ka0 in range(0, KA, 5):
                    gs = self.g_stream(Gt, None, ka0, gi_)
                    gi_ += 1
                    hs = hi_ % 2
                    hi_ += 1
                    nk = min(5, KA - ka0)
                    for kq in range(nk):
                        ka = ka0 + kq
                        bk = self.nb()
                        zr = self.ps[:, bk, 0:256]
                        zi = self.ps[:, bk, 256:512]
                        rr = yr + [('Gt', gs)]
                        self.mm(zr, Gt[gs][:, kq, 0, :], Ysb[:, 0, ka, :], True, False, rr, self.pr(bk))
                        self.mm(zr, Gt[gs][:, kq, 2, :], Ysb[:, 1, ka, :], False, True, rr, self.pr(bk))
                        self.mm(zi, Gt[gs][:, kq, 0, :], Ysb[:, 1, ka, :], True, False, rr, self.pr(bk))
                        self.mm(zi, Gt[gs][:, kq, 1, :], Ysb[:, 0, ka, :], False, True, rr, self.pr(bk))
                        self.cp(self.alt(), HfT[hs][:, kq, :, :], self.ps[:, bk, :].rearrange('p (r c) -> p r c', r=2), self.pr(bk), [('HfT', hs, kq)])
                    for g4 in range(4):
                        gg = hh * 4 + g4
                        dst = self.hfs[gg].rearrange('p (k r s c) -> p k r s c', k=KA, r=2, s=2)[:, ka0:ka0 + nk, :, sl, :]
                        self.dma('sp', dst, HfT[hs][:, 0:nk, :, g4 * 64:(g4 + 1) * 64], [('HfT', hs, kq) for kq in range(nk)],
                                 [('hfs', gg, sl, ka0)], 'f2hf%d_%d_st' % (hs, g4))
        self.end()

    def phase_b(self, l):
        self.begin()
        xa = [self.tile('xa%d' % s, [64, CG, 128], BF16) for s in range(2)]
        hft = self.tile('hft', [128, KA, 2, 2, CG], BF16)
        Ysb = self.tile('Ysb', [128, 2, KA, 2, CG], BF16)
        Wt = self.tile('Wt', [128, 2, CG, KA], BF16)
        U = self.tile('U', [KA, 2, 128, CG], BF16)
        Yo = self.tile('Yo', [64, CG, 128], BF16)
        A = self.tile('A', [128, 4, 2, 2 * CG], F32)
        B1 = self.tile('B1', [128, 4, 2 * CG], F32)
        B2 = self.tile('B2', [128, 4, 2 * CG], F32)
        Dr = self.tile('Dr', [128, 4, 2 * CG], F32)
        Di = self.tile('Di', [128, 4, 2 * CG], F32)
        Gt = [self.tile('Gt%d' % s, [128, 5, 3, 128], BF16) for s in range(2)]
        Ht = [self.tile('Ht%d' % s, [KA, 16, 2, 64], BF16) for s in range(2)]
        f1m = self.tile('f1m', [128, 130], BF16)
        e12 = self.tile('e12', [128, 2, 256], BF16)
        self.dma('sp', f1m[:], self.c_f1m, [], ['f1m'], 'bf1m')
        self.dma('sp', e12[:], self.c_e12, [], ['e12'], 'be12')
        hall = self.L('hall')
        gi_ = 0
        hi_ = 0
        for gg in range(NG):
            c0 = gg * CG
            for sl in range(2):
                self.dma('sp', xa[sl][:], self.vall[gg][sl * 64:(sl + 1) * 64, :].rearrange('a (c b) -> a c b', c=CG),
                         [], [('xa', sl)], 'bxa%d' % sl)
            self.dma('sp', hft[:], self.hfs[gg].rearrange('p (k r s c) -> p k r s c', k=KA, r=2, s=2), [], ['hft'], 'bhft')
            for sl in range(2):
                c = 0
                while c < CG:
                    n = min(3, CG - c)
                    bk = self.nb()
                    for u3 in range(n):
                        self.mm(self.ps[:, bk, u3 * 130:(u3 + 1) * 130], xa[sl][:, c + u3, :], f1m[0:64, :], True, True,
                                [('xa', sl), 'f1m'], self.pr(bk))
                    self.cp(self.alt(), Ysb[:, :, :, sl, c:c + n], self.ps[:, bk, 0:n * 130].rearrange('p (c r k) -> p r k c', c=n, r=2),
                            self.pr(bk), [('Ysb', sl, c)])
                    c += n
            yr = [('Ysb', sl, c) for sl in range(2) for c in range(0, CG, 3)]
            for ka0 in range(0, KA, 4):
                nk = min(4, KA - ka0)
                bz = self.nb(2)
                for kq in range(nk):
                    ka = ka0 + kq
                    if ka % 5 == 0:
                        gs = self.g_stream(Gt, None, ka, gi_)
                        gi_ += 1
                    gq_ = ka % 5
                    zr = self.ps[:, bz + kq // 2, (kq % 2) * 256:(kq % 2) * 256 + 128]
                    zi = self.ps[:, bz + kq // 2, (kq % 2) * 256 + 128:(kq % 2) * 256 + 256]
                    rr = yr + [('Gt', gs)]
                    yre = Ysb[:, 0, ka, :, :].rearrange('p s c -> p (s c)')
                    yim = Ysb[:, 1, ka, :, :].rearrange('p s c -> p (s c)')
                    w_ = self.pr(bz + kq // 2)
                    self.mm(zr, Gt[gs][:, gq_, 0, :], yre, True, False, rr, w_)
                    self.mm(zr, Gt[gs][:, gq_, 2, :], yim, False, True, rr, w_)
                    self.mm(zi, Gt[gs][:, gq_, 0, :], yim, True, False, rr, w_)
                    self.mm(zi, Gt[gs][:, gq_, 1, :], yre, False, True, rr, w_)
                zps = self.ps[:, bz:bz + 2, :].rearrange('p b (k r n) -> p (b k) r n', k=2, r=2)[:, 0:nk]
                hf_ = hft[:, ka0:ka0 + nk].rearrange('p k r s c -> p k r (s c)')
                pz = self.pr(bz, 2)
                self.tt('dve', A[:, 0:nk], zps, hf_, ALU.mult, pz + ['hft'], ['A'])
                self.tt('dve', B1[:, 0:nk], zps[:, :, 0, :], hf_[:, :, 1, :], ALU.mult, pz + ['hft'], ['B1'])
                self.tt('dve', B2[:, 0:nk], zps[:, :, 1, :], hf_[:, :, 0, :], ALU.mult, pz + ['hft'], ['B2'])
                self.tt('pool', Dr[:, 0:nk], A[:, 0:nk, 0, :], A[:, 0:nk, 1, :], ALU.subtract, ['A'], ['Dr'])
                self.tt('pool', Di[:, 0:nk], B1[:, 0:nk], B2[:, 0:nk], ALU.add, ['B1', 'B2'], ['Di'])
                for ri, Dx in ((0, Dr), (1, Di)):
                    self.tt('pool', Wt[:, ri, :, ka0:ka0 + nk], Dx[:, 0:nk, 0:CG].rearrange('p k c -> p c k'),
                            Dx[:, 0:nk, CG:2 * CG].rearrange('p k c -> p c k'), ALU.add, ['Dr' if ri == 0 else 'Di'], [('Wt', ka0, ri)])
            wr = [('Wt', ka0, ri) for ka0 in range(0, KA, 4) for ri in range(2)]
            for c in range(0, CG, 2):
                bk = self.nb()
                for u2 in range(2):
                    o_ = self.ps[0:KA, bk, u2 * 256:(u2 + 1) * 256]
                    self.mm(o_, Wt[:, 0, c + u2, :], e12[:, 0, :], True, False, wr + ['e12'], self.pr(bk))
                    self.mm(o_, Wt[:, 1, c + u2, :], e12[:, 1, :], False, True, wr + ['e12'], self.pr(bk))
                self.cp(self.alt(), U[:, :, :, c:c + 2], self.ps[0:KA, bk, :].rearrange('p (c r b) -> p r b c', c=2, r=2),
                        self.pr(bk), [('U', c)])
            ur = [('U', c) for c in range(0, CG, 2)]
            for b0 in range(0, 128, 8):
                if b0 % 16 == 0:
                    hs = hi_ % 2
                    hi_ += 1
                    self.dma('sp', Ht[hs][:], hall[:, b0:b0 + 16], [], [('Ht', hs)], 'bHt%d' % hs)
                bk = self.nb()
                for q8 in range(8):
                    bp = b0 + q8
                    o_ = self.ps[0:64, bk, q8 * 64:(q8 + 1) * 64]
                    self.mm(o_, Ht[hs][:, bp % 16, 0, :], U[:, 0, bp, :], True, False, ur + [('Ht', hs)], self.pr(bk))
                    self.mm(o_, Ht[hs][:, bp % 16, 1, :], U[:, 1, bp, :], False, True, ur + [('Ht', hs)], self.pr(bk))
                self.cp(self.alt(), Yo[:, :, b0:b0 + 8], self.ps[0:64, bk, :].rearrange('p (b c) -> p c b', b=8), self.pr(bk), [('Yo', b0)])
            self.dma('sp', self.yconv[c0:c0 + CG, :].rearrange('c (a b) -> a c b', a=64), Yo[:],
                     [('Yo', b0) for b0 in range(0, 128, 8)], [('yconv', gg)], 'bYo_st')
        self.end()

    def phase_c1a(self, l):
        self.begin()
        w_out = self.L('w_out')
        wo = self.tile('wo', [128, 8, D], BF16)
        stg = [self.tile('c1a_stg%d' % s, [128, D], F32) for s in range(4)]
        for kc in range(8):
            sc = self.col('g_ao', l, 1, kc) if kc < 4 else self.col('g_ho', l, 1, kc - 4)
            self.load_w(w_out[l, kc * 128:(kc + 1) * 128, :], D, sc, stg, 'c1astg', [(wo[:, kc, :], 0, D, None, ('wo', kc))])
        mix = [self.tile('mix%d' % s, [128, 8, 512], BF16) for s in range(2)]
        xy = [self.tile('xy%d' % s, [128, 2, 4, 512], BF16) for s in range(2)]
        ht = [self.tile('ht%d' % s, [128, 8, 512], F32) for s in range(3)]
        hy = self.tile('hy', [128, 4, 512], F32)
        sqh = self.tile('sqh', [128, 4, 512], BF16)
        rsh = self.tile('rsh', [128, 512], F32)
        sq2 = self.tile('sq2', [128, 8, 512], BF16)
        rs2 = self.tile('rs2', [128, 512], F32)
        n2 = [self.tile('n2_%d' % s, [128, 8, 512], BF16) for s in range(2)]
        ed = self.tile('ed', [128, 8, 2], BF16)
        def loads(i):
            s = i % 2
            cs = slice(i * 512, i * 512 + 512)
            self.dma('sp', mix[s][:, 0:4, :], self.attn_n[:, cs].rearrange('(c p) t -> p c t', p=128), [], [('mixa', s)], 'c1a_ma%d' % s)
            self.dma('sp', xy[s][:, 0], self.x0s[:, cs].rearrange('(c p) t -> p c t', p=128), [], [('xy', s, 0)], 'c1a_x%d' % s)
            self.dma('sp', xy[s][:, 1], self.yconv[:, cs].rearrange('(c p) t -> p c t', p=128), [], [('xy', s, 1)], 'c1a_y%d' % s)
            self.dma('sp', ht[i % 3][:], self.hres[:, cs].rearrange('(c p) t -> p c t', p=128), [], [('ht', i % 3)], 'c1a_h%d' % (i % 3))
        def P1(i):
            s = i % 2
            self.tt('dve', hy[:], xy[s][:, 0], xy[s][:, 1], ALU.mult, [('xy', s, 0), ('xy', s, 1)], ['hy'])
            self.act(sqh[:], hy[:], AF.Square, ['hy'], ['sqh'])
            bk = self.nb()
            for mc in range(4):
                self.mm(self.ps[:, bk, :], self.ones[:, 1, :], sqh[:, mc, :], mc == 0, mc == 3, ['sqh', 'g_ones'], self.pr(bk))
            self.rstd(rsh[:], self.ps[:, bk, :], self.pr(bk), ['rsh'])
            self.tt('pool', mix[s][:, 4:8, :], hy[:], _bc(rsh[:], [128, 4, 512], 1), ALU.mult, ['hy', 'rsh'], [('mixh', s)])

        def P2(i):
            s = i % 2
            cs = slice(i * 512, i * 512 + 512)
            for g2 in range(2):
                bo = self.nb(4)
                for j in range(4):
                    mc = g2 * 4 + j
                    for kc in range(8):
                        self.mm(self.ps[:, bo + j, :], wo[:, kc, mc * 128:(mc + 1) * 128], mix[s][:, kc, :], kc == 0, kc == 7,
                                [('mixa', s), ('mixh', s), ('wo', kc)], self.pr(bo + j))
                h3 = i % 3
                self.tt('dve', ht[h3][:, g2 * 4:(g2 + 1) * 4, :], ht[h3][:, g2 * 4:(g2 + 1) * 4, :], self.ps[:, bo:bo + 4, :], ALU.add,
                        [('ht', h3)] + self.pr(bo, 4), [('ht', h3)])
            self.dma('sp', self.hres[:, cs].rearrange('(c p) t -> p c t', p=128), ht[i % 3][:], [('ht', i % 3)], [('hres', i)], 'c1a_h%d_st' % (i % 3))

        def P3(i):
            s = i % 2
            t0 = i * 512
            h3 = i % 3
            self.act(sq2[:], ht[h3][:], AF.Square, [('ht', h3)], ['sq2'])
            bk = self.nb()
            for kc in range(8):
                self.mm(self.ps[:, bk, :], self.ones[:, 0, :], sq2[:, kc, :], kc == 0, kc == 7, ['sq2', 'g_ones'], self.pr(bk))
            self.rstd(rs2[:], self.ps[:, bk, :], self.pr(bk), ['rs2'])
            for hf in range(2):
                eng = 'dve' if hf == 0 else 'pool'
                self.tt(eng, n2[s][:, hf * 4:(hf + 1) * 4, :], ht[h3][:, hf * 4:(hf + 1) * 4, :], _bc(rs2[:], [128, 4, 512], 1), ALU.mult,
                        [('ht', h3), 'rs2'], [('n2', s, hf)])
            rd = [('n2', s, 0), ('n2', s, 1)]
            self.dma('sp', self.n2s[:, 1 + t0:1 + t0 + 512].rearrange('(c p) t -> p c t', p=128), n2[s][:], rd, [('n2s', i)], 'c1a_n%d_st' % s)
            if i == 0:
                self.dma('sp', self.xn_in[0:1, :].rearrange('o (c p) -> p c o', p=128), n2[s][:, :, 0:1], rd, ['xn0'], 'c1a_e0', slow=True)
            if i == NT - 1:
                self.dma('sp', self.xn_in[1:2, :].rearrange('o (c p) -> p c o', p=128), n2[s][:, :, 511:512], rd, ['xn1'], 'c1a_e1', slow=True)

        loads(0)
        loads(1)
        P1(0)
        for i in range(NT):
            if i + 1 < NT:
                P1(i + 1)
            P2(i)
            if i >= 1:
                P3(i - 1)
            if i + 2 < NT:
                loads(i + 2)
        P3(NT - 1)
        self.allgather(self.xn_in, self.xn_out, ['xn0', 'xn1'], ['xn_out'])
        for k, (row, col) in enumerate([(1, 0), (2, T + 1)]):
            self.dma('sp', ed[:, :, k:k + 1], self.xn_out[row:row + 1, :].rearrange('o (c p) -> p c o', p=128), ['xn_out'], [('ed', k)], 'c1a_ed%d' % k, slow=True)
            self.ts('dve', ed[:, :, k:k + 1], ed[:, :, k:k + 1], self.edge[:, k:k + 1], None, ALU.mult, None, [('ed', k), 'g_edge'], [('ed', k)])
            self.dma('sp', self.n2s[:, col:col + 1].rearrange('(c p) t -> p c t', p=128), ed[:, :, k:k + 1], [('ed', k)], [('n2sh', k)], 'c1a_ed%d_st' % k, slow=True)
        self.end()

    def phase_c1b(self, l):
        self.begin()
        w_up = self.L('w_up')
        wu = self.tile('wu', [128, 8, 2 * DFF], BF16)
        stg = [self.tile('c1b_stg%d' % s, [128, 2816], F32) for s in range(3)]
        for kc in range(8):
            for hh in range(2):
                self.load_w(w_up[l, kc * 128:(kc + 1) * 128, hh * DFF:(hh + 1) * DFF], DFF, self.col('g_ffn', l, 1, kc), stg, 'c1bstg',
                            [(wu[:, kc, hh * DFF:(hh + 1) * DFF], 0, DFF, None, ('wu', kc, hh))])
        n2t = [self.tile('n2t%d' % s, [128, 8, 512], BF16) for s in range(2)]
        at = [self.tile('at%d' % s, [128, 22, 510], BF16) for s in range(2)]
        ntl = (T + 509) // 510
        NQ = 3
        ua = [self.tile('ua%d' % q, [128, 510], F32) for q in range(NQ)]
        ug = [self.tile('ug%d' % q, [128, 510], F32) for q in range(NQ)]
        sg = [self.tile('sg%d' % q, [128, 510], F32) for q in range(NQ)]

        def loads(i):
            s = i % 2
            T0 = 510 * i
            nin = min(510, T - T0) + 2
            self.dma('sp', n2t[s][:, :, 0:nin], self.n2s[:, T0:T0 + nin].rearrange('(c p) t -> p c t', p=128), [], [('n2t', s)], 'c1b_n%d' % s)

        def front(i, j, q):
            s = i % 2
            nout = min(510, T - 510 * i)
            nin = nout + 2
            bk = self.nb(2)
            for hh in range(2):
                for kc in range(8):
                    self.mm(self.ps[:, bk + hh, 0:nin], wu[:, kc, hh * DFF + j * 128:hh * DFF + (j + 1) * 128], n2t[s][:, kc, 0:nin],
                            kc == 0, kc == 7, [('n2t', s), ('wu', kc, hh)], self.pr(bk + hh))
            for hh, ut in ((0, ua[q]), (1, ug[q])):
                wc = lambda k, hh=hh, j=j: self.col('w_ffc', l, 1, k * 44 + hh * 22 + j)
                pb = self.ps[:, bk + hh, :]
                rn = ('u', hh, q)
                self.act(ut[:, 0:nout], pb[:, 1:1 + nout], AF.Identity, self.pr(bk + hh) + ['g_cols'], [rn],
                         scale=wc(1), bias=self.col('b_ffc', l, 1, hh * 22 + j))
                self.stt('dve', ut[:, 0:nout], pb[:, 0:nout], wc(0), ut[:, 0:nout], ALU.mult, ALU.add, self.pr(bk + hh) + [rn, 'g_cols'], [rn])
                self.stt('dve', ut[:, 0:nout], pb[:, 2:2 + nout], wc(2), ut[:, 0:nout], ALU.mult, ALU.add, self.pr(bk + hh) + [rn, 'g_cols'], [rn])

        def back(i, j, q):
            s = i % 2
            nout = min(510, T - 510 * i)
            self.act(sg[q][:, 0:nout], ug[q][:, 0:nout], AF.Silu, [('u', 1, q)], [('sg', q)])
            self.tt('pool', at[s][:, j, 0:nout], sg[q][:, 0:nout], ua[q][:, 0:nout], ALU.mult, [('sg', q), ('u', 0, q)], [('at', s, j)])
            if j == 21:
                T0 = 510 * i
                self.dma('sp', self.acts[:, T0:T0 + nout].rearrange('(c p) t -> p c t', p=128), at[s][:, :, 0:nout],
                         [('at', s, jx) for jx in range(22)], [('acts', i)], 'c1b_a%d_st' % s)

        loads(0)
        seq = [(i, j) for i in range(ntl) for j in range(22)]
        for n_, (i, j) in enumerate(seq):
            if j == 0 and i + 1 < ntl:
                loads(i + 1)
            front(i, j, n_ % NQ)
            if n_ >= 1:
                pi, pj = seq[n_ - 1]
                back(pi, pj, (n_ - 1) % NQ)
        pi, pj = seq[-1]
        back(pi, pj, (len(seq) - 1) % NQ)
        self.end()

    def phase_c2(self, l, last):
        self.begin()
        wd = self.tile('wd', [128, 22, D], BF16)
        wg = self.tile('wg', [128, 8, D], BF16)
        wp = self.tile('wp', [128, 2, D], BF16)
        stg = [self.tile('c2_stg%d' % s, [128, D], F32) for s in range(4)]
        for kc in range(22):
            self.load_w(self.L('w_down')[l, kc * 128:(kc + 1) * 128, :], D, None, stg, 'c2stg', [(wd[:, kc, :], 0, D, None, ('wd', kc))])
        for kc in range(8):
            self.load_w(self.L('w_ple_gate')[l, kc * 128:(kc + 1) * 128, :], D, None, stg, 'c2stg', [(wg[:, kc, :], 0, D, None, ('wg', kc))])
        for kc in range(2):
            self.load_w(self.L('w_ple_proj')[l, kc * 128:(kc + 1) * 128, :], D, None, stg, 'c2stg', [(wp[:, kc, :], 0, D, None, ('wp', kc))])
        at = [self.tile('at%d' % s, [128, 22, 512], BF16) for s in range(2)]
        ht = [self.tile('ht%d' % s, [128, 8, 512], F32) for s in range(2)]
        pt = [self.tile('pt%d' % s, [128, 4, DPLE], F32) for s in range(2)]
        pT = self.tile('pT', [128, 2, 512], BF16)
        hb = self.tile('hb', [128, 8, 512], BF16)
        sgm = self.tile('sgm', [128, 4, 512], F32)
        yo = self.tile('yo', [128, 4, D], F32) if last else None
        p_d = self.L('p')
        def loads(i):
            s = i % 2
            cs = slice(i * 512, i * 512 + 512)
            self.dma('sp', at[s][:], self.acts[:, cs].rearrange('(c p) t -> p c t', p=128), [], [('at', s)], 'c2_a%d' % s)
            self.dma('sp', ht[s][:], self.hres[:, cs].rearrange('(c p) t -> p c t', p=128), [], [('ht', s, 0), ('ht', s, 1)], 'c2_h%d' % s)
            self.dma('sp', pt[s][:], p_d[l, cs, :].rearrange('(b p) f -> p b f', p=128), [], [('pt', s)], 'c2_p%d' % s)
        loads(0)
        for i in range(NT):
            s = i % 2
            t0 = i * 512
            cs = slice(t0, t0 + 512)
            if i + 1 < NT:
                loads(i + 1)
            for pc in range(2):
                bk = self.nb()
                for blk in range(4):
                    self.tp(self.ps[:, bk, blk * 128:(blk + 1) * 128], pt[s][:, blk, pc * 128:(pc + 1) * 128], self.ident[:],
                            [('pt', s), 'g_ident'], self.pr(bk))
                self.cp(self.alt(), pT[:, pc, :], self.ps[:, bk, :], self.pr(bk), [('pT', pc)])
            for g2 in range(2):
                bo = self.nb(4)
                for j in range(4):
                    mc = g2 * 4 + j
                    for kc in range(22):
                        self.mm(self.ps[:, bo + j, :], wd[:, kc, mc * 128:(mc + 1) * 128], at[s][:, kc, :], kc == 0, kc == 21,
                                [('at', s), ('wd', kc)], self.pr(bo + j))
                hs_ = ht[s][:, g2 * 4:(g2 + 1) * 4, :]
                self.tt('dve', hs_, hs_, self.ps[:, bo:bo + 4, :], ALU.add, [('ht', s, g2)] + self.pr(bo, 4), [('ht', s, g2)])
                self.cp('act', hb[:, g2 * 4:(g2 + 1) * 4, :], hs_, [('ht', s, g2)], [('hb', g2)])
            for g2 in range(2):
                bo = self.nb(4)
                for j in range(4):
                    mc = g2 * 4 + j
                    for kc in range(8):
                        self.mm(self.ps[:, bo + j, :], wg[:, kc, mc * 128:(mc + 1) * 128], hb[:, kc, :], kc == 0, kc == 7,
                                [('hb', 0), ('hb', 1), ('wg', kc)], self.pr(bo + j))
                self.act(sgm[:], self.ps[:, bo:bo + 4, :], AF.Sigmoid, self.pr(bo, 4), ['sgm'])
                bp = self.nb(4)
                for j in range(4):
                    mc = g2 * 4 + j
                    for kc in range(2):
                        self.mm(self.ps[:, bp + j, :], wp[:, kc, mc * 128:(mc + 1) * 128], pT[:, kc, :], kc == 0, kc == 1,
                                [('pT', 0), ('pT', 1), ('wp', kc)], self.pr(bp + j))
                self.tt('dve', sgm[:], sgm[:], self.ps[:, bp:bp + 4, :], ALU.mult, ['sgm'] + self.pr(bp, 4), ['sgm'])
                hs_ = ht[s][:, g2 * 4:(g2 + 1) * 4, :]
                self.tt('pool', hs_, hs_, sgm[:], ALU.add, [('ht', s, g2), 'sgm'], [('ht', s, g2)])
            hr_ = [('ht', s, 0), ('ht', s, 1)]
            if not last:
                self.dma('sp', self.hres[:, cs].rearrange('(c p) t -> p c t', p=128), ht[s][:], hr_, [('hres', i)], 'c2_h%d_st' % s)
            else:
                for blk in range(4):
                    for g2 in range(2):
                        bk = self.nb()
                        for j in range(4):
                            mc = g2 * 4 + j
                            self.tp(self.ps[:, bk, j * 128:(j + 1) * 128], ht[s][:, mc, blk * 128:(blk + 1) * 128], self.ident[:],
                                    hr_ + ['g_ident'], self.pr(bk))
                        self.cp(self.alt(), yo[:, blk, g2 * 512:(g2 + 1) * 512], self.ps[:, bk, :], self.pr(bk), [('yo', blk, g2)])
                self.dma('sp', self.y[cs, :].rearrange('(b p) f -> p b f', p=128), yo[:],
                         [('yo', blk, g2) for blk in range(4) for g2 in range(2)], [('y', i)], 'c2_y_st')
        self.end()

    def build(self):
        self.phase_p0()
        for l in range(self.depth):
            last = (l == self.depth - 1)
            for name in ['a0', 'a', 'a2', 'f1', 'f2', 'b', 'c1a', 'c1b', 'c2']:
                fn = getattr(self, 'phase_' + name, None)
                if fn is None:
                    return
                if name == 'c2':
                    fn(l, last)
                else:
                    fn(l)
                if self.stop == (name, l):
                    return


def _cols_table(b, W):
    L = DEPTH
    tab = np.zeros((128, b.ncol), np.float32)

    def put(name, arr):
        o, n = b.colspec[name]
        assert arr.shape == (128, n), (name, arr.shape, n)
        tab[:, o:o + n] = arr

    def chunks(v, nch):
        return v.reshape(L, nch, 128).transpose(2, 0, 1).reshape(128, L * nch)

    put('g_mix', chunks(W['rms_mix'], 8))
    put('g_ffn', chunks(W['rms_ffn'], 8))
    put('gq', np.tile(W['q_norm'].T, (2, 1)))
    put('gk', np.tile(W['k_norm'].T, (2, 1)))
    sk = W['sink'].reshape(L, 2, 4)[:, :, [0, 2, 1, 3]].reshape(1, L * 8)
    put('sink', np.tile(sk, (128, 1)))
    put('w_short', W['w_short'].reshape(L, 3, 12, 128).transpose(3, 0, 1, 2).reshape(128, L * 36))
    put('b_short', chunks(W['b_short'], 12))
    put('g_ao', chunks(W['norm_attn_out'], 4))
    put('g_ho', chunks(W['norm_hyena_out'], 4))
    put('w_ffc', W['w_ffconv'].reshape(L, 3, 44, 128).transpose(3, 0, 1, 2).reshape(128, L * 132))
    put('b_ffc', chunks(W['b_ffconv'], 44))
    for nm, key in [('fb1', 'filt_b1'), ('ffr1', 'filt_freq1'), ('fb2', 'filt_b2'), ('ffr2', 'filt_freq2')]:
        put(nm, np.tile(W[key].T, (2, 1)))
    return tab


_BUILD_CACHE = {}


def _get_builder(depth, debug, stop):
    key = (depth, debug, stop)
    if key not in _BUILD_CACHE:
        b = Builder(depth, debug, stop)
        b.declare()
        b.setup_globals()
        b.build()
        es = contextlib.ExitStack()
        b.P.emit(es)
        b._es = es
        _BUILD_CACHE[key] = b
    return _BUILD_CACHE[key]


def _run(inputs, depth=DEPTH, debug=False, stop=None):
    W = {k: np.asarray(v, dtype=np.float32) for k, v in inputs.items()}
    b = _get_builder(depth, debug, stop)
    sh = _shared_consts()
    cols = _cols_table(b, W)
    xp = W['x_prompt'][0]
    xs = W['x_sample']
    pp = W['p_prompt'][:, 0]
    psm = W['p_sample']
    in_maps = []
    for rank in range(N_CORES):
        kind, idx = _unit_of_rank(rank)
        rc = _rank_consts(rank)
        if kind == 'p':
            x = xp[idx * T:(idx + 1) * T]
            p = pp[:, idx * T:(idx + 1) * T]
        else:
            x = xs[idx]
            p = psm[:, idx]
        m = {
            'x': np.ascontiguousarray(x), 'p': np.ascontiguousarray(p),
            'w_in': W['w_in'], 'w_out': W['w_out'], 'w_up': W['w_up'], 'w_down': W['w_down'],
            'w_ple_gate': W['w_ple_gate'], 'w_ple_proj': W['w_ple_proj'],
            'filt_w1': W['filt_w1'], 'filt_w2': W['filt_w2'], 'filt_w3': W['filt_w3'],
            'cols': cols, 'hbias': W['hyena_bias'].reshape(1, -1),
            'ident_f': sh['ident_f'], 'ones_b': sh['ones_b'], 'prot_b': sh['prot_b'],
            'masks': rc['masks'], 'edge': rc['edge'], 'cstab': rc['cstab'],
            'f1m': sh['f1m'], 'gall': sh['gall'], 'e12': sh['e12'], 'hall': sh['hall'], 'negd': sh['negd'],
            'zf': rc['zf'], 'mm': rc['mm'], 'tdec': rc['tdec'], 'e0': rc['e0'],
        }
        in_maps.append({k: v for k, v in m.items() if k in b.inputs})
    res = run_bass_kernel_spmd(b.nc, in_maps, core_ids=list(range(N_CORES)))
    return b, res


def kernel(**inputs):
    b, res = _run(inputs)
    ys = [np.asarray(res.results[r]['y'], dtype=np.float32) for r in range(6)]
    y_prompt = np.concatenate([ys[0], ys[1]], axis=0)[None]
    y_sample = np.stack(ys[2:6], axis=0)
    return (y_prompt, y_sample)
```

```python
import os
import math
import contextlib
import numpy as np
import ml_dtypes
import concourse.bass as bass
import concourse.mybir as mybir
from concourse.bass_utils import run_bass_kernel_spmd

F32 = mybir.dt.float32
BF16 = mybir.dt.bfloat16
AF = mybir.ActivationFunctionType
ALU = mybir.AluOpType
BF = ml_dtypes.bfloat16

D = 1024
DEPTH = 4
T = 8192
NT = 16
HALO = 128
NW = T + 2 * HALO
DQ = 512
DH = 512
DFF = 2816
DPLE = 256
EPS = 1e-6
NF = 16384
KA = 65
CG = 64
NG = DH // CG
ROPE_THETA = 500000.0
N_CORES = 8


class Prog:
    def __init__(self, nc):
        self.nc = nc
        self.ops = []
        self.lastw = {}
        self.readers = {}
        self.lastdma = {}
        self.eng = {'pe': nc.tensor, 'act': nc.scalar, 'dve': nc.vector, 'pool': nc.gpsimd, 'sp': nc.sync}
        self.last_on = {}
        self.n_cc = 0

    def op(self, eng, fn, r=(), w=(), dma=None, cc=False):
        idx = len(self.ops)
        deps = set()
        for x in r:
            p = self.lastw.get(x)
            if p is not None:
                deps.add(p)
        for x in w:
            p = self.lastw.get(x)
            if p is not None:
                deps.add(p)
            deps.update(self.readers.get(x, ()))
        if dma is not None:
            p = self.lastdma.get(dma)
            if p is not None:
                deps.add(p)
            self.lastdma[dma] = idx
        for x in r:
            self.readers.setdefault(x, []).append(idx)
        for x in w:
            self.lastw[x] = idx
            self.readers[x] = []
        deps.discard(idx)
        self.ops.append(dict(eng=eng, fn=fn, deps=deps, dma=dma, cc=cc, bar=False))
        if not cc:
            self.last_on[eng] = idx
        else:
            self.sticky = getattr(self, 'sticky', {})
            for x in w:
                self.sticky[x] = idx
        return idx

    def barrier(self):
        deps = set(self.last_on.values()) | set(self.lastdma.values())
        for k, e in enumerate(('pe', 'act', 'dve', 'pool', 'sp')):
            self.ops.append(dict(eng=e, fn=None, deps=set(deps), dma=None, cc=False, bar=True, reset=(k == 0)))
        self.lastw = dict(getattr(self, 'sticky', {}))
        self.readers = {}

    def emit(self, es):
        nc = self.nc
        ops = self.ops
        def pe2pe(p, o):
            return (p['dma'] is None and not p['cc'] and p['eng'] == 'pe' and o['eng'] == 'pe'
                    and o['dma'] is None and o['fn'] is not None)
        sig = [False] * len(ops)
        for o in ops:
            latest = {}
            for d in o['deps']:
                p = ops[d]
                if pe2pe(p, o):
                    continue
                if p['dma'] is not None or p['cc']:
                    sig[d] = True
                else:
                    if latest.get(p['eng'], -1) < d:
                        latest[p['eng']] = d
            for d in latest.values():
                sig[d] = True
            o['bind'] = set(latest.values())
        sems = {}

        def getsem(name):
            if name not in sems:
                sems[name] = es.enter_context(nc.semaphore(name))
            return sems[name]

        cnt = {}
        val = [None] * len(ops)
        chan = [None] * len(ops)
        waited = {e: {} for e in self.eng}
        n_wait = 0
        keyslot = {}
        ncc = 0
        for i, o in enumerate(ops):
            e = o['eng']
            eobj = self.eng[e]
            if o.get('reset'):
                keyslot = {}
            need = {}
            for d in o['deps']:
                if val[d] is None:
                    continue
                p = ops[d]
                if p['dma'] is None and not p['cc'] and d not in o['bind']:
                    continue
                c = chan[d]
                if need.get(c, 0) < val[d]:
                    need[c] = val[d]
            for c, v in need.items():
                if waited[e].get(c, 0) >= v:
                    continue
                eobj.wait_ge(getsem(c), v)
                waited[e][c] = v
                n_wait += 1
            if o['fn'] is None:
                continue
            ins = o['fn'](eobj)
            if o['cc']:
                c = 'cc%d' % (ncc % 8)
                ncc += 1
                cnt[c] = cnt.get(c, 0) + 1
                ins.then_inc(getsem(c), 1)
                chan[i] = c
                val[i] = cnt[c]
            elif o['dma'] is not None:
                if o['dma'] not in keyslot:
                    keyslot[o['dma']] = len(keyslot)
                c = 'dslot%d' % keyslot[o['dma']]
                cnt[c] = cnt.get(c, 0) + 16
                ins.then_inc(getsem(c), 16)
                chan[i] = c
                val[i] = cnt[c]
            elif sig[i]:
                c = 'e_' + e
                cnt[c] = cnt.get(c, 0) + 1
                ins.then_inc(getsem(c), 1)
                chan[i] = c
                val[i] = cnt[c]
        for e in ('sp',):
            eobj = self.eng[e]
            for c, v in cnt.items():
                if waited[e].get(c, 0) < v:
                    eobj.wait_ge(getsem(c), v)
        self.stats = dict(n_ops=len(ops), n_wait=n_wait, n_sems=len(sems))


def _unit_of_rank(rank):
    return [('p', 0), ('p', 1), ('s', 0), ('s', 1), ('s', 2), ('s', 3), ('s', 2), ('s', 3)][rank]


_CONST_CACHE = {}


def _shared_consts():
    if 'shared' in _CONST_CACHE:
        return _CONST_CACHE['shared']
    c = {}
    c['ident_f'] = np.eye(128, dtype=np.float32)
    ones = np.zeros((128, 4, 128), np.float32)
    ones[:, 0, :] = 1.0 / 1024.0
    ones[:, 1, :] = 1.0 / 512.0
    ones[:, 2, :] = 1.0
    blk = np.zeros((128, 128), np.float32)
    blk[:64, :64] = 1.0 / 64.0
    blk[64:, 64:] = 1.0 / 64.0
    ones[:, 3, :] = blk
    c['ones_b'] = ones.astype(BF)
    prot = np.zeros((128, 128), np.float32)
    for p in range(128):
        d = p % 64
        if d < 8:
            prot[p + 8, p] = -1.0
        elif d < 16:
            prot[p - 8, p] = 1.0
    c['prot_b'] = prot.astype(BF)
    j = np.arange(128)[:, None]
    i = np.arange(128)[None, :]
    c['mprev'] = (j >= i).astype(np.float32)
    c['mnext'] = (j <= i).astype(np.float32)
    a = np.arange(128, dtype=np.float64)[:, None]
    ka = np.arange(KA, dtype=np.float64)[None, :]
    th = 2 * np.pi * a * ka / 128.0
    c['f1m'] = np.concatenate([np.cos(th), -np.sin(th)], axis=1).astype(BF)
    b = np.arange(128, dtype=np.float64)[:, None, None]
    kav = np.arange(KA, dtype=np.float64)[None, :, None]
    kb = np.arange(128, dtype=np.float64)[None, None, :]
    th = 2 * np.pi * b * (kav + 128.0 * kb) / NF
    gall = np.stack([np.cos(th), -np.sin(th), np.sin(th)], axis=2)
    c['gall'] = gall.astype(BF)
    kbv = np.arange(128, dtype=np.float64)[:, None]
    bp = np.arange(128, dtype=np.float64)[None, :]
    th = 2 * np.pi * kbv * bp / 128.0
    e1 = np.concatenate([np.cos(th), np.sin(th)], axis=1)
    e2 = np.concatenate([-np.sin(th), np.cos(th)], axis=1)
    c['e12'] = np.stack([e1, e2], axis=1).astype(BF)
    kav = np.arange(KA, dtype=np.float64)[:, None, None]
    bpv = np.arange(128, dtype=np.float64)[None, :, None]
    ap = np.arange(64, dtype=np.float64)[None, None, :]
    ph = 2 * np.pi * kav * (bpv + 128.0 * ap) / NF
    wt = np.full((KA, 1, 1), 2.0)
    wt[0] = 1.0
    wt[64] = 1.0
    hall = np.stack([wt / NF * np.cos(ph), -wt / NF * np.sin(ph)], axis=2)
    c['hall'] = hall.astype(BF)
    deltas = np.linspace(math.log(1e-2) / 1.5, math.log(1e-2) / 0.3, DH).astype(np.float32)
    c['negd'] = np.tile(-np.abs(deltas)[None, :], (128, 1)).astype(np.float32)
    _CONST_CACHE['shared'] = c
    return c


def _zfeat(L):
    key = ('z', L)
    if key in _CONST_CACHE:
        return _CONST_CACHE[key]
    t = np.linspace(0.0, 1.0, L).astype(np.float32)
    w = (2.0 * np.pi * np.arange(L, dtype=np.float64) / L)
    f = np.linspace(1e-4, 15.0, 16)
    fw = f[None, :] * w[:, None]
    z = np.concatenate([t[:, None].astype(np.float64), np.cos(fw), -np.sin(fw)], axis=1).astype(np.float32)
    _CONST_CACHE[key] = (t, z)
    return t, z


def _rank_consts(rank):
    key = ('rank', rank)
    if key in _CONST_CACHE:
        return _CONST_CACHE[key]
    kind, idx = _unit_of_rank(rank)
    sh = _shared_consts()
    c = {}
    pos0 = 8192 if (kind == 'p' and idx == 1) else 0
    mL = 1.0 if (kind == 'p' and idx == 1) else 0.0
    mR = 1.0 if (kind == 'p' and idx == 0) else 0.0
    c['edge'] = np.tile(np.array([[mL, mR]], np.float32), (128, 1))
    masks = np.stack([sh['mprev'], sh['mnext'], sh['mprev'] * mL, sh['mnext'] * mR], axis=1)
    c['masks'] = masks.astype(BF)
    pos = (pos0 - HALO + np.arange(NW)).astype(np.float32)
    inv = (ROPE_THETA ** (-np.arange(0, 16, 2, dtype=np.float32) / 16.0)).astype(np.float32)
    ang = pos[None, :] * inv[:, None]
    cs = np.zeros((128, 2, NW), np.float32)
    cs[:, 0, :] = 1.0
    for p in range(128):
        d = p % 64
        if d < 16:
            cs[p, 0, :] = np.cos(ang[d % 8])
            cs[p, 1, :] = np.sin(ang[d % 8])
    c['cstab'] = cs
    L = 16384 if kind == 'p' else 8192
    t_all, z_all = _zfeat(L)
    n = np.arange(NF)
    zf = np.zeros((2, 33, NF), np.float32)
    mm = np.zeros((2, 128, NF), np.float32)
    td = np.zeros((2, NF), np.float32)
    e0 = np.zeros((2,), np.float32)
    own = idx % 2 if kind == 's' else idx
    for s in range(2):
        lag = np.zeros(NF, np.int64)
        dr = np.zeros(NF, np.int64)
        if s == own:
            lo = n < 8192
            hi = n > 8192
            lag[lo] = n[lo]
            dr[lo] = 1
            lag[hi] = NF - n[hi]
            dr[hi] = 2
            e0[s] = 1.0
        elif kind == 'p':
            lo = n < 8192
            hi = n > 8192
            if idx == 0:
                lag[lo] = 8192 - n[lo]
                dr[lo] = 2
                lag[hi] = 24576 - n[hi]
                dr[hi] = 2
            else:
                lag[lo] = n[lo] + 8192
                dr[lo] = 1
                lag[hi] = n[hi] - 8192
                dr[hi] = 1
        valid = dr > 0
        zf[s][:, valid] = z_all[lag[valid]].T
        td[s][valid] = t_all[lag[valid]]
        mm[s][:64, :] = (dr == 1).astype(np.float32)[None, :]
        mm[s][64:, :] = (dr == 2).astype(np.float32)[None, :]
    c['zf'] = zf
    c['mm'] = mm.astype(BF)
    c['tdec'] = np.ascontiguousarray(td.reshape(2, 128, 128).transpose(1, 0, 2))
    c['e0'] = np.tile(e0[None, :], (128, 1)).astype(np.float32)
    _CONST_CACHE[key] = c
    return c


def _bc(ap, shape, axis):
    return ap.unsqueeze(axis).to_broadcast(shape)


class Builder:
    def __init__(self, depth=DEPTH, debug=False, stop=None):
        self.depth = depth
        self.debug = debug
        self.stop = stop
        self.nc = bass.Bass("TRN2", target_bir_lowering=False)
        self.P = Prog(self.nc)
        self.ges = contextlib.ExitStack()
        self.scope = None
        self.bank = 0
        self.rr = 0
        self.inputs = {}
        self.dbg_out = []
        self.sub = float(os.environ.get('MK_SUB', '99'))
        self.ntr = int(os.environ.get('MK_NT', str(NT)))

    def din(self, name, shape, dt=F32):
        t = self.nc.dram_tensor(name, list(shape), dt, kind="ExternalInput")
        self.inputs[name] = (tuple(shape), dt)
        return t.ap()

    def L(self, name):
        if name not in self.lazy_ap:
            shp, dt = self.lazy[name]
            self.lazy_ap[name] = self.din(name, shp, dt)
        return self.lazy_ap[name]

    def dscr(self, name, shape, dt, dbg=True):
        if self.debug and dbg and name in self.debug:
            self.dbg_out.append(name)
            return self.nc.dram_tensor(name, list(shape), dt, kind="ExternalOutput").ap()
        return self.nc.dram_tensor(name, list(shape), dt).ap()

    def gtile(self, name, shape, dt):
        return self.ges.enter_context(self.nc.sbuf_tensor('sbg_' + name, list(shape), dt))

    def tile(self, name, shape, dt):
        self.uid = getattr(self, 'uid', 0) + 1
        return self.scope.enter_context(self.nc.sbuf_tensor('sb%d_%s' % (self.uid, name), list(shape), dt))

    def begin(self):
        self.scope = contextlib.ExitStack()
        import inspect
        nm = inspect.stack()[1].function
        self.marks = getattr(self, 'marks', [])
        self.marks.append((nm, sum(1 for o in self.P.ops if o['eng'] == 'pe' and o['fn'] is not None)))

    def end(self):
        self.P.barrier()
        self.scope.close()
        self.scope = None

    def nb(self, k=1):
        if self.bank + k > 8:
            self.bank = 0
        b = self.bank
        self.bank = (self.bank + k) % 8
        return b

    def pr(self, b, k=1):
        return [('ps', b + j) for j in range(k)]

    def alt(self, engs=('act', 'dve')):
        self.rr += 1
        return engs[self.rr % len(engs)]

    def mm(self, out, lhsT, rhs, start, stop, r, w):
        self.P.op('pe', lambda e, o=out, l=lhsT, x=rhs, s=start, t=stop: e.matmul(o, l, x, start=s, stop=t), r, w)

    def tp(self, out, in_, ident, r, w):
        self.P.op('pe', lambda e, o=out, i=in_, d=ident: e.transpose(o, i, d), r, w)

    def act(self, out, in_, func, r, w, scale=None, bias=None):
        kw = {}
        if scale is not None:
            kw['scale'] = scale
        if bias is not None:
            kw['bias'] = bias
        self.P.op('act', lambda e, o=out, i=in_, f=func, k=kw: e.activation(o, i, f, **k), r, w)

    def cp(self, eng, out, in_, r, w):
        if eng == 'act':
            self.act(out, in_, AF.Copy, r, w)
        else:
            self.P.op(eng, lambda e, o=out, i=in_: e.tensor_copy(o, i), r, w)

    def tt(self, eng, out, in0, in1, op, r, w):
        self.P.op(eng, lambda e, o=out, a=in0, b=in1, p=op: e.tensor_tensor(o, a, b, p), r, w)

    def ts(self, eng, out, in0, s1, s2, op0, op1, r, w):
        if op1 is None:
            self.P.op(eng, lambda e, o=out, a=in0, x=s1, p=op0: e.tensor_scalar(o, a, x, None, p), r, w)
        else:
            self.P.op(eng, lambda e, o=out, a=in0, x=s1, y=s2, p=op0, q=op1: e.tensor_scalar(o, a, x, y, p, q), r, w)

    def stt(self, eng, out, in0, scalar, in1, op0, op1, r, w):
        self.P.op(eng, lambda e, o=out, a=in0, s=scalar, b=in1, p=op0, q=op1:
                  e.scalar_tensor_tensor(o, a, s, b, p, q), r, w)

    def rstd(self, out, in_, r, w, eng='dve'):
        np_ = out.shape[0]
        self.act(out, in_, AF.Ln, list(r) + ['g_epsc'], list(w), bias=self.epsc[0:np_, 0:1], scale=1.0)
        self.act(out, out, AF.Exp, list(w), list(w), scale=-0.5)

    def dma(self, q, out, in_, r, w, key, slow=False):
        if slow:
            self.P.op(q, lambda e, o=out, i=in_: e.dma_start(out=o, in_=i, allow_slow_non_contiguous=True), r, w, dma=key)
        else:
            self.P.op(q, lambda e, o=out, i=in_: e.dma_start(out=o, in_=i), r, w, dma=key)

    def allgather(self, in2d, out2d, r, w):
        self.P.op('pool', lambda e, i=in2d, o=out2d: e.collective_compute(
            "AllGather", ALU.bypass, replica_groups=[[0, 1], [2, 3], [4, 5], [6, 7]],
            ins=[i.opt()], outs=[o.opt()]), r, w, cc=True)

    def declare(self):
        L = DEPTH
        d = self.din
        self.lazy = {'x': ([T, D], F32), 'p': ([L, T, DPLE], F32), 'w_in': ([L, D, 2304], F32),
                     'w_out': ([L, D, D], F32), 'w_up': ([L, D, 2 * DFF], F32), 'w_down': ([L, DFF, D], F32),
                     'w_ple_gate': ([L, D, D], F32), 'w_ple_proj': ([L, DPLE, D], F32),
                     'filt_w1': ([L, 33, 64], F32), 'filt_w2': ([L, 64, 64], F32), 'filt_w3': ([L, 64, 1024], F32),
                     'gall': ([128, KA, 3, 128], BF16), 'hall': ([KA, 128, 2, 64], BF16),
                     'zf': ([2, 33, NF], F32), 'mm': ([2, 128, NF], BF16), 'cstab': ([128, 2, NW], F32)}
        self.lazy_ap = {}
        self.colspec = {}
        off = 0
        for name, n in [('g_mix', L * 8), ('g_ffn', L * 8), ('gq', L), ('gk', L), ('sink', L * 8),
                        ('w_short', L * 36), ('b_short', L * 12), ('g_ao', L * 4), ('g_ho', L * 4),
                        ('w_ffc', L * 132), ('b_ffc', L * 44), ('fb1', L), ('ffr1', L), ('fb2', L), ('ffr2', L)]:
            self.colspec[name] = (off, n)
            off += n
        self.ncol = off
        self.cols_d = d('cols', [128, self.ncol])
        self.hbias_d = d('hbias', [1, L * DH])
        self.c_ident = d('ident_f', [128, 128])
        self.c_ones = d('ones_b', [128, 4, 128], BF16)
        self.c_prot = d('prot_b', [128, 128], BF16)
        self.c_masks = d('masks', [128, 4, 128], BF16)
        self.c_edge = d('edge', [128, 2])
        self.c_f1m = d('f1m', [128, 130], BF16)
        self.c_e12 = d('e12', [128, 2, 256], BF16)
        self.c_negd = d('negd', [128, DH])
        self.c_tdec = d('tdec', [128, 2, 128])
        self.c_e0 = d('e0', [128, 2])
        self.y = self.nc.dram_tensor('y', [T, D], F32, kind="ExternalOutput").ap()
        s = self.dscr
        self.hres = s('hres', [D, T], F32)
        self.nrm = s('nrm', [D, NW], BF16)
        self.xh_in = s('xh_in', [2 * D, 128], BF16, dbg=False)
        self.xh_out = s('xh_out', [4 * D, 128], BF16, dbg=False)
        self.zhy = s('zhy', [3 * DH, T + 2], BF16)
        self.attn_n = s('attn_n', [DQ, T], BF16)
        self.x0s = s('x0s', [DH, T], BF16)
        self.vfft2 = [s('vfft%d' % g_, [64 * 8, 1024], BF16, dbg=False) for g_ in range(NG)]
        self.vall2 = [s('vall%d' % g_, [128 * 8, 1024], BF16, dbg=False) for g_ in range(NG)]
        self.vfft = [t_.rearrange('(a x) y -> a (x y)', x=8) for t_ in self.vfft2]
        self.vall = [t_.rearrange('(a x) y -> a (x y)', x=8) for t_ in self.vall2]
        self.hfs = s('hfs', [NG, 128, KA * 2 * 2 * CG], BF16)
        self.hdn = s('hdn', [2, 128, NF], BF16)
        self.yconv = s('yconv', [DH, T], BF16)
        self.n2s = s('n2s', [D, T + 2], BF16)
        self.xn_in = s('xn_in', [2, D], BF16, dbg=False)
        self.xn_out = s('xn_out', [4, D], BF16, dbg=False)
        self.acts = s('acts', [DFF, T], BF16)

    def col(self, name, l=None, k=1, j=0):
        off, n = self.colspec[name]
        per = n // DEPTH
        if l is None:
            return self.cols[:, off:off + n]
        a = off + l * per + j
        return self.cols[:, a:a + k]

    def setup_globals(self):
        g = self.gtile
        self.ps = self.ges.enter_context(self.nc.psum_tensor('ps', [128, 8, 512], F32))
        self.ident = g('ident', [128, 128], F32)
        self.ones = g('ones', [128, 4, 128], BF16)
        self.prot = g('prot', [128, 128], BF16)
        self.masks = g('masks', [128, 4, 128], BF16)
        self.edge = g('edge', [128, 2], F32)
        self.cols = g('cols', [128, self.ncol], F32)
        self.esk = g('esk', [128, DEPTH * 8], F32)
        self.fsc = g('fsc', [128, DEPTH * 4], F32)
        self.e0 = g('e0', [128, 2], F32)
        self.epsc = g('epsc', [128, 1], F32)
        self.P.op('dve', lambda e: e.memset(self.epsc[:], EPS), [], ['g_epsc'])
        for t, dsrc, nm in [(self.ident, self.c_ident, 'ident'), (self.ones, self.c_ones, 'ones'),
                            (self.prot, self.c_prot, 'prot'), (self.masks, self.c_masks, 'masks'),
                            (self.edge, self.c_edge, 'edge'), (self.cols, self.cols_d, 'cols'),
                            (self.e0, self.c_e0, 'e0')]:
            self.dma('sp', t[:], dsrc, [], ['g_' + nm], 'g_' + nm)
        o, n = self.colspec['sink']
        self.act(self.esk[:], self.cols[:, o:o + n], AF.Exp, ['g_cols'], ['g_esk'])
        for l in range(self.depth):
            for k, (fr, fb) in enumerate([('ffr1', 'fb1'), ('ffr2', 'fb2')]):
                self.ts('dve', self.fsc[:, l * 4 + 2 * k:l * 4 + 2 * k + 1], self.col(fr, l), 1.0 / 3.0, None, ALU.mult, None,
                        ['g_cols'], [('fsc', l, 2 * k)])
                self.tt('dve', self.fsc[:, l * 4 + 2 * k + 1:l * 4 + 2 * k + 2], self.fsc[:, l * 4 + 2 * k:l * 4 + 2 * k + 1],
                        self.col(fb, l), ALU.mult, [('fsc', l, 2 * k), 'g_cols'], [('fsc', l, 2 * k + 1)])
        self.P.barrier()

    def load_w(self, src_rows, ncols, scale, stg, sname, pieces):
        self.wslot = getattr(self, 'wslot', 0) + 1
        s = self.wslot % len(stg)
        st = stg[s]
        rn = (sname, s)
        self.dma('sp', st[:, 0:ncols], src_rows, [], [rn], '%s%d' % (sname, s))
        for (d_ap, c0, c1, vf, wn) in pieces:
            src = st[:, c0:c1]
            if vf is not None:
                src = vf(src)
            eng = self.alt(('act', 'dve'))
            if scale is None:
                self.cp(eng, d_ap, src, [rn], [wn])
            elif eng == 'act':
                self.act(d_ap, src, AF.Copy, [rn, 'g_cols'], [wn], scale=scale)
            else:
                self.ts(eng, d_ap, src, scale, None, ALU.mult, None, [rn, 'g_cols'], [wn])

    def phase_p0(self):
        self.begin()
        xt = [self.tile('p0_xt%d' % s, [128, 4, D], F32) for s in range(2)]
        ht = [self.tile('p0_ht%d' % s, [128, 8, 512], F32) for s in range(2)]
        for i in range(NT):
            s = i % 2
            self.dma('sp', xt[s][:], self.L('x')[i * 512:(i + 1) * 512, :].rearrange('(b p) f -> p b f', p=128),
                     [], [('xt', s)], 'xt%d' % s)
            for fc in range(8):
                bk = self.nb()
                for blk in range(4):
                    self.tp(self.ps[:, bk, blk * 128:(blk + 1) * 128], xt[s][:, blk, fc * 128:(fc + 1) * 128],
                            self.ident[:], [('xt', s), 'g_ident'], self.pr(bk))
                self.cp(self.alt(), ht[s][:, fc, :], self.ps[:, bk, :], self.pr(bk), [('ht', s, fc)])
            self.dma('sp', self.hres[:, i * 512:(i + 1) * 512].rearrange('(c p) t -> p c t', p=128), ht[s][:],
                     [('ht', s, fc) for fc in range(8)], [('hres', i)], 'ht%d_st' % s)
        self.end()

    def phase_a0(self, l):
        self.begin()
        ht = [self.tile('a0_ht%d' % s, [128, 8, 512], F32) for s in range(2)]
        sq = [self.tile('a0_sq%d' % s, [128, 8, 512], BF16) for s in range(2)]
        rs = [self.tile('a0_rs%d' % s, [128, 512], F32) for s in range(2)]
        nt = [self.tile('a0_nt%d' % s, [128, 8, 512], BF16) for s in range(2)]
        hl = self.tile('a0_hl', [128, 8, 128], BF16)
        hr = self.tile('a0_hr', [128, 8, 128], BF16)
        def loads(i):
            s = i % 2
            self.dma('sp', ht[s][:], self.hres[:, i * 512:(i + 1) * 512].rearrange('(c p) t -> p c t', p=128),
                     [('hres', i)], [('ht', s)], 'a0ht%d' % s)
        loads(0)
        for i in range(NT):
            s = i % 2
            if i + 1 < NT:
                loads(i + 1)
            self.act(sq[s][:], ht[s][:], AF.Square, [('ht', s)], [('sq', s)])
            bk = self.nb()
            for fc in range(8):
                self.mm(self.ps[:, bk, :], self.ones[:, 0, :], sq[s][:, fc, :], fc == 0, fc == 7,
                        [('sq', s), 'g_ones'], self.pr(bk))
            self.rstd(rs[s][:], self.ps[:, bk, :], self.pr(bk), [('rs', s)])
            for hf in range(2):
                eng = 'dve' if hf == 0 else 'pool'
                self.tt(eng, nt[s][:, hf * 4:(hf + 1) * 4, :], ht[s][:, hf * 4:(hf + 1) * 4, :],
                        _bc(rs[s][:], [128, 4, 512], 1), ALU.mult, [('ht', s), ('rs', s)], [('nt', s, hf)])
            rd = [('nt', s, 0), ('nt', s, 1)]
            self.dma('sp', self.nrm[:, HALO + i * 512:HALO + (i + 1) * 512].rearrange('(c p) t -> p c t', p=128),
                     nt[s][:], rd, [('nrm', i)], 'a0nt%d_st' % s)
            if i == 0:
                self.dma('sp', self.xh_in[0:D, :].rearrange('(c p) t -> p c t', p=128), nt[s][:, :, 0:128],
                         rd, ['xh_in0'], 'a0x0')
            if i == NT - 1:
                self.dma('sp', self.xh_in[D:2 * D, :].rearrange('(c p) t -> p c t', p=128), nt[s][:, :, 384:512],
                         rd, ['xh_in1'], 'a0x1')
        self.allgather(self.xh_in, self.xh_out, ['xh_in0', 'xh_in1'], ['xh_out'])
        for k, (tl, r0, col) in enumerate([(hl, D, 0), (hr, 2 * D, NW - HALO)]):
            self.dma('sp', tl[:], self.xh_out[r0:r0 + D, :].rearrange('(c p) t -> p c t', p=128),
                     ['xh_out'], [('hal', k)], 'a0h%d' % k)
            self.ts('dve', tl[:], tl[:], self.edge[:, k:k + 1], None, ALU.mult, None, [('hal', k), 'g_edge'], [('hal', k)])
            self.dma('sp', self.nrm[:, col:col + HALO].rearrange('(c p) t -> p c t', p=128), tl[:],
                     [('hal', k)], [('nrmh', k)], 'a0h%d_st' % k)
        self.end()


    def capture(self, fn):
        rec = []
        real = self.P.op
        self.P.op = lambda *a, **k: rec.append((a, k))
        try:
            fn()
        finally:
            self.P.op = real
        return rec

    def replay(self, recs):
        for a, k in recs:
            self.P.op(*a, **k)

    @staticmethod
    def merge(a, b):
        def split(x):
            segs = [[]]
            for r in x:
                if r[0][0] == 'MARK':
                    segs.append([])
                else:
                    segs[-1].append(r)
            return segs
        sa = split(a)
        sb = [g_ for g_ in split(b) if g_]
        ncut = max(1, len(sa) - 1)
        out = []
        ib = 0
        for k, sg in enumerate(sa):
            out.extend(sg)
            if k < len(sa) - 1:
                tgt = (k + 1) * len(sb) // ncut
                while ib < min(tgt, len(sb)):
                    out.extend(sb[ib])
                    ib += 1
        while ib < len(sb):
            out.extend(sb[ib])
            ib += 1
        return out

    def phase_a(self, l):
        self.begin()
        w_in = self.L('w_in')
        cstab = self.L('cstab')
        wq = self.tile('wq', [128, 8, 512], BF16)
        wk = self.tile('wk', [128, 8, 256], BF16)
        wv = self.tile('wv', [128, 8, 256], BF16)
        why = self.tile('why', [128, 8, 1536], BF16)
        stg = [self.tile('a_stg%d' % s_, [128, 1152], F32) for s_ in range(2)]
        dup = lambda a: _bc(a.rearrange('p (k d) -> p k d', k=2), [128, 2, 2, 64], 2)
        for kc in range(8):
            self.load_w(w_in[l, kc * 128:(kc + 1) * 128, 0:1152], 1152, self.col('g_mix', l, 1, kc), stg, 'astg', [
                (wq[:, kc, :], 0, 512, None, ('wq', kc)),
                (wk[:, kc, :].rearrange('p (k u d) -> p k u d', k=2, u=2), 512, 640, dup, ('wk', kc)),
                (wv[:, kc, :].rearrange('p (k u d) -> p k u d', k=2, u=2), 640, 768, dup, ('wv', kc)),
                (why[:, kc, 0:384], 768, 1152, None, ('why', kc, 0))])
            self.load_w(w_in[l, kc * 128:(kc + 1) * 128, 1152:2304], 1152, self.col('g_mix', l, 1, kc), stg, 'astg', [
                (why[:, kc, 384:1536], 0, 1152, None, ('why', kc, 1))])
        nt = [self.tile('nt%d' % s, [128, 8, 768], BF16) for s in range(2)]
        ct = [self.tile('ct%d' % s, [128, 2, 768], F32) for s in range(2)]
        qraw = self.tile('qraw', [128, 4, 512], F32)
        kraw = self.tile('kraw', [128, 2, 768], F32)
        sqq = self.tile('sqq', [128, 4, 512], BF16)
        sqk = self.tile('sqk', [128, 2, 768], BF16)
        rq = self.tile('rq', [128, 4, 512], F32)
        rk = self.tile('rk', [128, 2, 768], F32)
        qn = self.tile('qn', [128, 4, 512], BF16)
        kn = self.tile('kn', [128, 2, 768], BF16)
        qr = [self.tile('qr%d' % s, [128, 4, 512], BF16) for s in range(2)]
        krl = [self.tile('krl%d' % s, [128, 2, 768], BF16) for s in range(2)]
        krh = [self.tile('krh%d' % s, [128, 2, 768], BF16) for s in range(2)]
        for s in range(2):
            self.P.op('pool', lambda e, t_=krl[s]: e.memset(t_[64:128], 0.0), [], [('krl0', s)])
            self.P.op('pool', lambda e, t_=krh[s]: e.memset(t_[0:64], 0.0), [], [('krh0', s)])
        vd = [self.tile('vd%d' % s, [128, 6, 256], BF16) for s in range(2)]
        zt = self.tile('zt', [128, 12, 512], BF16)
        zh = self.tile('zh', [128, 12, 2], BF16)
        E = [self.tile('E%d' % s, [128, 3, 512], BF16) for s in range(2)]
        rec = [self.tile('rec%d' % s, [128, 512], F32) for s in range(2)]
        attn = self.tile('attn', [128, 4, 512], F32)
        sqa = self.tile('sqa', [128, 4, 512], BF16)
        rsa = self.tile('rsa', [128, 512], F32)
        an = [self.tile('an%d' % s, [128, 4, 512], BF16) for s in range(2)]
        gq = self.col('gq', l)
        gk = self.col('gk', l)
        st_ = {'ei': 0}
        kseg = [(0, 0, 512, 0, 0), (0, 512, 768, 1, 0), (1, 0, 256, 1, 256), (1, 256, 768, 2, 0)]
        whyr = lambda kc: [('why', kc, 0), ('why', kc, 1)]

        def loads(i):
            s = i % 2
            t0 = i * 512
            self.dma('sp', nt[s][:], self.nrm[:, t0:t0 + 768].rearrange('(c p) t -> p c t', p=128),
                     [], [('nt', s)], 'a_nt%d' % s)
            self.dma('sp', ct[s][:], cstab[:, :, t0:t0 + 768], [], [('ct', s)], 'a_ct%d' % s)

        def X(i):
            s = i % 2
            t0 = i * 512
            rw = [('nt', s)]
            bq = self.nb(4)
            for mc in range(4):
                for kc in range(8):
                    self.mm(self.ps[:, bq + mc, :], wq[:, kc, mc * 128:(mc + 1) * 128], nt[s][:, kc, 128:640],
                            kc == 0, kc == 7, rw + [('wq', kc)], self.pr(bq + mc))
            self.cp('act', qraw[:], self.ps[:, bq:bq + 4, :], self.pr(bq, 4), ['qraw'])
            self.P.op('MARK', None)
            bkk = self.nb(3)
            for (c2, c0, c1, bo, pc0) in kseg:
                for kc in range(8):
                    self.mm(self.ps[:, bkk + bo, pc0:pc0 + (c1 - c0)], wk[:, kc, c2 * 128:(c2 + 1) * 128],
                            nt[s][:, kc, c0:c1], kc == 0, kc == 7, rw + [('wk', kc)], self.pr(bkk + bo))
            psk = self.ps[:, bkk:bkk + 3, :].rearrange('p b t -> p (b t)').rearrange('p (c t) -> p c t', c=2)
            self.cp('dve', kraw[:], psk, self.pr(bkk, 3), ['kraw'])
            self.P.op('MARK', None)
            bv = self.nb(3)
            for kb in range(6):
                for kc in range(8):
                    self.mm(self.ps[:, bv + kb // 2, (kb % 2) * 256:(kb % 2 + 1) * 256], nt[s][:, kc, kb * 128:(kb + 1) * 128],
                            wv[:, kc, :], kc == 0, kc == 7, rw + [('wv', kc)], self.pr(bv + kb // 2))
            self.cp('act', vd[s][:], self.ps[:, bv:bv + 3, :].rearrange('p b (u t) -> p (b u) t', u=2), self.pr(bv, 3), [('vd', s)])
            for g3 in range(3):
                self.P.op('MARK', None)
                bh = self.nb(4)
                for j in range(4):
                    mc = g3 * 4 + j
                    for kc in range(8):
                        self.mm(self.ps[:, bh + j, :], why[:, kc, mc * 128:(mc + 1) * 128], nt[s][:, kc, 128:640],
                                kc == 0, kc == 7, rw + whyr(kc), self.pr(bh + j))
                self.cp(self.alt(), zt[:, g3 * 4:(g3 + 1) * 4, :], self.ps[:, bh:bh + 4, :], self.pr(bh, 4), [('zt', g3)])
            self.dma('sp', self.zhy[:, 1 + t0:1 + t0 + 512].rearrange('(c p) t -> p c t', p=128), zt[:],
                     [('zt', g3) for g3 in range(3)], [('zhy', i)], 'a_zt_st')
            if i == 0 or i == NT - 1:
                self.P.op('MARK', None)
                colh = 127 if i == 0 else 640
                hi = 0 if i == 0 else 1
                bh = self.nb(1)
                for mc in range(12):
                    for kc in range(8):
                        self.mm(self.ps[:, bh, mc:mc + 1], why[:, kc, mc * 128:(mc + 1) * 128], nt[s][:, kc, colh:colh + 1],
                                kc == 0, kc == 7, rw + whyr(kc), self.pr(bh))
                self.cp('dve', zh[:, :, hi], self.ps[:, bh, 0:12], self.pr(bh), [('zh', hi)])
                dcol = 0 if i == 0 else T + 1
                self.dma('sp', self.zhy[:, dcol:dcol + 1].rearrange('(c p) t -> p c t', p=128), zh[:, :, hi:hi + 1],
                         [('zh', hi)], [('zhyh', hi)], 'a_zh%d_st' % hi, slow=True)

        def Y(i):
            s = i % 2
            self.act(sqq[:], qraw[:], AF.Square, ['qraw'], ['sqq'])
            self.act(sqk[:], kraw[:], AF.Square, ['kraw'], ['sqk'])
            self.P.op('MARK', None)
            bs = self.nb(4)
            for mc in range(4):
                self.mm(self.ps[:, bs + mc, :], self.ones[:, 3, :], sqq[:, mc, :], True, True, ['sqq', 'g_ones'], self.pr(bs + mc))
            self.rstd(rq[:], self.ps[:, bs:bs + 4, :], self.pr(bs, 4), ['rq'])
            self.P.op('MARK', None)
            bs2 = self.nb(3)
            for (c2, c0, c1, bo, pc0) in kseg:
                self.mm(self.ps[:, bs2 + bo, pc0:pc0 + (c1 - c0)], self.ones[:, 3, :], sqk[:, c2, c0:c1], True, True,
                        ['sqk', 'g_ones'], self.pr(bs2 + bo))
            psk = self.ps[:, bs2:bs2 + 3, :].rearrange('p b t -> p (b t)').rearrange('p (c t) -> p c t', c=2)
            self.rstd(rk[:], psk, self.pr(bs2, 3), ['rk'])
            self.P.op('MARK', None)
            self.stt('dve', qn[:], qraw[:], gq, rq[:], ALU.mult, ALU.mult, ['qraw', 'rq', 'g_cols'], ['qn'])
            self.stt('dve', kn[:], kraw[:], gk, rk[:], ALU.mult, ALU.mult, ['kraw', 'rk', 'g_cols'], ['kn'])
            self.P.op('MARK', None)
            br = self.nb(4)
            for mc in range(4):
                self.mm(self.ps[:, br + mc, :], self.prot[:], qn[:, mc, :], True, True, ['qn', 'g_prot'], self.pr(br + mc))
            self.tt('pool', qraw[:], qn[:], _bc(ct[s][:, 0, 128:640], [128, 4, 512], 1), ALU.mult, ['qn', ('ct', s)], ['qraw'])
            self.tt('dve', rq[:], self.ps[:, br:br + 4, :], _bc(ct[s][:, 1, 128:640], [128, 4, 512], 1), ALU.mult,
                    self.pr(br, 4) + [('ct', s)], ['rq'])
            self.tt('pool', qr[s][:], qraw[:], rq[:], ALU.add, ['qraw', 'rq'], [('qr', s)])
            self.P.op('MARK', None)
            br2 = self.nb(3)
            for (c2, c0, c1, bo, pc0) in kseg:
                self.mm(self.ps[:, br2 + bo, pc0:pc0 + (c1 - c0)], self.prot[:], kn[:, c2, c0:c1], True, True,
                        ['kn', 'g_prot'], self.pr(br2 + bo))
            psk = self.ps[:, br2:br2 + 3, :].rearrange('p b t -> p (b t)').rearrange('p (c t) -> p c t', c=2)
            self.tt('pool', kraw[:], kn[:], _bc(ct[s][:, 0, :], [128, 2, 768], 1), ALU.mult, ['kn', ('ct', s)], ['kraw'])
            self.tt('dve', rk[:], psk, _bc(ct[s][:, 1, :], [128, 2, 768], 1), ALU.mult, self.pr(br2, 3) + [('ct', s)], ['rk'])
            self.tt('pool', krl[s][0:64], kraw[0:64], rk[0:64], ALU.add, ['kraw', 'rk'], [('krl', s)])
            self.tt('dve', krh[s][64:128], kraw[64:128], rk[64:128], ALU.add, ['kraw', 'rk'], [('krh', s)])

        def Z(i):
            s = i % 2
            t0 = i * 512
            kread = [('krl', s), ('krh', s), ('krl0', s), ('krh0', s), ('qr', s)]
            its = [(qb, kvh) for qb in range(4) for kvh in range(2)]

            def SE(k):
                qb, kvh = its[k]
                es_ = k % 2
                b0 = self.nb(3)
                for kk in range(3):
                    kbw = qb + kk
                    for half in range(2):
                        kt_ = krl[s] if half == 0 else krh[s]
                        self.mm(self.ps[:, b0 + kk, half * 256:(half + 1) * 256],
                                kt_[:, kvh, kbw * 128:(kbw + 1) * 128],
                                qr[s][:, 2 * kvh:2 * kvh + 2, qb * 128:(qb + 1) * 128],
                                True, True, kread, self.pr(b0 + kk))
                self.act(E[es_][:], self.ps[:, b0:b0 + 3, :], AF.Exp, self.pr(b0, 3), [('E', es_)], scale=0.125)
                first = (i == 0 and qb == 0)
                lastb = (i == NT - 1 and qb == 3)
                for (kk, mi) in [(0, 2 if first else 0), (2, 3 if lastb else 1)]:
                    ev = E[es_][:, kk, :].rearrange('p (g q) -> p g q', g=4)
                    self.tt('dve', ev, ev, _bc(self.masks[:, mi, :], [128, 4, 128], 1), ALU.mult,
                            [('E', es_), 'g_masks'], [('E', es_)])

            def PN(k):
                qb, kvh = its[k]
                es_ = k % 2
                bpv = self.nb()
                bdn = self.nb()
                for kk in range(3):
                    self.mm(self.ps[:, bpv, :], vd[s][:, qb + kk, kvh * 128:(kvh + 1) * 128], E[es_][:, kk, :],
                            kk == 0, kk == 2, [('vd', s), ('E', es_)], self.pr(bpv))
                for kk in range(3):
                    self.mm(self.ps[:, bdn, :], self.ones[:, 2, :], E[es_][:, kk, :], kk == 0, kk == 2,
                            [('E', es_), 'g_ones'], self.pr(bdn))
                rv = rec[es_][:].rearrange('p (g q) -> p g q', g=4)
                sk = self.esk[:, l * 8 + kvh * 4:l * 8 + kvh * 4 + 4]
                self.tt('dve', rv, self.ps[:, bdn, :].rearrange('p (g q) -> p g q', g=4), _bc(sk, [128, 4, 128], 2),
                        ALU.add, self.pr(bdn) + ['g_esk'], [('rec', es_)])
                self.act(rec[es_][:], rec[es_][:], AF.Ln, [('rec', es_)], [('rec', es_)])
                self.act(rec[es_][:], rec[es_][:], AF.Exp, [('rec', es_)], [('rec', es_)], scale=-1.0)
                for half in range(2):
                    pp = slice(half * 64, (half + 1) * 64)
                    self.tt('dve', attn[pp, 2 * kvh:2 * kvh + 2, qb * 128:(qb + 1) * 128],
                            self.ps[pp, bpv, half * 256:(half + 1) * 256].rearrange('p (g q) -> p g q', g=2),
                            rec[es_][pp, half * 256:(half + 1) * 256].rearrange('p (g q) -> p g q', g=2), ALU.mult,
                            self.pr(bpv) + [('rec', es_)], [('attn', qb, kvh, half)])

            SE(0)
            for k in range(8):
                if k + 1 < 8:
                    SE(k + 1)
                PN(k)
                self.P.op('MARK', None)
            ar = [('attn', qb, kvh, half) for qb in range(4) for kvh in range(2) for half in range(2)]
            self.act(sqa[:], attn[:], AF.Square, ar, ['sqa'])
            bs = self.nb()
            for mc in range(4):
                self.mm(self.ps[:, bs, :], self.ones[:, 1, :], sqa[:, mc, :], mc == 0, mc == 3, ['sqa', 'g_ones'], self.pr(bs))
            self.rstd(rsa[:], self.ps[:, bs, :], self.pr(bs), ['rsa'])
            self.tt('pool', an[s][:], attn[:], _bc(rsa[:], [128, 4, 512], 1), ALU.mult, ar + ['rsa'], [('an', s)])
            self.dma('sp', self.attn_n[:, t0:t0 + 512].rearrange('(c p) t -> p c t', p=128), an[s][:],
                     [('an', s)], [('attn_n', i)], 'a_an%d_st' % s)

        loads(0)
        loads(1)
        self.replay([r_ for r_ in self.capture(lambda: (X(0), Y(0))) if r_[0][0] != 'MARK'])
        for i in range(NT):
            zs = self.capture(lambda: Z(i))
            if i + 1 < NT:
                xs = self.capture(lambda: X(i + 1))
                ys = self.capture(lambda: Y(i + 1))
                self.replay(self.merge(zs, xs + [(('MARK', None), {})] + ys))
            else:
                self.replay([r_ for r_ in zs if r_[0][0] != 'MARK'])
            if i + 2 < NT:
                loads(i + 2)
        self.end()

    def phase_a2(self, l):
        self.begin()
        zw = [self.tile('zw%d' % s, [128, 12, 514], BF16) for s in range(2)]
        u = self.tile('u', [128, 12, 512], F32)
        xo = [self.tile('xo%d' % s, [128, 4, 512], BF16) for s in range(2)]
        vt = [self.tile('vt%d' % s, [128, 4, 512], BF16) for s in range(2)]
        def loads(i):
            s = i % 2
            self.dma('sp', zw[s][:], self.zhy[:, i * 512:i * 512 + 514].rearrange('(c p) t -> p c t', p=128), [], [('zw', s)], 'a2zw%d' % s)
        loads(0)
        for i in range(NT):
            s = i % 2
            t0 = i * 512
            if i + 1 < NT:
                loads(i + 1)
            for j in range(12):
                wc = lambda k, j=j: self.col('w_short', l, 1, k * 12 + j)
                self.act(u[:, j, :], zw[s][:, j, 1:513], AF.Identity, [('zw', s), 'g_cols'], [('u', j)],
                         scale=wc(1), bias=self.col('b_short', l, 1, j))
                e1 = 'dve'
                self.stt(e1, u[:, j, :], zw[s][:, j, 0:512], wc(0), u[:, j, :], ALU.mult, ALU.add, [('zw', s), ('u', j), 'g_cols'], [('u', j)])
                self.stt(e1, u[:, j, :], zw[s][:, j, 2:514], wc(2), u[:, j, :], ALU.mult, ALU.add, [('zw', s), ('u', j), 'g_cols'], [('u', j)])
            self.cp('act', xo[s][:], u[:, 0:4, :], [('u', j) for j in range(4)], [('xo', s)])
            self.tt('dve', vt[s][:], u[:, 4:8, :], u[:, 8:12, :], ALU.mult, [('u', j) for j in range(4, 12)], [('vt', s)])
            self.dma('sp', self.x0s[:, t0:t0 + 512].rearrange('(c p) t -> p c t', p=128), xo[s][:], [('xo', s)], [('x0s', i)], 'a2xo%d_st' % s)
            for j in range(4):
                for h2 in range(2):
                    gg = 2 * j + h2
                    self.dma('sp', self.vfft[gg][4 * i:4 * i + 4, :].rearrange('a (p b) -> p a b', p=64),
                             vt[s][h2 * 64:(h2 + 1) * 64, j, :].rearrange('p (a b) -> p a b', a=4), [('vt', s)], [('vfft', gg, i)],
                             'a2vt%d_%d_st' % (s, gg))
        if os.environ.get('MK_NOAG') is None:
            for gg in range(NG):
                self.allgather(self.vfft2[gg], self.vall2[gg], [('vfft', gg, i) for i in range(NT)], [('vall', gg)])
        self.end()

    def phase_f1(self, l):
        self.begin()
        zf = self.L('zf')
        mmk = self.L('mm')
        w1t = self.tile('w1t', [33, 64], F32)
        w2d = self.tile('w2d', [64, 128], F32)
        self.dma('sp', w1t[:], self.L('filt_w1')[l], [], ['w1t'], 'f1w1')
        for k in range(2):
            self.dma('sp', w2d[:, k * 64:(k + 1) * 64], self.L('filt_w2')[l], [], [('w2d', k)], 'f1w2%d' % k)
        zt_ = [self.tile('zt_%d' % s, [33, 2048], F32) for s in range(2)]
        mk = [self.tile('mk%d' % s, [128, 2048], BF16) for s in range(2)]
        s1 = self.tile('s1', [64, 2048], F32)
        t1 = self.tile('t1', [64, 2048], F32)
        s2 = self.tile('s2', [128, 2048], F32)
        t2 = self.tile('t2', [128, 2048], F32)
        hd = [self.tile('hd%d' % s, [128, 2048], BF16) for s in range(2)]
        f = lambda k: self.fsc[:, l * 4 + k:l * 4 + k + 1]
        it = 0
        for sl in range(2):
            for cch in range(8):
                s = it % 2
                it += 1
                c0 = cch * 2048
                self.dma('sp', zt_[s][:], zf[sl, :, c0:c0 + 2048], [], [('zt_', s)], 'f1zt%d' % s)
                self.dma('sp', mk[s][:], mmk[sl, :, c0:c0 + 2048], [], [('mk', s)], 'f1mk%d' % s)
                b1 = self.nb(4)
                for q4 in range(4):
                    self.mm(self.ps[0:64, b1 + q4, :], w1t[:], zt_[s][:, q4 * 512:(q4 + 1) * 512], True, True,
                            ['w1t', ('zt_', s)], self.pr(b1 + q4))
                self.act(s1[:], self.ps[0:64, b1:b1 + 4, :], AF.Sin, self.pr(b1, 4) + [('fsc', l, 0), ('fsc', l, 1)], ['s1'],
                         scale=f(0)[0:64], bias=f(1)[0:64])
                self.tt('dve', t1[:], s1[:], s1[:], ALU.mult, ['s1'], ['t1'])
                self.ts('dve', t1[:], t1[:], -4.0, 3.0, ALU.mult, ALU.add, ['t1'], ['t1'])
                self.tt('pool', t1[:], t1[:], s1[:], ALU.mult, ['t1', 's1'], ['t1'])
                b2 = self.nb(4)
                for q4 in range(4):
                    self.mm(self.ps[:, b2 + q4, :], w2d[:], t1[:, q4 * 512:(q4 + 1) * 512], True, True,
                            [('w2d', 0), ('w2d', 1), 't1'], self.pr(b2 + q4))
                self.act(s2[:], self.ps[:, b2:b2 + 4, :], AF.Sin, self.pr(b2, 4) + [('fsc', l, 2), ('fsc', l, 3)], ['s2'],
                         scale=f(2), bias=f(3))
                self.tt('dve', t2[:], s2[:], s2[:], ALU.mult, ['s2'], ['t2'])
                self.ts('dve', t2[:], t2[:], -4.0, 3.0, ALU.mult, ALU.add, ['t2'], ['t2'])
                self.tt('pool', t2[:], t2[:], s2[:], ALU.mult, ['t2', 's2'], ['t2'])
                self.tt('pool', hd[s][:], t2[:], mk[s][:], ALU.mult, ['t2', ('mk', s)], [('hd', s)])
                self.dma('sp', self.hdn[sl, :, c0:c0 + 2048], hd[s][:], [('hd', s)], [('hdn', sl, cch)], 'f1hd%d_st' % s)
        self.end()

    def g_stream(self, Gt, kab, ka0, gi_):
        s = gi_ % 2
        n = min(5, KA - ka0)
        self.dma('sp', Gt[s][:, 0:n], self.L('gall')[:, ka0:ka0 + n], [], [('Gt', s)], 'Gt%d' % s)
        return s

    def phase_f2(self, l):
        self.begin()
        hdn = self.tile('hdn', [128, NF], BF16)
        g = self.tile('g', [128, 128, 256], BF16)
        Ysb = self.tile('Ysb', [128, 2, KA, 256], BF16)
        HfT = [self.tile('HfT%d' % s, [128, 5, 2, 256], BF16) for s in range(2)]
        Gt = [self.tile('Gt%d' % s, [128, 5, 3, 128], BF16) for s in range(2)]
        dc = [self.tile('dc%d' % s, [128, 2, 256], F32) for s in range(2)]
        w3f = self.tile('w3f', [128, 256], F32)
        w3s = self.tile('w3s', [128, 256], BF16)
        f1m = self.tile('f1m', [128, 130], BF16)
        negd = self.tile('negd', [128, DH], F32)
        tdec = self.tile('tdec', [128, 2, 128], F32)
        hbt = self.tile('hbt', [1, 256], F32)
        self.dma('sp', f1m[:], self.c_f1m, [], ['f1m'], 'f2f1m')
        self.dma('sp', negd[:], self.c_negd, [], ['negd'], 'f2negd')
        self.dma('sp', tdec[:], self.c_tdec, [], ['tdec'], 'f2tdec')
        w3 = self.L('filt_w3')
        gi_ = 0
        hi_ = 0
        for sl in range(2):
            self.dma('sp', hdn[:], self.hdn[sl], [], ['hdn'], 'f2hdn')
            for hh in range(2):
                for k in range(2):
                    self.dma('sp', w3f[k * 64:(k + 1) * 64, :], w3[l, :, k * 512 + hh * 256:k * 512 + (hh + 1) * 256], [], [('w3f', k)], 'f2w3%d' % k)
                self.cp('dve', w3s[:], w3f[:], [('w3f', 0), ('w3f', 1)], ['w3s'])
                self.dma('sp', hbt[:], self.hbias_d[0:1, l * DH + hh * 256:l * DH + (hh + 1) * 256], [], ['hbt'], 'f2hbt')
                for b2 in range(64):
                    bk = self.nb()
                    ds_ = b2 % 2
                    for u2 in range(2):
                        b = b2 * 2 + u2
                        self.mm(self.ps[:, bk, u2 * 256:(u2 + 1) * 256], hdn[:].rearrange('p (a b) -> p b a', b=128)[:, b, :],
                                w3s[:], True, True, ['hdn', 'w3s'], self.pr(bk))
                        self.act(dc[ds_][:, u2, :], negd[:, hh * 256:(hh + 1) * 256], AF.Exp, ['negd', 'tdec'], [('dc', ds_, u2)],
                                 scale=tdec[:, sl, b:b + 1])
                    self.tt('dve', g[:, 2 * b2:2 * b2 + 2, :], self.ps[:, bk, :].rearrange('p (u c) -> p u c', u=2), dc[ds_][:],
                            ALU.mult, self.pr(bk) + [('dc', ds_, 0), ('dc', ds_, 1)], [('g', b2)])
                    if b2 == 0:
                        self.stt('dve', g[0:1, 0, :], hbt[:], self.e0[0:1, sl:sl + 1],
                                 g[0:1, 0, :], ALU.mult, ALU.add, [('g', 0), 'hbt', 'g_e0'], [('g', 0)])
                gr = [('g', b2) for b2 in range(64)]
                c = 0
                while c < 256:
                    n = min(3, 256 - c)
                    bk = self.nb()
                    for u3 in range(n):
                        self.mm(self.ps[:, bk, u3 * 130:(u3 + 1) * 130], g[:, :, c + u3], f1m[:], True, True, gr + ['f1m'], self.pr(bk))
                    self.cp(self.alt(), Ysb[:, :, :, c:c + n], self.ps[:, bk, 0:n * 130].rearrange('p (c r k) -> p r k c', c=n, r=2),
                            self.pr(bk), [('Ysb', c)])
                    c += n
                yr = [('Ysb', c) for c in range(0, 256, 3)]
                for ka0 in range(0, KA, 5):
                    gs = self.g_stream(Gt, None, ka0, gi_)
                    gi_ += 1
                    hs = hi_ % 2
                    hi_ += 1
                    nk = min(5, KA - ka0)
                    for kq in range(nk):
                        ka = ka0 + kq
                        bk = self.nb()
                        zr = self.ps[:, bk, 0:256]
                        zi = self.ps[:, bk, 256:512]
                        rr = yr + [('Gt', gs)]
                        self.mm(zr, Gt[gs][:, kq, 0, :], Ysb[:, 0, ka, :], True, False, rr, self.pr(bk))
                        self.mm(zr, Gt[gs][:, kq, 2, :], Ysb[:, 1, ka, :], False, True, rr, self.pr(bk))
                        self.mm(zi, Gt[gs][:, kq, 0, :], Ysb[:, 1, ka, :], True, False, rr, self.pr(bk))
                        self.mm(zi, Gt[gs][:, kq, 1, :], Ysb[:, 0, ka, :], False, True, rr, self.pr(bk))
                        self.cp(self.alt(), HfT[hs][:, kq, :, :], self.ps[:, bk, :].rearrange('p (r c) -> p r c', r=2), self.pr(bk), [('HfT', hs, kq)])
                    for g4 in range(4):
                        gg = hh * 4 + g4
                        dst = self.hfs[gg].rearrange('p (k r s c) -> p k r s c', k=KA, r=2, s=2)[:, ka0:ka0 + nk, :, sl, :]
                        self.dma('sp', dst, HfT[hs][:, 0:nk, :, g4 * 64:(g4 + 1) * 64], [('HfT', hs, kq) for kq in range(nk)],
                                 [('hfs', gg, sl, ka0)], 'f2hf%d_%d_st' % (hs, g4))
        self.end()

    def phase_b(self, l):
        self.begin()
        xa = [self.tile('xa%d' % s, [64, CG, 128], BF16) for s in range(2)]
        hft = self.tile('hft', [128, KA, 2, 2, CG], BF16)
        Ysb = self.tile('Ysb', [128, 2, KA, 2, CG], BF16)
        Wt = self.tile('Wt', [128, 2, CG, KA], BF16)
        U = self.tile('U', [KA, 2, 128, CG], BF16)
        Yo = self.tile('Yo', [64, CG, 128], BF16)
        A = self.tile('A', [128, 4, 2, 2 * CG], F32)
        B1 = self.tile('B1', [128, 4, 2 * CG], F32)
        B2 = self.tile('B2', [128, 4, 2 * CG], F32)
        Dr = self.tile('Dr', [128, 4, 2 * CG], F32)
        Di = self.tile('Di', [128, 4, 2 * CG], F32)
        Gt = [self.tile('Gt%d' % s, [128, 5, 3, 128], BF16) for s in range(2)]
        Ht = [self.tile('Ht%d' % s, [KA, 16, 2, 64], BF16) for s in range(2)]
        f1m = self.tile('f1m', [128, 130], BF16)
        e12 = self.tile('e12', [128, 2, 256], BF16)
        self.dma('sp', f1m[:], self.c_f1m, [], ['f1m'], 'bf1m')
        self.dma('sp', e12[:], self.c_e12, [], ['e12'], 'be12')
        hall = self.L('hall')
        gi_ = 0
        hi_ = 0
        for gg in range(NG):
            c0 = gg * CG
            for sl in range(2):
                self.dma('sp', xa[sl][:], self.vall[gg][sl * 64:(sl + 1) * 64, :].rearrange('a (c b) -> a c b', c=CG),
                         [], [('xa', sl)], 'bxa%d' % sl)
            self.dma('sp', hft[:], self.hfs[gg].rearrange('p (k r s c) -> p k r s c', k=KA, r=2, s=2), [], ['hft'], 'bhft')
            for sl in range(2):
                c = 0
                while c < CG:
                    n = min(3, CG - c)
                    bk = self.nb()
                    for u3 in range(n):
                        self.mm(self.ps[:, bk, u3 * 130:(u3 + 1) * 130], xa[sl][:, c + u3, :], f1m[0:64, :], True, True,
                                [('xa', sl), 'f1m'], self.pr(bk))
                    self.cp(self.alt(), Ysb[:, :, :, sl, c:c + n], self.ps[:, bk, 0:n * 130].rearrange('p (c r k) -> p r k c', c=n, r=2),
                            self.pr(bk), [('Ysb', sl, c)])
                    c += n
            yr = [('Ysb', sl, c) for sl in range(2) for c in range(0, CG, 3)]
            for ka0 in range(0, KA, 4):
                nk = min(4, KA - ka0)
                bz = self.nb(2)
                for kq in range(nk):
                    ka = ka0 + kq
                    if ka % 5 == 0:
                        gs = self.g_stream(Gt, None, ka, gi_)
                        gi_ += 1
                    gq_ = ka % 5
                    zr = self.ps[:, bz + kq // 2, (kq % 2) * 256:(kq % 2) * 256 + 128]
                    zi = self.ps[:, bz + kq // 2, (kq % 2) * 256 + 128:(kq % 2) * 256 + 256]
                    rr = yr + [('Gt', gs)]
                    yre = Ysb[:, 0, ka, :, :].rearrange('p s c -> p (s c)')
                    yim = Ysb[:, 1, ka, :, :].rearrange('p s c -> p (s c)')
                    w_ = self.pr(bz + kq // 2)
                    self.mm(zr, Gt[gs][:, gq_, 0, :], yre, True, False, rr, w_)
                    self.mm(zr, Gt[gs][:, gq_, 2, :], yim, False, True, rr, w_)
                    self.mm(zi, Gt[gs][:, gq_, 0, :], yim, True, False, rr, w_)
                    self.mm(zi, Gt[gs][:, gq_, 1, :], yre, False, True, rr, w_)
                zps = self.ps[:, bz:bz + 2, :].rearrange('p b (k r n) -> p (b k) r n', k=2, r=2)[:, 0:nk]
                hf_ = hft[:, ka0:ka0 + nk].rearrange('p k r s c -> p k r (s c)')
                pz = self.pr(bz, 2)
                self.tt('dve', A[:, 0:nk], zps, hf_, ALU.mult, pz + ['hft'], ['A'])
                self.tt('dve', B1[:, 0:nk], zps[:, :, 0, :], hf_[:, :, 1, :], ALU.mult, pz + ['hft'], ['B1'])
                self.tt('dve', B2[:, 0:nk], zps[:, :, 1, :], hf_[:, :, 0, :], ALU.mult, pz + ['hft'], ['B2'])
                self.tt('pool', Dr[:, 0:nk], A[:, 0:nk, 0, :], A[:, 0:nk, 1, :], ALU.subtract, ['A'], ['Dr'])
                self.tt('pool', Di[:, 0:nk], B1[:, 0:nk], B2[:, 0:nk], ALU.add, ['B1', 'B2'], ['Di'])
                for ri, Dx in ((0, Dr), (1, Di)):
                    self.tt('pool', Wt[:, ri, :, ka0:ka0 + nk], Dx[:, 0:nk, 0:CG].rearrange('p k c -> p c k'),
                            Dx[:, 0:nk, CG:2 * CG].rearrange('p k c -> p c k'), ALU.add, ['Dr' if ri == 0 else 'Di'], [('Wt', ka0, ri)])
            wr = [('Wt', ka0, ri) for ka0 in range(0, KA, 4) for ri in range(2)]
            for c in range(0, CG, 2):
                bk = self.nb()
                for u2 in range(2):
                    o_ = self.ps[0:KA, bk, u2 * 256:(u2 + 1) * 256]
                    self.mm(o_, Wt[:, 0, c + u2, :], e12[:, 0, :], True, False, wr + ['e12'], self.pr(bk))
                    self.mm(o_, Wt[:, 1, c + u2, :], e12[:, 1, :], False, True, wr + ['e12'], self.pr(bk))
                self.cp(self.alt(), U[:, :, :, c:c + 2], self.ps[0:KA, bk, :].rearrange('p (c r b) -> p r b c', c=2, r=2),
                        self.pr(bk), [('U', c)])
            ur = [('U', c) for c in range(0, CG, 2)]
            for b0 in range(0, 128, 8):
                if b0 % 16 == 0:
                    hs = hi_ % 2
                    hi_ += 1
                    self.dma('sp', Ht[hs][:], hall[:, b0:b0 + 16], [], [('Ht', hs)], 'bHt%d' % hs)
                bk = self.nb()
                for q8 in range(8):
                    bp = b0 + q8
                    o_ = self.ps[0:64, bk, q8 * 64:(q8 + 1) * 64]
                    self.mm(o_, Ht[hs][:, bp % 16, 0, :], U[:, 0, bp, :], True, False, ur + [('Ht', hs)], self.pr(bk))
                    self.mm(o_, Ht[hs][:, bp % 16, 1, :], U[:, 1, bp, :], False, True, ur + [('Ht', hs)], self.pr(bk))
                self.cp(self.alt(), Yo[:, :, b0:b0 + 8], self.ps[0:64, bk, :].rearrange('p (b c) -> p c b', b=8), self.pr(bk), [('Yo', b0)])
            self.dma('sp', self.yconv[c0:c0 + CG, :].rearrange('c (a b) -> a c b', a=64), Yo[:],
                     [('Yo', b0) for b0 in range(0, 128, 8)], [('yconv', gg)], 'bYo_st')
        self.end()

    def phase_c1a(self, l):
        self.begin()
        w_out = self.L('w_out')
        wo = self.tile('wo', [128, 8, D], BF16)
        stg = [self.tile('c1a_stg%d' % s, [128, D], F32) for s in range(4)]
        for kc in range(8):
            sc = self.col('g_ao', l, 1, kc) if kc < 4 else self.col('g_ho', l, 1, kc - 4)
            self.load_w(w_out[l, kc * 128:(kc + 1) * 128, :], D, sc, stg, 'c1astg', [(wo[:, kc, :], 0, D, None, ('wo', kc))])
        mix = [self.tile('mix%d' % s, [128, 8, 512], BF16) for s in range(2)]
        xy = [self.tile('xy%d' % s, [128, 2, 4, 512], BF16) for s in range(2)]
        ht = [self.tile('ht%d' % s, [128, 8, 512], F32) for s in range(3)]
        hy = self.tile('hy', [128, 4, 512], F32)
        sqh = self.tile('sqh', [128, 4, 512], BF16)
        rsh = self.tile('rsh', [128, 512], F32)
        sq2 = self.tile('sq2', [128, 8, 512], BF16)
        rs2 = self.tile('rs2', [128, 512], F32)
        n2 = [self.tile('n2_%d' % s, [128, 8, 512], BF16) for s in range(2)]
        ed = self.tile('ed', [128, 8, 2], BF16)
        def loads(i):
            s = i % 2
            cs = slice(i * 512, i * 512 + 512)
            self.dma('sp', mix[s][:, 0:4, :], self.attn_n[:, cs].rearrange('(c p) t -> p c t', p=128), [], [('mixa', s)], 'c1a_ma%d' % s)
            self.dma('sp', xy[s][:, 0], self.x0s[:, cs].rearrange('(c p) t -> p c t', p=128), [], [('xy', s, 0)], 'c1a_x%d' % s)
            self.dma('sp', xy[s][:, 1], self.yconv[:, cs].rearrange('(c p) t -> p c t', p=128), [], [('xy', s, 1)], 'c1a_y%d' % s)
            self.dma('sp', ht[i % 3][:], self.hres[:, cs].rearrange('(c p) t -> p c t', p=128), [], [('ht', i % 3)], 'c1a_h%d' % (i % 3))
        def P1(i):
            s = i % 2
            self.tt('dve', hy[:], xy[s][:, 0], xy[s][:, 1], ALU.mult, [('xy', s, 0), ('xy', s, 1)], ['hy'])
            self.act(sqh[:], hy[:], AF.Square, ['hy'], ['sqh'])
            bk = self.nb()
            for mc in range(4):
                self.mm(self.ps[:, bk, :], self.ones[:, 1, :], sqh[:, mc, :], mc == 0, mc == 3, ['sqh', 'g_ones'], self.pr(bk))
            self.rstd(rsh[:], self.ps[:, bk, :], self.pr(bk), ['rsh'])
            self.tt('pool', mix[s][:, 4:8, :], hy[:], _bc(rsh[:], [128, 4, 512], 1), ALU.mult, ['hy', 'rsh'], [('mixh', s)])

        def P2(i):
            s = i % 2
            cs = slice(i * 512, i * 512 + 512)
            for g2 in range(2):
                bo = self.nb(4)
                for j in range(4):
                    mc = g2 * 4 + j
                    for kc in range(8):
                        self.mm(self.ps[:, bo + j, :], wo[:, kc, mc * 128:(mc + 1) * 128], mix[s][:, kc, :], kc == 0, kc == 7,
                                [('mixa', s), ('mixh', s), ('wo', kc)], self.pr(bo + j))
                h3 = i % 3
                self.tt('dve', ht[h3][:, g2 * 4:(g2 + 1) * 4, :], ht[h3][:, g2 * 4:(g2 + 1) * 4, :], self.ps[:, bo:bo + 4, :], ALU.add,
                        [('ht', h3)] + self.pr(bo, 4), [('ht', h3)])
            self.dma('sp', self.hres[:, cs].rearrange('(c p) t -> p c t', p=128), ht[i % 3][:], [('ht', i % 3)], [('hres', i)], 'c1a_h%d_st' % (i % 3))

        def P3(i):
            s = i % 2
            t0 = i * 512
            h3 = i % 3
            self.act(sq2[:], ht[h3][:], AF.Square, [('ht', h3)], ['sq2'])
            bk = self.nb()
            for kc in range(8):
                self.mm(self.ps[:, bk, :], self.ones[:, 0, :], sq2[:, kc, :], kc == 0, kc == 7, ['sq2', 'g_ones'], self.pr(bk))
            self.rstd(rs2[:], self.ps[:, bk, :], self.pr(bk), ['rs2'])
            for hf in range(2):
                eng = 'dve' if hf == 0 else 'pool'
                self.tt(eng, n2[s][:, hf * 4:(hf + 1) * 4, :], ht[h3][:, hf * 4:(hf + 1) * 4, :], _bc(rs2[:], [128, 4, 512], 1), ALU.mult,
                        [('ht', h3), 'rs2'], [('n2', s, hf)])
            rd = [('n2', s, 0), ('n2', s, 1)]
            self.dma('sp', self.n2s[:, 1 + t0:1 + t0 + 512].rearrange('(c p) t -> p c t', p=128), n2[s][:], rd, [('n2s', i)], 'c1a_n%d_st' % s)
            if i == 0:
                self.dma('sp', self.xn_in[0:1, :].rearrange('o (c p) -> p c o', p=128), n2[s][:, :, 0:1], rd, ['xn0'], 'c1a_e0', slow=True)
            if i == NT - 1:
                self.dma('sp', self.xn_in[1:2, :].rearrange('o (c p) -> p c o', p=128), n2[s][:, :, 511:512], rd, ['xn1'], 'c1a_e1', slow=True)

        loads(0)
        loads(1)
        P1(0)
        for i in range(NT):
            if i + 1 < NT:
                P1(i + 1)
            P2(i)
            if i >= 1:
                P3(i - 1)
            if i + 2 < NT:
                loads(i + 2)
        P3(NT - 1)
        self.allgather(self.xn_in, self.xn_out, ['xn0', 'xn1'], ['xn_out'])
        for k, (row, col) in enumerate([(1, 0), (2, T + 1)]):
            self.dma('sp', ed[:, :, k:k + 1], self.xn_out[row:row + 1, :].rearrange('o (c p) -> p c o', p=128), ['xn_out'], [('ed', k)], 'c1a_ed%d' % k, slow=True)
            self.ts('dve', ed[:, :, k:k + 1], ed[:, :, k:k + 1], self.edge[:, k:k + 1], None, ALU.mult, None, [('ed', k), 'g_edge'], [('ed', k)])
            self.dma('sp', self.n2s[:, col:col + 1].rearrange('(c p) t -> p c t', p=128), ed[:, :, k:k + 1], [('ed', k)], [('n2sh', k)], 'c1a_ed%d_st' % k, slow=True)
        self.end()

    def phase_c1b(self, l):
        self.begin()
        w_up = self.L('w_up')
        wu = self.tile('wu', [128, 8, 2 * DFF], BF16)
        stg = [self.tile('c1b_stg%d' % s, [128, 2816], F32) for s in range(3)]
        for kc in range(8):
            for hh in range(2):
                self.load_w(w_up[l, kc * 128:(kc + 1) * 128, hh * DFF:(hh + 1) * DFF], DFF, self.col('g_ffn', l, 1, kc), stg, 'c1bstg',
                            [(wu[:, kc, hh * DFF:(hh + 1) * DFF], 0, DFF, None, ('wu', kc, hh))])
        n2t = [self.tile('n2t%d' % s, [128, 8, 512], BF16) for s in range(2)]
        at = [self.tile('at%d' % s, [128, 22, 510], BF16) for s in range(2)]
        ntl = (T + 509) // 510
        NQ = 3
        ua = [self.tile('ua%d' % q, [128, 510], F32) for q in range(NQ)]
        ug = [self.tile('ug%d' % q, [128, 510], F32) for q in range(NQ)]
        sg = [self.tile('sg%d' % q, [128, 510], F32) for q in range(NQ)]

        def loads(i):
            s = i % 2
            T0 = 510 * i
            nin = min(510, T - T0) + 2
            self.dma('sp', n2t[s][:, :, 0:nin], self.n2s[:, T0:T0 + nin].rearrange('(c p) t -> p c t', p=128), [], [('n2t', s)], 'c1b_n%d' % s)

        def front(i, j, q):
            s = i % 2
            nout = min(510, T - 510 * i)
            nin = nout + 2
            bk = self.nb(2)
            for hh in range(2):
                for kc in range(8):
                    self.mm(self.ps[:, bk + hh, 0:nin], wu[:, kc, hh * DFF + j * 128:hh * DFF + (j + 1) * 128], n2t[s][:, kc, 0:nin],
                            kc == 0, kc == 7, [('n2t', s), ('wu', kc, hh)], self.pr(bk + hh))
            for hh, ut in ((0, ua[q]), (1, ug[q])):
                wc = lambda k, hh=hh, j=j: self.col('w_ffc', l, 1, k * 44 + hh * 22 + j)
                pb = self.ps[:, bk + hh, :]
                rn = ('u', hh, q)
                self.act(ut[:, 0:nout], pb[:, 1:1 + nout], AF.Identity, self.pr(bk + hh) + ['g_cols'], [rn],
                         scale=wc(1), bias=self.col('b_ffc', l, 1, hh * 22 + j))
                self.stt('dve', ut[:, 0:nout], pb[:, 0:nout], wc(0), ut[:, 0:nout], ALU.mult, ALU.add, self.pr(bk + hh) + [rn, 'g_cols'], [rn])
                self.stt('dve', ut[:, 0:nout], pb[:, 2:2 + nout], wc(2), ut[:, 0:nout], ALU.mult, ALU.add, self.pr(bk + hh) + [rn, 'g_cols'], [rn])

        def back(i, j, q):
            s = i % 2
            nout = min(510, T - 510 * i)
            self.act(sg[q][:, 0:nout], ug[q][:, 0:nout], AF.Silu, [('u', 1, q)], [('sg', q)])
            self.tt('pool', at[s][:, j, 0:nout], sg[q][:, 0:nout], ua[q][:, 0:nout], ALU.mult, [('sg', q), ('u', 0, q)], [('at', s, j)])
            if j == 21:
                T0 = 510 * i
                self.dma('sp', self.acts[:, T0:T0 + nout].rearrange('(c p) t -> p c t', p=128), at[s][:, :, 0:nout],
                         [('at', s, jx) for jx in range(22)], [('acts', i)], 'c1b_a%d_st' % s)

        loads(0)
        seq = [(i, j) for i in range(ntl) for j in range(22)]
        for n_, (i, j) in enumerate(seq):
            if j == 0 and i + 1 < ntl:
                loads(i + 1)
            front(i, j, n_ % NQ)
            if n_ >= 1:
                pi, pj = seq[n_ - 1]
                back(pi, pj, (n_ - 1) % NQ)
        pi, pj = seq[-1]
        back(pi, pj, (len(seq) - 1) % NQ)
        self.end()

    def phase_c2(self, l, last):
        self.begin()
        wd = self.tile('wd', [128, 22, D], BF16)
        wg = self.tile('wg', [128, 8, D], BF16)
        wp = self.tile('wp', [128, 2, D], BF16)
        stg = [self.tile('c2_stg%d' % s, [128, D], F32) for s in range(4)]
        for kc in range(22):
            self.load_w(self.L('w_down')[l, kc * 128:(kc + 1) * 128, :], D, None, stg, 'c2stg', [(wd[:, kc, :], 0, D, None, ('wd', kc))])
        for kc in range(8):
            self.load_w(self.L('w_ple_gate')[l, kc * 128:(kc + 1) * 128, :], D, None, stg, 'c2stg', [(wg[:, kc, :], 0, D, None, ('wg', kc))])
        for kc in range(2):
            self.load_w(self.L('w_ple_proj')[l, kc * 128:(kc + 1) * 128, :], D, None, stg, 'c2stg', [(wp[:, kc, :], 0, D, None, ('wp', kc))])
        at = [self.tile('at%d' % s, [128, 22, 512], BF16) for s in range(2)]
        ht = [self.tile('ht%d' % s, [128, 8, 512], F32) for s in range(2)]
        pt = [self.tile('pt%d' % s, [128, 4, DPLE], F32) for s in range(2)]
        pT = self.tile('pT', [128, 2, 512], BF16)
        hb = self.tile('hb', [128, 8, 512], BF16)
        sgm = self.tile('sgm', [128, 4, 512], F32)
        yo = self.tile('yo', [128, 4, D], F32) if last else None
        p_d = self.L('p')
        def loads(i):
            s = i % 2
            cs = slice(i * 512, i * 512 + 512)
            self.dma('sp', at[s][:], self.acts[:, cs].rearrange('(c p) t -> p c t', p=128), [], [('at', s)], 'c2_a%d' % s)
            self.dma('sp', ht[s][:], self.hres[:, cs].rearrange('(c p) t -> p c t', p=128), [], [('ht', s, 0), ('ht', s, 1)], 'c2_h%d' % s)
            self.dma('sp', pt[s][:], p_d[l, cs, :].rearrange('(b p) f -> p b f', p=128), [], [('pt', s)], 'c2_p%d' % s)
        loads(0)
        for i in range(NT):
            s = i % 2
            t0 = i * 512
            cs = slice(t0, t0 + 512)
            if i + 1 < NT:
                loads(i + 1)
            for pc in range(2):
                bk = self.nb()
                for blk in range(4):
                    self.tp(self.ps[:, bk, blk * 128:(blk + 1) * 128], pt[s][:, blk, pc * 128:(pc + 1) * 128], self.ident[:],
                            [('pt', s), 'g_ident'], self.pr(bk))
                self.cp(self.alt(), pT[:, pc, :], self.ps[:, bk, :], self.pr(bk), [('pT', pc)])
            for g2 in range(2):
                bo = self.nb(4)
                for j in range(4):
                    mc = g2 * 4 + j
                    for kc in range(22):
                        self.mm(self.ps[:, bo + j, :], wd[:, kc, mc * 128:(mc + 1) * 128], at[s][:, kc, :], kc == 0, kc == 21,
                                [('at', s), ('wd', kc)], self.pr(bo + j))
                hs_ = ht[s][:, g2 * 4:(g2 + 1) * 4, :]
                self.tt('dve', hs_, hs_, self.ps[:, bo:bo + 4, :], ALU.add, [('ht', s, g2)] + self.pr(bo, 4), [('ht', s, g2)])
                self.cp('act', hb[:, g2 * 4:(g2 + 1) * 4, :], hs_, [('ht', s, g2)], [('hb', g2)])
            for g2 in range(2):
                bo = self.nb(4)
                for j in range(4):
                    mc = g2 * 4 + j
                    for kc in range(8):
                        self.mm(self.ps[:, bo + j, :], wg[:, kc, mc * 128:(mc + 1) * 128], hb[:, kc, :], kc == 0, kc == 7,
                                [('hb', 0), ('hb', 1), ('wg', kc)], self.pr(bo + j))
                self.act(sgm[:], self.ps[:, bo:bo + 4, :], AF.Sigmoid, self.pr(bo, 4), ['sgm'])
                bp = self.nb(4)
                for j in range(4):
                    mc = g2 * 4 + j
                    for kc in range(2):
                        self.mm(self.ps[:, bp + j, :], wp[:, kc, mc * 128:(mc + 1) * 128], pT[:, kc, :], kc == 0, kc == 1,
                                [('pT', 0), ('pT', 1), ('wp', kc)], self.pr(bp + j))
                self.tt('dve', sgm[:], sgm[:], self.ps[:, bp:bp + 4, :], ALU.mult, ['sgm'] + self.pr(bp, 4), ['sgm'])
                hs_ = ht[s][:, g2 * 4:(g2 + 1) * 4, :]
                self.tt('pool', hs_, hs_, sgm[:], ALU.add, [('ht', s, g2), 'sgm'], [('ht', s, g2)])
            hr_ = [('ht', s, 0), ('ht', s, 1)]
            if not last:
                self.dma('sp', self.hres[:, cs].rearrange('(c p) t -> p c t', p=128), ht[s][:], hr_, [('hres', i)], 'c2_h%d_st' % s)
            else:
                for blk in range(4):
                    for g2 in range(2):
                        bk = self.nb()
                        for j in range(4):
                            mc = g2 * 4 + j
                            self.tp(self.ps[:, bk, j * 128:(j + 1) * 128], ht[s][:, mc, blk * 128:(blk + 1) * 128], self.ident[:],
                                    hr_ + ['g_ident'], self.pr(bk))
                        self.cp(self.alt(), yo[:, blk, g2 * 512:(g2 + 1) * 512], self.ps[:, bk, :], self.pr(bk), [('yo', blk, g2)])
                self.dma('sp', self.y[cs, :].rearrange('(b p) f -> p b f', p=128), yo[:],
                         [('yo', blk, g2) for blk in range(4) for g2 in range(2)], [('y', i)], 'c2_y_st')
        self.end()

    def build(self):
        self.phase_p0()
        for l in range(self.depth):
            last = (l == self.depth - 1)
            for name in ['a0', 'a', 'a2', 'f1', 'f2', 'b', 'c1a', 'c1b', 'c2']:
                fn = getattr(self, 'phase_' + name, None)
                if fn is None:
                    return
                if name == 'c2':
                    fn(l, last)
                else:
                    fn(l)
                if self.stop == (name, l):
                    return


def _cols_table(b, W):
    L = DEPTH
    tab = np.zeros((128, b.ncol), np.float32)

    def put(name, arr):
        o, n = b.colspec[name]
        assert arr.shape == (128, n), (name, arr.shape, n)
        tab[:, o:o + n] = arr

    def chunks(v, nch):
        return v.reshape(L, nch, 128).transpose(2, 0, 1).reshape(128, L * nch)

    put('g_mix', chunks(W['rms_mix'], 8))
    put('g_ffn', chunks(W['rms_ffn'], 8))
    put('gq', np.tile(W['q_norm'].T, (2, 1)))
    put('gk', np.tile(W['k_norm'].T, (2, 1)))
    sk = W['sink'].reshape(L, 2, 4)[:, :, [0, 2, 1, 3]].reshape(1, L * 8)
    put('sink', np.tile(sk, (128, 1)))
    put('w_short', W['w_short'].reshape(L, 3, 12, 128).transpose(3, 0, 1, 2).reshape(128, L * 36))
    put('b_short', chunks(W['b_short'], 12))
    put('g_ao', chunks(W['norm_attn_out'], 4))
    put('g_ho', chunks(W['norm_hyena_out'], 4))
    put('w_ffc', W['w_ffconv'].reshape(L, 3, 44, 128).transpose(3, 0, 1, 2).reshape(128, L * 132))
    put('b_ffc', chunks(W['b_ffconv'], 44))
    for nm, key in [('fb1', 'filt_b1'), ('ffr1', 'filt_freq1'), ('fb2', 'filt_b2'), ('ffr2', 'filt_freq2')]:
        put(nm, np.tile(W[key].T, (2, 1)))
    return tab


_BUILD_CACHE = {}


def _get_builder(depth, debug, stop):
    key = (depth, debug, stop)
    if key not in _BUILD_CACHE:
        b = Builder(depth, debug, stop)
        b.declare()
        b.setup_globals()
        b.build()
        es = contextlib.ExitStack()
        b.P.emit(es)
        b._es = es
        _BUILD_CACHE[key] = b
    return _BUILD_CACHE[key]


def _run(inputs, depth=DEPTH, debug=False, stop=None):
    W = {k: np.asarray(v, dtype=np.float32) for k, v in inputs.items()}
    b = _get_builder(depth, debug, stop)
    sh = _shared_consts()
    cols = _cols_table(b, W)
    xp = W['x_prompt'][0]
    xs = W['x_sample']
    pp = W['p_prompt'][:, 0]
    psm = W['p_sample']
    in_maps = []
    for rank in range(N_CORES):
        kind, idx = _unit_of_rank(rank)
        rc = _rank_consts(rank)
        if kind == 'p':
            x = xp[idx * T:(idx + 1) * T]
            p = pp[:, idx * T:(idx + 1) * T]
        else:
            x = xs[idx]
            p = psm[:, idx]
        m = {
            'x': np.ascontiguousarray(x), 'p': np.ascontiguousarray(p),
            'w_in': W['w_in'], 'w_out': W['w_out'], 'w_up': W['w_up'], 'w_down': W['w_down'],
            'w_ple_gate': W['w_ple_gate'], 'w_ple_proj': W['w_ple_proj'],
            'filt_w1': W['filt_w1'], 'filt_w2': W['filt_w2'], 'filt_w3': W['filt_w3'],
            'cols': cols, 'hbias': W['hyena_bias'].reshape(1, -1),
            'ident_f': sh['ident_f'], 'ones_b': sh['ones_b'], 'prot_b': sh['prot_b'],
            'masks': rc['masks'], 'edge': rc['edge'], 'cstab': rc['cstab'],
            'f1m': sh['f1m'], 'gall': sh['gall'], 'e12': sh['e12'], 'hall': sh['hall'], 'negd': sh['negd'],
            'zf': rc['zf'], 'mm': rc['mm'], 'tdec': rc['tdec'], 'e0': rc['e0'],
        }
        in_maps.append({k: v for k, v in m.items() if k in b.inputs})
    res = run_bass_kernel_spmd(b.nc, in_maps, core_ids=list(range(N_CORES)))
    return b, res


def kernel(**inputs):
    b, res = _run(inputs)
    ys = [np.asarray(res.results[r]['y'], dtype=np.float32) for r in range(6)]
    y_prompt = np.concatenate([ys[0], ys[1]], axis=0)[None]
    y_sample = np.stack(ys[2:6], axis=0)
    return (y_prompt, y_sample)
```

```python
import os
import math
import contextlib
import numpy as np
import ml_dtypes
import concourse.bass as bass
import concourse.mybir as mybir
from concourse.bass_utils import run_bass_kernel_spmd

F32 = mybir.dt.float32
BF16 = mybir.dt.bfloat16
AF = mybir.ActivationFunctionType
ALU = mybir.AluOpType
BF = ml_dtypes.bfloat16

D = 1024
DEPTH = 4
T = 8192
NT = 16
HALO = 128
NW = T + 2 * HALO
DQ = 512
DH = 512
DFF = 2816
DPLE = 256
EPS = 1e-6
NF = 16384
KA = 65
CG = 64
NG = DH // CG
ROPE_THETA = 500000.0
N_CORES = 8


class Prog:
    def __init__(self, nc):
        self.nc = nc
        self.ops = []
        self.lastw = {}
        self.readers = {}
        self.lastdma = {}
        self.eng = {'pe': nc.tensor, 'act': nc.scalar, 'dve': nc.vector, 'pool': nc.gpsimd, 'sp': nc.sync}
        self.last_on = {}
        self.n_cc = 0

    def op(self, eng, fn, r=(), w=(), dma=None, cc=False):
        idx = len(self.ops)
        deps = set()
        for x in r:
            p = self.lastw.get(x)
            if p is not None:
                deps.add(p)
        for x in w:
            p = self.lastw.get(x)
            if p is not None:
                deps.add(p)
            deps.update(self.readers.get(x, ()))
        if dma is not None:
            p = self.lastdma.get(dma)
            if p is not None:
                deps.add(p)
            self.lastdma[dma] = idx
        for x in r:
            self.readers.setdefault(x, []).append(idx)
        for x in w:
            self.lastw[x] = idx
            self.readers[x] = []
        deps.discard(idx)
        self.ops.append(dict(eng=eng, fn=fn, deps=deps, dma=dma, cc=cc, bar=False))
        if not cc:
            self.last_on[eng] = idx
        else:
            self.sticky = getattr(self, 'sticky', {})
            for x in w:
                self.sticky[x] = idx
        return idx

    def barrier(self):
        deps = set(self.last_on.values()) | set(self.lastdma.values())
        for k, e in enumerate(('pe', 'act', 'dve', 'pool', 'sp')):
            self.ops.append(dict(eng=e, fn=None, deps=set(deps), dma=None, cc=False, bar=True, reset=(k == 0)))
        self.lastw = dict(getattr(self, 'sticky', {}))
        self.readers = {}

    def emit(self, es):
        nc = self.nc
        ops = self.ops
        def pe2pe(p, o):
            return (p['dma'] is None and not p['cc'] and p['eng'] == 'pe' and o['eng'] == 'pe'
                    and o['dma'] is None and o['fn'] is not None)
        sig = [False] * len(ops)
        for o in ops:
            latest = {}
            for d in o['deps']:
                p = ops[d]
                if pe2pe(p, o):
                    continue
                if p['dma'] is not None or p['cc']:
                    sig[d] = True
                else:
                    if latest.get(p['eng'], -1) < d:
                        latest[p['eng']] = d
            for d in latest.values():
                sig[d] = True
            o['bind'] = set(latest.values())
        sems = {}

        def getsem(name):
            if name not in sems:
                sems[name] = es.enter_context(nc.semaphore(name))
            return sems[name]

        cnt = {}
        val = [None] * len(ops)
        chan = [None] * len(ops)
        waited = {e: {} for e in self.eng}
        n_wait = 0
        keyslot = {}
        ncc = 0
        for i, o in enumerate(ops):
            e = o['eng']
            eobj = self.eng[e]
            if o.get('reset'):
                keyslot = {}
            need = {}
            for d in o['deps']:
                if val[d] is None:
                    continue
                p = ops[d]
                if p['dma'] is None and not p['cc'] and d not in o['bind']:
                    continue
                c = chan[d]
                if need.get(c, 0) < val[d]:
                    need[c] = val[d]
            for c, v in need.items():
                if waited[e].get(c, 0) >= v:
                    continue
                eobj.wait_ge(getsem(c), v)
                waited[e][c] = v
                n_wait += 1
            if o['fn'] is None:
                continue
            ins = o['fn'](eobj)
            if o['cc']:
                c = 'cc%d' % (ncc % 8)
                ncc += 1
                cnt[c] = cnt.get(c, 0) + 1
                ins.then_inc(getsem(c), 1)
                chan[i] = c
                val[i] = cnt[c]
            elif o['dma'] is not None:
                if o['dma'] not in keyslot:
                    keyslot[o['dma']] = len(keyslot)
                c = 'dslot%d' % keyslot[o['dma']]
                cnt[c] = cnt.get(c, 0) + 16
                ins.then_inc(getsem(c), 16)
                chan[i] = c
                val[i] = cnt[c]
            elif sig[i]:
                c = 'e_' + e
                cnt[c] = cnt.get(c, 0) + 1
                ins.then_inc(getsem(c), 1)
                chan[i] = c
                val[i] = cnt[c]
        for e in ('sp',):
            eobj = self.eng[e]
            for c, v in cnt.items():
                if waited[e].get(c, 0) < v:
                    eobj.wait_ge(getsem(c), v)
        self.stats = dict(n_ops=len(ops), n_wait=n_wait, n_sems=len(sems))


def _unit_of_rank(rank):
    return [('p', 0), ('p', 1), ('s', 0), ('s', 1), ('s', 2), ('s', 3), ('s', 2), ('s', 3)][rank]


_CONST_CACHE = {}


def _shared_consts():
    if 'shared' in _CONST_CACHE:
        return _CONST_CACHE['shared']
    c = {}
    c['ident_f'] = np.eye(128, dtype=np.float32)
    ones = np.zeros((128, 4, 128), np.float32)
    ones[:, 0, :] = 1.0 / 1024.0
    ones[:, 1, :] = 1.0 / 512.0
    ones[:, 2, :] = 1.0
    blk = np.zeros((128, 128), np.float32)
    blk[:64, :64] = 1.0 / 64.0
    blk[64:, 64:] = 1.0 / 64.0
    ones[:, 3, :] = blk
    c['ones_b'] = ones.astype(BF)
    prot = np.zeros((128, 128), np.float32)
    for p in range(128):
        d = p % 64
        if d < 8:
            prot[p + 8, p] = -1.0
        elif d < 16:
            prot[p - 8, p] = 1.0
    c['prot_b'] = prot.astype(BF)
    j = np.arange(128)[:, None]
    i = np.arange(128)[None, :]
    c['mprev'] = (j >= i).astype(np.float32)
    c['mnext'] = (j <= i).astype(np.float32)
    a = np.arange(128, dtype=np.float64)[:, None]
    ka = np.arange(KA, dtype=np.float64)[None, :]
    th = 2 * np.pi * a * ka / 128.0
    c['f1m'] = np.concatenate([np.cos(th), -np.sin(th)], axis=1).astype(BF)
    b = np.arange(128, dtype=np.float64)[:, None, None]
    kav = np.arange(KA, dtype=np.float64)[None, :, None]
    kb = np.arange(128, dtype=np.float64)[None, None, :]
    th = 2 * np.pi * b * (kav + 128.0 * kb) / NF
    gall = np.stack([np.cos(th), -np.sin(th), np.sin(th)], axis=2)
    c['gall'] = gall.astype(BF)
    kbv = np.arange(128, dtype=np.float64)[:, None]
    bp = np.arange(128, dtype=np.float64)[None, :]
    th = 2 * np.pi * kbv * bp / 128.0
    e1 = np.concatenate([np.cos(th), np.sin(th)], axis=1)
    e2 = np.concatenate([-np.sin(th), np.cos(th)], axis=1)
    c['e12'] = np.stack([e1, e2], axis=1).astype(BF)
    kav = np.arange(KA, dtype=np.float64)[:, None, None]
    bpv = np.arange(128, dtype=np.float64)[None, :, None]
    ap = np.arange(64, dtype=np.float64)[None, None, :]
    ph = 2 * np.pi * kav * (bpv + 128.0 * ap) / NF
    wt = np.full((KA, 1, 1), 2.0)
    wt[0] = 1.0
    wt[64] = 1.0
    hall = np.stack([wt / NF * np.cos(ph), -wt / NF * np.sin(ph)], axis=2)
    c['hall'] = hall.astype(BF)
    deltas = np.linspace(math.log(1e-2) / 1.5, math.log(1e-2) / 0.3, DH).astype(np.float32)
    c['negd'] = np.tile(-np.abs(deltas)[None, :], (128, 1)).astype(np.float32)
    _CONST_CACHE['shared'] = c
    return c


def _zfeat(L):
    key = ('z', L)
    if key in _CONST_CACHE:
        return _CONST_CACHE[key]
    t = np.linspace(0.0, 1.0, L).astype(np.float32)
    w = (2.0 * np.pi * np.arange(L, dtype=np.float64) / L)
    f = np.linspace(1e-4, 15.0, 16)
    fw = f[None, :] * w[:, None]
    z = np.concatenate([t[:, None].astype(np.float64), np.cos(fw), -np.sin(fw)], axis=1).astype(np.float32)
    _CONST_CACHE[key] = (t, z)
    return t, z


def _rank_consts(rank):
    key = ('rank', rank)
    if key in _CONST_CACHE:
        return _CONST_CACHE[key]
    kind, idx = _unit_of_rank(rank)
    sh = _shared_consts()
    c = {}
    pos0 = 8192 if (kind == 'p' and idx == 1) else 0
    mL = 1.0 if (kind == 'p' and idx == 1) else 0.0
    mR = 1.0 if (kind == 'p' and idx == 0) else 0.0
    c['edge'] = np.tile(np.array([[mL, mR]], np.float32), (128, 1))
    masks = np.stack([sh['mprev'], sh['mnext'], sh['mprev'] * mL, sh['mnext'] * mR], axis=1)
    c['masks'] = masks.astype(BF)
    pos = (pos0 - HALO + np.arange(NW)).astype(np.float32)
    inv = (ROPE_THETA ** (-np.arange(0, 16, 2, dtype=np.float32) / 16.0)).astype(np.float32)
    ang = pos[None, :] * inv[:, None]
    cs = np.zeros((128, 2, NW), np.float32)
    cs[:, 0, :] = 1.0
    for p in range(128):
        d = p % 64
        if d < 16:
            cs[p, 0, :] = np.cos(ang[d % 8])
            cs[p, 1, :] = np.sin(ang[d % 8])
    c['cstab'] = cs
    L = 16384 if kind == 'p' else 8192
    t_all, z_all = _zfeat(L)
    n = np.arange(NF)
    zf = np.zeros((2, 33, NF), np.float32)
    mm = np.zeros((2, 128, NF), np.float32)
    td = np.zeros((2, NF), np.float32)
    e0 = np.zeros((2,), np.float32)
    own = idx % 2 if kind == 's' else idx
    for s in range(2):
        lag = np.zeros(NF, np.int64)
        dr = np.zeros(NF, np.int64)
        if s == own:
            lo = n < 8192
            hi = n > 8192
            lag[lo] = n[lo]
            dr[lo] = 1
            lag[hi] = NF - n[hi]
            dr[hi] = 2
            e0[s] = 1.0
        elif kind == 'p':
            lo = n < 8192
            hi = n > 8192
            if idx == 0:
                lag[lo] = 8192 - n[lo]
                dr[lo] = 2
                lag[hi] = 24576 - n[hi]
                dr[hi] = 2
            else:
                lag[lo] = n[lo] + 8192
                dr[lo] = 1
                lag[hi] = n[hi] - 8192
                dr[hi] = 1
        valid = dr > 0
        zf[s][:, valid] = z_all[lag[valid]].T
        td[s][valid] = t_all[lag[valid]]
        mm[s][:64, :] = (dr == 1).astype(np.float32)[None, :]
        mm[s][64:, :] = (dr == 2).astype(np.float32)[None, :]
    c['zf'] = zf
    c['mm'] = mm.astype(BF)
    c['tdec'] = np.ascontiguousarray(td.reshape(2, 128, 128).transpose(1, 0, 2))
    c['e0'] = np.tile(e0[None, :], (128, 1)).astype(np.float32)
    _CONST_CACHE[key] = c
    return c


def _bc(ap, shape, axis):
    return ap.unsqueeze(axis).to_broadcast(shape)


class Builder:
    def __init__(self, depth=DEPTH, debug=False, stop=None):
        self.depth = depth
        self.debug = debug
        self.stop = stop
        self.nc = bass.Bass("TRN2", target_bir_lowering=False)
        self.P = Prog(self.nc)
        self.ges = contextlib.ExitStack()
        self.scope = None
        self.bank = 0
        self.rr = 0
        self.inputs = {}
        self.dbg_out = []
        self.sub = float(os.environ.get('MK_SUB', '99'))
        self.ntr = int(os.environ.get('MK_NT', str(NT)))

    def din(self, name, shape, dt=F32):
        t = self.nc.dram_tensor(name, list(shape), dt, kind="ExternalInput")
        self.inputs[name] = (tuple(shape), dt)
        return t.ap()

    def L(self, name):
        if name not in self.lazy_ap:
            shp, dt = self.lazy[name]
            self.lazy_ap[name] = self.din(name, shp, dt)
        return self.lazy_ap[name]

    def dscr(self, name, shape, dt, dbg=True):
        if self.debug and dbg and name in self.debug:
            self.dbg_out.append(name)
            return self.nc.dram_tensor(name, list(shape), dt, kind="ExternalOutput").ap()
        return self.nc.dram_tensor(name, list(shape), dt).ap()

    def gtile(self, name, shape, dt):
        return self.ges.enter_context(self.nc.sbuf_tensor('sbg_' + name, list(shape), dt))

    def tile(self, name, shape, dt):
        self.uid = getattr(self, 'uid', 0) + 1
        return self.scope.enter_context(self.nc.sbuf_tensor('sb%d_%s' % (self.uid, name), list(shape), dt))

    def begin(self):
        self.scope = contextlib.ExitStack()
        import inspect
        nm = inspect.stack()[1].function
        self.marks = getattr(self, 'marks', [])
        self.marks.append((nm, sum(1 for o in self.P.ops if o['eng'] == 'pe' and o['fn'] is not None)))

    def end(self):
        self.P.barrier()
        self.scope.close()
        self.scope = None

    def nb(self, k=1):
        if self.bank + k > 8:
            self.bank = 0
        b = self.bank
        self.bank = (self.bank + k) % 8
        return b

    def pr(self, b, k=1):
        return [('ps', b + j) for j in range(k)]

    def alt(self, engs=('act', 'dve')):
        self.rr += 1
        return engs[self.rr % len(engs)]

    def mm(self, out, lhsT, rhs, start, stop, r, w):
        self.P.op('pe', lambda e, o=out, l=lhsT, x=rhs, s=start, t=stop: e.matmul(o, l, x, start=s, stop=t), r, w)

    def tp(self, out, in_, ident, r, w):
        self.P.op('pe', lambda e, o=out, i=in_, d=ident: e.transpose(o, i, d), r, w)

    def act(self, out, in_, func, r, w, scale=None, bias=None):
        kw = {}
        if scale is not None:
            kw['scale'] = scale
        if bias is not None:
            kw['bias'] = bias
        self.P.op('act', lambda e, o=out, i=in_, f=func, k=kw: e.activation(o, i, f, **k), r, w)

    def cp(self, eng, out, in_, r, w):
        if eng == 'act':
            self.act(out, in_, AF.Copy, r, w)
        else:
            self.P.op(eng, lambda e, o=out, i=in_: e.tensor_copy(o, i), r, w)

    def tt(self, eng, out, in0, in1, op, r, w):
        self.P.op(eng, lambda e, o=out, a=in0, b=in1, p=op: e.tensor_tensor(o, a, b, p), r, w)

    def ts(self, eng, out, in0, s1, s2, op0, op1, r, w):
        if op1 is None:
            self.P.op(eng, lambda e, o=out, a=in0, x=s1, p=op0: e.tensor_scalar(o, a, x, None, p), r, w)
        else:
            self.P.op(eng, lambda e, o=out, a=in0, x=s1, y=s2, p=op0, q=op1: e.tensor_scalar(o, a, x, y, p, q), r, w)

    def stt(self, eng, out, in0, scalar, in1, op0, op1, r, w):
        self.P.op(eng, lambda e, o=out, a=in0, s=scalar, b=in1, p=op0, q=op1:
                  e.scalar_tensor_tensor(o, a, s, b, p, q), r, w)

    def rstd(self, out, in_, r, w, eng='dve'):
        np_ = out.shape[0]
        self.act(out, in_, AF.Ln, list(r) + ['g_epsc'], list(w), bias=self.epsc[0:np_, 0:1], scale=1.0)
        self.act(out, out, AF.Exp, list(w), list(w), scale=-0.5)

    def dma(self, q, out, in_, r, w, key, slow=False):
        if slow:
            self.P.op(q, lambda e, o=out, i=in_: e.dma_start(out=o, in_=i, allow_slow_non_contiguous=True), r, w, dma=key)
        else:
            self.P.op(q, lambda e, o=out, i=in_: e.dma_start(out=o, in_=i), r, w, dma=key)

    def allgather(self, in2d, out2d, r, w):
        self.P.op('pool', lambda e, i=in2d, o=out2d: e.collective_compute(
            "AllGather", ALU.bypass, replica_groups=[[0, 1], [2, 3], [4, 5], [6, 7]],
            ins=[i.opt()], outs=[o.opt()]), r, w, cc=True)

    def declare(self):
        L = DEPTH
        d = self.din
        self.lazy = {'x': ([T, D], F32), 'p': ([L, T, DPLE], F32), 'w_in': ([L, D, 2304], F32),
                     'w_out': ([L, D, D], F32), 'w_up': ([L, D, 2 * DFF], F32), 'w_down': ([L, DFF, D], F32),
                     'w_ple_gate': ([L, D, D], F32), 'w_ple_proj': ([L, DPLE, D], F32),
                     'filt_w1': ([L, 33, 64], F32), 'filt_w2': ([L, 64, 64], F32), 'filt_w3': ([L, 64, 1024], F32),
                     'gall': ([128, KA, 3, 128], BF16), 'hall': ([KA, 128, 2, 64], BF16),
                     'zf': ([2, 33, NF], F32), 'mm': ([2, 128, NF], BF16), 'cstab': ([128, 2, NW], F32)}
        self.lazy_ap = {}
        self.colspec = {}
        off = 0
        for name, n in [('g_mix', L * 8), ('g_ffn', L * 8), ('gq', L), ('gk', L), ('sink', L * 8),
                        ('w_short', L * 36), ('b_short', L * 12), ('g_ao', L * 4), ('g_ho', L * 4),
                        ('w_ffc', L * 132), ('b_ffc', L * 44), ('fb1', L), ('ffr1', L), ('fb2', L), ('ffr2', L)]:
            self.colspec[name] = (off, n)
            off += n
        self.ncol = off
        self.cols_d = d('cols', [128, self.ncol])
        self.hbias_d = d('hbias', [1, L * DH])
        self.c_ident = d('ident_f', [128, 128])
        self.c_ones = d('ones_b', [128, 4, 128], BF16)
        self.c_prot = d('prot_b', [128, 128], BF16)
        self.c_masks = d('masks', [128, 4, 128], BF16)
        self.c_edge = d('edge', [128, 2])
        self.c_f1m = d('f1m', [128, 130], BF16)
        self.c_e12 = d('e12', [128, 2, 256], BF16)
        self.c_negd = d('negd', [128, DH])
        self.c_tdec = d('tdec', [128, 2, 128])
        self.c_e0 = d('e0', [128, 2])
        self.y = self.nc.dram_tensor('y', [T, D], F32, kind="ExternalOutput").ap()
        s = self.dscr
        self.hres = s('hres', [D, T], F32)
        self.nrm = s('nrm', [D, NW], BF16)
        self.xh_in = s('xh_in', [2 * D, 128], BF16, dbg=False)
        self.xh_out = s('xh_out', [4 * D, 128], BF16, dbg=False)
        self.zhy = s('zhy', [3 * DH, T + 2], BF16)
        self.attn_n = s('attn_n', [DQ, T], BF16)
        self.x0s = s('x0s', [DH, T], BF16)
        self.vfft2 = [s('vfft%d' % g_, [64 * 8, 1024], BF16, dbg=False) for g_ in range(NG)]
        self.vall2 = [s('vall%d' % g_, [128 * 8, 1024], BF16, dbg=False) for g_ in range(NG)]
        self.vfft = [t_.rearrange('(a x) y -> a (x y)', x=8) for t_ in self.vfft2]
        self.vall = [t_.rearrange('(a x) y -> a (x y)', x=8) for t_ in self.vall2]
        self.hfs = s('hfs', [NG, 128, KA * 2 * 2 * CG], BF16)
        self.hdn = s('hdn', [2, 128, NF], BF16)
        self.yconv = s('yconv', [DH, T], BF16)
        self.n2s = s('n2s', [D, T + 2], BF16)
        self.xn_in = s('xn_in', [2, D], BF16, dbg=False)
        self.xn_out = s('xn_out', [4, D], BF16, dbg=False)
        self.acts = s('acts', [DFF, T], BF16)

    def col(self, name, l=None, k=1, j=0):
        off, n = self.colspec[name]
        per = n // DEPTH
        if l is None:
            return self.cols[:, off:off + n]
        a = off + l * per + j
        return self.cols[:, a:a + k]

    def setup_globals(self):
        g = self.gtile
        self.ps = self.ges.enter_context(self.nc.psum_tensor('ps', [128, 8, 512], F32))
        self.ident = g('ident', [128, 128], F32)
        self.ones = g('ones', [128, 4, 128], BF16)
        self.prot = g('prot', [128, 128], BF16)
        self.masks = g('masks', [128, 4, 128], BF16)
        self.edge = g('edge', [128, 2], F32)
        self.cols = g('cols', [128, self.ncol], F32)
        self.esk = g('esk', [128, DEPTH * 8], F32)
        self.fsc = g('fsc', [128, DEPTH * 4], F32)
        self.e0 = g('e0', [128, 2], F32)
        self.epsc = g('epsc', [128, 1], F32)
        self.P.op('dve', lambda e: e.memset(self.epsc[:], EPS), [], ['g_epsc'])
        for t, dsrc, nm in [(self.ident, self.c_ident, 'ident'), (self.ones, self.c_ones, 'ones'),
                            (self.prot, self.c_prot, 'prot'), (self.masks, self.c_masks, 'masks'),
                            (self.edge, self.c_edge, 'edge'), (self.cols, self.cols_d, 'cols'),
                            (self.e0, self.c_e0, 'e0')]:
            self.dma('sp', t[:], dsrc, [], ['g_' + nm], 'g_' + nm)
        o, n = self.colspec['sink']
        self.act(self.esk[:], self.cols[:, o:o + n], AF.Exp, ['g_cols'], ['g_esk'])
        for l in range(self.depth):
            for k, (fr, fb) in enumerate([('ffr1', 'fb1'), ('ffr2', 'fb2')]):
                self.ts('dve', self.fsc[:, l * 4 + 2 * k:l * 4 + 2 * k + 1], self.col(fr, l), 1.0 / 3.0, None, ALU.mult, None,
                        ['g_cols'], [('fsc', l, 2 * k)])
                self.tt('dve', self.fsc[:, l * 4 + 2 * k + 1:l * 4 + 2 * k + 2], self.fsc[:, l * 4 + 2 * k:l * 4 + 2 * k + 1],
                        self.col(fb, l), ALU.mult, [('fsc', l, 2 * k), 'g_cols'], [('fsc', l, 2 * k + 1)])
        self.P.barrier()

    def load_w(self, src_rows, ncols, scale, stg, sname, pieces):
        self.wslot = getattr(self, 'wslot', 0) + 1
        s = self.wslot % len(stg)
        st = stg[s]
        rn = (sname, s)
        self.dma('sp', st[:, 0:ncols], src_rows, [], [rn], '%s%d' % (sname, s))
        for (d_ap, c0, c1, vf, wn) in pieces:
            src = st[:, c0:c1]
            if vf is not None:
                src = vf(src)
            eng = self.alt(('act', 'dve'))
            if scale is None:
                self.cp(eng, d_ap, src, [rn], [wn])
            elif eng == 'act':
                self.act(d_ap, src, AF.Copy, [rn, 'g_cols'], [wn], scale=scale)
            else:
                self.ts(eng, d_ap, src, scale, None, ALU.mult, None, [rn, 'g_cols'], [wn])

    def phase_p0(self):
        self.begin()
        xt = [self.tile('p0_xt%d' % s, [128, 4, D], F32) for s in range(2)]
        ht = [self.tile('p0_ht%d' % s, [128, 8, 512], F32) for s in range(2)]
        for i in range(NT):
            s = i % 2
            self.dma('sp', xt[s][:], self.L('x')[i * 512:(i + 1) * 512, :].rearrange('(b p) f -> p b f', p=128),
                     [], [('xt', s)], 'xt%d' % s)
            for fc in range(8):
                bk = self.nb()
                for blk in range(4):
                    self.tp(self.ps[:, bk, blk * 128:(blk + 1) * 128], xt[s][:, blk, fc * 128:(fc + 1) * 128],
                            self.ident[:], [('xt', s), 'g_ident'], self.pr(bk))
                self.cp(self.alt(), ht[s][:, fc, :], self.ps[:, bk, :], self.pr(bk), [('ht', s, fc)])
            self.dma('sp', self.hres[:, i * 512:(i + 1) * 512].rearrange('(c p) t -> p c t', p=128), ht[s][:],
                     [('ht', s, fc) for fc in range(8)], [('hres', i)], 'ht%d_st' % s)
        self.end()

    def phase_a0(self, l):
        self.begin()
        ht = [self.tile('a0_ht%d' % s, [128, 8, 512], F32) for s in range(2)]
        sq = [self.tile('a0_sq%d' % s, [128, 8, 512], BF16) for s in range(2)]
        rs = [self.tile('a0_rs%d' % s, [128, 512], F32) for s in range(2)]
        nt = [self.tile('a0_nt%d' % s, [128, 8, 512], BF16) for s in range(2)]
        hl = self.tile('a0_hl', [128, 8, 128], BF16)
        hr = self.tile('a0_hr', [128, 8, 128], BF16)
        def loads(i):
            s = i % 2
            self.dma('sp', ht[s][:], self.hres[:, i * 512:(i + 1) * 512].rearrange('(c p) t -> p c t', p=128),
                     [('hres', i)], [('ht', s)], 'a0ht%d' % s)
        loads(0)
        for i in range(NT):
            s = i % 2
            if i + 1 < NT:
                loads(i + 1)
            self.act(sq[s][:], ht[s][:], AF.Square, [('ht', s)], [('sq', s)])
            bk = self.nb()
            for fc in range(8):
                self.mm(self.ps[:, bk, :], self.ones[:, 0, :], sq[s][:, fc, :], fc == 0, fc == 7,
                        [('sq', s), 'g_ones'], self.pr(bk))
            self.rstd(rs[s][:], self.ps[:, bk, :], self.pr(bk), [('rs', s)])
            for hf in range(2):
                eng = 'dve' if hf == 0 else 'pool'
                self.tt(eng, nt[s][:, hf * 4:(hf + 1) * 4, :], ht[s][:, hf * 4:(hf + 1) * 4, :],
                        _bc(rs[s][:], [128, 4, 512], 1), ALU.mult, [('ht', s), ('rs', s)], [('nt', s, hf)])
            rd = [('nt', s, 0), ('nt', s, 1)]
            self.dma('sp', self.nrm[:, HALO + i * 512:HALO + (i + 1) * 512].rearrange('(c p) t -> p c t', p=128),
                     nt[s][:], rd, [('nrm', i)], 'a0nt%d_st' % s)
            if i == 0:
                self.dma('sp', self.xh_in[0:D, :].rearrange('(c p) t -> p c t', p=128), nt[s][:, :, 0:128],
                         rd, ['xh_in0'], 'a0x0')
            if i == NT - 1:
                self.dma('sp', self.xh_in[D:2 * D, :].rearrange('(c p) t -> p c t', p=128), nt[s][:, :, 384:512],
                         rd, ['xh_in1'], 'a0x1')
        self.allgather(self.xh_in, self.xh_out, ['xh_in0', 'xh_in1'], ['xh_out'])
        for k, (tl, r0, col) in enumerate([(hl, D, 0), (hr, 2 * D, NW - HALO)]):
            self.dma('sp', tl[:], self.xh_out[r0:r0 + D, :].rearrange('(c p) t -> p c t', p=128),
                     ['xh_out'], [('hal', k)], 'a0h%d' % k)
            self.ts('dve', tl[:], tl[:], self.edge[:, k:k + 1], None, ALU.mult, None, [('hal', k), 'g_edge'], [('hal', k)])
            self.dma('sp', self.nrm[:, col:col + HALO].rearrange('(c p) t -> p c t', p=128), tl[:],
                     [('hal', k)], [('nrmh', k)], 'a0h%d_st' % k)
        self.end()


    def capture(self, fn):
        rec = []
        real = self.P.op
        self.P.op = lambda *a, **k: rec.append((a, k))
        try:
            fn()
        finally:
            self.P.op = real
        return rec

    def replay(self, recs):
        for a, k in recs:
            self.P.op(*a, **k)

    @staticmethod
    def merge(a, b):
        def split(x):
            segs = [[]]
            for r in x:
                if r[0][0] == 'MARK':
                    segs.append([])
                else:
                    segs[-1].append(r)
            return segs
        sa = split(a)
        sb = [g_ for g_ in split(b) if g_]
        ncut = max(1, len(sa) - 1)
        out = []
        ib = 0
        for k, sg in enumerate(sa):
            out.extend(sg)
            if k < len(sa) - 1:
                tgt = (k + 1) * len(sb) // ncut
                while ib < min(tgt, len(sb)):
                    out.extend(sb[ib])
                    ib += 1
        while ib < len(sb):
            out.extend(sb[ib])
            ib += 1
        return out

    def phase_a(self, l):
        self.begin()
        w_in = self.L('w_in')
        cstab = self.L('cstab')
        wq = self.tile('wq', [128, 8, 512], BF16)
        wk = self.tile('wk', [128, 8, 256], BF16)
        wv = self.tile('wv', [128, 8, 256], BF16)
        why = self.tile('why', [128, 8, 1536], BF16)
        stg = [self.tile('a_stg%d' % s_, [128, 1152], F32) for s_ in range(2)]
        dup = lambda a: _bc(a.rearrange('p (k d) -> p k d', k=2), [128, 2, 2, 64], 2)
        for kc in range(8):
            self.load_w(w_in[l, kc * 128:(kc + 1) * 128, 0:1152], 1152, self.col('g_mix', l, 1, kc), stg, 'astg', [
                (wq[:, kc, :], 0, 512, None, ('wq', kc)),
                (wk[:, kc, :].rearrange('p (k u d) -> p k u d', k=2, u=2), 512, 640, dup, ('wk', kc)),
                (wv[:, kc, :].rearrange('p (k u d) -> p k u d', k=2, u=2), 640, 768, dup, ('wv', kc)),
                (why[:, kc, 0:384], 768, 1152, None, ('why', kc, 0))])
            self.load_w(w_in[l, kc * 128:(kc + 1) * 128, 1152:2304], 1152, self.col('g_mix', l, 1, kc), stg, 'astg', [
                (why[:, kc, 384:1536], 0, 1152, None, ('why', kc, 1))])
        nt = [self.tile('nt%d' % s, [128, 8, 768], BF16) for s in range(2)]
        ct = [self.tile('ct%d' % s, [128, 2, 768], F32) for s in range(2)]
        qraw = self.tile('qraw', [128, 4, 512], F32)
        kraw = self.tile('kraw', [128, 2, 768], F32)
        sqq = self.tile('sqq', [128, 4, 512], BF16)
        sqk = self.tile('sqk', [128, 2, 768], BF16)
        rq = self.tile('rq', [128, 4, 512], F32)
        rk = self.tile('rk', [128, 2, 768], F32)
        qn = self.tile('qn', [128, 4, 512], BF16)
        kn = self.tile('kn', [128, 2, 768], BF16)
        qr = [self.tile('qr%d' % s, [128, 4, 512], BF16) for s in range(2)]
        krl = [self.tile('krl%d' % s, [128, 2, 768], BF16) for s in range(2)]
        krh = [self.tile('krh%d' % s, [128, 2, 768], BF16) for s in range(2)]
        for s in range(2):
            self.P.op('pool', lambda e, t_=krl[s]: e.memset(t_[64:128], 0.0), [], [('krl0', s)])
            self.P.op('pool', lambda e, t_=krh[s]: e.memset(t_[0:64], 0.0), [], [('krh0', s)])
        vd = [self.tile('vd%d' % s, [128, 6, 256], BF16) for s in range(2)]
        zt = self.tile('zt', [128, 12, 512], BF16)
        zh = self.tile('zh', [128, 12, 2], BF16)
        E = [self.tile('E%d' % s, [128, 3, 512], BF16) for s in range(2)]
        rec = [self.tile('rec%d' % s, [128, 512], F32) for s in range(2)]
        attn = self.tile('attn', [128, 4, 512], F32)
        sqa = self.tile('sqa', [128, 4, 512], BF16)
        rsa = self.tile('rsa', [128, 512], F32)
        an = [self.tile('an%d' % s, [128, 4, 512], BF16) for s in range(2)]
        gq = self.col('gq', l)
        gk = self.col('gk', l)
        st_ = {'ei': 0}
        kseg = [(0, 0, 512, 0, 0), (0, 512, 768, 1, 0), (1, 0, 256, 1, 256), (1, 256, 768, 2, 0)]
        whyr = lambda kc: [('why', kc, 0), ('why', kc, 1)]

        def loads(i):
            s = i % 2
            t0 = i * 512
            self.dma('sp', nt[s][:], self.nrm[:, t0:t0 + 768].rearrange('(c p) t -> p c t', p=128),
                     [], [('nt', s)], 'a_nt%d' % s)
            self.dma('sp', ct[s][:], cstab[:, :, t0:t0 + 768], [], [('ct', s)], 'a_ct%d' % s)

        def X(i):
            s = i % 2
            t0 = i * 512
            rw = [('nt', s)]
            bq = self.nb(4)
            for mc in range(4):
                for kc in range(8):
                    self.mm(self.ps[:, bq + mc, :], wq[:, kc, mc * 128:(mc + 1) * 128], nt[s][:, kc, 128:640],
                            kc == 0, kc == 7, rw + [('wq', kc)], self.pr(bq + mc))
            self.cp('act', qraw[:], self.ps[:, bq:bq + 4, :], self.pr(bq, 4), ['qraw'])
            self.P.op('MARK', None)
            bkk = self.nb(3)
            for (c2, c0, c1, bo, pc0) in kseg:
                for kc in range(8):
                    self.mm(self.ps[:, bkk + bo, pc0:pc0 + (c1 - c0)], wk[:, kc, c2 * 128:(c2 + 1) * 128],
                            nt[s][:, kc, c0:c1], kc == 0, kc == 7, rw + [('wk', kc)], self.pr(bkk + bo))
            psk = self.ps[:, bkk:bkk + 3, :].rearrange('p b t -> p (b t)').rearrange('p (c t) -> p c t', c=2)
            self.cp('dve', kraw[:], psk, self.pr(bkk, 3), ['kraw'])
            self.P.op('MARK', None)
            bv = self.nb(3)
            for kb in range(6):
                for kc in range(8):
                    self.mm(self.ps[:, bv + kb // 2, (kb % 2) * 256:(kb % 2 + 1) * 256], nt[s][:, kc, kb * 128:(kb + 1) * 128],
                            wv[:, kc, :], kc == 0, kc == 7, rw + [('wv', kc)], self.pr(bv + kb // 2))
            self.cp('act', vd[s][:], self.ps[:, bv:bv + 3, :].rearrange('p b (u t) -> p (b u) t', u=2), self.pr(bv, 3), [('vd', s)])
            for g3 in range(3):
                self.P.op('MARK', None)
                bh = self.nb(4)
                for j in range(4):
                    mc = g3 * 4 + j
                    for kc in range(8):
                        self.mm(self.ps[:, bh + j, :], why[:, kc, mc * 128:(mc + 1) * 128], nt[s][:, kc, 128:640],
                                kc == 0, kc == 7, rw + whyr(kc), self.pr(bh + j))
                self.cp(self.alt(), zt[:, g3 * 4:(g3 + 1) * 4, :], self.ps[:, bh:bh + 4, :], self.pr(bh, 4), [('zt', g3)])
            self.dma('sp', self.zhy[:, 1 + t0:1 + t0 + 512].rearrange('(c p) t -> p c t', p=128), zt[:],
                     [('zt', g3) for g3 in range(3)], [('zhy', i)], 'a_zt_st')
            if i == 0 or i == NT - 1:
                self.P.op('MARK', None)
                colh = 127 if i == 0 else 640
                hi = 0 if i == 0 else 1
                bh = self.nb(1)
                for mc in range(12):
                    for kc in range(8):
                        self.mm(self.ps[:, bh, mc:mc + 1], why[:, kc, mc * 128:(mc + 1) * 128], nt[s][:, kc, colh:colh + 1],
                                kc == 0, kc == 7, rw + whyr(kc), self.pr(bh))
                self.cp('dve', zh[:, :, hi], self.ps[:, bh, 0:12], self.pr(bh), [('zh', hi)])
                dcol = 0 if i == 0 else T + 1
                self.dma('sp', self.zhy[:, dcol:dcol + 1].rearrange('(c p) t -> p c t', p=128), zh[:, :, hi:hi + 1],
                         [('zh', hi)], [('zhyh', hi)], 'a_zh%d_st' % hi, slow=True)

        def Y(i):
            s = i % 2
            self.act(sqq[:], qraw[:], AF.Square, ['qraw'], ['sqq'])
            self.act(sqk[:], kraw[:], AF.Square, ['kraw'], ['sqk'])
            self.P.op('MARK', None)
            bs = self.nb(4)
            for mc in range(4):
                self.mm(self.ps[:, bs + mc, :], self.ones[:, 3, :], sqq[:, mc, :], True, True, ['sqq', 'g_ones'], self.pr(bs + mc))
            self.rstd(rq[:], self.ps[:, bs:bs + 4, :], self.pr(bs, 4), ['rq'])
            self.P.op('MARK', None)
            bs2 = self.nb(3)
            for (c2, c0, c1, bo, pc0) in kseg:
                self.mm(self.ps[:, bs2 + bo, pc0:pc0 + (c1 - c0)], self.ones[:, 3, :], sqk[:, c2, c0:c1], True, True,
                        ['sqk', 'g_ones'], self.pr(bs2 + bo))
            psk = self.ps[:, bs2:bs2 + 3, :].rearrange('p b t -> p (b t)').rearrange('p (c t) -> p c t', c=2)
            self.rstd(rk[:], psk, self.pr(bs2, 3), ['rk'])
            self.P.op('MARK', None)
            self.stt('dve', qn[:], qraw[:], gq, rq[:], ALU.mult, ALU.mult, ['qraw', 'rq', 'g_cols'], ['qn'])
            self.stt('dve', kn[:], kraw[:], gk, rk[:], ALU.mult, ALU.mult, ['kraw', 'rk', 'g_cols'], ['kn'])
            self.P.op('MARK', None)
            br = self.nb(4)
            for mc in range(4):
                self.mm(self.ps[:, br + mc, :], self.prot[:], qn[:, mc, :], True, True, ['qn', 'g_prot'], self.pr(br + mc))
            self.tt('pool', qraw[:], qn[:], _bc(ct[s][:, 0, 128:640], [128, 4, 512], 1), ALU.mult, ['qn', ('ct', s)], ['qraw'])
            self.tt('dve', rq[:], self.ps[:, br:br + 4, :], _bc(ct[s][:, 1, 128:640], [128, 4, 512], 1), ALU.mult,
                    self.pr(br, 4) + [('ct', s)], ['rq'])
            self.tt('pool', qr[s][:], qraw[:], rq[:], ALU.add, ['qraw', 'rq'], [('qr', s)])
            self.P.op('MARK', None)
            br2 = self.nb(3)
            for (c2, c0, c1, bo, pc0) in kseg:
                self.mm(self.ps[:, br2 + bo, pc0:pc0 + (c1 - c0)], self.prot[:], kn[:, c2, c0:c1], True, True,
                        ['kn', 'g_prot'], self.pr(br2 + bo))
            psk = self.ps[:, br2:br2 + 3, :].rearrange('p b t -> p (b t)').rearrange('p (c t) -> p c t', c=2)
            self.tt('pool', kraw[:], kn[:], _bc(ct[s][:, 0, :], [128, 2, 768], 1), ALU.mult, ['kn', ('ct', s)], ['kraw'])
            self.tt('dve', rk[:], psk, _bc(ct[s][:, 1, :], [128, 2, 768], 1), ALU.mult, self.pr(br2, 3) + [('ct', s)], ['rk'])
            self.tt('pool', krl[s][0:64], kraw[0:64], rk[0:64], ALU.add, ['kraw', 'rk'], [('krl', s)])
            self.tt('dve', krh[s][64:128], kraw[64:128], rk[64:128], ALU.add, ['kraw', 'rk'], [('krh', s)])

        def Z(i):
            s = i % 2
            t0 = i * 512
            kread = [('krl', s), ('krh', s), ('krl0', s), ('krh0', s), ('qr', s)]
            its = [(qb, kvh) for qb in range(4) for kvh in range(2)]

            def SE(k):
                qb, kvh = its[k]
                es_ = k % 2
                b0 = self.nb(3)
                for kk in range(3):
                    kbw = qb + kk
                    for half in range(2):
                        kt_ = krl[s] if half == 0 else krh[s]
                        self.mm(self.ps[:, b0 + kk, half * 256:(half + 1) * 256],
                                kt_[:, kvh, kbw * 128:(kbw + 1) * 128],
                                qr[s][:, 2 * kvh:2 * kvh + 2, qb * 128:(qb + 1) * 128],
                                True, True, kread, self.pr(b0 + kk))
                self.act(E[es_][:], self.ps[:, b0:b0 + 3, :], AF.Exp, self.pr(b0, 3), [('E', es_)], scale=0.125)
                first = (i == 0 and qb == 0)
                lastb = (i == NT - 1 and qb == 3)
                for (kk, mi) in [(0, 2 if first else 0), (2, 3 if lastb else 1)]:
                    ev = E[es_][:, kk, :].rearrange('p (g q) -> p g q', g=4)
                    self.tt('dve', ev, ev, _bc(self.masks[:, mi, :], [128, 4, 128], 1), ALU.mult,
                            [('E', es_), 'g_masks'], [('E', es_)])

            def PN(k):
                qb, kvh = its[k]
                es_ = k % 2
                bpv = self.nb()
                bdn = self.nb()
                for kk in range(3):
                    self.mm(self.ps[:, bpv, :], vd[s][:, qb + kk, kvh * 128:(kvh + 1) * 128], E[es_][:, kk, :],
                            kk == 0, kk == 2, [('vd', s), ('E', es_)], self.pr(bpv))
                for kk in range(3):
                    self.mm(self.ps[:, bdn, :], self.ones[:, 2, :], E[es_][:, kk, :], kk == 0, kk == 2,
                            [('E', es_), 'g_ones'], self.pr(bdn))
                rv = rec[es_][:].rearrange('p (g q) -> p g q', g=4)
                sk = self.esk[:, l * 8 + kvh * 4:l * 8 + kvh * 4 + 4]
                self.tt('dve', rv, self.ps[:, bdn, :].rearrange('p (g q) -> p g q', g=4), _bc(sk, [128, 4, 128], 2),
                        ALU.add, self.pr(bdn) + ['g_esk'], [('rec', es_)])
                self.act(rec[es_][:], rec[es_][:], AF.Ln, [('rec', es_)], [('rec', es_)])
                self.act(rec[es_][:], rec[es_][:], AF.Exp, [('rec', es_)], [('rec', es_)], scale=-1.0)
                for half in range(2):
                    pp = slice(half * 64, (half + 1) * 64)
                    self.tt('dve', attn[pp, 2 * kvh:2 * kvh + 2, qb * 128:(qb + 1) * 128],
                            self.ps[pp, bpv, half * 256:(half + 1) * 256].rearrange('p (g q) -> p g q', g=2),
                            rec[es_][pp, half * 256:(half + 1) * 256].rearrange('p (g q) -> p g q', g=2), ALU.mult,
                            self.pr(bpv) + [('rec', es_)], [('attn', qb, kvh, half)])

            SE(0)
            for k in range(8):
                if k + 1 < 8:
                    SE(k + 1)
                PN(k)
                self.P.op('MARK', None)
            ar = [('attn', qb, kvh, half) for qb in range(4) for kvh in range(2) for half in range(2)]
            self.act(sqa[:], attn[:], AF.Square, ar, ['sqa'])
            bs = self.nb()
            for mc in range(4):
                self.mm(self.ps[:, bs, :], self.ones[:, 1, :], sqa[:, mc, :], mc == 0, mc == 3, ['sqa', 'g_ones'], self.pr(bs))
            self.rstd(rsa[:], self.ps[:, bs, :], self.pr(bs), ['rsa'])
            self.tt('pool', an[s][:], attn[:], _bc(rsa[:], [128, 4, 512], 1), ALU.mult, ar + ['rsa'], [('an', s)])
            self.dma('sp', self.attn_n[:, t0:t0 + 512].rearrange('(c p) t -> p c t', p=128), an[s][:],
                     [('an', s)], [('attn_n', i)], 'a_an%d_st' % s)

        loads(0)
        loads(1)
        self.replay([r_ for r_ in self.capture(lambda: (X(0), Y(0))) if r_[0][0] != 'MARK'])
        for i in range(NT):
            zs = self.capture(lambda: Z(i))
            if i + 1 < NT:
                xs = self.capture(lambda: X(i + 1))
                ys = self.capture(lambda: Y(i + 1))
                self.replay(self.merge(zs, xs + [(('MARK', None), {})] + ys))
            else:
                self.replay([r_ for r_ in zs if r_[0][0] != 'MARK'])
            if i + 2 < NT:
                loads(i + 2)
        self.end()

    def body_a2(self, l):
        zw = [self.tile('zw%d' % s, [128, 12, 514], BF16) for s in range(2)]
        u = self.tile('u', [128, 12, 512], F32)
        xo = [self.tile('xo%d' % s, [128, 4, 512], BF16) for s in range(2)]
        vt = [self.tile('vt%d' % s, [128, 4, 512], BF16) for s in range(2)]
        def loads(i):
            s = i % 2
            self.dma('sp', zw[s][:], self.zhy[:, i * 512:i * 512 + 514].rearrange('(c p) t -> p c t', p=128), [], [('zw', s)], 'a2zw%d' % s)
        loads(0)
        for i in range(NT):
            s = i % 2
            t0 = i * 512
            if i + 1 < NT:
                loads(i + 1)
            for j in range(12):
                wc = lambda k, j=j: self.col('w_short', l, 1, k * 12 + j)
                self.act(u[:, j, :], zw[s][:, j, 1:513], AF.Identity, [('zw', s), 'g_cols'], [('u', j)],
                         scale=wc(1), bias=self.col('b_short', l, 1, j))
                e1 = 'dve'
                self.stt(e1, u[:, j, :], zw[s][:, j, 0:512], wc(0), u[:, j, :], ALU.mult, ALU.add, [('zw', s), ('u', j), 'g_cols'], [('u', j)])
                self.stt(e1, u[:, j, :], zw[s][:, j, 2:514], wc(2), u[:, j, :], ALU.mult, ALU.add, [('zw', s), ('u', j), 'g_cols'], [('u', j)])
            self.cp('act', xo[s][:], u[:, 0:4, :], [('u', j) for j in range(4)], [('xo', s)])
            self.tt('dve', vt[s][:], u[:, 4:8, :], u[:, 8:12, :], ALU.mult, [('u', j) for j in range(4, 12)], [('vt', s)])
            self.dma('sp', self.x0s[:, t0:t0 + 512].rearrange('(c p) t -> p c t', p=128), xo[s][:], [('xo', s)], [('x0s', i)], 'a2xo%d_st' % s)
            for j in range(4):
                for h2 in range(2):
                    gg = 2 * j + h2
                    self.dma('sp', self.vfft[gg][4 * i:4 * i + 4, :].rearrange('a (p b) -> p a b', p=64),
                             vt[s][h2 * 64:(h2 + 1) * 64, j, :].rearrange('p (a b) -> p a b', a=4), [('vt', s)], [('vfft', gg, i)],
                             'a2vt%d_%d_st' % (s, gg))
        if os.environ.get('MK_NOAG') is None:
            for gg in range(NG):
                self.allgather(self.vfft2[gg], self.vall2[gg], [('vfft', gg, i) for i in range(NT)], [('vall', gg)])

    def body_f1(self, l):
        zf = self.L('zf')
        mmk = self.L('mm')
        w1t = self.tile('w1t', [33, 64], F32)
        w2d = self.tile('w2d', [64, 128], F32)
        self.dma('sp', w1t[:], self.L('filt_w1')[l], [], ['w1t'], 'f1w1')
        for k in range(2):
            self.dma('sp', w2d[:, k * 64:(k + 1) * 64], self.L('filt_w2')[l], [], [('w2d', k)], 'f1w2%d' % k)
        zt_ = [self.tile('zt_%d' % s, [33, 2048], F32) for s in range(2)]
        mk = [self.tile('mk%d' % s, [128, 2048], BF16) for s in range(2)]
        s1 = self.tile('s1', [64, 2048], F32)
        t1 = self.tile('t1', [64, 2048], F32)
        s2 = self.tile('s2', [128, 2048], F32)
        t2 = self.tile('t2', [128, 2048], F32)
        hd = [self.tile('hd%d' % s, [128, 2048], BF16) for s in range(2)]
        f = lambda k: self.fsc[:, l * 4 + k:l * 4 + k + 1]
        it = 0
        for sl in range(2):
            for cch in range(8):
                s = it % 2
                it += 1
                c0 = cch * 2048
                self.dma('sp', zt_[s][:], zf[sl, :, c0:c0 + 2048], [], [('zt_', s)], 'f1zt%d' % s)
                self.dma('sp', mk[s][:], mmk[sl, :, c0:c0 + 2048], [], [('mk', s)], 'f1mk%d' % s)
                b1 = self.nb(4)
                for q4 in range(4):
                    self.mm(self.ps[0:64, b1 + q4, :], w1t[:], zt_[s][:, q4 * 512:(q4 + 1) * 512], True, True,
                            ['w1t', ('zt_', s)], self.pr(b1 + q4))
                self.act(s1[:], self.ps[0:64, b1:b1 + 4, :], AF.Sin, self.pr(b1, 4) + [('fsc', l, 0), ('fsc', l, 1)], ['s1'],
                         scale=f(0)[0:64], bias=f(1)[0:64])
                self.tt('dve', t1[:], s1[:], s1[:], ALU.mult, ['s1'], ['t1'])
                self.ts('dve', t1[:], t1[:], -4.0, 3.0, ALU.mult, ALU.add, ['t1'], ['t1'])
                self.tt('pool', t1[:], t1[:], s1[:], ALU.mult, ['t1', 's1'], ['t1'])
                b2 = self.nb(4)
                for q4 in range(4):
                    self.mm(self.ps[:, b2 + q4, :], w2d[:], t1[:, q4 * 512:(q4 + 1) * 512], True, True,
                            [('w2d', 0), ('w2d', 1), 't1'], self.pr(b2 + q4))
                self.act(s2[:], self.ps[:, b2:b2 + 4, :], AF.Sin, self.pr(b2, 4) + [('fsc', l, 2), ('fsc', l, 3)], ['s2'],
                         scale=f(2), bias=f(3))
                self.tt('dve', t2[:], s2[:], s2[:], ALU.mult, ['s2'], ['t2'])
                self.ts('dve', t2[:], t2[:], -4.0, 3.0, ALU.mult, ALU.add, ['t2'], ['t2'])
                self.tt('pool', t2[:], t2[:], s2[:], ALU.mult, ['t2', 's2'], ['t2'])
                self.tt('pool', hd[s][:], t2[:], mk[s][:], ALU.mult, ['t2', ('mk', s)], [('hd', s)])
                self.dma('sp', self.hdn[sl, :, c0:c0 + 2048], hd[s][:], [('hd', s)], [('hdn', sl, cch)], 'f1hd%d_st' % s)

    def phase_a2f1(self, l):
        self.begin()
        a2 = self.capture(lambda: self.body_a2(l))
        f1 = self.capture(lambda: self.body_f1(l))
        out = []
        ib = 0
        for ia, x in enumerate(a2):
            out.append(x)
            tgt = (ia + 1) * len(f1) // len(a2)
            out.extend(f1[ib:tgt])
            ib = tgt
        out.extend(f1[ib:])
        self.replay(out)
        self.end()

    def g_stream(self, Gt, kab, ka0, gi_):
        s = gi_ % 2
        n = min(5, KA - ka0)
        self.dma('sp', Gt[s][:, 0:n], self.L('gall')[:, ka0:ka0 + n], [], [('Gt', s)], 'Gt%d' % s)
        return s

    def phase_f2(self, l):
        self.begin()
        hdn = self.tile('hdn', [128, NF], BF16)
        g = self.tile('g', [128, 128, 256], BF16)
        Ysb = self.tile('Ysb', [128, 2, KA, 256], BF16)
        HfT = [self.tile('HfT%d' % s, [128, 5, 2, 256], BF16) for s in range(2)]
        Gt = [self.tile('Gt%d' % s, [128, 5, 3, 128], BF16) for s in range(2)]
        dc = [self.tile('dc%d' % s, [128, 2, 256], F32) for s in range(2)]
        w3f = self.tile('w3f', [128, 256], F32)
        w3s = self.tile('w3s', [128, 256], BF16)
        f1m = self.tile('f1m', [128, 130], BF16)
        negd = self.tile('negd', [128, DH], F32)
        tdec = self.tile('tdec', [128, 2, 128], F32)
        hbt = self.tile('hbt', [1, 256], F32)
        self.dma('sp', f1m[:], self.c_f1m, [], ['f1m'], 'f2f1m')
        self.dma('sp', negd[:], self.c_negd, [], ['negd'], 'f2negd')
        self.dma('sp', tdec[:], self.c_tdec, [], ['tdec'], 'f2tdec')
        w3 = self.L('filt_w3')
        gi_ = 0
        hi_ = 0
        for sl in range(2):
            self.dma('sp', hdn[:], self.hdn[sl], [], ['hdn'], 'f2hdn')
            for hh in range(2):
                for k in range(2):
                    self.dma('sp', w3f[k * 64:(k + 1) * 64, :], w3[l, :, k * 512 + hh * 256:k * 512 + (hh + 1) * 256], [], [('w3f', k)], 'f2w3%d' % k)
                self.cp('dve', w3s[:], w3f[:], [('w3f', 0), ('w3f', 1)], ['w3s'])
                self.dma('sp', hbt[:], self.hbias_d[0:1, l * DH + hh * 256:l * DH + (hh + 1) * 256], [], ['hbt'], 'f2hbt')
                for b2 in range(64):
                    bk = self.nb()
                    ds_ = b2 % 2
                    for u2 in range(2):
                        b = b2 * 2 + u2
                        self.mm(self.ps[:, bk, u2 * 256:(u2 + 1) * 256], hdn[:].rearrange('p (a b) -> p b a', b=128)[:, b, :],
                                w3s[:], True, True, ['hdn', 'w3s'], self.pr(bk))
                        self.act(dc[ds_][:, u2, :], negd[:, hh * 256:(hh + 1) * 256], AF.Exp, ['negd', 'tdec'], [('dc', ds_, u2)],
                                 scale=tdec[:, sl, b:b + 1])
                    self.tt('dve', g[:, 2 * b2:2 * b2 + 2, :], self.ps[:, bk, :].rearrange('p (u c) -> p u c', u=2), dc[ds_][:],
                            ALU.mult, self.pr(bk) + [('dc', ds_, 0), ('dc', ds_, 1)], [('g', b2)])
                    if b2 == 0:
                        self.stt('dve', g[0:1, 0, :], hbt[:], self.e0[0:1, sl:sl + 1],
                                 g[0:1, 0, :], ALU.mult, ALU.add, [('g', 0), 'hbt', 'g_e0'], [('g', 0)])
                gr = [('g', b2) for b2 in range(64)]
                c = 0
                while c < 256:
                    n = min(3, 256 - c)
                    bk = self.nb()
                    for u3 in range(n):
                        self.mm(self.ps[:, bk, u3 * 130:(u3 + 1) * 130], g[:, :, c + u3], f1m[:], True, True, gr + ['f1m'], self.pr(bk))
                    self.cp(self.alt(), Ysb[:, :, :, c:c + n], self.ps[:, bk, 0:n * 130].rearrange('p (c r k) -> p r k c', c=n, r=2),
                            self.pr(bk), [('Ysb', c)])
                    c += n
                yr = [('Ysb', c) for c in range(0, 256, 3)]
                for ka0 in range(0, KA, 5):
                    gs = self.g_stream(Gt, None, ka0, gi_)
                    gi_ += 1
                    hs = hi_ % 2
                    hi_ += 1
                    nk = min(5, KA - ka0)
                    for kq in range(nk):
                        ka = ka0 + kq
                        bk = self.nb()
                        zr = self.ps[:, bk, 0:256]
                        zi = self.ps[:, bk, 256:512]
                        rr = yr + [('Gt', gs)]
                        self.mm(zr, Gt[gs][:, kq, 0, :], Ysb[:, 0, ka, :], True, False, rr, self.pr(bk))
                        self.mm(zr, Gt[gs][:, kq, 2, :], Ysb[:, 1, ka, :], False, True, rr, self.pr(bk))
                        self.mm(zi, Gt[gs][:, kq, 0, :], Ysb[:, 1, ka, :], True, False, rr, self.pr(bk))
                        self.mm(zi, Gt[gs][:, kq, 1, :], Ysb[:, 0, ka, :], False, True, rr, self.pr(bk))
                        self.cp(self.alt(), HfT[hs][:, kq, :, :], self.ps[:, bk, :].rearrange('p (r c) -> p r c', r=2), self.pr(bk), [('HfT', hs, kq)])
                    for g4 in range(4):
                        gg = hh * 4 + g4
                        dst = self.hfs[gg].rearrange('p (k r s c) -> p k r s c', k=KA, r=2, s=2)[:, ka0:ka0 + nk, :, sl, :]
                        self.dma('sp', dst, HfT[hs][:, 0:nk, :, g4 * 64:(g4 + 1) * 64], [('HfT', hs, kq) for kq in range(nk)],
                                 [('hfs', gg, sl, ka0)], 'f2hf%d_%d_st' % (hs, g4))
        self.end()

    def phase_b(self, l):
        self.begin()
        xa = [self.tile('xa%d' % s, [64, CG, 128], BF16) for s in range(2)]
        hft = self.tile('hft', [128, KA, 2, 2, CG], BF16)
        Ysb = self.tile('Ysb', [128, 2, KA, 2, CG], BF16)
        Wt = self.tile('Wt', [128, 2, CG, KA], BF16)
        U = self.tile('U', [KA, 2, 128, CG], BF16)
        Yo = self.tile('Yo', [64, CG, 128], BF16)
        A = self.tile('A', [128, 4, 2, 2 * CG], F32)
        B1 = self.tile('B1', [128, 4, 2 * CG], F32)
        B2 = self.tile('B2', [128, 4, 2 * CG], F32)
        Dr = self.tile('Dr', [128, 4, 2 * CG], F32)
        Di = self.tile('Di', [128, 4, 2 * CG], F32)
        Gt = [self.tile('Gt%d' % s, [128, 5, 3, 128], BF16) for s in range(2)]
        Ht = [self.tile('Ht%d' % s, [KA, 16, 2, 64], BF16) for s in range(2)]
        f1m = self.tile('f1m', [128, 130], BF16)
        e12 = self.tile('e12', [128, 2, 256], BF16)
        self.dma('sp', f1m[:], self.c_f1m, [], ['f1m'], 'bf1m')
        self.dma('sp', e12[:], self.c_e12, [], ['e12'], 'be12')
        hall = self.L('hall')
        gi_ = 0
        hi_ = 0
        for gg in range(NG):
            c0 = gg * CG
            for sl in range(2):
                self.dma('sp', xa[sl][:], self.vall[gg][sl * 64:(sl + 1) * 64, :].rearrange('a (c b) -> a c b', c=CG),
                         [], [('xa', sl)], 'bxa%d' % sl)
            self.dma('sp', hft[:], self.hfs[gg].rearrange('p (k r s c) -> p k r s c', k=KA, r=2, s=2), [], ['hft'], 'bhft')
            for sl in range(2):
                c = 0
                while c < CG:
                    n = min(3, CG - c)
                    bk = self.nb()
                    for u3 in range(n):
                        self.mm(self.ps[:, bk, u3 * 130:(u3 + 1) * 130], xa[sl][:, c + u3, :], f1m[0:64, :], True, True,
                                [('xa', sl), 'f1m'], self.pr(bk))
                    self.cp(self.alt(), Ysb[:, :, :, sl, c:c + n], self.ps[:, bk, 0:n * 130].rearrange('p (c r k) -> p r k c', c=n, r=2),
                            self.pr(bk), [('Ysb', sl, c)])
                    c += n
            yr = [('Ysb', sl, c) for sl in range(2) for c in range(0, CG, 3)]
            for ka0 in range(0, KA, 4):
                nk = min(4, KA - ka0)
                bz = self.nb(2)
                for kq in range(nk):
                    ka = ka0 + kq
                    if ka % 5 == 0:
                        gs = self.g_stream(Gt, None, ka, gi_)
                        gi_ += 1
                    gq_ = ka % 5
                    zr = self.ps[:, bz + kq // 2, (kq % 2) * 256:(kq % 2) * 256 + 128]
                    zi = self.ps[:, bz + kq // 2, (kq % 2) * 256 + 128:(kq % 2) * 256 + 256]
                    rr = yr + [('Gt', gs)]
                    yre = Ysb[:, 0, ka, :, :].rearrange('p s c -> p (s c)')
                    yim = Ysb[:, 1, ka, :, :].rearrange('p s c -> p (s c)')
                    w_ = self.pr(bz + kq // 2)
                    self.mm(zr, Gt[gs][:, gq_, 0, :], yre, True, False, rr, w_)
                    self.mm(zr, Gt[gs][:, gq_, 2, :], yim, False, True, rr, w_)
                    self.mm(zi, Gt[gs][:, gq_, 0, :], yim, True, False, rr, w_)
                    self.mm(zi, Gt[gs][:, gq_, 1, :], yre, False, True, rr, w_)
                zps = self.ps[:, bz:bz + 2, :].rearrange('p b (k r n) -> p (b k) r n', k=2, r=2)[:, 0:nk]
                hf_ = hft[:, ka0:ka0 + nk].rearrange('p k r s c -> p k r (s c)')
                pz = self.pr(bz, 2)
                self.tt('dve', A[:, 0:nk], zps, hf_, ALU.mult, pz + ['hft'], ['A'])
                self.tt('dve', B1[:, 0:nk], zps[:, :, 0, :], hf_[:, :, 1, :], ALU.mult, pz + ['hft'], ['B1'])
                self.tt('dve', B2[:, 0:nk], zps[:, :, 1, :], hf_[:, :, 0, :], ALU.mult, pz + ['hft'], ['B2'])
                self.tt('pool', Dr[:, 0:nk], A[:, 0:nk, 0, :], A[:, 0:nk, 1, :], ALU.subtract, ['A'], ['Dr'])
                self.tt('pool', Di[:, 0:nk], B1[:, 0:nk], B2[:, 0:nk], ALU.add, ['B1', 'B2'], ['Di'])
                for ri, Dx in ((0, Dr), (1, Di)):
                    self.tt('pool' if ri == 0 else 'dve', Wt[:, ri, :, ka0:ka0 + nk], Dx[:, 0:nk, 0:CG].rearrange('p k c -> p c k'),
                            Dx[:, 0:nk, CG:2 * CG].rearrange('p k c -> p c k'), ALU.add, ['Dr' if ri == 0 else 'Di'], [('Wt', ka0, ri)])
            wr = [('Wt', ka0, ri) for ka0 in range(0, KA, 4) for ri in range(2)]
            for c in range(0, CG, 2):
                bk = self.nb()
                for u2 in range(2):
                    o_ = self.ps[0:KA, bk, u2 * 256:(u2 + 1) * 256]
                    self.mm(o_, Wt[:, 0, c + u2, :], e12[:, 0, :], True, False, wr + ['e12'], self.pr(bk))
                    self.mm(o_, Wt[:, 1, c + u2, :], e12[:, 1, :], False, True, wr + ['e12'], self.pr(bk))
                self.cp(self.alt(), U[:, :, :, c:c + 2], self.ps[0:KA, bk, :].rearrange('p (c r b) -> p r b c', c=2, r=2),
                        self.pr(bk), [('U', c)])
            ur = [('U', c) for c in range(0, CG, 2)]
            for b0 in range(0, 128, 8):
                if b0 % 16 == 0:
                    hs = hi_ % 2
                    hi_ += 1
                    self.dma('sp', Ht[hs][:], hall[:, b0:b0 + 16], [], [('Ht', hs)], 'bHt%d' % hs)
                bk = self.nb()
                for q8 in range(8):
                    bp = b0 + q8
                    o_ = self.ps[0:64, bk, q8 * 64:(q8 + 1) * 64]
                    self.mm(o_, Ht[hs][:, bp % 16, 0, :], U[:, 0, bp, :], True, False, ur + [('Ht', hs)], self.pr(bk))
                    self.mm(o_, Ht[hs][:, bp % 16, 1, :], U[:, 1, bp, :], False, True, ur + [('Ht', hs)], self.pr(bk))
                self.cp(self.alt(), Yo[:, :, b0:b0 + 8], self.ps[0:64, bk, :].rearrange('p (b c) -> p c b', b=8), self.pr(bk), [('Yo', b0)])
            self.dma('sp', self.yconv[c0:c0 + CG, :].rearrange('c (a b) -> a c b', a=64), Yo[:],
                     [('Yo', b0) for b0 in range(0, 128, 8)], [('yconv', gg)], 'bYo_st')
        self.end()

    def phase_c1a(self, l):
        self.begin()
        w_out = self.L('w_out')
        wo = self.tile('wo', [128, 8, D], BF16)
        stg = [self.tile('c1a_stg%d' % s, [128, D], F32) for s in range(4)]
        for kc in range(8):
            sc = self.col('g_ao', l, 1, kc) if kc < 4 else self.col('g_ho', l, 1, kc - 4)
            self.load_w(w_out[l, kc * 128:(kc + 1) * 128, :], D, sc, stg, 'c1astg', [(wo[:, kc, :], 0, D, None, ('wo', kc))])
        mix = [self.tile('mix%d' % s, [128, 8, 512], BF16) for s in range(2)]
        xy = [self.tile('xy%d' % s, [128, 2, 4, 512], BF16) for s in range(2)]
        ht = [self.tile('ht%d' % s, [128, 8, 512], F32) for s in range(3)]
        hy = self.tile('hy', [128, 4, 512], F32)
        sqh = self.tile('sqh', [128, 4, 512], BF16)
        rsh = self.tile('rsh', [128, 512], F32)
        sq2 = self.tile('sq2', [128, 8, 512], BF16)
        rs2 = self.tile('rs2', [128, 512], F32)
        n2 = [self.tile('n2_%d' % s, [128, 8, 512], BF16) for s in range(2)]
        ed = self.tile('ed', [128, 8, 2], BF16)
        def loads(i):
            s = i % 2
            cs = slice(i * 512, i * 512 + 512)
            self.dma('sp', mix[s][:, 0:4, :], self.attn_n[:, cs].rearrange('(c p) t -> p c t', p=128), [], [('mixa', s)], 'c1a_ma%d' % s)
            self.dma('sp', xy[s][:, 0], self.x0s[:, cs].rearrange('(c p) t -> p c t', p=128), [], [('xy', s, 0)], 'c1a_x%d' % s)
            self.dma('sp', xy[s][:, 1], self.yconv[:, cs].rearrange('(c p) t -> p c t', p=128), [], [('xy', s, 1)], 'c1a_y%d' % s)
            self.dma('sp', ht[i % 3][:], self.hres[:, cs].rearrange('(c p) t -> p c t', p=128), [], [('ht', i % 3)], 'c1a_h%d' % (i % 3))
        def P1(i):
            s = i % 2
            self.tt('dve', hy[:], xy[s][:, 0], xy[s][:, 1], ALU.mult, [('xy', s, 0), ('xy', s, 1)], ['hy'])
            self.act(sqh[:], hy[:], AF.Square, ['hy'], ['sqh'])
            bk = self.nb()
            for mc in range(4):
                self.mm(self.ps[:, bk, :], self.ones[:, 1, :], sqh[:, mc, :], mc == 0, mc == 3, ['sqh', 'g_ones'], self.pr(bk))
            self.rstd(rsh[:], self.ps[:, bk, :], self.pr(bk), ['rsh'])
            self.tt('pool', mix[s][:, 4:8, :], hy[:], _bc(rsh[:], [128, 4, 512], 1), ALU.mult, ['hy', 'rsh'], [('mixh', s)])

        def P2(i):
            s = i % 2
            cs = slice(i * 512, i * 512 + 512)
            for g2 in range(2):
                bo = self.nb(4)
                for j in range(4):
                    mc = g2 * 4 + j
                    for kc in range(8):
                        self.mm(self.ps[:, bo + j, :], wo[:, kc, mc * 128:(mc + 1) * 128], mix[s][:, kc, :], kc == 0, kc == 7,
                                [('mixa', s), ('mixh', s), ('wo', kc)], self.pr(bo + j))
                h3 = i % 3
                self.tt('dve', ht[h3][:, g2 * 4:(g2 + 1) * 4, :], ht[h3][:, g2 * 4:(g2 + 1) * 4, :], self.ps[:, bo:bo + 4, :], ALU.add,
                        [('ht', h3)] + self.pr(bo, 4), [('ht', h3)])
            self.dma('sp', self.hres[:, cs].rearrange('(c p) t -> p c t', p=128), ht[i % 3][:], [('ht', i % 3)], [('hres', i)], 'c1a_h%d_st' % (i % 3))

        def P3(i):
            s = i % 2
            t0 = i * 512
            h3 = i % 3
            self.act(sq2[:], ht[h3][:], AF.Square, [('ht', h3)], ['sq2'])
            bk = self.nb()
            for kc in range(8):
                self.mm(self.ps[:, bk, :], self.ones[:, 0, :], sq2[:, kc, :], kc == 0, kc == 7, ['sq2', 'g_ones'], self.pr(bk))
            self.rstd(rs2[:], self.ps[:, bk, :], self.pr(bk), ['rs2'])
            for hf in range(2):
                eng = 'dve' if hf == 0 else 'pool'
                self.tt(eng, n2[s][:, hf * 4:(hf + 1) * 4, :], ht[h3][:, hf * 4:(hf + 1) * 4, :], _bc(rs2[:], [128, 4, 512], 1), ALU.mult,
                        [('ht', h3), 'rs2'], [('n2', s, hf)])
            rd = [('n2', s, 0), ('n2', s, 1)]
            self.dma('sp', self.n2s[:, 1 + t0:1 + t0 + 512].rearrange('(c p) t -> p c t', p=128), n2[s][:], rd, [('n2s', i)], 'c1a_n%d_st' % s)
            if i == 0:
                self.dma('sp', self.xn_in[0:1, :].rearrange('o (c p) -> p c o', p=128), n2[s][:, :, 0:1], rd, ['xn0'], 'c1a_e0', slow=True)
            if i == NT - 1:
                self.dma('sp', self.xn_in[1:2, :].rearrange('o (c p) -> p c o', p=128), n2[s][:, :, 511:512], rd, ['xn1'], 'c1a_e1', slow=True)

        loads(0)
        loads(1)
        P1(0)
        for i in range(NT):
            if i + 1 < NT:
                P1(i + 1)
            P2(i)
            if i >= 1:
                P3(i - 1)
            if i + 2 < NT:
                loads(i + 2)
        P3(NT - 1)
        self.allgather(self.xn_in, self.xn_out, ['xn0', 'xn1'], ['xn_out'])
        for k, (row, col) in enumerate([(1, 0), (2, T + 1)]):
            self.dma('sp', ed[:, :, k:k + 1], self.xn_out[row:row + 1, :].rearrange('o (c p) -> p c o', p=128), ['xn_out'], [('ed', k)], 'c1a_ed%d' % k, slow=True)
            self.ts('dve', ed[:, :, k:k + 1], ed[:, :, k:k + 1], self.edge[:, k:k + 1], None, ALU.mult, None, [('ed', k), 'g_edge'], [('ed', k)])
            self.dma('sp', self.n2s[:, col:col + 1].rearrange('(c p) t -> p c t', p=128), ed[:, :, k:k + 1], [('ed', k)], [('n2sh', k)], 'c1a_ed%d_st' % k, slow=True)
        self.end()

    def phase_c1b(self, l):
        self.begin()
        w_up = self.L('w_up')
        wu = self.tile('wu', [128, 8, 2 * DFF], BF16)
        stg = [self.tile('c1b_stg%d' % s, [128, 2816], F32) for s in range(3)]
        for kc in range(8):
            for hh in range(2):
                self.load_w(w_up[l, kc * 128:(kc + 1) * 128, hh * DFF:(hh + 1) * DFF], DFF, self.col('g_ffn', l, 1, kc), stg, 'c1bstg',
                            [(wu[:, kc, hh * DFF:(hh + 1) * DFF], 0, DFF, None, ('wu', kc, hh))])
        n2t = [self.tile('n2t%d' % s, [128, 8, 512], BF16) for s in range(2)]
        at = [self.tile('at%d' % s, [128, 22, 510], BF16) for s in range(2)]
        ntl = (T + 509) // 510
        NQ = 3
        ua = [self.tile('ua%d' % q, [128, 510], F32) for q in range(NQ)]
        ug = [self.tile('ug%d' % q, [128, 510], F32) for q in range(NQ)]
        sg = [self.tile('sg%d' % q, [128, 510], F32) for q in range(NQ)]

        def loads(i):
            s = i % 2
            T0 = 510 * i
            nin = min(510, T - T0) + 2
            self.dma('sp', n2t[s][:, :, 0:nin], self.n2s[:, T0:T0 + nin].rearrange('(c p) t -> p c t', p=128), [], [('n2t', s)], 'c1b_n%d' % s)

        def front(i, j, q):
            s = i % 2
            nout = min(510, T - 510 * i)
            nin = nout + 2
            bk = self.nb(2)
            for hh in range(2):
                for kc in range(8):
                    self.mm(self.ps[:, bk + hh, 0:nin], wu[:, kc, hh * DFF + j * 128:hh * DFF + (j + 1) * 128], n2t[s][:, kc, 0:nin],
                            kc == 0, kc == 7, [('n2t', s), ('wu', kc, hh)], self.pr(bk + hh))
            for hh, ut in ((0, ua[q]), (1, ug[q])):
                wc = lambda k, hh=hh, j=j: self.col('w_ffc', l, 1, k * 44 + hh * 22 + j)
                pb = self.ps[:, bk + hh, :]
                rn = ('u', hh, q)
                self.act(ut[:, 0:nout], pb[:, 1:1 + nout], AF.Identity, self.pr(bk + hh) + ['g_cols'], [rn],
                         scale=wc(1), bias=self.col('b_ffc', l, 1, hh * 22 + j))
                self.stt('dve', ut[:, 0:nout], pb[:, 0:nout], wc(0), ut[:, 0:nout], ALU.mult, ALU.add, self.pr(bk + hh) + [rn, 'g_cols'], [rn])
                self.stt('dve', ut[:, 0:nout], pb[:, 2:2 + nout], wc(2), ut[:, 0:nout], ALU.mult, ALU.add, self.pr(bk + hh) + [rn, 'g_cols'], [rn])

        def back(i, j, q):
            s = i % 2
            nout = min(510, T - 510 * i)
            self.act(sg[q][:, 0:nout], ug[q][:, 0:nout], AF.Silu, [('u', 1, q)], [('sg', q)])
            self.tt('pool', at[s][:, j, 0:nout], sg[q][:, 0:nout], ua[q][:, 0:nout], ALU.mult, [('sg', q), ('u', 0, q)], [('at', s, j)])
            if j == 21:
                T0 = 510 * i
                self.dma('sp', self.acts[:, T0:T0 + nout].rearrange('(c p) t -> p c t', p=128), at[s][:, :, 0:nout],
                         [('at', s, jx) for jx in range(22)], [('acts', i)], 'c1b_a%d_st' % s)

        loads(0)
        seq = [(i, j) for i in range(ntl) for j in range(22)]
        for n_, (i, j) in enumerate(seq):
            if j == 0 and i + 1 < ntl:
                loads(i + 1)
            front(i, j, n_ % NQ)
            if n_ >= 1:
                pi, pj = seq[n_ - 1]
                back(pi, pj, (n_ - 1) % NQ)
        pi, pj = seq[-1]
        back(pi, pj, (len(seq) - 1) % NQ)
        self.end()

    def phase_c2(self, l, last):
        self.begin()
        wd = self.tile('wd', [128, 22, D], BF16)
        wg = self.tile('wg', [128, 8, D], BF16)
        wp = self.tile('wp', [128, 2, D], BF16)
        stg = [self.tile('c2_stg%d' % s, [128, D], F32) for s in range(4)]
        for kc in range(22):
            self.load_w(self.L('w_down')[l, kc * 128:(kc + 1) * 128, :], D, None, stg, 'c2stg', [(wd[:, kc, :], 0, D, None, ('wd', kc))])
        for kc in range(8):
            self.load_w(self.L('w_ple_gate')[l, kc * 128:(kc + 1) * 128, :], D, None, stg, 'c2stg', [(wg[:, kc, :], 0, D, None, ('wg', kc))])
        for kc in range(2):
            self.load_w(self.L('w_ple_proj')[l, kc * 128:(kc + 1) * 128, :], D, None, stg, 'c2stg', [(wp[:, kc, :], 0, D, None, ('wp', kc))])
        at = [self.tile('at%d' % s, [128, 22, 512], BF16) for s in range(2)]
        ht = [self.tile('ht%d' % s, [128, 8, 512], F32) for s in range(2)]
        pt = [self.tile('pt%d' % s, [128, 4, DPLE], F32) for s in range(2)]
        pT = self.tile('pT', [128, 2, 512], BF16)
        hb = self.tile('hb', [128, 8, 512], BF16)
        sgm = self.tile('sgm', [128, 4, 512], F32)
        yo = self.tile('yo', [128, 4, D], F32) if last else None
        p_d = self.L('p')
        def loads(i):
            s = i % 2
            cs = slice(i * 512, i * 512 + 512)
            self.dma('sp', at[s][:], self.acts[:, cs].rearrange('(c p) t -> p c t', p=128), [], [('at', s)], 'c2_a%d' % s)
            self.dma('sp', ht[s][:], self.hres[:, cs].rearrange('(c p) t -> p c t', p=128), [], [('ht', s, 0), ('ht', s, 1)], 'c2_h%d' % s)
            self.dma('sp', pt[s][:], p_d[l, cs, :].rearrange('(b p) f -> p b f', p=128), [], [('pt', s)], 'c2_p%d' % s)
        loads(0)
        for i in range(NT):
            s = i % 2
            t0 = i * 512
            cs = slice(t0, t0 + 512)
            if i + 1 < NT:
                loads(i + 1)
            for pc in range(2):
                bk = self.nb()
                for blk in range(4):
                    self.tp(self.ps[:, bk, blk * 128:(blk + 1) * 128], pt[s][:, blk, pc * 128:(pc + 1) * 128], self.ident[:],
                            [('pt', s), 'g_ident'], self.pr(bk))
                self.cp(self.alt(), pT[:, pc, :], self.ps[:, bk, :], self.pr(bk), [('pT', pc)])
            for g2 in range(2):
                bo = self.nb(4)
                for j in range(4):
                    mc = g2 * 4 + j
                    for kc in range(22):
                        self.mm(self.ps[:, bo + j, :], wd[:, kc, mc * 128:(mc + 1) * 128], at[s][:, kc, :], kc == 0, kc == 21,
                                [('at', s), ('wd', kc)], self.pr(bo + j))
                hs_ = ht[s][:, g2 * 4:(g2 + 1) * 4, :]
                self.tt('dve', hs_, hs_, self.ps[:, bo:bo + 4, :], ALU.add, [('ht', s, g2)] + self.pr(bo, 4), [('ht', s, g2)])
                self.cp('act', hb[:, g2 * 4:(g2 + 1) * 4, :], hs_, [('ht', s, g2)], [('hb', g2)])
            for g2 in range(2):
                bo = self.nb(4)
                for j in range(4):
                    mc = g2 * 4 + j
                    for kc in range(8):
                        self.mm(self.ps[:, bo + j, :], wg[:, kc, mc * 128:(mc + 1) * 128], hb[:, kc, :], kc == 0, kc == 7,
                                [('hb', 0), ('hb', 1), ('wg', kc)], self.pr(bo + j))
                self.act(sgm[:], self.ps[:, bo:bo + 4, :], AF.Sigmoid, self.pr(bo, 4), ['sgm'])
                bp = self.nb(4)
                for j in range(4):
                    mc = g2 * 4 + j
                    for kc in range(2):
                        self.mm(self.ps[:, bp + j, :], wp[:, kc, mc * 128:(mc + 1) * 128], pT[:, kc, :], kc == 0, kc == 1,
                                [('pT', 0), ('pT', 1), ('wp', kc)], self.pr(bp + j))
                self.tt('dve', sgm[:], sgm[:], self.ps[:, bp:bp + 4, :], ALU.mult, ['sgm'] + self.pr(bp, 4), ['sgm'])
                hs_ = ht[s][:, g2 * 4:(g2 + 1) * 4, :]
                self.tt('pool', hs_, hs_, sgm[:], ALU.add, [('ht', s, g2), 'sgm'], [('ht', s, g2)])
            hr_ = [('ht', s, 0), ('ht', s, 1)]
            if not last:
                self.dma('sp', self.hres[:, cs].rearrange('(c p) t -> p c t', p=128), ht[s][:], hr_, [('hres', i)], 'c2_h%d_st' % s)
            else:
                for blk in range(4):
                    for g2 in range(2):
                        bk = self.nb()
                        for j in range(4):
                            mc = g2 * 4 + j
                            self.tp(self.ps[:, bk, j * 128:(j + 1) * 128], ht[s][:, mc, blk * 128:(blk + 1) * 128], self.ident[:],
                                    hr_ + ['g_ident'], self.pr(bk))
                        self.cp(self.alt(), yo[:, blk, g2 * 512:(g2 + 1) * 512], self.ps[:, bk, :], self.pr(bk), [('yo', blk, g2)])
                self.dma('sp', self.y[cs, :].rearrange('(b p) f -> p b f', p=128), yo[:],
                         [('yo', blk, g2) for blk in range(4) for g2 in range(2)], [('y', i)], 'c2_y_st')
        self.end()

    def build(self):
        self.phase_p0()
        for l in range(self.depth):
            last = (l == self.depth - 1)
            for name in ['a0', 'a', 'a2f1', 'f2', 'b', 'c1a', 'c1b', 'c2']:
                fn = getattr(self, 'phase_' + name, None)
                if fn is None:
                    return
                if name == 'c2':
                    fn(l, last)
                else:
                    fn(l)
                if self.stop == (name, l):
                    return


def _cols_table(b, W):
    L = DEPTH
    tab = np.zeros((128, b.ncol), np.float32)

    def put(name, arr):
        o, n = b.colspec[name]
        assert arr.shape == (128, n), (name, arr.shape, n)
        tab[:, o:o + n] = arr

    def chunks(v, nch):
        return v.reshape(L, nch, 128).transpose(2, 0, 1).reshape(128, L * nch)

    put('g_mix', chunks(W['rms_mix'], 8))
    put('g_ffn', chunks(W['rms_ffn'], 8))
    put('gq', np.tile(W['q_norm'].T, (2, 1)))
    put('gk', np.tile(W['k_norm'].T, (2, 1)))
    sk = W['sink'].reshape(L, 2, 4)[:, :, [0, 2, 1, 3]].reshape(1, L * 8)
    put('sink', np.tile(sk, (128, 1)))
    put('w_short', W['w_short'].reshape(L, 3, 12, 128).transpose(3, 0, 1, 2).reshape(128, L * 36))
    put('b_short', chunks(W['b_short'], 12))
    put('g_ao', chunks(W['norm_attn_out'], 4))
    put('g_ho', chunks(W['norm_hyena_out'], 4))
    put('w_ffc', W['w_ffconv'].reshape(L, 3, 44, 128).transpose(3, 0, 1, 2).reshape(128, L * 132))
    put('b_ffc', chunks(W['b_ffconv'], 44))
    for nm, key in [('fb1', 'filt_b1'), ('ffr1', 'filt_freq1'), ('fb2', 'filt_b2'), ('ffr2', 'filt_freq2')]:
        put(nm, np.tile(W[key].T, (2, 1)))
    return tab


_BUILD_CACHE = {}


def _get_builder(depth, debug, stop):
    key = (depth, debug, stop)
    if key not in _BUILD_CACHE:
        b = Builder(depth, debug, stop)
        b.declare()
        b.setup_globals()
        b.build()
        es = contextlib.ExitStack()
        b.P.emit(es)
        b._es = es
        _BUILD_CACHE[key] = b
    return _BUILD_CACHE[key]


def _run(inputs, depth=DEPTH, debug=False, stop=None):
    W = {k: np.asarray(v, dtype=np.float32) for k, v in inputs.items()}
    b = _get_builder(depth, debug, stop)
    sh = _shared_consts()
    cols = _cols_table(b, W)
    xp = W['x_prompt'][0]
    xs = W['x_sample']
    pp = W['p_prompt'][:, 0]
    psm = W['p_sample']
    in_maps = []
    for rank in range(N_CORES):
        kind, idx = _unit_of_rank(rank)
        rc = _rank_consts(rank)
        if kind == 'p':
            x = xp[idx * T:(idx + 1) * T]
            p = pp[:, idx * T:(idx + 1) * T]
        else:
            x = xs[idx]
            p = psm[:, idx]
        m = {
            'x': np.ascontiguousarray(x), 'p': np.ascontiguousarray(p),
            'w_in': W['w_in'], 'w_out': W['w_out'], 'w_up': W['w_up'], 'w_down': W['w_down'],
            'w_ple_gate': W['w_ple_gate'], 'w_ple_proj': W['w_ple_proj'],
            'filt_w1': W['filt_w1'], 'filt_w2': W['filt_w2'], 'filt_w3': W['filt_w3'],
            'cols': cols, 'hbias': W['hyena_bias'].reshape(1, -1),
            'ident_f': sh['ident_f'], 'ones_b': sh['ones_b'], 'prot_b': sh['prot_b'],
            'masks': rc['masks'], 'edge': rc['edge'], 'cstab': rc['cstab'],
            'f1m': sh['f1m'], 'gall': sh['gall'], 'e12': sh['e12'], 'hall': sh['hall'], 'negd': sh['negd'],
            'zf': rc['zf'], 'mm': rc['mm'], 'tdec': rc['tdec'], 'e0': rc['e0'],
        }
        in_maps.append({k: v for k, v in m.items() if k in b.inputs})
    res = run_bass_kernel_spmd(b.nc, in_maps, core_ids=list(range(N_CORES)))
    return b, res


def kernel(**inputs):
    b, res = _run(inputs)
    ys = [np.asarray(res.results[r]['y'], dtype=np.float32) for r in range(6)]
    y_prompt = np.concatenate([ys[0], ys[1]], axis=0)[None]
    y_sample = np.stack(ys[2:6], axis=0)
    return (y_prompt, y_sample)
```
